# Optimizing a Trainium2 kernel written in Bass

```python
import jax, jax.numpy as jnp
from jax import lax
import numpy as np

D_MODEL = 1024
BATCH = 16
SEQ = 256
DEPTH = 2
DEC_BATCH = 2
DEC_SEQ = 2048
PAST_LEN = 256

GRID_W = 64
EPS = 1e-6
N_AB = (DEPTH + 1) // 2
N_C = DEPTH // 2
MLA_HEADS = 8
Q_RANK = 256
KV_RANK = 256
NOPE_DIM = 64
ROPE_DIM = 32
V_DIM = 64
ROPE_THETA = 10000.0
Q_BLOCK = 128
SSD_HEADS = 8
SSD_GROUPS = 2
SSD_HPG = SSD_HEADS // SSD_GROUPS
SSD_HEAD_DIM = 64
SSD_STATE = 128
D_SSD = SSD_HEADS * SSD_HEAD_DIM
CONV_W = 5
CONV_CH = D_SSD + 2 * SSD_GROUPS * SSD_STATE
CHUNK = 128
IN_SPLITS = (Q_RANK, KV_RANK, ROPE_DIM, D_SSD, CONV_CH, SSD_HEADS, SSD_HEADS)
IN_AB = sum(IN_SPLITS)
OUT_AB = MLA_HEADS * V_DIM + D_SSD
POOL_WINDOWS = (2, 4, 8, 16)
POOL_GC = D_MODEL // len(POOL_WINDOWS)
D_FF = ((8 * D_MODEL + 3 * 256 - 1) // (3 * 256)) * 256

kernel_name = "hybrid_mla_ssd_pool_diffusion_step"


def rmsnorm(x, g):
    xf = x.astype(jnp.float32)
    y = xf * lax.rsqrt(jnp.mean(xf * xf, axis=-1, keepdims=True) + EPS)
    return (y * g.astype(jnp.float32)).astype(x.dtype)


def split_last(x, sizes):
    idx = np.cumsum(np.array(sizes))[:-1].tolist()
    return jnp.split(x, idx, axis=-1)


def modulation(cvec, w_mod, b_mod):
    m = jnp.expand_dims(jax.nn.silu(cvec) @ w_mod + b_mod, -2)
    return split_last(m, (D_MODEL,) * 6)


def modulate(x, g, shift, scale):
    return rmsnorm(x, g) * (1 + scale) + shift


def axial_rope_tables(n_tokens):
    rows = n_tokens // GRID_W
    row = jnp.repeat(jnp.arange(rows, dtype=jnp.float32), GRID_W)
    col = jnp.tile(jnp.arange(GRID_W, dtype=jnp.float32), rows)
    half = ROPE_DIM // 2
    inv_freq = jnp.power(ROPE_THETA, -jnp.arange(0, half, 2, dtype=jnp.float32) / half)
    ang = jnp.concatenate([row[:, None] * inv_freq, col[:, None] * inv_freq], axis=-1)
    return jnp.cos(ang), jnp.sin(ang)


def apply_rope(x, cos, sin):
    xf = x.astype(jnp.float32)
    x1, x2 = xf[..., : ROPE_DIM // 2], xf[..., ROPE_DIM // 2:]
    return jnp.concatenate([x1 * cos - x2 * sin, x1 * sin + x2 * cos], axis=-1).astype(x.dtype)


def ab_project(h, w_in, q_norm, w_uq, kv_norm):
    b, L, _ = h.shape
    cq, ckv, k_pe, z, xbc, dt_f, dt_b = split_last(h @ w_in, IN_SPLITS)
    q = (rmsnorm(cq, q_norm) @ w_uq).reshape(b, L, MLA_HEADS, NOPE_DIM + ROPE_DIM)
    return q[..., :NOPE_DIM], q[..., NOPE_DIM:], rmsnorm(ckv, kv_norm), k_pe, z, xbc, dt_f, dt_b


def mla_expand_kv(ckv_n, w_ukv):
    b, L, _ = ckv_n.shape
    kv = (ckv_n @ w_ukv).reshape(b, L, MLA_HEADS, NOPE_DIM + V_DIM)
    return kv[..., :NOPE_DIM], kv[..., NOPE_DIM:]


def mla_attend(q_nope, q_pe, k_nope, k_pe, v):
    b, lq, h, _ = q_nope.shape
    nb = lq // Q_BLOCK
    scale = (NOPE_DIM + ROPE_DIM) ** -0.5

    def block(qs):
        qn, qp = qs
        s = jnp.einsum('bqhd,bkhd->bhqk', qn, k_nope) + jnp.einsum('bqhr,bkr->bhqk', qp, k_pe)
        p = jax.nn.softmax(s.astype(jnp.float32) * scale, axis=-1)
        return jnp.einsum('bhqk,bkhd->bqhd', p.astype(v.dtype), v)

    qn_b = q_nope.reshape(b, nb, Q_BLOCK, h, NOPE_DIM).transpose(1, 0, 2, 3, 4)
    qp_b = q_pe.reshape(b, nb, Q_BLOCK, h, ROPE_DIM).transpose(1, 0, 2, 3, 4)
    out = lax.map(block, (qn_b, qp_b))
    return out.transpose(1, 0, 2, 3, 4).reshape(b, lq, h * V_DIM)


def dwconv_centred(x, w, bias):
    y = lax.conv_general_dilated(x, w[:, None, :], window_strides=(1,),
                                 padding=[(CONV_W // 2, CONV_W // 2)],
                                 dimension_numbers=('NWC', 'WIO', 'NWC'),
                                 feature_group_count=x.shape[-1])
    return y + bias


def ssd_chunked(x, dt, A, Bm, Cm, h0):
    b, L, g, hg, p = x.shape
    n = Bm.shape[-1]
    nc = L // CHUNK
    f32 = jnp.float32
    dtf = dt.astype(f32)
    xdt = (x.astype(f32) * dtf[..., None]).reshape(b, nc, CHUNK, g, hg, p)
    a = (dtf * A.astype(f32)).reshape(b, nc, CHUNK, g, hg)
    Bc = Bm.astype(f32).reshape(b, nc, CHUNK, g, n)
    Cc = Cm.astype(f32).reshape(b, nc, CHUNK, g, n)
    acum = jnp.cumsum(a, axis=2)
    seg = acum[:, :, :, None] - acum[:, :, None, :]
    lower = jnp.tril(jnp.ones((CHUNK, CHUNK), dtype=bool))[:, :, None, None]
    decay = jnp.where(lower, jnp.exp(jnp.where(lower, seg, 0.0)), 0.0)
    cb = jnp.einsum('bcign,bcjgn->bcijg', Cc, Bc)
    y_diag = jnp.einsum('bcijg,bcijgh,bcjghp->bcighp', cb, decay, xdt)
    decay_end = jnp.exp(acum[:, :, -1:] - acum)
    chunk_states = jnp.einsum('bcjgn,bcjgh,bcjghp->bcghpn', Bc, decay_end, xdt)
    chunk_decay = jnp.exp(acum[:, :, -1])

    def step(h, inp):
        s, d = inp
        return d[..., None, None] * h + s, h

    h_final, h_prev = lax.scan(step, h0.astype(f32),
                               (jnp.moveaxis(chunk_states, 1, 0), jnp.moveaxis(chunk_decay, 1, 0)))
    h_prev = jnp.moveaxis(h_prev, 0, 1)
    y_off = jnp.einsum('bcign,bcghpn,bcigh->bcighp', Cc, h_prev, jnp.exp(acum))
    y = (y_diag + y_off).reshape(b, L, g, hg, p)
    return y.astype(x.dtype), h_final.astype(x.dtype)


def ssd_mixer(z, xbc, dt_f, dt_b, conv_w, conv_b, dt_bias_f, dt_bias_b, a_log_f, a_log_b,
              d_skip, norm_g, h0_f, h0_b):
    b, L, _ = z.shape
    xbc = jax.nn.silu(dwconv_centred(xbc, conv_w, conv_b))
    xs, Bm, Cm = split_last(xbc, (D_SSD, SSD_GROUPS * SSD_STATE, SSD_GROUPS * SSD_STATE))
    xs = xs.reshape(b, L, SSD_GROUPS, SSD_HPG, SSD_HEAD_DIM)
    Bm = Bm.reshape(b, L, SSD_GROUPS, SSD_STATE)
    Cm = Cm.reshape(b, L, SSD_GROUPS, SSD_STATE)

    def run_dir(dt_raw, dt_bias, a_log, h0, reverse):
        dt = jax.nn.softplus((dt_raw + dt_bias).astype(jnp.float32)).reshape(b, L, SSD_GROUPS, SSD_HPG)
        A = -jnp.exp(a_log.astype(jnp.float32)).reshape(SSD_GROUPS, SSD_HPG)
        xd, dd, Bd, Cd = xs, dt, Bm, Cm
        if reverse:
            xd, dd, Bd, Cd = jnp.flip(xd, 1), jnp.flip(dd, 1), jnp.flip(Bd, 1), jnp.flip(Cd, 1)
        y, hN = ssd_chunked(xd, dd, A, Bd, Cd,
                            h0.reshape(b, SSD_GROUPS, SSD_HPG, SSD_HEAD_DIM, SSD_STATE))
        if reverse:
            y = jnp.flip(y, 1)
        return y, hN.reshape(b, SSD_HEADS, SSD_HEAD_DIM, SSD_STATE)

    y_f, h_f = run_dir(dt_f, dt_bias_f, a_log_f, h0_f, False)
    y_b, h_b = run_dir(dt_b, dt_bias_b, a_log_b, h0_b, True)
    y = y_f + y_b + d_skip.reshape(SSD_GROUPS, SSD_HPG)[..., None] * xs
    y = y.reshape(b, L, D_SSD) * jax.nn.silu(z)
    y = rmsnorm(y.reshape(b, L, SSD_GROUPS, D_SSD // SSD_GROUPS),
                norm_g.reshape(SSD_GROUPS, D_SSD // SSD_GROUPS)).reshape(b, L, D_SSD)
    return y, h_f, h_b


def pool_mixer(h, w_pool, pool_scale):
    b, L, d = h.shape
    hf = h.astype(jnp.float32)
    cs = jnp.concatenate([jnp.zeros((b, 1, d), jnp.float32), jnp.cumsum(hf, axis=1)], axis=1)
    t = np.arange(L)
    outs = []
    for gi, w in enumerate(POOL_WINDOWS):
        lo = np.clip(t - w // 2, 0, L)
        hi = np.clip(t + w // 2, 0, L)
        cnt = jnp.asarray((hi - lo).astype(np.float32))[None, :, None]
        csg = cs[..., gi * POOL_GC:(gi + 1) * POOL_GC]
        mean = (jnp.take(csg, jnp.asarray(hi), axis=1) - jnp.take(csg, jnp.asarray(lo), axis=1)) / cnt
        outs.append(mean - hf[..., gi * POOL_GC:(gi + 1) * POOL_GC])
    pooled = jnp.stack(outs, axis=2).astype(h.dtype)
    out = jnp.einsum('blgc,gcd->blgd', pooled, w_pool).reshape(b, L, d)
    return out * pool_scale


def swiglu(h, w_gate, w_up, w_down):
    return (jax.nn.silu(h @ w_gate) * (h @ w_up)) @ w_down


def setup_inputs(seed: int = 0) -> dict:
    key = jax.random.key(seed)
    ks = jax.random.split(key, 40)
    nrm = jax.random.normal
    D = D_MODEL
    dt0 = jnp.exp(jax.random.uniform(ks[14], (N_AB, SSD_HEADS), minval=np.log(1e-3), maxval=np.log(1e-1)))
    dt1 = jnp.exp(jax.random.uniform(ks[15], (N_AB, SSD_HEADS), minval=np.log(1e-3), maxval=np.log(1e-1)))
    return {
        "x_prompt": nrm(ks[0], (BATCH, SEQ, D), jnp.float32),
        "x_sample": nrm(ks[1], (DEC_BATCH, DEC_SEQ, D), jnp.float32),
        "c": nrm(ks[2], (DEC_BATCH, D), jnp.float32),
        "cache_mla_ckv": nrm(ks[3], (DEC_BATCH, N_AB, PAST_LEN, KV_RANK), jnp.float32),
        "cache_mla_krope": nrm(ks[4], (DEC_BATCH, N_AB, PAST_LEN, ROPE_DIM), jnp.float32),
        "state_ssd_fwd": 0.1 * nrm(ks[5], (DEC_BATCH, N_AB, SSD_HEADS, SSD_HEAD_DIM, SSD_STATE), jnp.float32),
        "state_ssd_bwd": 0.1 * nrm(ks[6], (DEC_BATCH, N_AB, SSD_HEADS, SSD_HEAD_DIM, SSD_STATE), jnp.float32),
        "c_ctx": nrm(ks[7], (D,), jnp.float32),
        "w_mod": 0.5 * D ** -0.5 * nrm(ks[8], (DEPTH, D, 6 * D), jnp.float32),
        "b_mod": 0.01 * nrm(ks[9], (DEPTH, 6 * D), jnp.float32),
        "norm_pre_mix": 1.0 + 0.05 * nrm(ks[10], (DEPTH, D), jnp.float32),
        "norm_post_mix": 1.0 + 0.05 * nrm(ks[11], (DEPTH, D), jnp.float32),
        "norm_pre_ffn": 1.0 + 0.05 * nrm(ks[12], (DEPTH, D), jnp.float32),
        "norm_post_ffn": 1.0 + 0.05 * nrm(ks[13], (DEPTH, D), jnp.float32),
        "w_in_ab": D ** -0.5 * nrm(ks[16], (N_AB, D, IN_AB), jnp.float32),
        "q_norm": 1.0 + 0.05 * nrm(ks[17], (N_AB, Q_RANK), jnp.float32),
        "w_uq": Q_RANK ** -0.5 * nrm(ks[18], (N_AB, Q_RANK, MLA_HEADS * (NOPE_DIM + ROPE_DIM)), jnp.float32),
        "kv_norm": 1.0 + 0.05 * nrm(ks[19], (N_AB, KV_RANK), jnp.float32),
        "w_ukv": KV_RANK ** -0.5 * nrm(ks[20], (N_AB, KV_RANK, MLA_HEADS * (NOPE_DIM + V_DIM)), jnp.float32),
        "ssd_conv_w": CONV_W ** -0.5 * nrm(ks[21], (N_AB, CONV_W, CONV_CH), jnp.float32),
        "ssd_conv_b": 0.01 * nrm(ks[22], (N_AB, CONV_CH), jnp.float32),
        "ssd_dt_bias_fwd": jnp.log(jnp.expm1(dt0)),
        "ssd_dt_bias_bwd": jnp.log(jnp.expm1(dt1)),
        "ssd_a_log_fwd": jnp.log(jax.random.uniform(ks[23], (N_AB, SSD_HEADS), minval=1.0, maxval=16.0)),
        "ssd_a_log_bwd": jnp.log(jax.random.uniform(ks[24], (N_AB, SSD_HEADS), minval=1.0, maxval=16.0)),
        "ssd_d": 1.0 + 0.1 * nrm(ks[25], (N_AB, SSD_HEADS), jnp.float32),
        "ssd_norm": 1.0 + 0.05 * nrm(ks[26], (N_AB, D_SSD), jnp.float32),
        "w_out_ab": OUT_AB ** -0.5 * nrm(ks[27], (N_AB, OUT_AB, D), jnp.float32),
        "pool_w": POOL_GC ** -0.5 * nrm(ks[28], (N_C, len(POOL_WINDOWS), POOL_GC, POOL_GC), jnp.float32),
        "pool_scale": 1.0 + 0.05 * nrm(ks[29], (N_C, D), jnp.float32),
        "ffn_w_gate": D ** -0.5 * nrm(ks[30], (DEPTH, D, D_FF), jnp.float32),
        "ffn_w_up": D ** -0.5 * nrm(ks[31], (DEPTH, D, D_FF), jnp.float32),
        "ffn_w_down": D_FF ** -0.5 * nrm(ks[32], (DEPTH, D_FF, D), jnp.float32),
    }


def reference(x_prompt, x_sample, c, cache_mla_ckv, cache_mla_krope, state_ssd_fwd, state_ssd_bwd, c_ctx,
              w_mod, b_mod, norm_pre_mix, norm_post_mix, norm_pre_ffn, norm_post_ffn,
              w_in_ab, q_norm, w_uq, kv_norm, w_ukv, ssd_conv_w, ssd_conv_b,
              ssd_dt_bias_fwd, ssd_dt_bias_bwd, ssd_a_log_fwd, ssd_a_log_bwd, ssd_d, ssd_norm, w_out_ab,
              pool_w, pool_scale, ffn_w_gate, ffn_w_up, ffn_w_down):
    xp, xs = x_prompt, x_sample
    cos, sin = axial_rope_tables(x_sample.shape[1])
    new_ckv, new_kpe, new_hf, new_hb = [], [], [], []
    for l in range(DEPTH):
        sh_p, sc_p, g_p, shf_p, scf_p, gf_p = modulation(c_ctx, w_mod[l], b_mod[l])
        sh_s, sc_s, g_s, shf_s, scf_s, gf_s = modulation(c, w_mod[l], b_mod[l])
        hp = modulate(xp, norm_pre_mix[l], sh_p, sc_p)
        hs = modulate(xs, norm_pre_mix[l], sh_s, sc_s)
        if l % 2 == 0:
            i = l // 2
            ssd_args = (ssd_conv_w[i], ssd_conv_b[i], ssd_dt_bias_fwd[i], ssd_dt_bias_bwd[i],
                        ssd_a_log_fwd[i], ssd_a_log_bwd[i], ssd_d[i], ssd_norm[i])
            qn, qpe, ckv_n, kpe, z, xbc, dtf, dtb = ab_project(hp, w_in_ab[i], q_norm[i], w_uq[i], kv_norm[i])
            kn, v = mla_expand_kv(ckv_n, w_ukv[i])
            att_p = mla_attend(qn, qpe, kn, kpe, v)
            h_zero = jnp.zeros((xp.shape[0], SSD_HEADS, SSD_HEAD_DIM, SSD_STATE), xp.dtype)
            ssd_p, hf, hb = ssd_mixer(z, xbc, dtf, dtb, *ssd_args, h_zero, h_zero)
            mix_p = jnp.concatenate([att_p, ssd_p], axis=-1) @ w_out_ab[i]
            new_ckv.append(ckv_n)
            new_kpe.append(kpe)
            new_hf.append(hf)
            new_hb.append(hb)
            qn, qpe, ckv_s, kpe_s, z, xbc, dtf, dtb = ab_project(hs, w_in_ab[i], q_norm[i], w_uq[i], kv_norm[i])
            qpe = apply_rope(qpe, cos[:, None, :], sin[:, None, :])
            kpe_s = apply_rope(kpe_s, cos, sin)
            ckv_all = jnp.concatenate([cache_mla_ckv[:, i].astype(ckv_s.dtype), ckv_s], axis=1)
            kpe_all = jnp.concatenate([cache_mla_krope[:, i].astype(kpe_s.dtype), kpe_s], axis=1)
            kn, v = mla_expand_kv(ckv_all, w_ukv[i])
            att_s = mla_attend(qn, qpe, kn, kpe_all, v)
            ssd_s, _, _ = ssd_mixer(z, xbc, dtf, dtb, *ssd_args, state_ssd_fwd[:, i], state_ssd_bwd[:, i])
            mix_s = jnp.concatenate([att_s, ssd_s], axis=-1) @ w_out_ab[i]
        else:
            j = l // 2
            mix_p = pool_mixer(hp, pool_w[j], pool_scale[j])
            mix_s = pool_mixer(hs, pool_w[j], pool_scale[j])
        xp = xp + g_p * rmsnorm(mix_p, norm_post_mix[l])
        xs = xs + g_s * rmsnorm(mix_s, norm_post_mix[l])
        fp = swiglu(modulate(xp, norm_pre_ffn[l], shf_p, scf_p), ffn_w_gate[l], ffn_w_up[l], ffn_w_down[l])
        fs = swiglu(modulate(xs, norm_pre_ffn[l], shf_s, scf_s), ffn_w_gate[l], ffn_w_up[l], ffn_w_down[l])
        xp = xp + gf_p * rmsnorm(fp, norm_post_ffn[l])
        xs = xs + gf_s * rmsnorm(fs, norm_post_ffn[l])
    new_mla_ckv = jnp.stack(new_ckv, axis=1)
    new_mla_krope = jnp.stack(new_kpe, axis=1)
    new_ssd_fwd = jnp.stack(new_hf, axis=1)
    new_ssd_bwd = jnp.stack(new_hb, axis=1)
    return (xp, xs, new_mla_ckv, new_mla_krope, new_ssd_fwd, new_ssd_bwd)
```

```python
import numpy as np
import concourse.bass as bass
import concourse.mybir as mybir

F32 = mybir.dt.float32
BF16 = mybir.dt.bfloat16
AF = mybir.ActivationFunctionType
ALU = mybir.AluOpType
AX = mybir.AxisListType

ENGS = ("pe", "act", "dve", "pool", "sp")


class Prog:
    def __init__(self, nc, n_dma_sems=24):
        self.nc = nc
        self.items = {e: [] for e in ENGS}
        self.cnt = {e: 0 for e in ENGS}
        self.waited = {e: {} for e in ENGS}
        self.lastw = {}
        self.readers = {}
        self.n_dma_sems = n_dma_sems
        self.dma_cnt = [0] * (n_dma_sems + 4)
        self.dma_i = 0
        self.dma_q = 0
        self.dma_c = 0
        self.nops = {e: 0 for e in ENGS}

    def _deps(self, eng, reads, writes):
        deps = []
        for r in reads:
            s = self.lastw.get(r)
            if s is not None:
                deps.append((s, "raw"))
            if isinstance(r, tuple) and r[0] in ("psF", "psB"):
                for s in self.readers.get(r, ()):
                    if s[2] != eng:
                        deps.append((s, "rar"))
        for w in writes:
            s = self.lastw.get(w)
            if s is not None:
                deps.append((s, "waw"))
            for s in self.readers.get(w, ()):
                deps.append((s, "war"))
        out = {}
        for (sem, val, peng), kind in deps:
            if peng == eng:
                if eng == "pe":
                    continue
            if out.get(sem, -1) < val:
                out[sem] = val
        return out

    def _emit_waits(self, eng, deps):
        for sem, val in deps.items():
            if self.waited[eng].get(sem, -1) >= val:
                continue
            self.waited[eng][sem] = val
            self.items[eng].append(("wait", sem, val))

    def _record(self, sig, reads, writes):
        for r in reads:
            self.readers.setdefault(r, []).append(sig)
        for w in writes:
            self.lastw[w] = sig
            self.readers[w] = []

    def op(self, eng, fn, reads=(), writes=(), signal=True):
        deps = self._deps(eng, reads, writes)
        self._emit_waits(eng, deps)
        self.nops[eng] += 1
        if signal:
            self.cnt[eng] += 1
            sig = ("E_" + eng, self.cnt[eng], eng)
            self.items[eng].append(("op", fn, True))
            self._record(sig, reads, writes)
        else:
            self.items[eng].append(("op", fn, False))
            sig = ("E_" + eng, self.cnt[eng] + 1, eng)
            self._record(sig, reads, writes)
        return sig

    def dma(self, eng, fn, reads=(), writes=(), inc=16):
        half = self.n_dma_sems // 2
        if inc == 1:
            i = self.n_dma_sems + (self.dma_c % 4); self.dma_c += 1
        elif eng == "pool":
            i = half + (self.dma_q % half); self.dma_q += 1
        else:
            i = self.dma_i % half; self.dma_i += 1
        sem = "D_%d" % i
        deps = self._deps(eng, reads, writes)
        if self.dma_cnt[i] > 0:
            if deps.get(sem, -1) < self.dma_cnt[i]:
                deps[sem] = self.dma_cnt[i]
        self._emit_waits(eng, deps)
        self.dma_cnt[i] += inc
        sig = (sem, self.dma_cnt[i], None)
        self.items[eng].append(("dma", fn, sem, inc))
        self.nops[eng] += 1
        self._record(sig, reads, writes)
        return sig

    def barrier(self):
        allsig = {}
        for e in ENGS:
            if self.cnt[e] > 0:
                allsig["E_" + e] = self.cnt[e]
        for i in range(self.n_dma_sems + 4):
            if self.dma_cnt[i] > 0:
                allsig["D_%d" % i] = self.dma_cnt[i]
        for e in ENGS:
            d = dict(allsig)
            self._emit_waits(e, d)

    def final_wait(self, eng="sp"):
        self.barrier()

    def emit(self, extra_ctx=()):
        nc = self.nc
        import contextlib
        with contextlib.ExitStack() as st:
            sems = {}
            for e in ENGS:
                sems["E_" + e] = st.enter_context(nc.semaphore("E_" + e))
            for i in range(self.n_dma_sems + 4):
                sems["D_%d" % i] = st.enter_context(nc.semaphore("D_%d" % i))
            block = st.enter_context(nc.Block())
            items = self.items

            def run(engh, ename):
                for it in items[ename]:
                    if it[0] == "wait":
                        engh.wait_ge(sems[it[1]], it[2])
                    elif it[0] == "op":
                        ins = it[1](engh)
                        if it[2]:
                            ins.then_inc(sems["E_" + ename], 1)
                    else:
                        ins = it[1](engh)
                        if it[3] == 1:
                            ins.then_inc(sems[it[2]])
                        else:
                            ins.then_inc(sems[it[2]], it[3])

            @block.sync
            def _(e):
                run(e, "sp")

            @block.scalar
            def _(e):
                run(e, "act")

            @block.vector
            def _(e):
                run(e, "dve")

            @block.gpsimd
            def _(e):
                run(e, "pool")

            @block.tensor
            def _(e):
                run(e, "pe")

from contextlib import ExitStack
import os
from concourse.bass_utils import run_bass_kernel_spmd
import ml_dtypes

D = 1024
NT = 512
EPS = 1e-6
IN_AB = 2096
D_FF = 2816
SCALE = 96 ** -0.5
RING_SLOTS = 3
RING_ELEMS = 4096

B_NPM, B_NPO, B_NFR, B_NFO = 0, 16, 32, 48
B_BMOD = 64
B_CVEC = 160
B_QN, B_KVN = 176, 178
B_CONVB = 180
B_CONVW = 188
B_SSDN = 228
B_PSC = 232
N_ROWS = 240


def build(n_layers=2, dbg=False):
    nc = bass.Bass("TRN2", target_bir_lowering=False)
    P = Prog(nc)
    es = ExitStack()

    def din(name, shape, dt=F32):
        return nc.dram_tensor(name, list(shape), dt, kind="ExternalInput").ap()

    def dout(name, shape, dt=F32):
        return nc.dram_tensor(name, list(shape), dt, kind="ExternalOutput").ap()

    def sb(name, shape, dt=F32):
        return es.enter_context(nc.sbuf_tensor("sb_" + name, list(shape), dt))

    xin = {"P": din("xp", [NT, D]), "S": din("xs", [NT, D])}
    cache_ckv = din("cache_ckv", [256, 256])
    cache_kpe = din("cache_kpe", [256, 32])
    h0_d = [din("h0f", [512, 128]), din("h0b", [512, 128])]
    w_mod = din("w_mod_sl", [2, D, 1536])
    w_in = din("w_in_ab", [1, D, IN_AB])
    w_uq = din("w_uq", [1, 256, 768]); w_ukv = din("w_ukv", [1, 256, 1024])
    w_out = din("w_out_ab", [1, D, D])
    pool_w = din("pool_w", [1, 4, 256, 256])
    w_gate = din("ffn_w_gate", [2, D, D_FF]); w_up = din("ffn_w_up", [2, D, D_FF])
    w_down = din("ffn_w_down", [2, D_FF, D])
    consts_d = din("consts", [128, 512])
    selc_d = din("selc", [8, 1024])
    padI_d = din("padI", [32, 96])
    rope_d = din("rope", [32, 2, NT])
    sel_d = din("sel", [128, 16])
    invcnt_d = din("invcnt", [2, 4, NT])

    yout = {"P": dout("yp", [NT, D]), "S": dout("ys", [NT, D])}
    ockv = dout("ockv", [NT, 256]); okpe = dout("okpe", [NT, 32])
    ohs = [dout("ohf", [2, 512, 128]), dout("ohb", [2, 512, 128])]

    NX1 = 1024 + 512 + 32
    x1_in = nc.dram_tensor("x1_in", [128, NX1], BF16, kind="Internal").ap()
    x1_out = nc.dram_tensor("x1_out", [4 * 128, NX1], BF16, kind="Internal").ap()
    x2_in = nc.dram_tensor("x2_in", [128, 1040], F32, kind="Internal").ap()
    x2_out = nc.dram_tensor("x2_out", [4 * 128, 1040], F32, kind="Internal").ap()
    x3_in = nc.dram_tensor("x3_in", [128, 128], F32, kind="Internal").ap()
    x3_out = nc.dram_tensor("x3_out", [4 * 128, 128], F32, kind="Internal").ap()
    GROUPS = [[0, 1, 2, 3], [4, 5, 6, 7]]

    consts = sb("consts", [128, 512]); consts_b = sb("consts_b", [128, 512], BF16)
    ident_f = consts[:, 0:128]; U_f = consts[:, 128:256]; L_f = consts[:, 256:384]; ones_f = consts[:, 384:512]
    ident_b = consts_b[:, 0:128]; ones_b = consts_b[:, 384:512]
    selc = sb("selc", [8, 1024])
    padI_f = sb("padI_f", [32, 96]); padI = sb("padI", [32, 96], BF16)
    rope_lo = sb("rope_lo", [32, 2, NT], BF16); rope_hi = sb("rope_hi", [96, 2, NT], BF16)
    sel = sb("sel", [128, 16])
    stg = sb("stg", [128, 2, 128]); cols = sb("cols", [128, 256])
    bcp = sb("bcp", [128, 808])
    kvn_bc = bcp[:, 0:256]; ssdn_bc = bcp[:, 256:768]; sm_bc = bcp[:, 768:808]
    A_bc = sb("A_bc", [128, 16]); dsk_bc = sb("dsk_bc", [128, 512], BF16)
    modT = sb("modT", [128, 2, 48, 2])
    csil = sb("csil", [128, 8, 2], BF16)
    mcol = sb("mcol", [128, 2, 6, 8, 2])
    qn32 = sb("qn32", [128, 2]); kvn32 = sb("kvn32", [128, 2])
    xT = {"P": sb("xT_P", [128, 8, NT]), "S": sb("xT_S", [128, 8, NT])}
    hT2 = sb("hT2", [128, 2, 8, NT], BF16)
    hT = {"P": hT2[:, 0], "S": hT2[:, 1]}
    sqT = sb("sqT", [128, 8, NT], BF16)
    rstd = sb("rstd", [128, NT]); tmpA = [sb("tmpA0", [128, NT]), sb("tmpA1", [128, NT])]
    mixT = sb("mixT", [128, 8, NT])
    xld = [mixT[:, 0:2, :].rearrange("p a b -> p (a b)"), mixT[:, 2:4, :].rearrange("p a b -> p (a b)")]
    ring = [sb("ring%d" % i, [128, RING_ELEMS], BF16) for i in range(RING_SLOTS)]
    wsm = sb("wsm", [128, 8, 48], BF16)
    wuq = sb("wuq", [128, 2, 8, 96], BF16); wuq_sw = sb("wuq_sw", [128, 2, 8, 96], BF16)
    wukv = sb("wukv", [128, 2, 8, 128], BF16)
    ARENA = 75 * 1024 // 2
    arena = sb("arena", [128, ARENA], BF16)
    mrow = arena[0:2, 0:6144].bitcast(F32).rearrange("p (l n) -> p l n", l=2)
    Gm = arena[0:16, 12288:12288 + 3072].bitcast(F32)

    rr = {"ring": 0, "tmp": 0, "xld": 0, "alt": 0}

    psF = [es.enter_context(nc.psum_tensor("psF%d" % i, [128, 512], F32)) for i in range(6)]
    psB = [es.enter_context(nc.psum_tensor("psB%d" % i, [128, 1024], BF16)) for i in range(2)]
    freeF = list(range(6)); freeB = [0, 1]

    def psf():
        i = freeF.pop(0); return i

    def psf_free(i):
        freeF.append(i)

    def psb():
        i = freeB.pop(0); return i

    def psb_free(i):
        freeB.append(i)

    def act(out, in_, func, reads, writes, bias=None, scale=None, accum=None):
        kw = {}
        if bias is not None: kw["bias"] = bias
        if scale is not None: kw["scale"] = scale
        if accum is not None: kw["accum_out"] = accum
        return P.op("act", lambda e: e.activation(out=out, in_=in_, func=func, **kw), reads, writes)

    def tt(eng, out, in0, in1, op, reads, writes):
        return P.op(eng, lambda e: e.tensor_tensor(out=out, in0=in0, in1=in1, op=op), reads, writes)

    def ts(eng, out, in0, s1, s2, op0, op1, reads, writes):
        if s2 is None:
            return P.op(eng, lambda e: e.tensor_scalar(out=out, in0=in0, scalar1=s1, scalar2=None, op0=op0), reads, writes)
        return P.op(eng, lambda e: e.tensor_scalar(out=out, in0=in0, scalar1=s1, scalar2=s2, op0=op0, op1=op1), reads, writes)

    def stt(eng, out, in0, scalar, in1, op0, op1, reads, writes):
        return P.op(eng, lambda e: e.scalar_tensor_tensor(out=out, in0=in0, scalar=scalar, in1=in1, op0=op0, op1=op1), reads, writes)

    def cp(eng, out, in_, reads, writes):
        if eng == "act":
            return P.op("act", lambda e: e.copy(out=out, in_=in_), reads, writes)
        return P.op(eng, lambda e: e.tensor_copy(out=out, in_=in_), reads, writes)

    def rsqrt_to(out, in_, c, reads, writes, scale=1.0):
        act(out, in_, AF.Ln, reads, writes, bias=float(c), scale=float(scale))
        act(out, out, AF.Exp, list(writes), list(writes), scale=-0.5)

    def alt_cp(out, in_, reads, writes):
        rr["alt"] ^= 1
        return cp("act" if rr["alt"] else "dve", out, in_, reads, writes)

    def mm(out, lhsT, rhs, start, stop, reads, writes, signal):
        return P.op("pe", lambda e: e.matmul(out, lhsT=lhsT, rhs=rhs, start=start, stop=stop), reads, writes, signal=signal)

    def mm_group(out, pairs, reads, writes):
        n = len(pairs)
        for i, (l, r) in enumerate(pairs):
            mm(out, l, r, i == 0, i == n - 1, reads if i == 0 else (), writes, i == n - 1)

    def tr(out, in_, ident, reads, writes, signal=True):
        return P.op("pe", lambda e: e.transpose(out=out, in_=in_, identity=ident), reads, writes, signal=signal)

    def dma(eng, out, in_, reads, writes):
        return P.dma(eng, lambda e: e.dma_start(out=out, in_=in_), reads, writes)

    def ring_load(parts, eng="pool"):
        s = rr["ring"] % RING_SLOTS
        rr["ring"] += 1
        for (off, shp, src) in parts:
            n = 1
            for v in shp[1:]:
                n *= v
            dst = ring[s][0:shp[0], off:off + n]
            if len(shp) == 3:
                dst = dst.rearrange("p (a b) -> p a b", a=shp[1])
            dma(eng, dst, src, (), [("ring", s)])
        return s

    def rview(s, off, shp):
        n = 1
        for v in shp[1:]:
            n *= v
        v = ring[s][0:shp[0], off:off + n]
        if len(shp) == 3:
            v = v.rearrange("p (a b) -> p a b", a=shp[1])
        return v

    dma("sp", consts[:], consts_d[:, :], (), ["consts"])
    dma("sp", selc[:], selc_d[:, :], (), ["selc"])
    dma("sp", padI_f[:], padI_d[:, :], (), ["padI_f"])
    dma("pool", rope_lo[:], rope_d[:, :, :], (), ["rope_lo"])
    dma("pool", rope_hi[64:96], rope_d[:, :, :], (), ["rope_hi"])
    dma("sp", sel[:], sel_d[:, :], (), ["sel"])
    cp("dve", consts_b[:], consts[:], ["consts"], ["consts_b"])
    cp("dve", padI[:], padI_f[:], ["padI_f"], ["padI"])

    def stage_rows(base, src2d, nrows):
        r = 0
        while r < nrows:
            row = base + r
            t, rin = row // 128, row % 128
            n = min(nrows - r, 128 - rin)
            dma("sp", stg[rin:rin + n, t, :], src2d[r:r + n, :], (), [("stg", t)])
            r += n

    stg_d = din("stg_all", [256, 128])
    bc_d = din("bc_all", [1, 808])
    dma("sp", stg[:, 0, :], stg_d[0:128, :], (), [("stg", 0)])
    dma("sp", stg[0:112, 1, :], stg_d[128:240, :], (), [("stg", 1)])
    pz = psf()
    tr(psF[pz][:, 0:128], stg[:, 0, :], ident_f, ["consts", ("stg", 0)], [("psF", pz)])
    tr(psF[pz][:, 128:128 + 112], stg[0:112, 1, :], ident_f[0:112, 0:112], ["consts", ("stg", 1)], [("psF", pz)])
    cp("dve", cols[:, 0:240], psF[pz][:, 0:240], [("psF", pz)], ["cols"])
    psf_free(pz)

    dma("sp", bcp[:], bc_d[0:1, :].to_broadcast([128, 808]), (), ["kvn_bc", "ssdn_bc", "sm_bc"])
    act(A_bc[:], sm_bc[:, 16:32], AF.Exp, ["sm_bc"], ["A_bc"])
    ts("dve", A_bc[:], A_bc[:], -1.0, None, ALU.mult, None, ["A_bc"], ["A_bc"])
    cp("dve", dsk_bc[:].rearrange("p (h c) -> p h c", h=8),
       sm_bc[:, 32:40].unsqueeze(2).to_broadcast([128, 8, 64]), ["sm_bc"], ["dsk_bc"])
    ts("dve", qn32[:], cols[:, B_QN:B_QN + 2], 16.0, None, ALU.mult, None, ["cols"], ["qn32"])
    ts("dve", kvn32[:], cols[:, B_KVN:B_KVN + 2], 16.0, None, ALU.mult, None, ["cols"], ["kvn32"])

    dma("pool", wsm[:, :, 0:32], w_in[0, :, 512:544].rearrange("(k p) n -> p k n", p=128), (), ["wsm"])
    dma("pool", wsm[:, :, 32:48], w_in[0, :, 2080:2096].rearrange("(k p) n -> p k n", p=128), (), ["wsm"])
    dma("pool", wuq[:].rearrange("p k h c -> p k (h c)"), w_uq[0].rearrange("(k p) n -> p k n", p=128), (), ["wuq"])
    dma("pool", wukv[:].rearrange("p k h c -> p k (h c)"), w_ukv[0].rearrange("(k p) n -> p k n", p=128), (), ["wukv"])
    P.op("dve", lambda e: e.memset(wuq_sw[:], 0.0), (), ["wuq_sw"])
    ts("dve", wuq_sw[:, :, :, 64:80], wuq[:, :, :, 80:96], -1.0, None, ALU.mult, None, ["wuq"], ["wuq_sw"])
    cp("dve", wuq_sw[:, :, :, 80:96], wuq[:, :, :, 64:80], ["wuq"], ["wuq_sw"])
    wsm_sw = sb("wsm_sw", [128, 8, 32], BF16)
    ts("dve", wsm_sw[:, :, 0:16], wsm[:, :, 16:32], -1.0, None, ALU.mult, None, ["wsm"], ["wsm_sw"])
    cp("dve", wsm_sw[:, :, 16:32], wsm[:, :, 0:16], ["wsm"], ["wsm_sw"])

    xm_in = nc.dram_tensor("xm_in", [4, 1536], F32, kind="Internal").ap()
    xm_out = nc.dram_tensor("xm_out", [16, 1536], F32, kind="Internal").ap()
    act(csil[:].rearrange("p k v -> p v k"),
        cols[:, B_CVEC:B_CVEC + 16].rearrange("p (v k) -> p v k", v=2), AF.Silu, ["cols"], ["csil"])
    for l in range(2):
        for cb in range(4):
            s = ring_load([(0, [128, 8, 384], w_mod[l, :, cb * 384:(cb + 1) * 384].rearrange("(k p) n -> p k n", p=128))])
            wv = rview(s, 0, [128, 8, 384])
            pz = psf()
            mm_group(psF[pz][0:2, 0:384], [(csil[:, k, :], wv[:, k, :]) for k in range(8)],
                     ["csil", ("ring", s)], [("psF", pz)])
            alt_cp(mrow[:, l, cb * 384:(cb + 1) * 384], psF[pz][0:2, 0:384], [("psF", pz)], ["mrow"])
            psf_free(pz)
    dma("sp", xm_in.rearrange("(v l) c -> v (l c)", v=2), mrow[:].rearrange("p l n -> p (l n)"), ["mrow"], ["xm_in"])
    P.dma("pool", lambda e: e.collective_compute("AllGather", ALU.bypass, replica_groups=GROUPS,
                                                 ins=[xm_in.opt()], outs=[xm_out.opt()]), ["xm_in"], ["xm_out"], inc=1)

    for t in ("P", "S"):
        for tb in range(4):
            xl = xld[rr["xld"] % 2]; xk = ("xld", rr["xld"] % 2); rr["xld"] += 1
            dma("sp", xl[:], xin[t][tb * 128:(tb + 1) * 128, :], (), [xk])
            for half in range(2):
                pz = psf()
                for q in range(4):
                    k = half * 4 + q
                    tr(psF[pz][:, q * 128:(q + 1) * 128], xl[:, k * 128:(k + 1) * 128], ident_f,
                       ["consts", xk], [("psF", pz)], signal=(q == 3))
                alt_cp(xT[t][:, half * 4:half * 4 + 4, tb * 128:(tb + 1) * 128],
                       psF[pz][:].rearrange("p (q c) -> p q c", q=4), [("psF", pz)], [("xT", t)])
                psf_free(pz)


    dma("sp", Gm[:, :], xm_out[:, :], ["xm_out"], ["Gm"])
    pz = psf()
    for cb in range(12):
        tr(psF[pz][:, cb * 16:(cb + 1) * 16], Gm[:, cb * 128:(cb + 1) * 128], ident_f[0:16, 0:16],
           ["consts", "Gm"], [("psF", pz)], signal=(cb == 11))
    pv = psF[pz][:, 0:192].rearrange("p (cb r v l) -> p r cb v l", cb=12, r=4, v=2, l=2)
    for l in range(2):
        for v2 in range(2):
            tt("dve", modT[:, l, :, v2].rearrange("p (r cb) -> p r cb", r=4), pv[:, :, :, v2, l],
               cols[:, B_BMOD + l * 48:B_BMOD + (l + 1) * 48].rearrange("p (r cb) -> p r cb", r=4), ALU.add,
               [("psF", pz), "cols"], [("modT", l)])
    psf_free(pz)
    for l in range(2):
        def ncol(base):
            return cols[:, base + l * 8:base + (l + 1) * 8].unsqueeze(2).to_broadcast([128, 8, 2])
        for (kind, jscale, nbase) in ((0, 1, B_NPM), (3, 4, B_NFR)):
            ts("dve", mcol[:, l, kind], modT[:, l, jscale * 8:(jscale + 1) * 8, :], 1.0, 32.0, ALU.add, ALU.mult,
               [("modT", l)], [("mcol", l)])
            tt("dve", mcol[:, l, kind], mcol[:, l, kind], ncol(nbase), ALU.mult, [("mcol", l), "cols"], [("mcol", l)])
        for (kind, jsh) in ((1, 0), (4, 3)):
            cp("dve", mcol[:, l, kind], modT[:, l, jsh * 8:(jsh + 1) * 8, :], [("modT", l)], [("mcol", l)])
        for (kind, jg, nbase) in ((2, 2, B_NPO), (5, 5, B_NFO)):
            stt("dve", mcol[:, l, kind], modT[:, l, jg * 8:(jg + 1) * 8, :], 32.0, ncol(nbase), ALU.mult, ALU.mult,
                [("modT", l), "cols"], [("mcol", l)])


    P.barrier()
    VI = {"P": 0, "S": 1}

    def rstd_from_sq(nchunks, scale_const, reads):
        pz = psf()
        mm_group(psF[pz][:, :], [(ones_b, sqT[:, k, :]) for k in range(nchunks)], ["consts_b"] + [("sqT", k_) for k_ in range(nchunks)] + list(reads), [("psF", pz)])
        rsqrt_to(rstd[:], psF[pz][:, :], scale_const * EPS, [("psF", pz)], ["rstd"])
        psf_free(pz)

    def modulate(t, l, kind_gs, kind_sh, outf=None):
        v = VI[t]
        act(sqT[:], xT[t][:], AF.Square, [("xT", t)], [("sqT", k_) for k_ in range(8)])
        rstd_from_sq(8, 1024.0, [])
        for k in range(8):
            tm = tmpA[rr["tmp"] % 2]; tk = ("tmpA", rr["tmp"] % 2); rr["tmp"] += 1
            stt("dve", tm[:], xT[t][:, k, :], mcol[:, l, kind_gs, k, v:v + 1], rstd[:], ALU.mult, ALU.mult,
                [("xT", t), ("mcol", l), "rstd"], [tk])
            if outf is None:
                act(hT[t][:, k, :], tm[:], AF.Identity, [tk, ("mcol", l)], [("hT", t)], bias=mcol[:, l, kind_sh, k, v:v + 1])
            else:
                o_ap, i_ap, o_key = outf(k, tm)
                act(o_ap, i_ap, AF.Identity, [tk, ("mcol", l)], [o_key], bias=mcol[:, l, kind_sh, k, v:v + 1])

    def post_norm_residual(t, l, kind_g):
        v = VI[t]
        rstd_from_sq(8, 1024.0, [])
        for k in range(8):
            tm = tmpA[rr["tmp"] % 2]; tk = ("tmpA", rr["tmp"] % 2); rr["tmp"] += 1
            stt("dve", tm[:], mixT[:, k, :], mcol[:, l, kind_g, k, v:v + 1], rstd[:], ALU.mult, ALU.mult,
                ["mixT", ("mcol", l), "rstd"], [tk])
            tt("dve", xT[t][:, k, :], xT[t][:, k, :], tm[:], ALU.add, [("xT", t), tk], [("xT", t)])

    def evac_mix(pz, k):
        cp("dve", mixT[:, k, :], psF[pz][:, :], [("psF", pz)], ["mixT"])
        act(sqT[:, k, :], mixT[:, k, :], AF.Square, ["mixT"], [("sqT", k)])

    def ffn(l):
        actT = arena[:, 0:22 * 2 * NT].rearrange("p (f n) -> p f n", f=22)
        for t in ("P", "S"):
            modulate(t, l, 3, 4)
        STAGE = int(os.environ.get('KFFN_STAGE', '3'))
        if STAGE < 2:
            return
        for nb in range(11):
            s = ring_load([(0, [128, 8, 256], w_gate[l, :, nb * 256:(nb + 1) * 256].rearrange("(k p) n -> p k n", p=128)),
                           (2048, [128, 8, 256], w_up[l, :, nb * 256:(nb + 1) * 256].rearrange("(k p) n -> p k n", p=128))])
            wg = rview(s, 0, [128, 8, 256]); wu = rview(s, 2048, [128, 8, 256])
            for c in range(2):
                f = nb * 2 + c
                for ti, t in enumerate(("P", "S")):
                    pg = psf(); pu = psf()
                    mm_group(psF[pg][:, :], [(wg[:, k, c * 128:(c + 1) * 128], hT[t][:, k, :]) for k in range(8)],
                             [("ring", s), ("hT", t)], [("psF", pg)])
                    mm_group(psF[pu][:, :], [(wu[:, k, c * 128:(c + 1) * 128], hT[t][:, k, :]) for k in range(8)],
                             [("ring", s), ("hT", t)], [("psF", pu)])
                    tm = tmpA[rr["tmp"] % 2]; tk = ("tmpA", rr["tmp"] % 2); rr["tmp"] += 1
                    KGU = int(os.environ.get("KGU", "0"))
                    if KGU in (0, 1):
                        act(tm[:], psF[pg][:, :], AF.Silu, [("psF", pg)], [tk])
                    if KGU in (0, 2):
                        tt("dve", actT[:, f, ti * NT:(ti + 1) * NT], psF[pu][:, :], tm[:], ALU.mult,
                           [tk, ("psF", pu)], [("actT", f)])
                    psf_free(pg); psf_free(pu)
        for ti, t in enumerate(("P", "S")):
            pass
        if STAGE < 3:
            return
        wdb = [arena[:, 26624 + i * 5632:26624 + (i + 1) * 5632].rearrange("p (f n) -> p f n", f=22) for i in range(2)]
        for db in range(4):
            wb = wdb[db % 2]; wk = ("wdblk", db % 2)
            for (f0, f1) in ((0, 11), (11, 22)):
                dma("pool", wb[:, f0:f1, :],
                    w_down[l, f0 * 128:f1 * 128, db * 256:(db + 1) * 256].rearrange("(f p) n -> p f n", p=128),
                    (), [wk])
            for d2 in range(2):
                dc = db * 2 + d2
                for ti, t in enumerate(("P", "S")):
                    pz = psf()
                    mm_group(psF[pz][:, :], [(wb[:, f, d2 * 128:(d2 + 1) * 128], actT[:, f, ti * NT:(ti + 1) * NT]) for f in range(22)],
                             [wk] + [("actT", f) for f in range(22)], [("psF", pz)])
                    mb = mixT2[t]
                    mkeys = [("mix2", t), ("hT", "P"), ("hT", "S")] if t == "S" else [("mix2", t)]
                    cp("dve", mb[:, dc, :], psF[pz][:, :], [("psF", pz)], mkeys)
                    act(sq2[t][:, dc, :], mb[:, dc, :], AF.Square, mkeys, [("sq2", t)])
                    psf_free(pz)
        for t in ("P", "S"):
            v = VI[t]
            pz = psf()
            mm_group(psF[pz][:, :], [(ones_b, sq2[t][:, k, :]) for k in range(8)], ["consts_b", ("sq2", t)], [("psF", pz)])
            rsqrt_to(rstd[:], psF[pz][:, :], 1024.0 * EPS, [("psF", pz)], ["rstd"])
            psf_free(pz)
            for k in range(8):
                tm = tmpA[rr["tmp"] % 2]; tk = ("tmpA", rr["tmp"] % 2); rr["tmp"] += 1
                stt("dve", tm[:], mixT2[t][:, k, :], mcol[:, l, 5, k, v:v + 1], rstd[:], ALU.mult, ALU.mult,
                    [("mix2", t), ("mcol", l), "rstd"], [tk])
                tt("dve", xT[t][:, k, :], xT[t][:, k, :], tm[:], ALU.add, [("xT", t), tk], [("xT", t)])

    mixT2 = {"P": mixT, "S": hT2[:].rearrange("p t k n -> p (t k n)").bitcast(F32).rearrange("p (k n) -> p k n", k=8)}
    sq2 = {"P": sqT, "S": arena[:, 22 * 2 * NT:22 * 2 * NT + 8 * NT].rearrange("p (k n) -> p k n", k=8)}

    def av(off, shape, dt=BF16):
        n = 1
        for v_ in shape[1:]:
            n *= v_
        if dt == F32:
            v = arena[0:shape[0], off:off + 2 * n].bitcast(F32)
        else:
            v = arena[0:shape[0], off:off + n]
        if len(shape) == 3:
            v = v.rearrange("p (a b) -> p a b", a=shape[1])
        elif len(shape) == 4:
            v = v.rearrange("p (a b c) -> p a b c", a=shape[1], b=shape[2])
        return v

    xbcpad = {"S": av(0, [128, 8, 516]), "P": av(19232, [128, 8, 520])}
    diagW = av(4128, [128, 40, 128])
    Vt = {"S": av(0, [128, 18, 512]), "P": av(19232, [128, 4, 512])}
    cqn = {"S": av(9248, [128, 2, 512]), "P": av(34656, [128, 2, 512])}
    Ksrc = av(12320, [128, 2, 2304]); kpeT = av(16928, [32, 2304])
    xconvT = av(23392, [128, 8, 512]); xtok = av(27488, [128, 4, 768]); hprev = av(30560, [128, 2, 4, 512])
    x1buf = av(27488, [128, 1568])
    gX = av(30560, [128, 4, 32])
    wuk96 = av(36864, [128, 2, 8, 96])
    smf = arena[:, 35680:36864].bitcast(F32)
    dtraw = {"P": smf[:, 0:64].rearrange("p (a b) -> p a b", a=4), "S": smf[:, 64:128].rearrange("p (a b) -> p a b", a=4)}
    dtw = smf[:, 128:192].rearrange("p (a b) -> p a b", a=4)
    a_ = smf[:, 192:256].rearrange("p (a b) -> p a b", a=4)
    smx = smf[:, 256:384].rearrange("p (a b) -> p a b", a=4)
    E_ = smf[:, 384:448].rearrange("p (a b) -> p a b", a=4)
    CD_ = smf[:, 448:512].rearrange("p (a b) -> p a b", a=4)
    ac2 = smf[:, 512:576].rearrange("p (a b) -> p a b", a=4)
    hflat = hT2[:].rearrange("p t k n -> p (t k n)")
    OT = hflat[0:64, 0:4096].rearrange("p (h n) -> p h n", h=8)
    ssdT = hflat[:, 4096:6144].rearrange("p (k n) -> p k n", k=4)
    zs = {"S": av(10272, [128, 4, 512]), "P": hflat[:, 6144:8192].rearrange("p (k n) -> p k n", k=4)}
    x2buf = hflat[:, 0:2080].bitcast(F32)
    sflat = sqT[:].rearrange("p k n -> p (k n)")
    Kh = sflat[0:96, 0:2304]; Qh = sflat[0:96, 2304:2816]
    PT = [sflat[:, 2816:3328], sflat[:, 3328:3840]]
    mflat = mixT[:].rearrange("p k n -> p (k n)")
    mbf = mflat.bitcast(BF16)
    MT = mbf[:, 0:2048].rearrange("p (h n) -> p h n", h=16)
    CBm = mflat[:, 1024:1536].rearrange("p (h n) -> p h n", h=4)
    ysb = mflat[:, 1536:2048]; yg = mflat[:, 2048:2560]
    xw = mbf[:, 5120:5632]; xd = mbf[:, 5632:6144]
    hrun = mflat[:, 3072:4096].rearrange("p (d n) -> p d n", d=2)
    hcand = mflat[:, 1536:2560].rearrange("p (d n) -> p d n", d=2)
    hin = mflat[:, 0:1024].rearrange("p (d n) -> p d n", d=2)
    acumT = rstd[0:8, 0:256]; stat2 = sb("stat2", [128, 16])
    gS = av(0, [128, 4, 1040], F32)
    SEGS = {"P": [[0, 1], [2, 3]], "S": [[0, 1, 2, 3]]}

    def h8(ap2d):
        return ap2d.rearrange("p (h c) -> p h c", h=8)

    def bc8(ap_8):
        return ap_8.unsqueeze(2).to_broadcast([128, 8, 64])

    def coll(in_ap, out_ap, rk, wk):
        P.dma("pool", lambda e: e.collective_compute("AllGather", ALU.bypass, replica_groups=GROUPS,
                                                     ins=[in_ap.opt()], outs=[out_ap.opt()]), rk, wk, inc=1)

    def inproj(t, l):
        modulate(t, l, 0, 1)
        hk = ("hT", t)
        sA = ring_load([(0, [128, 8, 512], w_in[0, :, 0:512].rearrange("(k p) n -> p k n", p=128))])
        wA = rview(sA, 0, [128, 8, 512])
        for j in range(4):
            pz = psf()
            mm_group(psF[pz][:, :], [(wA[:, k, j * 128:(j + 1) * 128], hT[t][:, k, :]) for k in range(8)],
                     [("ring", sA), hk], [("psF", pz)])
            cp("dve", mixT[:, j, :], psF[pz][:, :], [("psF", pz)], [("cqf", j)])
            act(sqT[:, j, :], mixT[:, j, :], AF.Square, [("cqf", j)], [("sqT", j)])
            psf_free(pz)
        for (j0, scl, dst) in ((0, qn32, None), (2, kvn32, None)):
            pz = psf()
            mm_group(psF[pz][:, :], [(ones_b, sqT[:, j0 + jj, :]) for jj in range(2)],
                     ["consts_b", ("sqT", j0), ("sqT", j0 + 1)], [("psF", pz)])
            rsqrt_to(rstd[:], psF[pz][:, :], 256.0 * EPS, [("psF", pz)], ["rstd"])
            psf_free(pz)
            for jj in range(2):
                if j0 == 0:
                    o, ok = cqn[t][:, jj, :], ("cqn", t)
                elif t == "P":
                    o, ok = Ksrc[:, jj, 0:512], "Ksrc"
                else:
                    o, ok = x1buf[:, jj * 512:(jj + 1) * 512], "x1buf"
                stt("dve", o, mixT[:, j0 + jj, :], scl[:, jj:jj + 1], rstd[:], ALU.mult, ALU.mult,
                    [("cqf", j0 + jj), "qn32", "kvn32", "rstd"], [ok])
        pz = psf()
        mm_group(psF[pz][0:32, :], [(wsm[:, k, 0:32], hT[t][:, k, :]) for k in range(8)], ["wsm", hk], [("psF", pz)])
        if t == "P":
            cp("dve", kpeT[0:32, 0:512], psF[pz][0:32, :], [("psF", pz)], ["kpeT"])
        else:
            pz2 = psf()
            mm_group(psF[pz2][0:32, :], [(wsm_sw[:, k, :], hT[t][:, k, :]) for k in range(8)], ["wsm_sw", hk], [("psF", pz2)])
            tt("dve", tmpA[0][0:32, :], psF[pz][0:32, :], rope_lo[:, 0, :], ALU.mult, [("psF", pz), "rope_lo"], [("tmpA", 0)])
            tt("dve", tmpA[1][0:32, :], psF[pz2][0:32, :], rope_lo[:, 1, :], ALU.mult, [("psF", pz2), "rope_lo"], [("tmpA", 1)])
            tt("dve", x1buf[0:32, 1024:1536], tmpA[0][0:32, :], tmpA[1][0:32, :], ALU.add, [("tmpA", 0), ("tmpA", 1)], ["x1buf"])
            psf_free(pz2)
        psf_free(pz)
        if t == "P":
            for tb in range(4):
                pz = psf()
                mm_group(psF[pz][:, 0:256], [(hT[t][:, k, tb * 128:(tb + 1) * 128], wA[:, k, 256:512]) for k in range(8)],
                         [("ring", sA), hk], [("psF", pz)])
                mm_group(psF[pz][:, 256:288], [(hT[t][:, k, tb * 128:(tb + 1) * 128], wsm[:, k, 0:32]) for k in range(8)],
                         ["wsm", hk], [("psF", pz)])
                ct = tmpA[tb % 2]; ck = ("tmpA", tb % 2)
                P.op("dve", lambda e: e.memset(stat2[:, 0:1], 0.0), (), ["stat2"])
                cp("dve", ct[:, 0:288], psF[pz][:, 0:288], [("psF", pz)], [ck])
                psf_free(pz)
                act(sqT[:, 4, 0:256], ct[:, 0:256], AF.Square, [ck, "stat2"], [("sqT", 4), "stat2"], accum=stat2[:, 0:1])
                rsqrt_to(stat2[:, 0:1], stat2[:, 0:1], EPS, ["stat2"], ["stat2"], scale=1.0 / 256.0)
                stt("dve", ct[:, 0:256], ct[:, 0:256], stat2[:, 0:1], kvn_bc[:], ALU.mult, ALU.mult,
                    [ck, "stat2", "kvn_bc"], [ck])
                dma("sp", ockv[tb * 128:(tb + 1) * 128, :], ct[:, 0:256], [ck], ["ockv"])
                dma("sp", okpe[tb * 128:(tb + 1) * 128, :], ct[:, 256:288], [ck], ["okpe"])
        sZ = ring_load([(0, [128, 8, 512], w_in[0, :, 544:1056].rearrange("(k p) n -> p k n", p=128))])
        wZ = rview(sZ, 0, [128, 8, 512])
        for tb in range(4):
            pz = psf()
            mm_group(psF[pz][:, :], [(hT[t][:, k, tb * 128:(tb + 1) * 128], wZ[:, k, :]) for k in range(8)],
                     [("ring", sZ), hk], [("psF", pz)])
            act(zs[t][:, tb, :], psF[pz][:, :], AF.Silu, [("psF", pz)], [("zs", t)])
            psf_free(pz)
        for tb in range(4):
            pz = psf()
            mm_group(psF[pz][:, 0:16], [(hT[t][:, k, tb * 128:(tb + 1) * 128], wsm[:, k, 32:48]) for k in range(8)],
                     ["wsm", hk], [("psF", pz)])
            tt("dve", dtraw[t][:, tb, :], psF[pz][:, 0:16], sm_bc[:, 0:16], ALU.add, [("psF", pz), "sm_bc"], [("dtraw", t)])
            psf_free(pz)
        for xb in range(2):
            sX = ring_load([(0, [128, 8, 512], w_in[0, :, 1056 + xb * 512:1568 + xb * 512].rearrange("(k p) n -> p k n", p=128))])
            wX = rview(sX, 0, [128, 8, 512])
            for jj in range(4):
                j = xb * 4 + jj
                pz = psf()
                mm_group(psF[pz][:, :], [(wX[:, k, jj * 128:(jj + 1) * 128], hT[t][:, k, :]) for k in range(8)],
                         [("ring", sX), hk], [("psF", pz)])
                if t == "P":
                    alt_cp(xbcpad["P"][:, j, :].rearrange("p (s c) -> p s c", s=2)[:, :, 2:258],
                           psF[pz][:, :].rearrange("p (s c) -> p s c", s=2), [("psF", pz)], [("xbcpad", t)])
                else:
                    alt_cp(xbcpad["S"][:, j, 2:514], psF[pz][:, :], [("psF", pz)], [("xbcpad", t)])
                psf_free(pz)
        if t == "S":
            xe = x1buf[:, 1536:1568].rearrange("p (j c) -> p j c", j=8)
            cp("dve", xe[:, :, 0:2], xbcpad["S"][:, :, 2:4], [("xbcpad", t)], ["x1buf"])
            cp("dve", xe[:, :, 2:4], xbcpad["S"][:, :, 512:514], [("xbcpad", t)], ["x1buf"])

    def x1_exchange():
        dma("sp", x1_in[:, :], x1buf[:, :], ["x1buf"], ["x1_in"])
        coll(x1_in, x1_out, ["x1_in"], ["x1_out"])

    def x1_receive():
        x1r = x1_out.rearrange("(r p) c -> p r c", p=128)
        for kc in range(2):
            dma("sp", Ksrc[:, kc, 256:2304].rearrange("p (r t) -> p r t", r=4), x1r[:, :, kc * 512:(kc + 1) * 512],
                ["x1_out"], ["Ksrc"])
        dma("sp", kpeT[0:32, 256:2304].rearrange("p (r t) -> p r t", r=4), x1r[0:32, :, 1024:1536], ["x1_out"], ["kpeT"])
        dma("sp", gX[:], x1r[:, :, 1536:1568], ["x1_out"], ["gX"])
        cc = mixT[:, 0:2, 0:288]
        for tbk in range(2):
            dma("sp", cc[:, tbk, 0:256], cache_ckv[tbk * 128:(tbk + 1) * 128, :], (), [("cc", tbk)])
            dma("sp", cc[:, tbk, 256:288], cache_kpe[tbk * 128:(tbk + 1) * 128, :], (), [("cc", tbk)])
        for tbk in range(2):
            pz = psf()
            tr(psF[pz][:, 0:128], cc[:, tbk, 0:128], ident_f, ["consts", ("cc", tbk)], [("psF", pz)], signal=False)
            tr(psF[pz][:, 128:256], cc[:, tbk, 128:256], ident_f, ["consts", ("cc", tbk)], [("psF", pz)], signal=False)
            tr(psF[pz][0:32, 256:384], cc[:, tbk, 256:288], ident_f, ["consts", ("cc", tbk)], [("psF", pz)])
            cp("dve", Ksrc[:, :, tbk * 128:(tbk + 1) * 128], psF[pz][:, 0:256].rearrange("p (k n) -> p k n", k=2),
               [("psF", pz)], ["Ksrc"])
            cp("dve", kpeT[0:32, tbk * 128:(tbk + 1) * 128], psF[pz][0:32, 256:384], [("psF", pz)], ["kpeT"])
            psf_free(pz)
        hal = tmpA[0][:, 0:32].rearrange("p (s j c) -> p s j c", s=2, j=8)
        for side, (c0, sb_) in enumerate(((2, 4), (0, 8))):
            for rp in range(4):
                src = gX[:, rp, :].rearrange("p (j c) -> p j c", j=8)[:, :, c0:c0 + 2]
                if rp == 0:
                    ts("dve", hal[:, side], src, sel[:, sb_ + rp:sb_ + rp + 1], None, ALU.mult, None, ["gX", "sel"], [("tmpA", 0)])
                else:
                    stt("dve", hal[:, side], src, sel[:, sb_ + rp:sb_ + rp + 1], hal[:, side], ALU.mult, ALU.add,
                        ["gX", "sel", ("tmpA", 0)], [("tmpA", 0)])
        cp("dve", xbcpad["S"][:, :, 0:2], hal[:, 0], [("tmpA", 0)], [("xbcpad", "S")])
        cp("dve", xbcpad["S"][:, :, 514:516], hal[:, 1], [("tmpA", 0)], [("xbcpad", "S")])

    def conv(t):
        segs = [(0, 0, 256), (260, 256, 256)] if t == "P" else [(0, 0, 512)]
        for j in range(8):
            pz = psf()
            for (pb_, oc, n) in segs:
                mm_group(psF[pz][:, oc:oc + n],
                         [(diagW[:, w * 8 + j, :], xbcpad[t][:, j, pb_ + w:pb_ + w + n]) for w in range(5)],
                         [("diagW", 0), ("xbcpad", t)], [("psF", pz)])
            act(xconvT[:, j, :], psF[pz][:, :], AF.Silu, [("psF", pz), "cols"], [("xconvT", j)],
                bias=cols[:, B_CONVB + j:B_CONVB + j + 1])
            psf_free(pz)
        for tb in range(4):
            pb_ = psb()
            for j in range(6):
                tr(psB[pb_][:, j * 128:(j + 1) * 128], xconvT[:, j, tb * 128:(tb + 1) * 128], ident_b,
                   ["consts_b", ("xconvT", j)], [("psB", pb_)], signal=(j == 5))
            alt_cp(xtok[:, tb, :], psB[pb_][:, 0:768], [("psB", pb_)], [("xtok", tb)])
            psb_free(pb_)

    def ssd_small(t):
        dk = ("dtraw", t)
        act(dtw[:], dtraw[t][:], AF.Exp, [dk], ["dtw"])
        act(dtw[:], dtw[:], AF.Ln, ["dtw"], ["dtw"], bias=1.0)
        tt("dve", a_[:], dtw[:], A_bc[:].unsqueeze(1).to_broadcast([128, 4, 16]), ALU.mult, ["dtw", "A_bc"], ["a_"])
        act(ac2[:], dtw[:], AF.Ln, ["dtw"], ["ac2"])
        for tb in range(4):
            pz = psf()
            mm(psF[pz][:, 0:8], U_f, a_[:, tb, 0:8], True, True, ["consts", "a_"], [("psF", pz)], False)
            mm(psF[pz][:, 8:16], L_f, a_[:, tb, 8:16], True, True, ["consts", "a_"], [("psF", pz)], False)
            mm(psF[pz][:, 16:32], ones_f, a_[:, tb, :], True, True, ["consts", "a_"], [("psF", pz)], True)
            cp("dve", smx[:, tb, :], psF[pz][:, 0:32], [("psF", pz)], ["smx"])
            psf_free(pz)
        act(E_[:], smx[:, :, 0:16], AF.Exp, ["smx"], ["E_"])
        act(CD_[:], smx[:, :, 16:32], AF.Exp, ["smx"], ["CD_"])
        tt("dve", ac2[:], smx[:, :, 0:16], ac2[:], ALU.subtract, ["smx", "ac2"], ["ac2"])
        wt = tmpA[0][:, 0:64].rearrange("p (a b) -> p a b", a=4)
        tt("dve", wt, smx[:, :, 16:32], smx[:, :, 0:16], ALU.subtract, ["smx"], [("tmpA", 0)])
        act(wt, wt, AF.Exp, [("tmpA", 0)], [("tmpA", 0)])
        tt("dve", dtw[:], dtw[:], wt, ALU.mult, ["dtw", ("tmpA", 0)], ["dtw"])

    def prepass(t, use_hin, final_cb, dirs=(0, 1)):
        for d in dirs:
            for seg in SEGS[t]:
                order = seg if d == 0 else list(reversed(seg))
                if use_hin:
                    cp("dve", hrun[:, d, :], hin[:, d, :], ["hin"], [("hrun", d)])
                else:
                    P.op("dve", lambda e, d=d: e.memset(hrun[:, d, :], 0.0), (), [("hrun", d)])
                for tb in order:
                    cp("act", hprev[:, d, tb, :], hrun[:, d, :], [("hrun", d)], [("hprev", d, tb)])
                    tt("dve", h8(xw), h8(xtok[:, tb, 0:512]), bc8(dtw[:, tb, d * 8:(d + 1) * 8]), ALU.mult,
                       [("xtok", tb), "dtw"], ["xw"])
                    pz = psf()
                    for g in range(2):
                        mm(psF[pz][:, g * 256:(g + 1) * 256], xtok[:, tb, 512 + g * 128:512 + (g + 1) * 128],
                           xw[:, g * 256:(g + 1) * 256], True, True, [("xtok", tb), "xw"], [("psF", pz)], g == 1)
                    tt("dve", h8(hrun[:, d, :]), h8(hrun[:, d, :]), bc8(CD_[:, tb, d * 8:(d + 1) * 8]), ALU.mult,
                       [("hrun", d), "CD_"], [("hrun", d)])
                    tt("dve", hrun[:, d, :], psF[pz][:, :], hrun[:, d, :], ALU.add, [("psF", pz), ("hrun", d)], [("hrun", d)])
                    psf_free(pz)
                final_cb(d, seg)

    def final_P(d, seg):
        s_ = seg[0] // 2
        pz = psf()
        for q in range(4):
            tr(psF[pz][:, q * 128:(q + 1) * 128], hrun[:, d, q * 128:(q + 1) * 128], ident_f,
               ["consts", ("hrun", d)], [("psF", pz)], signal=(q == 3))
        st_ = tmpA[d]
        cp("act", st_[:], psF[pz][:, :], [("psF", pz)], [("tmpA", d)])
        psf_free(pz)
        dma("sp", ohs[d][s_].rearrange("(q p) n -> p q n", p=128), st_[:].rearrange("p (q n) -> p q n", q=4),
            [("tmpA", d)], [("ohs", d)])

    def final_S1(d, seg):
        cp("dve", x2buf[:, d * 512:(d + 1) * 512], hrun[:, d, :], [("hrun", d)], ["x2buf"])
        tsum = tmpA[0][:, 0:8]
        tt("dve", tsum, smx[:, 0, 16 + d * 8:24 + d * 8], smx[:, 1, 16 + d * 8:24 + d * 8], ALU.add, ["smx"], [("tmpA", 0)])
        tt("dve", tsum, tsum, smx[:, 2, 16 + d * 8:24 + d * 8], ALU.add, ["smx", ("tmpA", 0)], [("tmpA", 0)])
        tt("dve", tsum, tsum, smx[:, 3, 16 + d * 8:24 + d * 8], ALU.add, ["smx", ("tmpA", 0)], [("tmpA", 0)])
        act(x2buf[:, 1024 + d * 8:1032 + d * 8], tsum, AF.Exp, [("tmpA", 0)], ["x2buf"])

    def x2_exchange():
        dma("sp", x2_in[:, :], x2buf[:, :], ["x2buf"], ["x2_in"])
        coll(x2_in, x2_out, ["x2_in"], ["x2_out"])

    gSb = [av(19232, [128, 1040], F32), av(19232 + 2080, [128, 1040], F32)]

    def x2_receive():
        nld = [0]

        def load_rank(rp):
            i = nld[0] % 2; nld[0] += 1
            dma("sp", gSb[i][:, :], x2_out[rp * 128:(rp + 1) * 128, :], ["x2_out"], [("gSb", i)])
            return gSb[i], ("gSb", i)
        for d in range(2):
            st_ = tmpA[d]
            dma("sp", st_[:].rearrange("p (q n) -> p q n", q=4), h0_d[d].rearrange("(q p) n -> p q n", p=128), (), [("tmpA", d)])
            pz = psf()
            for q in range(4):
                tr(psF[pz][:, q * 128:(q + 1) * 128], st_[:, q * 128:(q + 1) * 128], ident_f,
                   ["consts", ("tmpA", d)], [("psF", pz)], signal=(q == 3))
            cp("dve", hcand[:, d, :], psF[pz][:, :], [("psF", pz)], [("hcand", d)])
            psf_free(pz)
            ranks = [0, 1, 2] if d == 0 else [3, 2, 1]
            first = 0 if d == 0 else 3
            ts("dve", hin[:, d, :], hcand[:, d, :], sel[:, first:first + 1], None, ALU.mult, None,
               [("hcand", d), "sel"], ["hin"])
            for rp in ranks:
                nxt = rp + 1 if d == 0 else rp - 1
                g_, gk = load_rank(rp)
                tt("dve", h8(hcand[:, d, :]), h8(hcand[:, d, :]), bc8(g_[:, 1024 + d * 8:1032 + d * 8]), ALU.mult,
                   [("hcand", d), gk], [("hcand", d)])
                tt("dve", hcand[:, d, :], hcand[:, d, :], g_[:, d * 512:(d + 1) * 512], ALU.add,
                   [("hcand", d), gk], [("hcand", d)])
                stt("dve", hin[:, d, :], hcand[:, d, :], sel[:, nxt:nxt + 1], hin[:, d, :], ALU.mult, ALU.add,
                    [("hcand", d), "sel", "hin"], ["hin"])

    def ssd_main(t, tb):
        tbc = slice(tb * 128, (tb + 1) * 128)
        pa = psf()
        mm(psF[pa][0:8, 0:128], a_[:, tb, 0:8], U_f, True, True, ["consts", "a_"], [("psF", pa)], False)
        mm(psF[pa][0:8, 128:256], a_[:, tb, 8:16], L_f, True, True, ["consts", "a_"], [("psF", pa)], True)
        cp("act", acumT, psF[pa][0:8, 0:256], [("psF", pa)], ["acumT"])
        psf_free(pa)
        pc = psf()
        for g in range(2):
            mm(psF[pc][:, g * 128:(g + 1) * 128], xconvT[:, 4 + g, tbc], xconvT[:, 6 + g, tbc], True, True,
               [("xconvT", 4 + g), ("xconvT", 6 + g)], [("psF", pc)], g == 1)
        for g in range(2):
            for d in range(2):
                tt("dve", CBm[:, g * 2 + d, :], psF[pc][:, g * 128:(g + 1) * 128], U_f if d == 0 else L_f, ALU.mult,
                   [("psF", pc), "consts"], ["CBm"])
        psf_free(pc)
        for d in range(2):
            prs = []
            for g in range(2):
                pr = psf(); prs.append(pr)
                for hh in range(4):
                    h = g * 4 + hh
                    mm(psF[pr][:, hh * 128:(hh + 1) * 128], selc[0:8, h * 128:(h + 1) * 128], acumT[0:8, d * 128:(d + 1) * 128],
                       True, True, ["selc", "acumT"], [("psF", pr)], hh == 3)
            for g in range(2):
                pr = prs[g]
                for hh in range(4):
                    ci = d * 8 + g * 4 + hh
                    ts("dve", tmpA[g][:, hh * 128:(hh + 1) * 128], psF[pr][:, hh * 128:(hh + 1) * 128],
                       ac2[:, tb, ci:ci + 1], 30.0, ALU.subtract, ALU.min, [("psF", pr), "ac2"], [("tmpA", g)])
                psf_free(pr)
            for g in range(2):
                act(tmpA[g][:], tmpA[g][:], AF.Exp, [("tmpA", g)], [("tmpA", g)])
            for g in range(2):
                D3 = tmpA[g][:].rearrange("p (h n) -> p h n", h=4)
                stt("dve", MT[:, d * 8 + g * 4:d * 8 + g * 4 + 4, :], D3, 1e30,
                    CBm[:, g * 2 + d, :].unsqueeze(1).to_broadcast([128, 4, 128]), ALU.min, ALU.mult,
                    [("tmpA", g), "CBm"], [("MT", d, g)])
        tt("dve", xd, xtok[:, tb, 0:512], dsk_bc[:], ALU.mult, [("xtok", tb), "dsk_bc"], ["xd"])
        py = psf()
        mtk = [("MT", d, g) for d in range(2) for g in range(2)]
        for h in range(8):
            hc = slice(h * 64, (h + 1) * 64)
            mm(psF[py][:, hc], ident_b, xd[:, hc], True, False,
               (["consts_b", "xd"] + mtk + [("xtok", tb)]) if h == 0 else (), [("psF", py)], False)
            for d in range(2):
                mm(psF[py][:, hc], MT[:, d * 8 + h, :], xtok[:, tb, hc], False, d == 1,
                   (), [("psF", py)], (h == 7 and d == 1))
        pof = [psf(), psf()]
        for d in range(2):
            for g in range(2):
                mm(psF[pof[d]][:, g * 256:(g + 1) * 256], xconvT[:, 6 + g, tbc], hprev[:, d, tb, g * 256:(g + 1) * 256],
                   True, True, [("xconvT", 6 + g), ("hprev", d, tb)], [("psF", pof[d])], g == 1)
        cp("act", ysb, psF[py][:, :], [("psF", py)], ["ysb"])
        psf_free(py)
        for d in range(2):
            Dt = tmpA[d]; dk = ("tmpA", d)
            tt("dve", h8(Dt[:]), h8(psF[pof[d]][:, :]), bc8(E_[:, tb, d * 8:(d + 1) * 8]), ALU.mult,
               [("psF", pof[d]), "E_"], [dk])
            tt("dve", ysb, ysb, Dt[:], ALU.add, ["ysb", dk], ["ysb"])
            psf_free(pof[d])
        tt("dve", yg, ysb, zs[t][:, tb, :], ALU.mult, ["ysb", ("zs", t)], ["yg"])
        P.op("dve", lambda e: e.memset(stat2[:, 0:2], 0.0), (), ["stat2"])
        for g in range(2):
            act(tmpA[g][:, 0:256], yg[:, g * 256:(g + 1) * 256], AF.Square, ["yg"], [("tmpA", g), "stat2"],
                accum=stat2[:, g:g + 1])
        rsqrt_to(stat2[:, 0:2], stat2[:, 0:2], EPS, ["stat2"], ["stat2"], scale=1.0 / 256.0)
        for g in range(2):
            stt("dve", xw[:, g * 256:(g + 1) * 256], yg[:, g * 256:(g + 1) * 256], stat2[:, g:g + 1],
                ssdn_bc[:, g * 256:(g + 1) * 256], ALU.mult, ALU.mult, ["yg", "stat2", "ssdn_bc"], ["xw"])
        pb_ = psb()
        for k in range(4):
            tr(psB[pb_][:, k * 128:(k + 1) * 128], xw[:, k * 128:(k + 1) * 128], ident_b, ["consts_b", "xw"],
               [("psB", pb_)], signal=(k == 3))
        alt_cp(ssdT[:, :, tbc], psB[pb_][:, 0:512].rearrange("p (k n) -> p k n", k=4), [("psB", pb_)], ["ssdT"])
        psb_free(pb_)

    def attn_V(t):
        nkb = 18 if t == "S" else 4
        for kb in range(nkb):
            pz = psf()
            mm_group(psF[pz][:, :].rearrange("p (h c) -> p h c", h=8),
                     [(Ksrc[:, kc, kb * 128:(kb + 1) * 128], wukv[:, kc, :, 64:128]) for kc in range(2)],
                     ["Ksrc", "wukv"], [("psF", pz)])
            alt_cp(Vt[t][:, kb, :], psF[pz][:, :], [("psF", pz)], [("V", kb)])
            psf_free(pz)

    def attn_head(t, h):
        nkb = 18 if t == "S" else 4
        NK = nkb * 128
        if True:
            for kt in range((NK + 511) // 512):
                n = min(512, NK - kt * 512)
                kc_ = slice(kt * 512, kt * 512 + n)
                pz = psf()
                mm(psF[pz][0:96, 0:n], padI[0:32, 0:96], kpeT[0:32, kc_], True, False, ["padI", "kpeT"], [("psF", pz)], False)
                for kc in range(2):
                    mm(psF[pz][0:96, 0:n], wuk96[:, kc, h, :], Ksrc[:, kc, kc_], False, kc == 1,
                       ["wuk96", "Ksrc"], [("psF", pz)], kc == 1)
                alt_cp(Kh[:, kc_], psF[pz][0:96, 0:n], [("psF", pz)], ["Kh"])
                psf_free(pz)
            pq = psf()
            mm_group(psF[pq][0:96, :], [(wuq[:, kc, h, :], cqn[t][:, kc, :]) for kc in range(2)], ["wuq", ("cqn", t)], [("psF", pq)])
            if t == "P":
                alt_cp(Qh, psF[pq][0:96, :], [("psF", pq)], ["Qh"])
            else:
                pq2 = psf()
                mm_group(psF[pq2][0:96, :], [(wuq_sw[:, kc, h, :], cqn[t][:, kc, :]) for kc in range(2)],
                         ["wuq_sw", ("cqn", t)], [("psF", pq2)])
                cp("act", Qh[0:64, :], psF[pq][0:64, :], [("psF", pq)], ["Qh"])
                tt("dve", tmpA[0][64:96, :], psF[pq][64:96, :], rope_hi[64:96, 0, :], ALU.mult, [("psF", pq), "rope_hi"], [("tmpA", 0)])
                tt("dve", tmpA[1][64:96, :], psF[pq2][64:96, :], rope_hi[64:96, 1, :], ALU.mult, [("psF", pq2), "rope_hi"], [("tmpA", 1)])
                tt("dve", Qh[64:96, :], tmpA[0][64:96, :], tmpA[1][64:96, :], ALU.add, [("tmpA", 0), ("tmpA", 1)], ["Qh"])
                psf_free(pq2)
            psf_free(pq)
            po = psf(); pl = psf()
            if t == "P":
                plan = [(slice(s_ * 256, (s_ + 1) * 256), [2 * s_, 2 * s_ + 1]) for s_ in range(2)]
            else:
                plan = [(slice(0, 512), list(range(18)))]
            steps = []
            for (qc, kbs) in plan:
                for i, kb in enumerate(kbs):
                    steps.append((qc, kb, i == 0, i == len(kbs) - 1))
            pend = None
            for it, (qc, kb, first, lastk) in enumerate(steps):
                nq = qc.stop - qc.start
                psc = psf()
                mm(psF[psc][:, 0:nq], Kh[:, kb * 128:(kb + 1) * 128], Qh[:, qc], True, True, ["Kh", "Qh"], [("psF", psc)], True)
                if pend is not None:
                    pend()
                pt = PT[it % 2]; pk = ("PT", it % 2)
                act(pt[:, 0:nq], psF[psc][:, 0:nq], AF.Exp, [("psF", psc)], [pk], scale=SCALE)
                psf_free(psc)

                def pend(qc=qc, kb=kb, first=first, lastk=lastk, pt=pt, pk=pk, nq=nq):
                    mm(psF[po][0:64, qc], Vt[t][:, kb, h * 64:(h + 1) * 64], pt[:, 0:nq], first, lastk,
                       [("V", kb), pk], [("psF", po)], True)
                    mm(psF[pl][0:64, qc], ones_b[:, 0:64], pt[:, 0:nq], first, lastk, ["consts_b", pk], [("psF", pl)], True)
            pend()
            rl = tmpA[0][0:64, :]
            act(rl, psF[pl][0:64, :], AF.Ln, [("psF", pl)], [("tmpA", 0)])
            act(rl, rl, AF.Exp, [("tmpA", 0)], [("tmpA", 0)], scale=-1.0)
            tt("dve", OT[:, h, :], psF[po][0:64, :], rl, ALU.mult, [("psF", po), ("tmpA", 0)], [("OT", h)])
            psf_free(po); psf_free(pl)

    def outproj(t, l):
        for half in range(2):
            s1 = ring_load([(0, [64, 8, 512], w_out[0, 0:512, half * 512:(half + 1) * 512].rearrange("(h p) n -> p h n", p=64))])
            s2 = ring_load([(0, [128, 4, 512], w_out[0, 512:1024, half * 512:(half + 1) * 512].rearrange("(k p) n -> p k n", p=128))])
            wa = rview(s1, 0, [64, 8, 512]); ws_ = rview(s2, 0, [128, 4, 512])
            for d4 in range(4):
                dc = half * 4 + d4
                pz = psf()
                pairs = [(wa[:, h, d4 * 128:(d4 + 1) * 128], OT[:, h, :]) for h in range(8)]
                pairs += [(ws_[:, k, d4 * 128:(d4 + 1) * 128], ssdT[:, k, :]) for k in range(4)]
                mm_group(psF[pz][:, :], pairs, [("ring", s1), ("ring", s2), "ssdT"] + [("OT", h) for h in range(8)], [("psF", pz)])
                evac_mix(pz, dc)
                psf_free(pz)
        post_norm_residual(t, l, 2)

    def rest(t, l):
        conv(t)
        ssd_small(t)
        if t == "P":
            prepass(t, False, final_P)
            attn_V(t)
            for h in range(8):
                attn_head(t, h)
            for tb in range(4):
                ssd_main(t, tb)
        else:
            prepass(t, False, final_S1)
            x2_exchange()
            P.barrier()
            nop_ = lambda d, seg: None
            sched = {1: lambda: x2_receive(),
                     2: lambda: prepass(t, True, nop_, dirs=(0,)),
                     3: lambda: prepass(t, True, nop_, dirs=(1,)),
                     4: lambda: ssd_main(t, 0), 5: lambda: ssd_main(t, 1),
                     6: lambda: ssd_main(t, 2), 7: lambda: ssd_main(t, 3)}
            attn_V(t)
            for h in range(8):
                attn_head(t, h)
                if h in sched:
                    sched[h]()
        P.barrier()
        outproj(t, l)
        P.barrier()

    def layer0_mixer(l):
        P.op("dve", lambda e: e.memset(xbcpad["P"][:], 0.0), (), [("xbcpad", "P")])
        P.op("dve", lambda e: e.memset(wuk96[:], 0.0), (), ["wuk96"])
        cp("dve", wuk96[:, :, :, 0:64], wukv[:, :, :, 0:64], ["wukv", "wuk96"], ["wuk96"])
        for i in range(40):
            ts("dve", diagW[:, i, :], ident_f, cols[:, B_CONVW + i:B_CONVW + i + 1], None, ALU.mult, None,
               ["consts", "cols"], [("diagW", 0)])
        if RUN_S:
            P.op("dve", lambda e: e.memset(x1buf[32:64, 1024:1536], 0.0), (), ["x1buf"])
            P.op("dve", lambda e: e.memset(x1buf[64:128, 1024:1536], 0.0), (), ["x1buf"])
            inproj("S", l)
            x1_exchange()
            P.barrier()
        inproj("P", l)
        P.barrier()
        rest("P", l)
        if RUN_S:
            x1_receive()
            P.barrier()
            rest("S", l)

    RUN_S = not os.environ.get("KNO_S")

    def layer1_mixer(l):
        WP = {"P": 544, "S": 528}
        hpad = {"P": av(0, [128, 8, 544]), "S": av(8704, [128, 8, 528])}
        invc = av(21504, [128, 4, 512], F32)
        pooledT = av(25600, [128, 8, 512])
        x3buf = av(29696, [128, 8, 16], F32)
        gE = av(29952, [128, 4, 128], F32)

        def data(ap3, t):
            if t == "S":
                return ap3[:, :, 8:520]
            return ap3.rearrange("p k (s c) -> p k s c", s=2)[:, :, :, 8:264]

        def data2(ap2, t):
            if t == "S":
                return ap2[:, 8:520]
            return ap2.rearrange("p (s c) -> p s c", s=2)[:, :, 8:264]

        def seg2(ap2, t):
            if t == "S":
                return ap2
            return ap2.rearrange("p (s c) -> p s c", s=2)

        def fill(t):
            if t == "P":
                hp4 = hpad[t][:].rearrange("p k (s c) -> p k s c", s=2)
                P.op("dve", lambda e: e.memset(hp4[:, :, :, 0:8], 0.0), (), [("hpad", t)])
                P.op("dve", lambda e: e.memset(hp4[:, :, :, 264:272], 0.0), (), [("hpad", t)])
            modulate(t, l, 0, 1, outf=lambda k, tm, t=t: (data2(hpad[t][:, k, :], t), seg2(tm[:], t), ("hpad", t)))

        def pool_tile(t, ti):
            W = WP[t]
            dma("sp", invc[:], invcnt_d[ti:ti + 1].to_broadcast([128, 4, NT]), (), ["invc"])
            hk = ("hpad", t)
            segs = [(0, 0, 256), (272, 256, 256)] if t == "P" else [(0, 0, 512)]
            for gi, w in enumerate((2, 4, 8, 16)):
                for kk in range(2):
                    kch = 2 * gi + kk
                    pz = psf()
                    for (sb0, oc, n) in segs:
                        mm_group(psF[pz][:, oc:oc + n],
                                 [(ident_b, hpad[t][:, kch, sb0 + 8 + off:sb0 + 8 + off + n]) for off in range(-(w // 2), w // 2)],
                                 ["consts_b", hk], [("psF", pz)])
                    tm = tmpA[kk]; tk = ("tmpA", kk)
                    tt("dve", tm[:], psF[pz][:, :], invc[:, gi, :], ALU.mult, [("psF", pz), "invc"], [tk])
                    psf_free(pz)
                    tt("dve", seg2(pooledT[:, kch, :], t), seg2(tm[:], t), data2(hpad[t][:, kch, :], t), ALU.subtract,
                       [tk, hk], [("pooledT", kch)])
            s = ring_load([(0, [128, 8, 256], pool_w[0].rearrange("g (k p) n -> p (g k) n", p=128))])
            pw = rview(s, 0, [128, 8, 256])
            for dc in range(8):
                gi, co = dc // 2, dc % 2
                pz = psf()
                mm_group(psF[pz][:, :], [(pw[:, gi * 2 + kc, co * 128:(co + 1) * 128], pooledT[:, 2 * gi + kc, :]) for kc in range(2)],
                         [("ring", s), ("pooledT", 2 * gi), ("pooledT", 2 * gi + 1)], [("psF", pz)])
                act(mixT[:, dc, :], psF[pz][:, :], AF.Copy, [("psF", pz), "cols"], ["mixT"],
                    scale=cols[:, B_PSC + dc:B_PSC + dc + 1])
                act(sqT[:, dc, :], mixT[:, dc, :], AF.Square, ["mixT"], [("sqT", dc)])
                psf_free(pz)
            post_norm_residual(t, l, 2)

        if RUN_S:
            fill("S")
            cp("dve", x3buf[:, :, 0:8], hpad["S"][:, :, 8:16], [("hpad", "S")], ["x3buf"])
            cp("dve", x3buf[:, :, 8:16], hpad["S"][:, :, 512:520], [("hpad", "S")], ["x3buf"])
            dma("sp", x3_in[:, :], x3buf[:].rearrange("p k c -> p (k c)"), ["x3buf"], ["x3_in"])
            coll(x3_in, x3_out, ["x3_in"], ["x3_out"])
        fill("P")
        pool_tile("P", 0)
        if RUN_S:
            dma("sp", gE[:], x3_out.rearrange("(r p) c -> p r c", p=128), ["x3_out"], ["gE"])
            for (dst0, c0, sb_) in ((0, 8, 4), (520, 0, 8)):
                dst = hpad["S"][:, :, dst0:dst0 + 8]
                for rp in range(4):
                    src = gE[:, rp, :].rearrange("p (k c) -> p k c", k=8)[:, :, c0:c0 + 8]
                    if rp == 0:
                        ts("dve", dst, src, sel[:, sb_ + rp:sb_ + rp + 1], None, ALU.mult, None, ["gE", "sel"], [("hpad", "S")])
                    else:
                        stt("dve", dst, src, sel[:, sb_ + rp:sb_ + rp + 1], dst, ALU.mult, ALU.add,
                            ["gE", "sel", ("hpad", "S")], [("hpad", "S")])
            pool_tile("S", 1)
    for l in range(n_layers):
        if l % 2 == 0:
            layer0_mixer(l)
        else:
            layer1_mixer(l)
        P.barrier()
        if not os.environ.get('KSKIP_FFN'):
            ffn(l)
        P.barrier()

    P.barrier()
    for t in ("P", "S"):
        for tb in range(4):
            xl = xld[rr["xld"] % 2]; xk = ("xld", rr["xld"] % 2); rr["xld"] += 1
            for half in range(2):
                pz = psf()
                for q in range(4):
                    k = half * 4 + q
                    tr(psF[pz][:, q * 128:(q + 1) * 128], xT[t][:, k, tb * 128:(tb + 1) * 128], ident_f,
                       ["consts", ("xT", t)], [("psF", pz)], signal=(q == 3))
                alt_cp(xl[:, half * 512:(half + 1) * 512], psF[pz][:, :], [("psF", pz)], [xk])
                psf_free(pz)
            dma("sp", yout[t][tb * 128:(tb + 1) * 128, :], xl[:], [xk], [("yout", t)])
    P.final_wait()
    P.emit()
    es.close()
    return nc


def _consts():
    c = np.zeros((128, 512), np.float32)
    c[:, 0:128] = np.eye(128)
    t = np.arange(128)
    c[:, 128:256] = (t[:, None] <= t[None, :])
    c[:, 256:384] = (t[:, None] >= t[None, :])
    c[:, 384:512] = 1.0
    selc = np.zeros((8, 8, 128), np.float32)
    for h in range(8):
        selc[h, h, :] = 1.0
    padI = np.zeros((32, 96), np.float32)
    padI[np.arange(32), 64 + np.arange(32)] = 1.0
    return c, selc.reshape(8, 1024), padI


def _rope(pos):
    half = 16
    inv_freq = np.power(10000.0, -np.arange(0, half, 2, dtype=np.float64) / half)
    row = (pos // 64).astype(np.float64); col = (pos % 64).astype(np.float64)
    ang = np.concatenate([row[:, None] * inv_freq, col[:, None] * inv_freq], axis=-1)
    cos = np.cos(ang).T; sin = np.sin(ang).T
    r = np.zeros((32, 2, len(pos)), np.float32)
    r[0:16, 0] = cos; r[16:32, 0] = cos; r[0:16, 1] = sin; r[16:32, 1] = sin
    return r


def _invcnt(seg_len, nseg, lo_pad, hi_pad):
    out = np.zeros((4, NT), np.float32)
    for gi, w in enumerate((2, 4, 8, 16)):
        for s in range(nseg):
            L = seg_len
            t = np.arange(L)
            lo = t - w // 2; hi = t + w // 2
            if lo_pad: lo = np.clip(lo, 0, None)
            if hi_pad: hi = np.clip(hi, None, L)
            out[gi, s * L:(s + 1) * L] = 1.0 / (hi - lo)
    return out


_NC_CACHE = {}


def kernel(**inp):
    inp = {k: np.ascontiguousarray(np.asarray(v)) for k, v in inp.items()}
    if "nc" not in _NC_CACHE:
        _NC_CACHE["nc"] = build()
    nc = _NC_CACHE["nc"]
    c, selc, padI = _consts()
    shared = {k: inp[k] for k in (
          "w_in_ab",
         "w_uq",  "w_ukv",
         "w_out_ab", "pool_w",
        "ffn_w_gate", "ffn_w_up", "ffn_w_down")}
    shared.update(consts=c, selc=selc, padI=padI)
    bc_all = np.concatenate([inp["kv_norm"].reshape(-1), inp["ssd_norm"].reshape(-1), inp["ssd_dt_bias_fwd"].reshape(-1),
                             inp["ssd_dt_bias_bwd"].reshape(-1), inp["ssd_a_log_fwd"].reshape(-1),
                             inp["ssd_a_log_bwd"].reshape(-1), inp["ssd_d"].reshape(-1)]).reshape(1, 808)
    shared["bc_all"] = bc_all
    stg_common = [inp["norm_pre_mix"].reshape(16, 128), inp["norm_post_mix"].reshape(16, 128),
                  inp["norm_pre_ffn"].reshape(16, 128), inp["norm_post_ffn"].reshape(16, 128),
                  inp["b_mod"].reshape(96, 128)]
    stg_tail = [inp["q_norm"].reshape(2, 128), inp["kv_norm"].reshape(2, 128), inp["ssd_conv_b"].reshape(8, 128),
                inp["ssd_conv_w"][0].reshape(40, 128), inp["ssd_norm"].reshape(4, 128), inp["pool_scale"].reshape(8, 128),
                np.zeros((16, 128), np.float32)]
    in_maps = []
    for core in range(8):
        b, r = core // 4, core % 4
        m = dict(shared)
        m["xp"] = inp["x_prompt"][2 * core:2 * core + 2].reshape(NT, D)
        m["xs"] = inp["x_sample"][b, r * NT:(r + 1) * NT]
        m["stg_all"] = np.concatenate(stg_common + [np.stack([inp["c_ctx"], inp["c"][b]]).reshape(16, 128)] + stg_tail, axis=0)
        m["w_mod_sl"] = inp["w_mod"][:, :, r * 1536:(r + 1) * 1536]
        m["cache_ckv"] = inp["cache_mla_ckv"][b, 0]
        m["cache_kpe"] = inp["cache_mla_krope"][b, 0]
        m["h0f"] = inp["state_ssd_fwd"][b, 0].reshape(512, 128)
        m["h0b"] = inp["state_ssd_bwd"][b, 0].reshape(512, 128)
        m["rope"] = _rope(np.arange(r * NT, (r + 1) * NT))
        sel = np.zeros((128, 16), np.float32)
        sel[:, r] = 1.0
        if r > 0: sel[:, 4 + r - 1] = 1.0
        if r < 3: sel[:, 8 + r + 1] = 1.0
        m["sel"] = sel
        ic = np.zeros((2, 4, NT), np.float32)
        ic[0] = _invcnt(256, 2, True, True)
        ic[1] = _invcnt(NT, 1, r == 0, r == 3)
        m["invcnt"] = ic
        in_maps.append({k: np.ascontiguousarray(v, dtype=np.float32) for k, v in m.items()})
    ncores = int(os.environ.get("KCORES", "8"))
    res = run_bass_kernel_spmd(nc, in_maps[:ncores], core_ids=list(range(ncores)))
    R = list(res.results)
    while len(R) < 8:
        R.append(R[0])
    yp = np.concatenate([R[c_]["yp"].reshape(2, 256, D) for c_ in range(8)], axis=0)
    ys = np.stack([np.concatenate([R[b * 4 + r]["ys"] for r in range(4)], axis=0) for b in range(2)])
    ockv = np.concatenate([R[c_]["ockv"].reshape(2, 1, 256, 256) for c_ in range(8)], axis=0)
    okpe = np.concatenate([R[c_]["okpe"].reshape(2, 1, 256, 32) for c_ in range(8)], axis=0)
    ohf = np.concatenate([R[c_]["ohf"].reshape(2, 1, 8, 64, 128) for c_ in range(8)], axis=0)
    ohb = np.concatenate([R[c_]["ohb"].reshape(2, 1, 8, 64, 128) for c_ in range(8)], axis=0)
    return (yp.astype(np.float32), ys.astype(np.float32), ockv.astype(np.float32), okpe.astype(np.float32),
            ohf.astype(np.float32), ohb.astype(np.float32))
```

```python
import numpy as np
import concourse.bass as bass
import concourse.mybir as mybir

F32 = mybir.dt.float32
BF16 = mybir.dt.bfloat16
AF = mybir.ActivationFunctionType
ALU = mybir.AluOpType
AX = mybir.AxisListType

ENGS = ("pe", "act", "dve", "pool", "sp")


class Prog:
    def __init__(self, nc, n_dma_sems=24):
        self.nc = nc
        self.items = {e: [] for e in ENGS}
        self.cnt = {e: 0 for e in ENGS}
        self.waited = {e: {} for e in ENGS}
        self.lastw = {}
        self.readers = {}
        self.n_dma_sems = n_dma_sems
        self.dma_cnt = [0] * (n_dma_sems + 4)
        self.dma_i = 0
        self.dma_q = 0
        self.dma_c = 0
        self.nops = {e: 0 for e in ENGS}

    def _deps(self, eng, reads, writes):
        deps = []
        for r in reads:
            s = self.lastw.get(r)
            if s is not None:
                deps.append((s, "raw"))
            if isinstance(r, tuple) and r[0] in ("psF", "psB"):
                for s in self.readers.get(r, ()):
                    if s[2] != eng:
                        deps.append((s, "rar"))
        for w in writes:
            s = self.lastw.get(w)
            if s is not None:
                deps.append((s, "waw"))
            for s in self.readers.get(w, ()):
                deps.append((s, "war"))
        out = {}
        for (sem, val, peng), kind in deps:
            if peng == eng:
                if eng == "pe":
                    continue
            if out.get(sem, -1) < val:
                out[sem] = val
        return out

    def _emit_waits(self, eng, deps):
        for sem, val in deps.items():
            if self.waited[eng].get(sem, -1) >= val:
                continue
            self.waited[eng][sem] = val
            self.items[eng].append(("wait", sem, val))

    def _record(self, sig, reads, writes):
        for r in reads:
            self.readers.setdefault(r, []).append(sig)
        for w in writes:
            self.lastw[w] = sig
            self.readers[w] = []

    def op(self, eng, fn, reads=(), writes=(), signal=True):
        deps = self._deps(eng, reads, writes)
        self._emit_waits(eng, deps)
        self.nops[eng] += 1
        if signal:
            self.cnt[eng] += 1
            sig = ("E_" + eng, self.cnt[eng], eng)
            self.items[eng].append(("op", fn, True))
            self._record(sig, reads, writes)
        else:
            self.items[eng].append(("op", fn, False))
            sig = ("E_" + eng, self.cnt[eng] + 1, eng)
            self._record(sig, reads, writes)
        return sig

    def dma(self, eng, fn, reads=(), writes=(), inc=16):
        half = self.n_dma_sems // 2
        if inc == 1:
            i = self.n_dma_sems + (self.dma_c % 4); self.dma_c += 1
        elif eng == "pool":
            i = half + (self.dma_q % half); self.dma_q += 1
        else:
            i = self.dma_i % half; self.dma_i += 1
        sem = "D_%d" % i
        deps = self._deps(eng, reads, writes)
        if self.dma_cnt[i] > 0:
            if deps.get(sem, -1) < self.dma_cnt[i]:
                deps[sem] = self.dma_cnt[i]
        self._emit_waits(eng, deps)
        self.dma_cnt[i] += inc
        sig = (sem, self.dma_cnt[i], None)
        self.items[eng].append(("dma", fn, sem, inc))
        self.nops[eng] += 1
        self._record(sig, reads, writes)
        return sig

    def barrier(self):
        allsig = {}
        for e in ENGS:
            if self.cnt[e] > 0:
                allsig["E_" + e] = self.cnt[e]
        for i in range(self.n_dma_sems + 4):
            if self.dma_cnt[i] > 0:
                allsig["D_%d" % i] = self.dma_cnt[i]
        for e in ENGS:
            d = dict(allsig)
            self._emit_waits(e, d)

    def final_wait(self, eng="sp"):
        self.barrier()

    def emit(self, extra_ctx=()):
        nc = self.nc
        import contextlib
        with contextlib.ExitStack() as st:
            sems = {}
            for e in ENGS:
                sems["E_" + e] = st.enter_context(nc.semaphore("E_" + e))
            for i in range(self.n_dma_sems + 4):
                sems["D_%d" % i] = st.enter_context(nc.semaphore("D_%d" % i))
            block = st.enter_context(nc.Block())
            items = self.items

            def run(engh, ename):
                for it in items[ename]:
                    if it[0] == "wait":
                        engh.wait_ge(sems[it[1]], it[2])
                    elif it[0] == "op":
                        ins = it[1](engh)
                        if it[2]:
                            ins.then_inc(sems["E_" + ename], 1)
                    else:
                        ins = it[1](engh)
                        if it[3] == 1:
                            ins.then_inc(sems[it[2]])
                        else:
                            ins.then_inc(sems[it[2]], it[3])

            @block.sync
            def _(e):
                run(e, "sp")

            @block.scalar
            def _(e):
                run(e, "act")

            @block.vector
            def _(e):
                run(e, "dve")

            @block.gpsimd
            def _(e):
                run(e, "pool")

            @block.tensor
            def _(e):
                run(e, "pe")

from contextlib import ExitStack
import os
from concourse.bass_utils import run_bass_kernel_spmd
import ml_dtypes

D = 1024
NT = 512
EPS = 1e-6
IN_AB = 2096
D_FF = 2816
SCALE = 96 ** -0.5
RING_SLOTS = 3
RING_ELEMS = 4096

B_NPM, B_NPO, B_NFR, B_NFO = 0, 16, 32, 48
B_BMOD = 64
B_CVEC = 160
B_QN, B_KVN = 176, 178
B_CONVB = 180
B_CONVW = 188
B_SSDN = 228
B_PSC = 232
N_ROWS = 240


def build(n_layers=2, dbg=False):
    nc = bass.Bass("TRN2", target_bir_lowering=False)
    P = Prog(nc)
    es = ExitStack()

    def din(name, shape, dt=F32):
        return nc.dram_tensor(name, list(shape), dt, kind="ExternalInput").ap()

    def dout(name, shape, dt=F32):
        return nc.dram_tensor(name, list(shape), dt, kind="ExternalOutput").ap()

    def sb(name, shape, dt=F32):
        return es.enter_context(nc.sbuf_tensor("sb_" + name, list(shape), dt))

    xin = {"P": din("xp", [NT, D]), "S": din("xs", [NT, D])}
    cache_ckv = din("cache_ckv", [256, 256])
    cache_kpe = din("cache_kpe", [256, 32])
    h0_d = [din("h0f", [512, 128]), din("h0b", [512, 128])]
    w_mod = din("w_mod_sl", [2, D, 1536])
    w_in = din("w_in_ab", [1, D, IN_AB])
    w_uq = din("w_uq", [1, 256, 768]); w_ukv = din("w_ukv", [1, 256, 1024])
    w_out = din("w_out_ab", [1, D, D])
    pool_w = din("pool_w", [1, 4, 256, 256])
    w_gate = din("ffn_w_gate", [2, D, D_FF]); w_up = din("ffn_w_up", [2, D, D_FF])
    w_down = din("ffn_w_down", [2, D_FF, D])
    consts_d = din("consts", [128, 512])
    selc_d = din("selc", [8, 1024])
    padI_d = din("padI", [32, 96])
    rope_d = din("rope", [32, 2, NT])
    sel_d = din("sel", [128, 16])
    invcnt_d = din("invcnt", [2, 4, NT])

    yout = {"P": dout("yp", [NT, D]), "S": dout("ys", [NT, D])}
    ockv = dout("ockv", [NT, 256]); okpe = dout("okpe", [NT, 32])
    ohs = [dout("ohf", [2, 512, 128]), dout("ohb", [2, 512, 128])]

    NX1 = 1024 + 512 + 32
    x1_in = nc.dram_tensor("x1_in", [128, NX1], BF16, kind="Internal").ap()
    x1_out = nc.dram_tensor("x1_out", [4 * 128, NX1], BF16, kind="Internal").ap()
    x2_in = nc.dram_tensor("x2_in", [128, 1040], F32, kind="Internal").ap()
    x2_out = nc.dram_tensor("x2_out", [4 * 128, 1040], F32, kind="Internal").ap()
    x3_in = nc.dram_tensor("x3_in", [128, 128], F32, kind="Internal").ap()
    x3_out = nc.dram_tensor("x3_out", [4 * 128, 128], F32, kind="Internal").ap()
    GROUPS = [[0, 1, 2, 3], [4, 5, 6, 7]]

    consts = sb("consts", [128, 512]); consts_b = sb("consts_b", [128, 512], BF16)
    ident_f = consts[:, 0:128]; U_f = consts[:, 128:256]; L_f = consts[:, 256:384]; ones_f = consts[:, 384:512]
    ident_b = consts_b[:, 0:128]; ones_b = consts_b[:, 384:512]
    selc = sb("selc", [8, 1024])
    padI_f = sb("padI_f", [32, 96]); padI = sb("padI", [32, 96], BF16)
    rope_lo = sb("rope_lo", [32, 2, NT], BF16); rope_hi = sb("rope_hi", [96, 2, NT], BF16)
    sel = sb("sel", [128, 16])
    stg = sb("stg", [128, 2, 128]); cols = sb("cols", [128, 256])
    bcp = sb("bcp", [128, 808])
    kvn_bc = bcp[:, 0:256]; ssdn_bc = bcp[:, 256:768]; sm_bc = bcp[:, 768:808]
    A_bc = sb("A_bc", [128, 16]); dsk_bc = sb("dsk_bc", [128, 512], BF16)
    modT = sb("modT", [128, 2, 48, 2])
    csil = sb("csil", [128, 8, 2], BF16)
    mcol = sb("mcol", [128, 2, 6, 8, 2])
    qn32 = sb("qn32", [128, 2]); kvn32 = sb("kvn32", [128, 2])
    xT = {"P": sb("xT_P", [128, 8, NT]), "S": sb("xT_S", [128, 8, NT])}
    hT2 = sb("hT2", [128, 2, 8, NT], BF16)
    hT = {"P": hT2[:, 0], "S": hT2[:, 1]}
    sqT = sb("sqT", [128, 8, NT], BF16)
    rstd = sb("rstd", [128, NT]); tmpA = [sb("tmpA0", [128, NT]), sb("tmpA1", [128, NT])]
    mixT = sb("mixT", [128, 8, NT])
    xld = [mixT[:, 0:2, :].rearrange("p a b -> p (a b)"), mixT[:, 2:4, :].rearrange("p a b -> p (a b)")]
    ring = [sb("ring%d" % i, [128, RING_ELEMS], BF16) for i in range(RING_SLOTS)]
    wsm = sb("wsm", [128, 8, 48], BF16)
    wuq = sb("wuq", [128, 2, 8, 96], BF16); wuq_sw = sb("wuq_sw", [128, 2, 8, 96], BF16)
    wukv = sb("wukv", [128, 2, 8, 128], BF16)
    ARENA = 75 * 1024 // 2
    arena = sb("arena", [128, ARENA], BF16)
    mrow = arena[0:2, 0:6144].bitcast(F32).rearrange("p (l n) -> p l n", l=2)
    Gm = arena[0:16, 12288:12288 + 3072].bitcast(F32)

    rr = {"ring": 0, "tmp": 0, "xld": 0, "alt": 0}

    psF = [es.enter_context(nc.psum_tensor("psF%d" % i, [128, 512], F32)) for i in range(6)]
    psB = [es.enter_context(nc.psum_tensor("psB%d" % i, [128, 1024], BF16)) for i in range(2)]
    freeF = list(range(6)); freeB = [0, 1]

    def psf():
        i = freeF.pop(0); return i

    def psf_free(i):
        freeF.append(i)

    def psb():
        i = freeB.pop(0); return i

    def psb_free(i):
        freeB.append(i)

    def act(out, in_, func, reads, writes, bias=None, scale=None, accum=None):
        kw = {}
        if bias is not None: kw["bias"] = bias
        if scale is not None: kw["scale"] = scale
        if accum is not None: kw["accum_out"] = accum
        return P.op("act", lambda e: e.activation(out=out, in_=in_, func=func, **kw), reads, writes)

    def tt(eng, out, in0, in1, op, reads, writes):
        return P.op(eng, lambda e: e.tensor_tensor(out=out, in0=in0, in1=in1, op=op), reads, writes)

    def ts(eng, out, in0, s1, s2, op0, op1, reads, writes):
        if s2 is None:
            return P.op(eng, lambda e: e.tensor_scalar(out=out, in0=in0, scalar1=s1, scalar2=None, op0=op0), reads, writes)
        return P.op(eng, lambda e: e.tensor_scalar(out=out, in0=in0, scalar1=s1, scalar2=s2, op0=op0, op1=op1), reads, writes)

    def stt(eng, out, in0, scalar, in1, op0, op1, reads, writes):
        return P.op(eng, lambda e: e.scalar_tensor_tensor(out=out, in0=in0, scalar=scalar, in1=in1, op0=op0, op1=op1), reads, writes)

    def cp(eng, out, in_, reads, writes):
        if eng == "act":
            return P.op("act", lambda e: e.copy(out=out, in_=in_), reads, writes)
        return P.op(eng, lambda e: e.tensor_copy(out=out, in_=in_), reads, writes)

    def rsqrt_to(out, in_, c, reads, writes, scale=1.0):
        act(out, in_, AF.Ln, reads, writes, bias=float(c), scale=float(scale))
        act(out, out, AF.Exp, list(writes), list(writes), scale=-0.5)

    def alt_cp(out, in_, reads, writes):
        rr["alt"] ^= 1
        return cp("act" if rr["alt"] else "dve", out, in_, reads, writes)

    def mm(out, lhsT, rhs, start, stop, reads, writes, signal):
        return P.op("pe", lambda e: e.matmul(out, lhsT=lhsT, rhs=rhs, start=start, stop=stop), reads, writes, signal=signal)

    def mm_group(out, pairs, reads, writes):
        n = len(pairs)
        for i, (l, r) in enumerate(pairs):
            mm(out, l, r, i == 0, i == n - 1, reads if i == 0 else (), writes, i == n - 1)

    def tr(out, in_, ident, reads, writes, signal=True):
        return P.op("pe", lambda e: e.transpose(out=out, in_=in_, identity=ident), reads, writes, signal=signal)

    def dma(eng, out, in_, reads, writes):
        return P.dma(eng, lambda e: e.dma_start(out=out, in_=in_), reads, writes)

    def ring_load(parts, eng="pool"):
        s = rr["ring"] % RING_SLOTS
        rr["ring"] += 1
        for (off, shp, src) in parts:
            n = 1
            for v in shp[1:]:
                n *= v
            dst = ring[s][0:shp[0], off:off + n]
            if len(shp) == 3:
                dst = dst.rearrange("p (a b) -> p a b", a=shp[1])
            dma(eng, dst, src, (), [("ring", s)])
        return s

    def rview(s, off, shp):
        n = 1
        for v in shp[1:]:
            n *= v
        v = ring[s][0:shp[0], off:off + n]
        if len(shp) == 3:
            v = v.rearrange("p (a b) -> p a b", a=shp[1])
        return v

    dma("sp", consts[:], consts_d[:, :], (), ["consts"])
    dma("sp", selc[:], selc_d[:, :], (), ["selc"])
    dma("sp", padI_f[:], padI_d[:, :], (), ["padI_f"])
    dma("pool", rope_lo[:], rope_d[:, :, :], (), ["rope_lo"])
    dma("pool", rope_hi[64:96], rope_d[:, :, :], (), ["rope_hi"])
    dma("sp", sel[:], sel_d[:, :], (), ["sel"])
    cp("dve", consts_b[:], consts[:], ["consts"], ["consts_b"])
    cp("dve", padI[:], padI_f[:], ["padI_f"], ["padI"])

    def stage_rows(base, src2d, nrows):
        r = 0
        while r < nrows:
            row = base + r
            t, rin = row // 128, row % 128
            n = min(nrows - r, 128 - rin)
            dma("sp", stg[rin:rin + n, t, :], src2d[r:r + n, :], (), [("stg", t)])
            r += n

    stg_d = din("stg_all", [256, 128])
    bc_d = din("bc_all", [1, 808])
    dma("sp", stg[:, 0, :], stg_d[0:128, :], (), [("stg", 0)])
    dma("sp", stg[0:112, 1, :], stg_d[128:240, :], (), [("stg", 1)])
    pz = psf()
    tr(psF[pz][:, 0:128], stg[:, 0, :], ident_f, ["consts", ("stg", 0)], [("psF", pz)])
    tr(psF[pz][:, 128:128 + 112], stg[0:112, 1, :], ident_f[0:112, 0:112], ["consts", ("stg", 1)], [("psF", pz)])
    cp("dve", cols[:, 0:240], psF[pz][:, 0:240], [("psF", pz)], ["cols"])
    psf_free(pz)

    dma("sp", bcp[:], bc_d[0:1, :].to_broadcast([128, 808]), (), ["kvn_bc", "ssdn_bc", "sm_bc"])
    act(A_bc[:], sm_bc[:, 16:32], AF.Exp, ["sm_bc"], ["A_bc"])
    ts("dve", A_bc[:], A_bc[:], -1.0, None, ALU.mult, None, ["A_bc"], ["A_bc"])
    cp("dve", dsk_bc[:].rearrange("p (h c) -> p h c", h=8),
       sm_bc[:, 32:40].unsqueeze(2).to_broadcast([128, 8, 64]), ["sm_bc"], ["dsk_bc"])
    ts("dve", qn32[:], cols[:, B_QN:B_QN + 2], 16.0, None, ALU.mult, None, ["cols"], ["qn32"])
    ts("dve", kvn32[:], cols[:, B_KVN:B_KVN + 2], 16.0, None, ALU.mult, None, ["cols"], ["kvn32"])

    dma("pool", wsm[:, :, 0:32], w_in[0, :, 512:544].rearrange("(k p) n -> p k n", p=128), (), ["wsm"])
    dma("pool", wsm[:, :, 32:48], w_in[0, :, 2080:2096].rearrange("(k p) n -> p k n", p=128), (), ["wsm"])
    dma("pool", wuq[:].rearrange("p k h c -> p k (h c)"), w_uq[0].rearrange("(k p) n -> p k n", p=128), (), ["wuq"])
    dma("pool", wukv[:].rearrange("p k h c -> p k (h c)"), w_ukv[0].rearrange("(k p) n -> p k n", p=128), (), ["wukv"])
    P.op("dve", lambda e: e.memset(wuq_sw[:], 0.0), (), ["wuq_sw"])
    ts("dve", wuq_sw[:, :, :, 64:80], wuq[:, :, :, 80:96], -1.0, None, ALU.mult, None, ["wuq"], ["wuq_sw"])
    cp("dve", wuq_sw[:, :, :, 80:96], wuq[:, :, :, 64:80], ["wuq"], ["wuq_sw"])
    wsm_sw = sb("wsm_sw", [128, 8, 32], BF16)
    ts("dve", wsm_sw[:, :, 0:16], wsm[:, :, 16:32], -1.0, None, ALU.mult, None, ["wsm"], ["wsm_sw"])
    cp("dve", wsm_sw[:, :, 16:32], wsm[:, :, 0:16], ["wsm"], ["wsm_sw"])

    xm_in = nc.dram_tensor("xm_in", [4, 1536], F32, kind="Internal").ap()
    xm_out = nc.dram_tensor("xm_out", [16, 1536], F32, kind="Internal").ap()
    act(csil[:].rearrange("p k v -> p v k"),
        cols[:, B_CVEC:B_CVEC + 16].rearrange("p (v k) -> p v k", v=2), AF.Silu, ["cols"], ["csil"])
    for l in range(2):
        for cb in range(4):
            s = ring_load([(0, [128, 8, 384], w_mod[l, :, cb * 384:(cb + 1) * 384].rearrange("(k p) n -> p k n", p=128))])
            wv = rview(s, 0, [128, 8, 384])
            pz = psf()
            mm_group(psF[pz][0:2, 0:384], [(csil[:, k, :], wv[:, k, :]) for k in range(8)],
                     ["csil", ("ring", s)], [("psF", pz)])
            alt_cp(mrow[:, l, cb * 384:(cb + 1) * 384], psF[pz][0:2, 0:384], [("psF", pz)], ["mrow"])
            psf_free(pz)
    dma("sp", xm_in.rearrange("(v l) c -> v (l c)", v=2), mrow[:].rearrange("p l n -> p (l n)"), ["mrow"], ["xm_in"])
    P.dma("pool", lambda e: e.collective_compute("AllGather", ALU.bypass, replica_groups=GROUPS,
                                                 ins=[xm_in.opt()], outs=[xm_out.opt()]), ["xm_in"], ["xm_out"], inc=1)

    for t in ("P", "S"):
        for tb in range(4):
            xl = xld[rr["xld"] % 2]; xk = ("xld", rr["xld"] % 2); rr["xld"] += 1
            dma("sp", xl[:], xin[t][tb * 128:(tb + 1) * 128, :], (), [xk])
            for half in range(2):
                pz = psf()
                for q in range(4):
                    k = half * 4 + q
                    tr(psF[pz][:, q * 128:(q + 1) * 128], xl[:, k * 128:(k + 1) * 128], ident_f,
                       ["consts", xk], [("psF", pz)], signal=(q == 3))
                alt_cp(xT[t][:, half * 4:half * 4 + 4, tb * 128:(tb + 1) * 128],
                       psF[pz][:].rearrange("p (q c) -> p q c", q=4), [("psF", pz)], [("xT", t)])
                psf_free(pz)


    dma("sp", Gm[:, :], xm_out[:, :], ["xm_out"], ["Gm"])
    pz = psf()
    for cb in range(12):
        tr(psF[pz][:, cb * 16:(cb + 1) * 16], Gm[:, cb * 128:(cb + 1) * 128], ident_f[0:16, 0:16],
           ["consts", "Gm"], [("psF", pz)], signal=(cb == 11))
    pv = psF[pz][:, 0:192].rearrange("p (cb r v l) -> p r cb v l", cb=12, r=4, v=2, l=2)
    for l in range(2):
        for v2 in range(2):
            tt("dve", modT[:, l, :, v2].rearrange("p (r cb) -> p r cb", r=4), pv[:, :, :, v2, l],
               cols[:, B_BMOD + l * 48:B_BMOD + (l + 1) * 48].rearrange("p (r cb) -> p r cb", r=4), ALU.add,
               [("psF", pz), "cols"], [("modT", l)])
    psf_free(pz)
    for l in range(2):
        def ncol(base):
            return cols[:, base + l * 8:base + (l + 1) * 8].unsqueeze(2).to_broadcast([128, 8, 2])
        for (kind, jscale, nbase) in ((0, 1, B_NPM), (3, 4, B_NFR)):
            ts("dve", mcol[:, l, kind], modT[:, l, jscale * 8:(jscale + 1) * 8, :], 1.0, 32.0, ALU.add, ALU.mult,
               [("modT", l)], [("mcol", l)])
            tt("dve", mcol[:, l, kind], mcol[:, l, kind], ncol(nbase), ALU.mult, [("mcol", l), "cols"], [("mcol", l)])
        for (kind, jsh) in ((1, 0), (4, 3)):
            cp("dve", mcol[:, l, kind], modT[:, l, jsh * 8:(jsh + 1) * 8, :], [("modT", l)], [("mcol", l)])
        for (kind, jg, nbase) in ((2, 2, B_NPO), (5, 5, B_NFO)):
            stt("dve", mcol[:, l, kind], modT[:, l, jg * 8:(jg + 1) * 8, :], 32.0, ncol(nbase), ALU.mult, ALU.mult,
                [("modT", l), "cols"], [("mcol", l)])


    P.barrier()
    VI = {"P": 0, "S": 1}

    def rstd_from_sq(nchunks, scale_const, reads):
        pz = psf()
        mm_group(psF[pz][:, :], [(ones_b, sqT[:, k, :]) for k in range(nchunks)], ["consts_b"] + [("sqT", k_) for k_ in range(nchunks)] + list(reads), [("psF", pz)])
        rsqrt_to(rstd[:], psF[pz][:, :], scale_const * EPS, [("psF", pz)], ["rstd"])
        psf_free(pz)

    def modulate(t, l, kind_gs, kind_sh, outf=None):
        v = VI[t]
        act(sqT[:], xT[t][:], AF.Square, [("xT", t)], [("sqT", k_) for k_ in range(8)])
        rstd_from_sq(8, 1024.0, [])
        for k in range(8):
            tm = tmpA[rr["tmp"] % 2]; tk = ("tmpA", rr["tmp"] % 2); rr["tmp"] += 1
            stt("dve", tm[:], xT[t][:, k, :], mcol[:, l, kind_gs, k, v:v + 1], rstd[:], ALU.mult, ALU.mult,
                [("xT", t), ("mcol", l), "rstd"], [tk])
            if outf is None:
                act(hT[t][:, k, :], tm[:], AF.Identity, [tk, ("mcol", l)], [("hT", t)], bias=mcol[:, l, kind_sh, k, v:v + 1])
            else:
                o_ap, i_ap, o_key = outf(k, tm)
                act(o_ap, i_ap, AF.Identity, [tk, ("mcol", l)], [o_key], bias=mcol[:, l, kind_sh, k, v:v + 1])

    def post_norm_residual(t, l, kind_g):
        v = VI[t]
        rstd_from_sq(8, 1024.0, [])
        for k in range(8):
            tm = tmpA[rr["tmp"] % 2]; tk = ("tmpA", rr["tmp"] % 2); rr["tmp"] += 1
            stt("dve", tm[:], mixT[:, k, :], mcol[:, l, kind_g, k, v:v + 1], rstd[:], ALU.mult, ALU.mult,
                ["mixT", ("mcol", l), "rstd"], [tk])
            tt("dve", xT[t][:, k, :], xT[t][:, k, :], tm[:], ALU.add, [("xT", t), tk], [("xT", t)])

    def evac_mix(pz, k):
        cp("dve", mixT[:, k, :], psF[pz][:, :], [("psF", pz)], ["mixT"])
        act(sqT[:, k, :], mixT[:, k, :], AF.Square, ["mixT"], [("sqT", k)])

    def ffn(l):
        actT = arena[:, 0:22 * 2 * NT].rearrange("p (f n) -> p f n", f=22)
        for t in ("P", "S"):
            modulate(t, l, 3, 4)
        STAGE = int(os.environ.get('KFFN_STAGE', '3'))
        if STAGE < 2:
            return
        for nb in range(11):
            s = ring_load([(0, [128, 8, 256], w_gate[l, :, nb * 256:(nb + 1) * 256].rearrange("(k p) n -> p k n", p=128)),
                           (2048, [128, 8, 256], w_up[l, :, nb * 256:(nb + 1) * 256].rearrange("(k p) n -> p k n", p=128))])
            wg = rview(s, 0, [128, 8, 256]); wu = rview(s, 2048, [128, 8, 256])
            for c in range(2):
                f = nb * 2 + c
                for ti, t in enumerate(("P", "S")):
                    pg = psf(); pu = psf()
                    mm_group(psF[pg][:, :], [(wg[:, k, c * 128:(c + 1) * 128], hT[t][:, k, :]) for k in range(8)],
                             [("ring", s), ("hT", t)], [("psF", pg)])
                    mm_group(psF[pu][:, :], [(wu[:, k, c * 128:(c + 1) * 128], hT[t][:, k, :]) for k in range(8)],
                             [("ring", s), ("hT", t)], [("psF", pu)])
                    tm = tmpA[rr["tmp"] % 2]; tk = ("tmpA", rr["tmp"] % 2); rr["tmp"] += 1
                    KGU = int(os.environ.get("KGU", "0"))
                    if KGU in (0, 1):
                        act(tm[:], psF[pg][:, :], AF.Silu, [("psF", pg)], [tk])
                    if KGU in (0, 2):
                        tt("dve", actT[:, f, ti * NT:(ti + 1) * NT], psF[pu][:, :], tm[:], ALU.mult,
                           [tk, ("psF", pu)], [("actT", f)])
                    psf_free(pg); psf_free(pu)
        for ti, t in enumerate(("P", "S")):
            pass
        if STAGE < 3:
            return
        wdb = [arena[:, 26624 + i * 5632:26624 + (i + 1) * 5632].rearrange("p (f n) -> p f n", f=22) for i in range(2)]
        for db in range(4):
            wb = wdb[db % 2]; wk = ("wdblk", db % 2)
            for (f0, f1) in ((0, 11), (11, 22)):
                dma("pool", wb[:, f0:f1, :],
                    w_down[l, f0 * 128:f1 * 128, db * 256:(db + 1) * 256].rearrange("(f p) n -> p f n", p=128),
                    (), [wk])
            for d2 in range(2):
                dc = db * 2 + d2
                for ti, t in enumerate(("P", "S")):
                    pz = psf()
                    mm_group(psF[pz][:, :], [(wb[:, f, d2 * 128:(d2 + 1) * 128], actT[:, f, ti * NT:(ti + 1) * NT]) for f in range(22)],
                             [wk] + [("actT", f) for f in range(22)], [("psF", pz)])
                    mb = mixT2[t]
                    mkeys = [("mix2", t), ("hT", "P"), ("hT", "S")] if t == "S" else [("mix2", t)]
                    cp("dve", mb[:, dc, :], psF[pz][:, :], [("psF", pz)], mkeys)
                    act(sq2[t][:, dc, :], mb[:, dc, :], AF.Square, mkeys, [("sq2", t)])
                    psf_free(pz)
        for t in ("P", "S"):
            v = VI[t]
            pz = psf()
            mm_group(psF[pz][:, :], [(ones_b, sq2[t][:, k, :]) for k in range(8)], ["consts_b", ("sq2", t)], [("psF", pz)])
            rsqrt_to(rstd[:], psF[pz][:, :], 1024.0 * EPS, [("psF", pz)], ["rstd"])
            psf_free(pz)
            for k in range(8):
                tm = tmpA[rr["tmp"] % 2]; tk = ("tmpA", rr["tmp"] % 2); rr["tmp"] += 1
                stt("dve", tm[:], mixT2[t][:, k, :], mcol[:, l, 5, k, v:v + 1], rstd[:], ALU.mult, ALU.mult,
                    [("mix2", t), ("mcol", l), "rstd"], [tk])
                tt("dve", xT[t][:, k, :], xT[t][:, k, :], tm[:], ALU.add, [("xT", t), tk], [("xT", t)])

    mixT2 = {"P": mixT, "S": hT2[:].rearrange("p t k n -> p (t k n)").bitcast(F32).rearrange("p (k n) -> p k n", k=8)}
    sq2 = {"P": sqT, "S": arena[:, 22 * 2 * NT:22 * 2 * NT + 8 * NT].rearrange("p (k n) -> p k n", k=8)}

    def av(off, shape, dt=BF16):
        n = 1
        for v_ in shape[1:]:
            n *= v_
        if dt == F32:
            v = arena[0:shape[0], off:off + 2 * n].bitcast(F32)
        else:
            v = arena[0:shape[0], off:off + n]
        if len(shape) == 3:
            v = v.rearrange("p (a b) -> p a b", a=shape[1])
        elif len(shape) == 4:
            v = v.rearrange("p (a b c) -> p a b c", a=shape[1], b=shape[2])
        return v

    xbcpad = {"S": av(0, [128, 8, 516]), "P": av(19232, [128, 8, 520])}
    diagW = av(4128, [128, 40, 128])
    Vt = {"S": av(0, [128, 18, 512]), "P": av(19232, [128, 4, 512])}
    cqn = {"S": av(9248, [128, 2, 512]), "P": av(34656, [128, 2, 512])}
    Ksrc = av(12320, [128, 2, 2304]); kpeT = av(16928, [32, 2304])
    xconvT = av(23392, [128, 8, 512]); xtok = av(27488, [128, 4, 768]); hprev = av(30560, [128, 2, 4, 512])
    x1buf = av(27488, [128, 1568])
    gX = av(30560, [128, 4, 32])
    wuk96 = av(36864, [128, 2, 8, 96])
    smf = arena[:, 35680:36864].bitcast(F32)
    dtraw = {"P": smf[:, 0:64].rearrange("p (a b) -> p a b", a=4), "S": smf[:, 64:128].rearrange("p (a b) -> p a b", a=4)}
    dtw = smf[:, 128:192].rearrange("p (a b) -> p a b", a=4)
    a_ = smf[:, 192:256].rearrange("p (a b) -> p a b", a=4)
    smx = smf[:, 256:384].rearrange("p (a b) -> p a b", a=4)
    E_ = smf[:, 384:448].rearrange("p (a b) -> p a b", a=4)
    CD_ = smf[:, 448:512].rearrange("p (a b) -> p a b", a=4)
    ac2 = smf[:, 512:576].rearrange("p (a b) -> p a b", a=4)
    hflat = hT2[:].rearrange("p t k n -> p (t k n)")
    OT = hflat[0:64, 0:4096].rearrange("p (h n) -> p h n", h=8)
    ssdT = hflat[:, 4096:6144].rearrange("p (k n) -> p k n", k=4)
    zs = {"S": av(10272, [128, 4, 512]), "P": hflat[:, 6144:8192].rearrange("p (k n) -> p k n", k=4)}
    x2buf = hflat[:, 0:2080].bitcast(F32)
    sflat = sqT[:].rearrange("p k n -> p (k n)")
    Kh = sflat[0:96, 0:2304]; Qh = sflat[0:96, 2304:2816]
    PT = [sflat[:, 2816:3328], sflat[:, 3328:3840]]
    mflat = mixT[:].rearrange("p k n -> p (k n)")
    mbf = mflat.bitcast(BF16)
    MT = mbf[:, 0:2048].rearrange("p (h n) -> p h n", h=16)
    CBm = mflat[:, 1024:1536].rearrange("p (h n) -> p h n", h=4)
    ysb = mflat[:, 1536:2048]; yg = mflat[:, 2048:2560]
    xw = mbf[:, 5120:5632]; xd = mbf[:, 5632:6144]
    hrun = mflat[:, 3072:4096].rearrange("p (d n) -> p d n", d=2)
    hcand = mflat[:, 1536:2560].rearrange("p (d n) -> p d n", d=2)
    hin = mflat[:, 0:1024].rearrange("p (d n) -> p d n", d=2)
    acumT = rstd[0:8, 0:256]; stat2 = sb("stat2", [128, 16])
    gS = av(0, [128, 4, 1040], F32)
    SEGS = {"P": [[0, 1], [2, 3]], "S": [[0, 1, 2, 3]]}

    def h8(ap2d):
        return ap2d.rearrange("p (h c) -> p h c", h=8)

    def bc8(ap_8):
        return ap_8.unsqueeze(2).to_broadcast([128, 8, 64])

    def coll(in_ap, out_ap, rk, wk):
        P.dma("pool", lambda e: e.collective_compute("AllGather", ALU.bypass, replica_groups=GROUPS,
                                                     ins=[in_ap.opt()], outs=[out_ap.opt()]), rk, wk, inc=1)

    def inproj(t, l):
        modulate(t, l, 0, 1)
        hk = ("hT", t)
        sA = ring_load([(0, [128, 8, 512], w_in[0, :, 0:512].rearrange("(k p) n -> p k n", p=128))])
        wA = rview(sA, 0, [128, 8, 512])
        for j in range(4):
            pz = psf()
            mm_group(psF[pz][:, :], [(wA[:, k, j * 128:(j + 1) * 128], hT[t][:, k, :]) for k in range(8)],
                     [("ring", sA), hk], [("psF", pz)])
            cp("dve", mixT[:, j, :], psF[pz][:, :], [("psF", pz)], [("cqf", j)])
            act(sqT[:, j, :], mixT[:, j, :], AF.Square, [("cqf", j)], [("sqT", j)])
            psf_free(pz)
        for (j0, scl, dst) in ((0, qn32, None), (2, kvn32, None)):
            pz = psf()
            mm_group(psF[pz][:, :], [(ones_b, sqT[:, j0 + jj, :]) for jj in range(2)],
                     ["consts_b", ("sqT", j0), ("sqT", j0 + 1)], [("psF", pz)])
            rsqrt_to(rstd[:], psF[pz][:, :], 256.0 * EPS, [("psF", pz)], ["rstd"])
            psf_free(pz)
            for jj in range(2):
                if j0 == 0:
                    o, ok = cqn[t][:, jj, :], ("cqn", t)
                elif t == "P":
                    o, ok = Ksrc[:, jj, 0:512], "Ksrc"
                else:
                    o, ok = x1buf[:, jj * 512:(jj + 1) * 512], "x1buf"
                stt("dve", o, mixT[:, j0 + jj, :], scl[:, jj:jj + 1], rstd[:], ALU.mult, ALU.mult,
                    [("cqf", j0 + jj), "qn32", "kvn32", "rstd"], [ok])
        pz = psf()
        mm_group(psF[pz][0:32, :], [(wsm[:, k, 0:32], hT[t][:, k, :]) for k in range(8)], ["wsm", hk], [("psF", pz)])
        if t == "P":
            cp("dve", kpeT[0:32, 0:512], psF[pz][0:32, :], [("psF", pz)], ["kpeT"])
        else:
            pz2 = psf()
            mm_group(psF[pz2][0:32, :], [(wsm_sw[:, k, :], hT[t][:, k, :]) for k in range(8)], ["wsm_sw", hk], [("psF", pz2)])
            tt("dve", tmpA[0][0:32, :], psF[pz][0:32, :], rope_lo[:, 0, :], ALU.mult, [("psF", pz), "rope_lo"], [("tmpA", 0)])
            tt("dve", tmpA[1][0:32, :], psF[pz2][0:32, :], rope_lo[:, 1, :], ALU.mult, [("psF", pz2), "rope_lo"], [("tmpA", 1)])
            tt("dve", x1buf[0:32, 1024:1536], tmpA[0][0:32, :], tmpA[1][0:32, :], ALU.add, [("tmpA", 0), ("tmpA", 1)], ["x1buf"])
            psf_free(pz2)
        psf_free(pz)
        if t == "P":
            for tb in range(4):
                pz = psf()
                mm_group(psF[pz][:, 0:256], [(hT[t][:, k, tb * 128:(tb + 1) * 128], wA[:, k, 256:512]) for k in range(8)],
                         [("ring", sA), hk], [("psF", pz)])
                mm_group(psF[pz][:, 256:288], [(hT[t][:, k, tb * 128:(tb + 1) * 128], wsm[:, k, 0:32]) for k in range(8)],
                         ["wsm", hk], [("psF", pz)])
                ct = tmpA[tb % 2]; ck = ("tmpA", tb % 2)
                P.op("dve", lambda e: e.memset(stat2[:, 0:1], 0.0), (), ["stat2"])
                cp("dve", ct[:, 0:288], psF[pz][:, 0:288], [("psF", pz)], [ck])
                psf_free(pz)
                act(sqT[:, 4, 0:256], ct[:, 0:256], AF.Square, [ck, "stat2"], [("sqT", 4), "stat2"], accum=stat2[:, 0:1])
                rsqrt_to(stat2[:, 0:1], stat2[:, 0:1], EPS, ["stat2"], ["stat2"], scale=1.0 / 256.0)
                stt("dve", ct[:, 0:256], ct[:, 0:256], stat2[:, 0:1], kvn_bc[:], ALU.mult, ALU.mult,
                    [ck, "stat2", "kvn_bc"], [ck])
                dma("sp", ockv[tb * 128:(tb + 1) * 128, :], ct[:, 0:256], [ck], ["ockv"])
                dma("sp", okpe[tb * 128:(tb + 1) * 128, :], ct[:, 256:288], [ck], ["okpe"])
        sZ = ring_load([(0, [128, 8, 512], w_in[0, :, 544:1056].rearrange("(k p) n -> p k n", p=128))])
        wZ = rview(sZ, 0, [128, 8, 512])
        for tb in range(4):
            pz = psf()
            mm_group(psF[pz][:, :], [(hT[t][:, k, tb * 128:(tb + 1) * 128], wZ[:, k, :]) for k in range(8)],
                     [("ring", sZ), hk], [("psF", pz)])
            act(zs[t][:, tb, :], psF[pz][:, :], AF.Silu, [("psF", pz)], [("zs", t)])
            psf_free(pz)
        for tb in range(4):
            pz = psf()
            mm_group(psF[pz][:, 0:16], [(hT[t][:, k, tb * 128:(tb + 1) * 128], wsm[:, k, 32:48]) for k in range(8)],
                     ["wsm", hk], [("psF", pz)])
            tt("dve", dtraw[t][:, tb, :], psF[pz][:, 0:16], sm_bc[:, 0:16], ALU.add, [("psF", pz), "sm_bc"], [("dtraw", t)])
            psf_free(pz)
        for xb in range(2):
            sX = ring_load([(0, [128, 8, 512], w_in[0, :, 1056 + xb * 512:1568 + xb * 512].rearrange("(k p) n -> p k n", p=128))])
            wX = rview(sX, 0, [128, 8, 512])
            for jj in range(4):
                j = xb * 4 + jj
                pz = psf()
                mm_group(psF[pz][:, :], [(wX[:, k, jj * 128:(jj + 1) * 128], hT[t][:, k, :]) for k in range(8)],
                         [("ring", sX), hk], [("psF", pz)])
                if t == "P":
                    alt_cp(xbcpad["P"][:, j, :].rearrange("p (s c) -> p s c", s=2)[:, :, 2:258],
                           psF[pz][:, :].rearrange("p (s c) -> p s c", s=2), [("psF", pz)], [("xbcpad", t)])
                else:
                    alt_cp(xbcpad["S"][:, j, 2:514], psF[pz][:, :], [("psF", pz)], [("xbcpad", t)])
                psf_free(pz)
        if t == "S":
            xe = x1buf[:, 1536:1568].rearrange("p (j c) -> p j c", j=8)
            cp("dve", xe[:, :, 0:2], xbcpad["S"][:, :, 2:4], [("xbcpad", t)], ["x1buf"])
            cp("dve", xe[:, :, 2:4], xbcpad["S"][:, :, 512:514], [("xbcpad", t)], ["x1buf"])

    def x1_exchange():
        dma("sp", x1_in[:, :], x1buf[:, :], ["x1buf"], ["x1_in"])
        coll(x1_in, x1_out, ["x1_in"], ["x1_out"])

    def x1_receive():
        x1r = x1_out.rearrange("(r p) c -> p r c", p=128)
        for kc in range(2):
            dma("sp", Ksrc[:, kc, 256:2304].rearrange("p (r t) -> p r t", r=4), x1r[:, :, kc * 512:(kc + 1) * 512],
                ["x1_out"], ["Ksrc"])
        dma("sp", kpeT[0:32, 256:2304].rearrange("p (r t) -> p r t", r=4), x1r[0:32, :, 1024:1536], ["x1_out"], ["kpeT"])
        dma("sp", gX[:], x1r[:, :, 1536:1568], ["x1_out"], ["gX"])
        cc = mixT[:, 0:2, 0:288]
        for tbk in range(2):
            dma("sp", cc[:, tbk, 0:256], cache_ckv[tbk * 128:(tbk + 1) * 128, :], (), [("cc", tbk)])
            dma("sp", cc[:, tbk, 256:288], cache_kpe[tbk * 128:(tbk + 1) * 128, :], (), [("cc", tbk)])
        for tbk in range(2):
            pz = psf()
            tr(psF[pz][:, 0:128], cc[:, tbk, 0:128], ident_f, ["consts", ("cc", tbk)], [("psF", pz)], signal=False)
            tr(psF[pz][:, 128:256], cc[:, tbk, 128:256], ident_f, ["consts", ("cc", tbk)], [("psF", pz)], signal=False)
            tr(psF[pz][0:32, 256:384], cc[:, tbk, 256:288], ident_f, ["consts", ("cc", tbk)], [("psF", pz)])
            cp("dve", Ksrc[:, :, tbk * 128:(tbk + 1) * 128], psF[pz][:, 0:256].rearrange("p (k n) -> p k n", k=2),
               [("psF", pz)], ["Ksrc"])
            cp("dve", kpeT[0:32, tbk * 128:(tbk + 1) * 128], psF[pz][0:32, 256:384], [("psF", pz)], ["kpeT"])
            psf_free(pz)
        hal = tmpA[0][:, 0:32].rearrange("p (s j c) -> p s j c", s=2, j=8)
        for side, (c0, sb_) in enumerate(((2, 4), (0, 8))):
            for rp in range(4):
                src = gX[:, rp, :].rearrange("p (j c) -> p j c", j=8)[:, :, c0:c0 + 2]
                if rp == 0:
                    ts("dve", hal[:, side], src, sel[:, sb_ + rp:sb_ + rp + 1], None, ALU.mult, None, ["gX", "sel"], [("tmpA", 0)])
                else:
                    stt("dve", hal[:, side], src, sel[:, sb_ + rp:sb_ + rp + 1], hal[:, side], ALU.mult, ALU.add,
                        ["gX", "sel", ("tmpA", 0)], [("tmpA", 0)])
        cp("dve", xbcpad["S"][:, :, 0:2], hal[:, 0], [("tmpA", 0)], [("xbcpad", "S")])
        cp("dve", xbcpad["S"][:, :, 514:516], hal[:, 1], [("tmpA", 0)], [("xbcpad", "S")])

    def conv(t):
        segs = [(0, 0, 256), (260, 256, 256)] if t == "P" else [(0, 0, 512)]
        for j in range(8):
            pz = psf()
            for (pb_, oc, n) in segs:
                mm_group(psF[pz][:, oc:oc + n],
                         [(diagW[:, w * 8 + j, :], xbcpad[t][:, j, pb_ + w:pb_ + w + n]) for w in range(5)],
                         [("diagW", 0), ("xbcpad", t)], [("psF", pz)])
            act(xconvT[:, j, :], psF[pz][:, :], AF.Silu, [("psF", pz), "cols"], [("xconvT", j)],
                bias=cols[:, B_CONVB + j:B_CONVB + j + 1])
            psf_free(pz)
        for tb in range(4):
            pb_ = psb()
            for j in range(6):
                tr(psB[pb_][:, j * 128:(j + 1) * 128], xconvT[:, j, tb * 128:(tb + 1) * 128], ident_b,
                   ["consts_b", ("xconvT", j)], [("psB", pb_)], signal=(j == 5))
            alt_cp(xtok[:, tb, :], psB[pb_][:, 0:768], [("psB", pb_)], [("xtok", tb)])
            psb_free(pb_)

    def ssd_small(t):
        dk = ("dtraw", t)
        act(dtw[:], dtraw[t][:], AF.Exp, [dk], ["dtw"])
        act(dtw[:], dtw[:], AF.Ln, ["dtw"], ["dtw"], bias=1.0)
        tt("dve", a_[:], dtw[:], A_bc[:].unsqueeze(1).to_broadcast([128, 4, 16]), ALU.mult, ["dtw", "A_bc"], ["a_"])
        act(ac2[:], dtw[:], AF.Ln, ["dtw"], ["ac2"])
        for tb in range(4):
            pz = psf()
            mm(psF[pz][:, 0:8], U_f, a_[:, tb, 0:8], True, True, ["consts", "a_"], [("psF", pz)], False)
            mm(psF[pz][:, 8:16], L_f, a_[:, tb, 8:16], True, True, ["consts", "a_"], [("psF", pz)], False)
            mm(psF[pz][:, 16:32], ones_f, a_[:, tb, :], True, True, ["consts", "a_"], [("psF", pz)], True)
            cp("dve", smx[:, tb, :], psF[pz][:, 0:32], [("psF", pz)], ["smx"])
            psf_free(pz)
        act(E_[:], smx[:, :, 0:16], AF.Exp, ["smx"], ["E_"])
        act(CD_[:], smx[:, :, 16:32], AF.Exp, ["smx"], ["CD_"])
        tt("dve", ac2[:], smx[:, :, 0:16], ac2[:], ALU.subtract, ["smx", "ac2"], ["ac2"])
        wt = tmpA[0][:, 0:64].rearrange("p (a b) -> p a b", a=4)
        tt("dve", wt, smx[:, :, 16:32], smx[:, :, 0:16], ALU.subtract, ["smx"], [("tmpA", 0)])
        act(wt, wt, AF.Exp, [("tmpA", 0)], [("tmpA", 0)])
        tt("dve", dtw[:], dtw[:], wt, ALU.mult, ["dtw", ("tmpA", 0)], ["dtw"])

    def prepass(t, use_hin, final_cb, dirs=(0, 1)):
        xws = {0: xw, 1: xd}

        def step(d, tb):
            xw_, xk_ = xws[d], ("xw", d)
            cp("act", hprev[:, d, tb, :], hrun[:, d, :], [("hrun", d)], [("hprev", d, tb)])
            tt("dve", h8(xw_), h8(xtok[:, tb, 0:512]), bc8(dtw[:, tb, d * 8:(d + 1) * 8]), ALU.mult,
               [("xtok", tb), "dtw"], [xk_, "xw", "xd"] if False else [xk_])
            pz = psf()
            for g in range(2):
                mm(psF[pz][:, g * 256:(g + 1) * 256], xtok[:, tb, 512 + g * 128:512 + (g + 1) * 128],
                   xw_[:, g * 256:(g + 1) * 256], True, True, [("xtok", tb), xk_], [("psF", pz)], g == 1)
            tt("dve", h8(hrun[:, d, :]), h8(hrun[:, d, :]), bc8(CD_[:, tb, d * 8:(d + 1) * 8]), ALU.mult,
               [("hrun", d), "CD_"], [("hrun", d)])
            tt("dve", hrun[:, d, :], psF[pz][:, :], hrun[:, d, :], ALU.add, [("psF", pz), ("hrun", d)], [("hrun", d)])
            psf_free(pz)

        for seg in SEGS[t]:
            for d in dirs:
                if use_hin:
                    cp("dve", hrun[:, d, :], hin[:, d, :], ["hin"], [("hrun", d)])
                else:
                    P.op("dve", lambda e, d=d: e.memset(hrun[:, d, :], 0.0), (), [("hrun", d)])
            orders = {d: (seg if d == 0 else list(reversed(seg))) for d in dirs}
            for i in range(len(seg)):
                for d in dirs:
                    step(d, orders[d][i])
            for d in dirs:
                final_cb(d, seg)

    def final_P(d, seg):
        s_ = seg[0] // 2
        pz = psf()
        for q in range(4):
            tr(psF[pz][:, q * 128:(q + 1) * 128], hrun[:, d, q * 128:(q + 1) * 128], ident_f,
               ["consts", ("hrun", d)], [("psF", pz)], signal=(q == 3))
        st_ = tmpA[d]
        cp("act", st_[:], psF[pz][:, :], [("psF", pz)], [("tmpA", d)])
        psf_free(pz)
        dma("sp", ohs[d][s_].rearrange("(q p) n -> p q n", p=128), st_[:].rearrange("p (q n) -> p q n", q=4),
            [("tmpA", d)], [("ohs", d)])

    def final_S1(d, seg):
        cp("dve", x2buf[:, d * 512:(d + 1) * 512], hrun[:, d, :], [("hrun", d)], ["x2buf"])
        tsum = tmpA[0][:, 0:8]
        tt("dve", tsum, smx[:, 0, 16 + d * 8:24 + d * 8], smx[:, 1, 16 + d * 8:24 + d * 8], ALU.add, ["smx"], [("tmpA", 0)])
        tt("dve", tsum, tsum, smx[:, 2, 16 + d * 8:24 + d * 8], ALU.add, ["smx", ("tmpA", 0)], [("tmpA", 0)])
        tt("dve", tsum, tsum, smx[:, 3, 16 + d * 8:24 + d * 8], ALU.add, ["smx", ("tmpA", 0)], [("tmpA", 0)])
        act(x2buf[:, 1024 + d * 8:1032 + d * 8], tsum, AF.Exp, [("tmpA", 0)], ["x2buf"])

    def x2_exchange():
        dma("sp", x2_in[:, :], x2buf[:, :], ["x2buf"], ["x2_in"])
        coll(x2_in, x2_out, ["x2_in"], ["x2_out"])

    gSb = [av(19232, [128, 1040], F32), av(19232 + 2080, [128, 1040], F32)]

    def x2_receive():
        nld = [0]

        def load_rank(rp):
            i = nld[0] % 2; nld[0] += 1
            dma("sp", gSb[i][:, :], x2_out[rp * 128:(rp + 1) * 128, :], ["x2_out"], [("gSb", i)])
            return gSb[i], ("gSb", i)
        for d in range(2):
            st_ = tmpA[d]
            dma("sp", st_[:].rearrange("p (q n) -> p q n", q=4), h0_d[d].rearrange("(q p) n -> p q n", p=128), (), [("tmpA", d)])
            pz = psf()
            for q in range(4):
                tr(psF[pz][:, q * 128:(q + 1) * 128], st_[:, q * 128:(q + 1) * 128], ident_f,
                   ["consts", ("tmpA", d)], [("psF", pz)], signal=(q == 3))
            cp("dve", hcand[:, d, :], psF[pz][:, :], [("psF", pz)], [("hcand", d)])
            psf_free(pz)
            ranks = [0, 1, 2] if d == 0 else [3, 2, 1]
            first = 0 if d == 0 else 3
            ts("dve", hin[:, d, :], hcand[:, d, :], sel[:, first:first + 1], None, ALU.mult, None,
               [("hcand", d), "sel"], ["hin"])
            for rp in ranks:
                nxt = rp + 1 if d == 0 else rp - 1
                g_, gk = load_rank(rp)
                tt("dve", h8(hcand[:, d, :]), h8(hcand[:, d, :]), bc8(g_[:, 1024 + d * 8:1032 + d * 8]), ALU.mult,
                   [("hcand", d), gk], [("hcand", d)])
                tt("dve", hcand[:, d, :], hcand[:, d, :], g_[:, d * 512:(d + 1) * 512], ALU.add,
                   [("hcand", d), gk], [("hcand", d)])
                stt("dve", hin[:, d, :], hcand[:, d, :], sel[:, nxt:nxt + 1], hin[:, d, :], ALU.mult, ALU.add,
                    [("hcand", d), "sel", "hin"], ["hin"])

    def ssd_main(t, tb):
        tbc = slice(tb * 128, (tb + 1) * 128)
        pa = psf()
        mm(psF[pa][0:8, 0:128], a_[:, tb, 0:8], U_f, True, True, ["consts", "a_"], [("psF", pa)], False)
        mm(psF[pa][0:8, 128:256], a_[:, tb, 8:16], L_f, True, True, ["consts", "a_"], [("psF", pa)], True)
        cp("act", acumT, psF[pa][0:8, 0:256], [("psF", pa)], ["acumT"])
        psf_free(pa)
        pc = psf()
        for g in range(2):
            mm(psF[pc][:, g * 128:(g + 1) * 128], xconvT[:, 4 + g, tbc], xconvT[:, 6 + g, tbc], True, True,
               [("xconvT", 4 + g), ("xconvT", 6 + g)], [("psF", pc)], g == 1)
        for g in range(2):
            for d in range(2):
                tt("dve", CBm[:, g * 2 + d, :], psF[pc][:, g * 128:(g + 1) * 128], U_f if d == 0 else L_f, ALU.mult,
                   [("psF", pc), "consts"], ["CBm"])
        psf_free(pc)
        for d in range(2):
            prs = []
            for g in range(2):
                pr = psf(); prs.append(pr)
                for hh in range(4):
                    h = g * 4 + hh
                    mm(psF[pr][:, hh * 128:(hh + 1) * 128], selc[0:8, h * 128:(h + 1) * 128], acumT[0:8, d * 128:(d + 1) * 128],
                       True, True, ["selc", "acumT"], [("psF", pr)], hh == 3)
            for g in range(2):
                pr = prs[g]
                for hh in range(4):
                    ci = d * 8 + g * 4 + hh
                    ts("dve", tmpA[g][:, hh * 128:(hh + 1) * 128], psF[pr][:, hh * 128:(hh + 1) * 128],
                       ac2[:, tb, ci:ci + 1], 30.0, ALU.subtract, ALU.min, [("psF", pr), "ac2"], [("tmpA", g)])
                psf_free(pr)
            for g in range(2):
                act(tmpA[g][:], tmpA[g][:], AF.Exp, [("tmpA", g)], [("tmpA", g)])
            for g in range(2):
                D3 = tmpA[g][:].rearrange("p (h n) -> p h n", h=4)
                stt("dve", MT[:, d * 8 + g * 4:d * 8 + g * 4 + 4, :], D3, 1e30,
                    CBm[:, g * 2 + d, :].unsqueeze(1).to_broadcast([128, 4, 128]), ALU.min, ALU.mult,
                    [("tmpA", g), "CBm"], [("MT", d, g)])
        tt("dve", xd, xtok[:, tb, 0:512], dsk_bc[:], ALU.mult, [("xtok", tb), "dsk_bc"], ["xd"])
        py = psf()
        mtk = [("MT", d, g) for d in range(2) for g in range(2)]
        for h in range(8):
            hc = slice(h * 64, (h + 1) * 64)
            mm(psF[py][:, hc], ident_b, xd[:, hc], True, False,
               (["consts_b", "xd"] + mtk + [("xtok", tb)]) if h == 0 else (), [("psF", py)], False)
            for d in range(2):
                mm(psF[py][:, hc], MT[:, d * 8 + h, :], xtok[:, tb, hc], False, d == 1,
                   (), [("psF", py)], (h == 7 and d == 1))
        pof = [psf(), psf()]
        for d in range(2):
            for g in range(2):
                mm(psF[pof[d]][:, g * 256:(g + 1) * 256], xconvT[:, 6 + g, tbc], hprev[:, d, tb, g * 256:(g + 1) * 256],
                   True, True, [("xconvT", 6 + g), ("hprev", d, tb)], [("psF", pof[d])], g == 1)
        cp("act", ysb, psF[py][:, :], [("psF", py)], ["ysb"])
        psf_free(py)
        for d in range(2):
            Dt = tmpA[d]; dk = ("tmpA", d)
            tt("dve", h8(Dt[:]), h8(psF[pof[d]][:, :]), bc8(E_[:, tb, d * 8:(d + 1) * 8]), ALU.mult,
               [("psF", pof[d]), "E_"], [dk])
            tt("dve", ysb, ysb, Dt[:], ALU.add, ["ysb", dk], ["ysb"])
            psf_free(pof[d])
        tt("dve", yg, ysb, zs[t][:, tb, :], ALU.mult, ["ysb", ("zs", t)], ["yg"])
        P.op("dve", lambda e: e.memset(stat2[:, 0:2], 0.0), (), ["stat2"])
        for g in range(2):
            act(tmpA[g][:, 0:256], yg[:, g * 256:(g + 1) * 256], AF.Square, ["yg"], [("tmpA", g), "stat2"],
                accum=stat2[:, g:g + 1])
        rsqrt_to(stat2[:, 0:2], stat2[:, 0:2], EPS, ["stat2"], ["stat2"], scale=1.0 / 256.0)
        for g in range(2):
            stt("dve", xw[:, g * 256:(g + 1) * 256], yg[:, g * 256:(g + 1) * 256], stat2[:, g:g + 1],
                ssdn_bc[:, g * 256:(g + 1) * 256], ALU.mult, ALU.mult, ["yg", "stat2", "ssdn_bc"], ["xw"])
        pb_ = psb()
        for k in range(4):
            tr(psB[pb_][:, k * 128:(k + 1) * 128], xw[:, k * 128:(k + 1) * 128], ident_b, ["consts_b", "xw"],
               [("psB", pb_)], signal=(k == 3))
        alt_cp(ssdT[:, :, tbc], psB[pb_][:, 0:512].rearrange("p (k n) -> p k n", k=4), [("psB", pb_)], ["ssdT"])
        psb_free(pb_)

    def attn_V(t):
        nkb = 18 if t == "S" else 4
        for kb in range(nkb):
            pz = psf()
            mm_group(psF[pz][:, :].rearrange("p (h c) -> p h c", h=8),
                     [(Ksrc[:, kc, kb * 128:(kb + 1) * 128], wukv[:, kc, :, 64:128]) for kc in range(2)],
                     ["Ksrc", "wukv"], [("psF", pz)])
            alt_cp(Vt[t][:, kb, :], psF[pz][:, :], [("psF", pz)], [("V", kb)])
            psf_free(pz)

    def attn_head(t, h):
        nkb = 18 if t == "S" else 4
        NK = nkb * 128
        if True:
            for kt in range((NK + 511) // 512):
                n = min(512, NK - kt * 512)
                kc_ = slice(kt * 512, kt * 512 + n)
                pz = psf()
                mm(psF[pz][0:96, 0:n], padI[0:32, 0:96], kpeT[0:32, kc_], True, False, ["padI", "kpeT"], [("psF", pz)], False)
                for kc in range(2):
                    mm(psF[pz][0:96, 0:n], wuk96[:, kc, h, :], Ksrc[:, kc, kc_], False, kc == 1,
                       ["wuk96", "Ksrc"], [("psF", pz)], kc == 1)
                alt_cp(Kh[:, kc_], psF[pz][0:96, 0:n], [("psF", pz)], ["Kh"])
                psf_free(pz)
            pq = psf()
            mm_group(psF[pq][0:96, :], [(wuq[:, kc, h, :], cqn[t][:, kc, :]) for kc in range(2)], ["wuq", ("cqn", t)], [("psF", pq)])
            if t == "P":
                alt_cp(Qh, psF[pq][0:96, :], [("psF", pq)], ["Qh"])
            else:
                pq2 = psf()
                mm_group(psF[pq2][0:96, :], [(wuq_sw[:, kc, h, :], cqn[t][:, kc, :]) for kc in range(2)],
                         ["wuq_sw", ("cqn", t)], [("psF", pq2)])
                cp("act", Qh[0:64, :], psF[pq][0:64, :], [("psF", pq)], ["Qh"])
                tt("dve", tmpA[0][64:96, :], psF[pq][64:96, :], rope_hi[64:96, 0, :], ALU.mult, [("psF", pq), "rope_hi"], [("tmpA", 0)])
                tt("dve", tmpA[1][64:96, :], psF[pq2][64:96, :], rope_hi[64:96, 1, :], ALU.mult, [("psF", pq2), "rope_hi"], [("tmpA", 1)])
                tt("dve", Qh[64:96, :], tmpA[0][64:96, :], tmpA[1][64:96, :], ALU.add, [("tmpA", 0), ("tmpA", 1)], ["Qh"])
                psf_free(pq2)
            psf_free(pq)
            po = psf(); pl = psf()
            if t == "P":
                plan = [(slice(s_ * 256, (s_ + 1) * 256), [2 * s_, 2 * s_ + 1]) for s_ in range(2)]
            else:
                plan = [(slice(0, 512), list(range(18)))]
            steps = []
            for (qc, kbs) in plan:
                for i, kb in enumerate(kbs):
                    steps.append((qc, kb, i == 0, i == len(kbs) - 1))
            pend = None
            for it, (qc, kb, first, lastk) in enumerate(steps):
                nq = qc.stop - qc.start
                psc = psf()
                mm(psF[psc][:, 0:nq], Kh[:, kb * 128:(kb + 1) * 128], Qh[:, qc], True, True, ["Kh", "Qh"], [("psF", psc)], True)
                if pend is not None:
                    pend()
                pt = PT[it % 2]; pk = ("PT", it % 2)
                act(pt[:, 0:nq], psF[psc][:, 0:nq], AF.Exp, [("psF", psc)], [pk], scale=SCALE)
                psf_free(psc)

                def pend(qc=qc, kb=kb, first=first, lastk=lastk, pt=pt, pk=pk, nq=nq):
                    mm(psF[po][0:64, qc], Vt[t][:, kb, h * 64:(h + 1) * 64], pt[:, 0:nq], first, lastk,
                       [("V", kb), pk], [("psF", po)], True)
                    mm(psF[pl][0:64, qc], ones_b[:, 0:64], pt[:, 0:nq], first, lastk, ["consts_b", pk], [("psF", pl)], True)
            pend()
            rl = tmpA[0][0:64, :]
            act(rl, psF[pl][0:64, :], AF.Ln, [("psF", pl)], [("tmpA", 0)])
            act(rl, rl, AF.Exp, [("tmpA", 0)], [("tmpA", 0)], scale=-1.0)
            tt("dve", OT[:, h, :], psF[po][0:64, :], rl, ALU.mult, [("psF", po), ("tmpA", 0)], [("OT", h)])
            psf_free(po); psf_free(pl)

    def outproj(t, l):
        for half in range(2):
            s1 = ring_load([(0, [64, 8, 512], w_out[0, 0:512, half * 512:(half + 1) * 512].rearrange("(h p) n -> p h n", p=64))])
            s2 = ring_load([(0, [128, 4, 512], w_out[0, 512:1024, half * 512:(half + 1) * 512].rearrange("(k p) n -> p k n", p=128))])
            wa = rview(s1, 0, [64, 8, 512]); ws_ = rview(s2, 0, [128, 4, 512])
            for d4 in range(4):
                dc = half * 4 + d4
                pz = psf()
                pairs = [(wa[:, h, d4 * 128:(d4 + 1) * 128], OT[:, h, :]) for h in range(8)]
                pairs += [(ws_[:, k, d4 * 128:(d4 + 1) * 128], ssdT[:, k, :]) for k in range(4)]
                mm_group(psF[pz][:, :], pairs, [("ring", s1), ("ring", s2), "ssdT"] + [("OT", h) for h in range(8)], [("psF", pz)])
                evac_mix(pz, dc)
                psf_free(pz)
        post_norm_residual(t, l, 2)

    def rest(t, l):
        conv(t)
        ssd_small(t)
        if t == "P":
            prepass(t, False, final_P)
            attn_V(t)
            for h in range(8):
                attn_head(t, h)
            for tb in range(4):
                ssd_main(t, tb)
        else:
            prepass(t, False, final_S1)
            x2_exchange()
            P.barrier()
            nop_ = lambda d, seg: None
            sched = {1: lambda: x2_receive(),
                     2: lambda: prepass(t, True, nop_, dirs=(0,)),
                     3: lambda: prepass(t, True, nop_, dirs=(1,)),
                     4: lambda: ssd_main(t, 0), 5: lambda: ssd_main(t, 1),
                     6: lambda: ssd_main(t, 2), 7: lambda: ssd_main(t, 3)}
            attn_V(t)
            for h in range(8):
                attn_head(t, h)
                if h in sched:
                    sched[h]()
        P.barrier()
        outproj(t, l)
        P.barrier()

    def layer0_mixer(l):
        P.op("dve", lambda e: e.memset(xbcpad["P"][:], 0.0), (), [("xbcpad", "P")])
        P.op("dve", lambda e: e.memset(wuk96[:], 0.0), (), ["wuk96"])
        cp("dve", wuk96[:, :, :, 0:64], wukv[:, :, :, 0:64], ["wukv", "wuk96"], ["wuk96"])
        for i in range(40):
            ts("dve", diagW[:, i, :], ident_f, cols[:, B_CONVW + i:B_CONVW + i + 1], None, ALU.mult, None,
               ["consts", "cols"], [("diagW", 0)])
        if RUN_S:
            P.op("dve", lambda e: e.memset(x1buf[32:64, 1024:1536], 0.0), (), ["x1buf"])
            P.op("dve", lambda e: e.memset(x1buf[64:128, 1024:1536], 0.0), (), ["x1buf"])
            inproj("S", l)
            x1_exchange()
            P.barrier()
        inproj("P", l)
        P.barrier()
        rest("P", l)
        if RUN_S:
            x1_receive()
            P.barrier()
            rest("S", l)

    RUN_S = not os.environ.get("KNO_S")

    def layer1_mixer(l):
        WP = {"P": 544, "S": 528}
        hpad = {"P": av(0, [128, 8, 544]), "S": av(8704, [128, 8, 528])}
        invc = av(21504, [128, 4, 512], F32)
        pooledT = av(25600, [128, 8, 512])
        x3buf = av(29696, [128, 8, 16], F32)
        gE = av(29952, [128, 4, 128], F32)

        def data(ap3, t):
            if t == "S":
                return ap3[:, :, 8:520]
            return ap3.rearrange("p k (s c) -> p k s c", s=2)[:, :, :, 8:264]

        def data2(ap2, t):
            if t == "S":
                return ap2[:, 8:520]
            return ap2.rearrange("p (s c) -> p s c", s=2)[:, :, 8:264]

        def seg2(ap2, t):
            if t == "S":
                return ap2
            return ap2.rearrange("p (s c) -> p s c", s=2)

        def fill(t):
            if t == "P":
                hp4 = hpad[t][:].rearrange("p k (s c) -> p k s c", s=2)
                P.op("dve", lambda e: e.memset(hp4[:, :, :, 0:8], 0.0), (), [("hpad", t)])
                P.op("dve", lambda e: e.memset(hp4[:, :, :, 264:272], 0.0), (), [("hpad", t)])
            modulate(t, l, 0, 1, outf=lambda k, tm, t=t: (data2(hpad[t][:, k, :], t), seg2(tm[:], t), ("hpad", t)))

        def pool_tile(t, ti):
            W = WP[t]
            dma("sp", invc[:], invcnt_d[ti:ti + 1].to_broadcast([128, 4, NT]), (), ["invc"])
            hk = ("hpad", t)
            segs = [(0, 0, 256), (272, 256, 256)] if t == "P" else [(0, 0, 512)]
            for gi, w in enumerate((2, 4, 8, 16)):
                for kk in range(2):
                    kch = 2 * gi + kk
                    pz = psf()
                    for (sb0, oc, n) in segs:
                        mm_group(psF[pz][:, oc:oc + n],
                                 [(ident_b, hpad[t][:, kch, sb0 + 8 + off:sb0 + 8 + off + n]) for off in range(-(w // 2), w // 2)],
                                 ["consts_b", hk], [("psF", pz)])
                    tm = tmpA[kk]; tk = ("tmpA", kk)
                    tt("dve", tm[:], psF[pz][:, :], invc[:, gi, :], ALU.mult, [("psF", pz), "invc"], [tk])
                    psf_free(pz)
                    tt("dve", seg2(pooledT[:, kch, :], t), seg2(tm[:], t), data2(hpad[t][:, kch, :], t), ALU.subtract,
                       [tk, hk], [("pooledT", kch)])
            s = ring_load([(0, [128, 8, 256], pool_w[0].rearrange("g (k p) n -> p (g k) n", p=128))])
            pw = rview(s, 0, [128, 8, 256])
            for dc in range(8):
                gi, co = dc // 2, dc % 2
                pz = psf()
                mm_group(psF[pz][:, :], [(pw[:, gi * 2 + kc, co * 128:(co + 1) * 128], pooledT[:, 2 * gi + kc, :]) for kc in range(2)],
                         [("ring", s), ("pooledT", 2 * gi), ("pooledT", 2 * gi + 1)], [("psF", pz)])
                act(mixT[:, dc, :], psF[pz][:, :], AF.Copy, [("psF", pz), "cols"], ["mixT"],
                    scale=cols[:, B_PSC + dc:B_PSC + dc + 1])
                act(sqT[:, dc, :], mixT[:, dc, :], AF.Square, ["mixT"], [("sqT", dc)])
                psf_free(pz)
            post_norm_residual(t, l, 2)

        if RUN_S:
            fill("S")
            cp("dve", x3buf[:, :, 0:8], hpad["S"][:, :, 8:16], [("hpad", "S")], ["x3buf"])
            cp("dve", x3buf[:, :, 8:16], hpad["S"][:, :, 512:520], [("hpad", "S")], ["x3buf"])
            dma("sp", x3_in[:, :], x3buf[:].rearrange("p k c -> p (k c)"), ["x3buf"], ["x3_in"])
            coll(x3_in, x3_out, ["x3_in"], ["x3_out"])
        fill("P")
        pool_tile("P", 0)
        if RUN_S:
            dma("sp", gE[:], x3_out.rearrange("(r p) c -> p r c", p=128), ["x3_out"], ["gE"])
            for (dst0, c0, sb_) in ((0, 8, 4), (520, 0, 8)):
                dst = hpad["S"][:, :, dst0:dst0 + 8]
                for rp in range(4):
                    src = gE[:, rp, :].rearrange("p (k c) -> p k c", k=8)[:, :, c0:c0 + 8]
                    if rp == 0:
                        ts("dve", dst, src, sel[:, sb_ + rp:sb_ + rp + 1], None, ALU.mult, None, ["gE", "sel"], [("hpad", "S")])
                    else:
                        stt("dve", dst, src, sel[:, sb_ + rp:sb_ + rp + 1], dst, ALU.mult, ALU.add,
                            ["gE", "sel", ("hpad", "S")], [("hpad", "S")])
            pool_tile("S", 1)
    for l in range(n_layers):
        if l % 2 == 0:
            layer0_mixer(l)
        else:
            layer1_mixer(l)
        P.barrier()
        if not os.environ.get('KSKIP_FFN'):
            ffn(l)
        P.barrier()

    P.barrier()
    for t in ("P", "S"):
        for tb in range(4):
            xl = xld[rr["xld"] % 2]; xk = ("xld", rr["xld"] % 2); rr["xld"] += 1
            for half in range(2):
                pz = psf()
                for q in range(4):
                    k = half * 4 + q
                    tr(psF[pz][:, q * 128:(q + 1) * 128], xT[t][:, k, tb * 128:(tb + 1) * 128], ident_f,
                       ["consts", ("xT", t)], [("psF", pz)], signal=(q == 3))
                alt_cp(xl[:, half * 512:(half + 1) * 512], psF[pz][:, :], [("psF", pz)], [xk])
                psf_free(pz)
            dma("sp", yout[t][tb * 128:(tb + 1) * 128, :], xl[:], [xk], [("yout", t)])
    P.final_wait()
    P.emit()
    es.close()
    return nc


def _consts():
    c = np.zeros((128, 512), np.float32)
    c[:, 0:128] = np.eye(128)
    t = np.arange(128)
    c[:, 128:256] = (t[:, None] <= t[None, :])
    c[:, 256:384] = (t[:, None] >= t[None, :])
    c[:, 384:512] = 1.0
    selc = np.zeros((8, 8, 128), np.float32)
    for h in range(8):
        selc[h, h, :] = 1.0
    padI = np.zeros((32, 96), np.float32)
    padI[np.arange(32), 64 + np.arange(32)] = 1.0
    return c, selc.reshape(8, 1024), padI


def _rope(pos):
    half = 16
    inv_freq = np.power(10000.0, -np.arange(0, half, 2, dtype=np.float64) / half)
    row = (pos // 64).astype(np.float64); col = (pos % 64).astype(np.float64)
    ang = np.concatenate([row[:, None] * inv_freq, col[:, None] * inv_freq], axis=-1)
    cos = np.cos(ang).T; sin = np.sin(ang).T
    r = np.zeros((32, 2, len(pos)), np.float32)
    r[0:16, 0] = cos; r[16:32, 0] = cos; r[0:16, 1] = sin; r[16:32, 1] = sin
    return r


def _invcnt(seg_len, nseg, lo_pad, hi_pad):
    out = np.zeros((4, NT), np.float32)
    for gi, w in enumerate((2, 4, 8, 16)):
        for s in range(nseg):
            L = seg_len
            t = np.arange(L)
            lo = t - w // 2; hi = t + w // 2
            if lo_pad: lo = np.clip(lo, 0, None)
            if hi_pad: hi = np.clip(hi, None, L)
            out[gi, s * L:(s + 1) * L] = 1.0 / (hi - lo)
    return out


_NC_CACHE = {}


def kernel(**inp):
    inp = {k: np.ascontiguousarray(np.asarray(v)) for k, v in inp.items()}
    if "nc" not in _NC_CACHE:
        _NC_CACHE["nc"] = build()
    nc = _NC_CACHE["nc"]
    c, selc, padI = _consts()
    shared = {k: inp[k] for k in (
          "w_in_ab",
         "w_uq",  "w_ukv",
         "w_out_ab", "pool_w",
        "ffn_w_gate", "ffn_w_up", "ffn_w_down")}
    shared.update(consts=c, selc=selc, padI=padI)
    bc_all = np.concatenate([inp["kv_norm"].reshape(-1), inp["ssd_norm"].reshape(-1), inp["ssd_dt_bias_fwd"].reshape(-1),
                             inp["ssd_dt_bias_bwd"].reshape(-1), inp["ssd_a_log_fwd"].reshape(-1),
                             inp["ssd_a_log_bwd"].reshape(-1), inp["ssd_d"].reshape(-1)]).reshape(1, 808)
    shared["bc_all"] = bc_all
    stg_common = [inp["norm_pre_mix"].reshape(16, 128), inp["norm_post_mix"].reshape(16, 128),
                  inp["norm_pre_ffn"].reshape(16, 128), inp["norm_post_ffn"].reshape(16, 128),
                  inp["b_mod"].reshape(96, 128)]
    stg_tail = [inp["q_norm"].reshape(2, 128), inp["kv_norm"].reshape(2, 128), inp["ssd_conv_b"].reshape(8, 128),
                inp["ssd_conv_w"][0].reshape(40, 128), inp["ssd_norm"].reshape(4, 128), inp["pool_scale"].reshape(8, 128),
                np.zeros((16, 128), np.float32)]
    in_maps = []
    for core in range(8):
        b, r = core // 4, core % 4
        m = dict(shared)
        m["xp"] = inp["x_prompt"][2 * core:2 * core + 2].reshape(NT, D)
        m["xs"] = inp["x_sample"][b, r * NT:(r + 1) * NT]
        m["stg_all"] = np.concatenate(stg_common + [np.stack([inp["c_ctx"], inp["c"][b]]).reshape(16, 128)] + stg_tail, axis=0)
        m["w_mod_sl"] = inp["w_mod"][:, :, r * 1536:(r + 1) * 1536]
        m["cache_ckv"] = inp["cache_mla_ckv"][b, 0]
        m["cache_kpe"] = inp["cache_mla_krope"][b, 0]
        m["h0f"] = inp["state_ssd_fwd"][b, 0].reshape(512, 128)
        m["h0b"] = inp["state_ssd_bwd"][b, 0].reshape(512, 128)
        m["rope"] = _rope(np.arange(r * NT, (r + 1) * NT))
        sel = np.zeros((128, 16), np.float32)
        sel[:, r] = 1.0
        if r > 0: sel[:, 4 + r - 1] = 1.0
        if r < 3: sel[:, 8 + r + 1] = 1.0
        m["sel"] = sel
        ic = np.zeros((2, 4, NT), np.float32)
        ic[0] = _invcnt(256, 2, True, True)
        ic[1] = _invcnt(NT, 1, r == 0, r == 3)
        m["invcnt"] = ic
        in_maps.append({k: np.ascontiguousarray(v, dtype=np.float32) for k, v in m.items()})
    ncores = int(os.environ.get("KCORES", "8"))
    res = run_bass_kernel_spmd(nc, in_maps[:ncores], core_ids=list(range(ncores)))
    R = list(res.results)
    while len(R) < 8:
        R.append(R[0])
    yp = np.concatenate([R[c_]["yp"].reshape(2, 256, D) for c_ in range(8)], axis=0)
    ys = np.stack([np.concatenate([R[b * 4 + r]["ys"] for r in range(4)], axis=0) for b in range(2)])
    ockv = np.concatenate([R[c_]["ockv"].reshape(2, 1, 256, 256) for c_ in range(8)], axis=0)
    okpe = np.concatenate([R[c_]["okpe"].reshape(2, 1, 256, 32) for c_ in range(8)], axis=0)
    ohf = np.concatenate([R[c_]["ohf"].reshape(2, 1, 8, 64, 128) for c_ in range(8)], axis=0)
    ohb = np.concatenate([R[c_]["ohb"].reshape(2, 1, 8, 64, 128) for c_ in range(8)], axis=0)
    return (yp.astype(np.float32), ys.astype(np.float32), ockv.astype(np.float32), okpe.astype(np.float32),
            ohf.astype(np.float32), ohb.astype(np.float32))
```

```python
import numpy as np
import concourse.bass as bass
import concourse.mybir as mybir

F32 = mybir.dt.float32
BF16 = mybir.dt.bfloat16
AF = mybir.ActivationFunctionType
ALU = mybir.AluOpType
AX = mybir.AxisListType

ENGS = ("pe", "act", "dve", "pool", "sp")


class Prog:
    def __init__(self, nc, n_dma_sems=24):
        self.nc = nc
        self.items = {e: [] for e in ENGS}
        self.cnt = {e: 0 for e in ENGS}
        self.waited = {e: {} for e in ENGS}
        self.lastw = {}
        self.readers = {}
        self.n_dma_sems = n_dma_sems
        self.dma_cnt = [0] * (n_dma_sems + 4)
        self.dma_i = 0
        self.dma_q = 0
        self.dma_c = 0
        self.nops = {e: 0 for e in ENGS}

    def _deps(self, eng, reads, writes):
        deps = []
        for r in reads:
            s = self.lastw.get(r)
            if s is not None:
                deps.append((s, "raw"))
            if isinstance(r, tuple) and r[0] in ("psF", "psB"):
                for s in self.readers.get(r, ()):
                    if s[2] != eng:
                        deps.append((s, "rar"))
        for w in writes:
            s = self.lastw.get(w)
            if s is not None:
                deps.append((s, "waw"))
            for s in self.readers.get(w, ()):
                deps.append((s, "war"))
        out = {}
        for (sem, val, peng), kind in deps:
            if peng == eng:
                if eng == "pe":
                    continue
            if out.get(sem, -1) < val:
                out[sem] = val
        return out

    def _emit_waits(self, eng, deps):
        for sem, val in deps.items():
            if self.waited[eng].get(sem, -1) >= val:
                continue
            self.waited[eng][sem] = val
            self.items[eng].append(("wait", sem, val))

    def _record(self, sig, reads, writes):
        for r in reads:
            self.readers.setdefault(r, []).append(sig)
        for w in writes:
            self.lastw[w] = sig
            self.readers[w] = []

    def op(self, eng, fn, reads=(), writes=(), signal=True):
        deps = self._deps(eng, reads, writes)
        self._emit_waits(eng, deps)
        self.nops[eng] += 1
        if signal:
            self.cnt[eng] += 1
            sig = ("E_" + eng, self.cnt[eng], eng)
            self.items[eng].append(("op", fn, True))
            self._record(sig, reads, writes)
        else:
            self.items[eng].append(("op", fn, False))
            sig = ("E_" + eng, self.cnt[eng] + 1, eng)
            self._record(sig, reads, writes)
        return sig

    def dma(self, eng, fn, reads=(), writes=(), inc=16):
        half = self.n_dma_sems // 2
        if inc == 1:
            i = self.n_dma_sems + (self.dma_c % 4); self.dma_c += 1
        elif eng == "pool":
            i = half + (self.dma_q % half); self.dma_q += 1
        else:
            i = self.dma_i % half; self.dma_i += 1
        sem = "D_%d" % i
        deps = self._deps(eng, reads, writes)
        if self.dma_cnt[i] > 0:
            if deps.get(sem, -1) < self.dma_cnt[i]:
                deps[sem] = self.dma_cnt[i]
        self._emit_waits(eng, deps)
        self.dma_cnt[i] += inc
        sig = (sem, self.dma_cnt[i], None)
        self.items[eng].append(("dma", fn, sem, inc))
        self.nops[eng] += 1
        self._record(sig, reads, writes)
        return sig

    def barrier(self):
        allsig = {}
        for e in ENGS:
            if self.cnt[e] > 0:
                allsig["E_" + e] = self.cnt[e]
        for i in range(self.n_dma_sems + 4):
            if self.dma_cnt[i] > 0:
                allsig["D_%d" % i] = self.dma_cnt[i]
        for e in ENGS:
            d = dict(allsig)
            self._emit_waits(e, d)

    def final_wait(self, eng="sp"):
        self.barrier()

    def emit(self, extra_ctx=()):
        nc = self.nc
        import contextlib
        with contextlib.ExitStack() as st:
            sems = {}
            for e in ENGS:
                sems["E_" + e] = st.enter_context(nc.semaphore("E_" + e))
            for i in range(self.n_dma_sems + 4):
                sems["D_%d" % i] = st.enter_context(nc.semaphore("D_%d" % i))
            block = st.enter_context(nc.Block())
            items = self.items

            def run(engh, ename):
                for it in items[ename]:
                    if it[0] == "wait":
                        engh.wait_ge(sems[it[1]], it[2])
                    elif it[0] == "op":
                        ins = it[1](engh)
                        if it[2]:
                            ins.then_inc(sems["E_" + ename], 1)
                    else:
                        ins = it[1](engh)
                        if it[3] == 1:
                            ins.then_inc(sems[it[2]])
                        else:
                            ins.then_inc(sems[it[2]], it[3])

            @block.sync
            def _(e):
                run(e, "sp")

            @block.scalar
            def _(e):
                run(e, "act")

            @block.vector
            def _(e):
                run(e, "dve")

            @block.gpsimd
            def _(e):
                run(e, "pool")

            @block.tensor
            def _(e):
                run(e, "pe")

from contextlib import ExitStack
import os
from concourse.bass_utils import run_bass_kernel_spmd
import ml_dtypes

D = 1024
NT = 512
EPS = 1e-6
IN_AB = 2096
D_FF = 2816
SCALE = 96 ** -0.5
RING_SLOTS = 3
RING_ELEMS = 4096

B_NPM, B_NPO, B_NFR, B_NFO = 0, 16, 32, 48
B_BMOD = 64
B_CVEC = 160
B_QN, B_KVN = 176, 178
B_CONVB = 180
B_CONVW = 188
B_SSDN = 228
B_PSC = 232
N_ROWS = 240


def build(n_layers=2, dbg=False):
    nc = bass.Bass("TRN2", target_bir_lowering=False)
    P = Prog(nc)
    es = ExitStack()

    def din(name, shape, dt=F32):
        return nc.dram_tensor(name, list(shape), dt, kind="ExternalInput").ap()

    def dout(name, shape, dt=F32):
        return nc.dram_tensor(name, list(shape), dt, kind="ExternalOutput").ap()

    def sb(name, shape, dt=F32):
        return es.enter_context(nc.sbuf_tensor("sb_" + name, list(shape), dt))

    xin = {"P": din("xp", [NT, D]), "S": din("xs", [NT, D])}
    cache_ckv = din("cache_ckv", [256, 256])
    cache_kpe = din("cache_kpe", [256, 32])
    h0_d = [din("h0f", [512, 128]), din("h0b", [512, 128])]
    w_mod = din("w_mod_sl", [2, D, 1536])
    w_in = din("w_in_ab", [1, D, IN_AB])
    w_uq = din("w_uq", [1, 256, 768]); w_ukv = din("w_ukv", [1, 256, 1024])
    w_out = din("w_out_ab", [1, D, D])
    pool_w = din("pool_w", [1, 4, 256, 256])
    w_gate = din("ffn_w_gate", [2, D, D_FF]); w_up = din("ffn_w_up", [2, D, D_FF])
    w_down = din("ffn_w_down", [2, D_FF, D])
    consts_d = din("consts", [128, 512])
    selc_d = din("selc", [8, 1024])
    padI_d = din("padI", [32, 96])
    rope_d = din("rope", [32, 2, NT])
    sel_d = din("sel", [128, 16])
    invcnt_d = din("invcnt", [2, 4, NT])

    yout = {"P": dout("yp", [NT, D]), "S": dout("ys", [NT, D])}
    ockv = dout("ockv", [NT, 256]); okpe = dout("okpe", [NT, 32])
    ohs = [dout("ohf", [2, 512, 128]), dout("ohb", [2, 512, 128])]

    NX1 = 1024 + 512 + 32
    x1_in = nc.dram_tensor("x1_in", [128, NX1], BF16, kind="Internal").ap()
    x1_out = nc.dram_tensor("x1_out", [4 * 128, NX1], BF16, kind="Internal").ap()
    x2_in = nc.dram_tensor("x2_in", [128, 1040], F32, kind="Internal").ap()
    x2_out = nc.dram_tensor("x2_out", [4 * 128, 1040], F32, kind="Internal").ap()
    x3_in = nc.dram_tensor("x3_in", [128, 128], F32, kind="Internal").ap()
    x3_out = nc.dram_tensor("x3_out", [4 * 128, 128], F32, kind="Internal").ap()
    GROUPS = [[0, 1, 2, 3], [4, 5, 6, 7]]

    consts = sb("consts", [128, 512]); consts_b = sb("consts_b", [128, 512], BF16)
    ident_f = consts[:, 0:128]; U_f = consts[:, 128:256]; L_f = consts[:, 256:384]; ones_f = consts[:, 384:512]
    ident_b = consts_b[:, 0:128]; ones_b = consts_b[:, 384:512]
    selc = sb("selc", [8, 1024])
    padI_f = sb("padI_f", [32, 96]); padI = sb("padI", [32, 96], BF16)
    rope_lo = sb("rope_lo", [32, 2, NT], BF16); rope_hi = sb("rope_hi", [96, 2, NT], BF16)
    sel = sb("sel", [128, 16])
    stg = sb("stg", [128, 2, 128]); cols = sb("cols", [128, 256])
    bcp = sb("bcp", [128, 808])
    kvn_bc = bcp[:, 0:256]; ssdn_bc = bcp[:, 256:768]; sm_bc = bcp[:, 768:808]
    A_bc = sb("A_bc", [128, 16]); dsk_bc = sb("dsk_bc", [128, 512], BF16)
    modT = sb("modT", [128, 2, 48, 2])
    csil = sb("csil", [128, 8, 2], BF16)
    mcol = sb("mcol", [128, 2, 6, 8, 2])
    qn32 = sb("qn32", [128, 2]); kvn32 = sb("kvn32", [128, 2])
    xT = {"P": sb("xT_P", [128, 8, NT]), "S": sb("xT_S", [128, 8, NT])}
    hT2 = sb("hT2", [128, 2, 8, NT], BF16)
    hT = {"P": hT2[:, 0], "S": hT2[:, 1]}
    sqT = sb("sqT", [128, 8, NT], BF16)
    rstd = sb("rstd", [128, NT]); tmpA = [sb("tmpA0", [128, NT]), sb("tmpA1", [128, NT])]
    mixT = sb("mixT", [128, 8, NT])
    xld = [mixT[:, 0:2, :].rearrange("p a b -> p (a b)"), mixT[:, 2:4, :].rearrange("p a b -> p (a b)")]
    ring = [sb("ring%d" % i, [128, RING_ELEMS], BF16) for i in range(RING_SLOTS)]
    wsm = sb("wsm", [128, 8, 48], BF16)
    wuq = sb("wuq", [128, 2, 8, 96], BF16); wuq_sw = sb("wuq_sw", [128, 2, 8, 96], BF16)
    wukv = sb("wukv", [128, 2, 8, 128], BF16)
    ARENA = 75 * 1024 // 2
    arena = sb("arena", [128, ARENA], BF16)
    mrow = arena[0:2, 0:6144].bitcast(F32).rearrange("p (l n) -> p l n", l=2)
    Gm = arena[0:16, 12288:12288 + 3072].bitcast(F32)

    rr = {"ring": 0, "tmp": 0, "xld": 0, "alt": 0}

    psF = [es.enter_context(nc.psum_tensor("psF%d" % i, [128, 512], F32)) for i in range(6)]
    psB = [es.enter_context(nc.psum_tensor("psB%d" % i, [128, 1024], BF16)) for i in range(2)]
    freeF = list(range(6)); freeB = [0, 1]

    def psf():
        i = freeF.pop(0); return i

    def psf_free(i):
        freeF.append(i)

    def psb():
        i = freeB.pop(0); return i

    def psb_free(i):
        freeB.append(i)

    def act(out, in_, func, reads, writes, bias=None, scale=None, accum=None):
        kw = {}
        if bias is not None: kw["bias"] = bias
        if scale is not None: kw["scale"] = scale
        if accum is not None: kw["accum_out"] = accum
        return P.op("act", lambda e: e.activation(out=out, in_=in_, func=func, **kw), reads, writes)

    def tt(eng, out, in0, in1, op, reads, writes):
        return P.op(eng, lambda e: e.tensor_tensor(out=out, in0=in0, in1=in1, op=op), reads, writes)

    def ts(eng, out, in0, s1, s2, op0, op1, reads, writes):
        if s2 is None:
            return P.op(eng, lambda e: e.tensor_scalar(out=out, in0=in0, scalar1=s1, scalar2=None, op0=op0), reads, writes)
        return P.op(eng, lambda e: e.tensor_scalar(out=out, in0=in0, scalar1=s1, scalar2=s2, op0=op0, op1=op1), reads, writes)

    def stt(eng, out, in0, scalar, in1, op0, op1, reads, writes):
        return P.op(eng, lambda e: e.scalar_tensor_tensor(out=out, in0=in0, scalar=scalar, in1=in1, op0=op0, op1=op1), reads, writes)

    def cp(eng, out, in_, reads, writes):
        if eng == "act":
            return P.op("act", lambda e: e.copy(out=out, in_=in_), reads, writes)
        return P.op(eng, lambda e: e.tensor_copy(out=out, in_=in_), reads, writes)

    def rsqrt_to(out, in_, c, reads, writes, scale=1.0):
        act(out, in_, AF.Ln, reads, writes, bias=float(c), scale=float(scale))
        act(out, out, AF.Exp, list(writes), list(writes), scale=-0.5)

    def alt_cp(out, in_, reads, writes):
        rr["alt"] ^= 1
        return cp("act" if rr["alt"] else "dve", out, in_, reads, writes)

    def mm(out, lhsT, rhs, start, stop, reads, writes, signal):
        return P.op("pe", lambda e: e.matmul(out, lhsT=lhsT, rhs=rhs, start=start, stop=stop), reads, writes, signal=signal)

    def mm_group(out, pairs, reads, writes):
        n = len(pairs)
        for i, (l, r) in enumerate(pairs):
            mm(out, l, r, i == 0, i == n - 1, reads if i == 0 else (), writes, i == n - 1)

    def tr(out, in_, ident, reads, writes, signal=True):
        return P.op("pe", lambda e: e.transpose(out=out, in_=in_, identity=ident), reads, writes, signal=signal)

    def dma(eng, out, in_, reads, writes):
        return P.dma(eng, lambda e: e.dma_start(out=out, in_=in_), reads, writes)

    def ring_load(parts, eng="pool"):
        s = rr["ring"] % RING_SLOTS
        rr["ring"] += 1
        for (off, shp, src) in parts:
            n = 1
            for v in shp[1:]:
                n *= v
            dst = ring[s][0:shp[0], off:off + n]
            if len(shp) == 3:
                dst = dst.rearrange("p (a b) -> p a b", a=shp[1])
            dma(eng, dst, src, (), [("ring", s)])
        return s

    def rview(s, off, shp):
        n = 1
        for v in shp[1:]:
            n *= v
        v = ring[s][0:shp[0], off:off + n]
        if len(shp) == 3:
            v = v.rearrange("p (a b) -> p a b", a=shp[1])
        return v

    dma("sp", consts[:], consts_d[:, :], (), ["consts"])
    dma("sp", selc[:], selc_d[:, :], (), ["selc"])
    dma("sp", padI_f[:], padI_d[:, :], (), ["padI_f"])
    dma("pool", rope_lo[:], rope_d[:, :, :], (), ["rope_lo"])
    dma("pool", rope_hi[64:96], rope_d[:, :, :], (), ["rope_hi"])
    dma("sp", sel[:], sel_d[:, :], (), ["sel"])
    cp("dve", consts_b[:], consts[:], ["consts"], ["consts_b"])
    cp("dve", padI[:], padI_f[:], ["padI_f"], ["padI"])

    def stage_rows(base, src2d, nrows):
        r = 0
        while r < nrows:
            row = base + r
            t, rin = row // 128, row % 128
            n = min(nrows - r, 128 - rin)
            dma("sp", stg[rin:rin + n, t, :], src2d[r:r + n, :], (), [("stg", t)])
            r += n

    stg_d = din("stg_all", [256, 128])
    bc_d = din("bc_all", [1, 808])
    dma("sp", stg[:, 0, :], stg_d[0:128, :], (), [("stg", 0)])
    dma("sp", stg[0:112, 1, :], stg_d[128:240, :], (), [("stg", 1)])
    pz = psf()
    tr(psF[pz][:, 0:128], stg[:, 0, :], ident_f, ["consts", ("stg", 0)], [("psF", pz)])
    tr(psF[pz][:, 128:128 + 112], stg[0:112, 1, :], ident_f[0:112, 0:112], ["consts", ("stg", 1)], [("psF", pz)])
    cp("dve", cols[:, 0:240], psF[pz][:, 0:240], [("psF", pz)], ["cols"])
    psf_free(pz)

    dma("sp", bcp[:], bc_d[0:1, :].to_broadcast([128, 808]), (), ["kvn_bc", "ssdn_bc", "sm_bc"])
    act(A_bc[:], sm_bc[:, 16:32], AF.Exp, ["sm_bc"], ["A_bc"])
    ts("dve", A_bc[:], A_bc[:], -1.0, None, ALU.mult, None, ["A_bc"], ["A_bc"])
    cp("dve", dsk_bc[:].rearrange("p (h c) -> p h c", h=8),
       sm_bc[:, 32:40].unsqueeze(2).to_broadcast([128, 8, 64]), ["sm_bc"], ["dsk_bc"])
    ts("dve", qn32[:], cols[:, B_QN:B_QN + 2], 16.0, None, ALU.mult, None, ["cols"], ["qn32"])
    ts("dve", kvn32[:], cols[:, B_KVN:B_KVN + 2], 16.0, None, ALU.mult, None, ["cols"], ["kvn32"])

    dma("pool", wsm[:, :, 0:32], w_in[0, :, 512:544].rearrange("(k p) n -> p k n", p=128), (), ["wsm"])
    dma("pool", wsm[:, :, 32:48], w_in[0, :, 2080:2096].rearrange("(k p) n -> p k n", p=128), (), ["wsm"])
    dma("pool", wuq[:].rearrange("p k h c -> p k (h c)"), w_uq[0].rearrange("(k p) n -> p k n", p=128), (), ["wuq"])
    dma("pool", wukv[:].rearrange("p k h c -> p k (h c)"), w_ukv[0].rearrange("(k p) n -> p k n", p=128), (), ["wukv"])
    P.op("dve", lambda e: e.memset(wuq_sw[:], 0.0), (), ["wuq_sw"])
    ts("dve", wuq_sw[:, :, :, 64:80], wuq[:, :, :, 80:96], -1.0, None, ALU.mult, None, ["wuq"], ["wuq_sw"])
    cp("dve", wuq_sw[:, :, :, 80:96], wuq[:, :, :, 64:80], ["wuq"], ["wuq_sw"])
    wsm_sw = sb("wsm_sw", [128, 8, 32], BF16)
    ts("dve", wsm_sw[:, :, 0:16], wsm[:, :, 16:32], -1.0, None, ALU.mult, None, ["wsm"], ["wsm_sw"])
    cp("dve", wsm_sw[:, :, 16:32], wsm[:, :, 0:16], ["wsm"], ["wsm_sw"])

    xm_in = nc.dram_tensor("xm_in", [4, 1536], F32, kind="Internal").ap()
    xm_out = nc.dram_tensor("xm_out", [16, 1536], F32, kind="Internal").ap()
    act(csil[:].rearrange("p k v -> p v k"),
        cols[:, B_CVEC:B_CVEC + 16].rearrange("p (v k) -> p v k", v=2), AF.Silu, ["cols"], ["csil"])
    for l in range(2):
        for cb in range(4):
            s = ring_load([(0, [128, 8, 384], w_mod[l, :, cb * 384:(cb + 1) * 384].rearrange("(k p) n -> p k n", p=128))])
            wv = rview(s, 0, [128, 8, 384])
            pz = psf()
            mm_group(psF[pz][0:2, 0:384], [(csil[:, k, :], wv[:, k, :]) for k in range(8)],
                     ["csil", ("ring", s)], [("psF", pz)])
            alt_cp(mrow[:, l, cb * 384:(cb + 1) * 384], psF[pz][0:2, 0:384], [("psF", pz)], ["mrow"])
            psf_free(pz)
    dma("sp", xm_in.rearrange("(v l) c -> v (l c)", v=2), mrow[:].rearrange("p l n -> p (l n)"), ["mrow"], ["xm_in"])
    P.dma("pool", lambda e: e.collective_compute("AllGather", ALU.bypass, replica_groups=GROUPS,
                                                 ins=[xm_in.opt()], outs=[xm_out.opt()]), ["xm_in"], ["xm_out"], inc=1)

    for t in ("P", "S"):
        for tb in range(4):
            xl = xld[rr["xld"] % 2]; xk = ("xld", rr["xld"] % 2); rr["xld"] += 1
            dma("sp", xl[:], xin[t][tb * 128:(tb + 1) * 128, :], (), [xk])
            for half in range(2):
                pz = psf()
                for q in range(4):
                    k = half * 4 + q
                    tr(psF[pz][:, q * 128:(q + 1) * 128], xl[:, k * 128:(k + 1) * 128], ident_f,
                       ["consts", xk], [("psF", pz)], signal=(q == 3))
                alt_cp(xT[t][:, half * 4:half * 4 + 4, tb * 128:(tb + 1) * 128],
                       psF[pz][:].rearrange("p (q c) -> p q c", q=4), [("psF", pz)], [("xT", t)])
                psf_free(pz)


    dma("sp", Gm[:, :], xm_out[:, :], ["xm_out"], ["Gm"])
    pz = psf()
    for cb in range(12):
        tr(psF[pz][:, cb * 16:(cb + 1) * 16], Gm[:, cb * 128:(cb + 1) * 128], ident_f[0:16, 0:16],
           ["consts", "Gm"], [("psF", pz)], signal=(cb == 11))
    pv = psF[pz][:, 0:192].rearrange("p (cb r v l) -> p r cb v l", cb=12, r=4, v=2, l=2)
    for l in range(2):
        for v2 in range(2):
            tt("dve", modT[:, l, :, v2].rearrange("p (r cb) -> p r cb", r=4), pv[:, :, :, v2, l],
               cols[:, B_BMOD + l * 48:B_BMOD + (l + 1) * 48].rearrange("p (r cb) -> p r cb", r=4), ALU.add,
               [("psF", pz), "cols"], [("modT", l)])
    psf_free(pz)
    for l in range(2):
        def ncol(base):
            return cols[:, base + l * 8:base + (l + 1) * 8].unsqueeze(2).to_broadcast([128, 8, 2])
        for (kind, jscale, nbase) in ((0, 1, B_NPM), (3, 4, B_NFR)):
            ts("dve", mcol[:, l, kind], modT[:, l, jscale * 8:(jscale + 1) * 8, :], 1.0, 32.0, ALU.add, ALU.mult,
               [("modT", l)], [("mcol", l)])
            tt("dve", mcol[:, l, kind], mcol[:, l, kind], ncol(nbase), ALU.mult, [("mcol", l), "cols"], [("mcol", l)])
        for (kind, jsh) in ((1, 0), (4, 3)):
            cp("dve", mcol[:, l, kind], modT[:, l, jsh * 8:(jsh + 1) * 8, :], [("modT", l)], [("mcol", l)])
        for (kind, jg, nbase) in ((2, 2, B_NPO), (5, 5, B_NFO)):
            stt("dve", mcol[:, l, kind], modT[:, l, jg * 8:(jg + 1) * 8, :], 32.0, ncol(nbase), ALU.mult, ALU.mult,
                [("modT", l), "cols"], [("mcol", l)])


    P.barrier()
    VI = {"P": 0, "S": 1}

    def rstd_from_sq(nchunks, scale_const, reads):
        pz = psf()
        mm_group(psF[pz][:, :], [(ones_b, sqT[:, k, :]) for k in range(nchunks)], ["consts_b"] + [("sqT", k_) for k_ in range(nchunks)] + list(reads), [("psF", pz)])
        rsqrt_to(rstd[:], psF[pz][:, :], scale_const * EPS, [("psF", pz)], ["rstd"])
        psf_free(pz)

    def modulate(t, l, kind_gs, kind_sh, outf=None):
        v = VI[t]
        act(sqT[:], xT[t][:], AF.Square, [("xT", t)], [("sqT", k_) for k_ in range(8)])
        rstd_from_sq(8, 1024.0, [])
        for k in range(8):
            tm = tmpA[rr["tmp"] % 2]; tk = ("tmpA", rr["tmp"] % 2); rr["tmp"] += 1
            stt("dve", tm[:], xT[t][:, k, :], mcol[:, l, kind_gs, k, v:v + 1], rstd[:], ALU.mult, ALU.mult,
                [("xT", t), ("mcol", l), "rstd"], [tk])
            if outf is None:
                act(hT[t][:, k, :], tm[:], AF.Identity, [tk, ("mcol", l)], [("hT", t)], bias=mcol[:, l, kind_sh, k, v:v + 1])
            else:
                o_ap, i_ap, o_key = outf(k, tm)
                act(o_ap, i_ap, AF.Identity, [tk, ("mcol", l)], [o_key], bias=mcol[:, l, kind_sh, k, v:v + 1])

    def post_norm_residual(t, l, kind_g):
        v = VI[t]
        rstd_from_sq(8, 1024.0, [])
        for k in range(8):
            tm = tmpA[rr["tmp"] % 2]; tk = ("tmpA", rr["tmp"] % 2); rr["tmp"] += 1
            stt("dve", tm[:], mixT[:, k, :], mcol[:, l, kind_g, k, v:v + 1], rstd[:], ALU.mult, ALU.mult,
                ["mixT", ("mcol", l), "rstd"], [tk])
            tt("dve", xT[t][:, k, :], xT[t][:, k, :], tm[:], ALU.add, [("xT", t), tk], [("xT", t)])

    def evac_mix(pz, k):
        cp("dve", mixT[:, k, :], psF[pz][:, :], [("psF", pz)], ["mixT"])
        act(sqT[:, k, :], mixT[:, k, :], AF.Square, ["mixT"], [("sqT", k)])

    def ffn(l):
        actT = arena[:, 0:22 * 2 * NT].rearrange("p (f n) -> p f n", f=22)
        for t in ("P", "S"):
            modulate(t, l, 3, 4)
        STAGE = int(os.environ.get('KFFN_STAGE', '3'))
        if STAGE < 2:
            return
        for nb in range(11):
            s = ring_load([(0, [128, 8, 256], w_gate[l, :, nb * 256:(nb + 1) * 256].rearrange("(k p) n -> p k n", p=128)),
                           (2048, [128, 8, 256], w_up[l, :, nb * 256:(nb + 1) * 256].rearrange("(k p) n -> p k n", p=128))])
            wg = rview(s, 0, [128, 8, 256]); wu = rview(s, 2048, [128, 8, 256])
            for ti, t in enumerate(("P", "S")):
                for c in range(2):
                    f = nb * 2 + c
                    pg = psf(); pu = psf()
                    mm_group(psF[pg][:, :], [(wg[:, k, c * 128:(c + 1) * 128], hT[t][:, k, :]) for k in range(8)],
                             [("ring", s), ("hT", t)], [("psF", pg)])
                    mm_group(psF[pu][:, :], [(wu[:, k, c * 128:(c + 1) * 128], hT[t][:, k, :]) for k in range(8)],
                             [("ring", s), ("hT", t)], [("psF", pu)])
                    tm = tmpA[rr["tmp"] % 2]; tk = ("tmpA", rr["tmp"] % 2); rr["tmp"] += 1
                    KGU = int(os.environ.get("KGU", "0"))
                    if KGU in (0, 1):
                        act(tm[:], psF[pg][:, :], AF.Silu, [("psF", pg)], [tk])
                    if KGU in (0, 2):
                        tt("dve", actT[:, f, ti * NT:(ti + 1) * NT], psF[pu][:, :], tm[:], ALU.mult,
                           [tk, ("psF", pu)], [("actT", f)])
                    psf_free(pg); psf_free(pu)
        for ti, t in enumerate(("P", "S")):
            pass
        if STAGE < 3:
            return
        wdb = [arena[:, 26624 + i * 5632:26624 + (i + 1) * 5632].rearrange("p (f n) -> p f n", f=22) for i in range(2)]
        for db in range(4):
            wb = wdb[db % 2]; wk = ("wdblk", db % 2)
            for (f0, f1) in ((0, 11), (11, 22)):
                dma("pool", wb[:, f0:f1, :],
                    w_down[l, f0 * 128:f1 * 128, db * 256:(db + 1) * 256].rearrange("(f p) n -> p f n", p=128),
                    (), [wk])
            for d2 in range(2):
                dc = db * 2 + d2
                for ti, t in enumerate(("P", "S")):
                    pz = psf()
                    mm_group(psF[pz][:, :], [(wb[:, f, d2 * 128:(d2 + 1) * 128], actT[:, f, ti * NT:(ti + 1) * NT]) for f in range(22)],
                             [wk] + [("actT", f) for f in range(22)], [("psF", pz)])
                    mb = mixT2[t]
                    mkeys = [("mix2", t), ("hT", "P"), ("hT", "S")] if t == "S" else [("mix2", t)]
                    cp("dve", mb[:, dc, :], psF[pz][:, :], [("psF", pz)], mkeys)
                    act(sq2[t][:, dc, :], mb[:, dc, :], AF.Square, mkeys, [("sq2", t)])
                    psf_free(pz)
        for t in ("P", "S"):
            v = VI[t]
            pz = psf()
            mm_group(psF[pz][:, :], [(ones_b, sq2[t][:, k, :]) for k in range(8)], ["consts_b", ("sq2", t)], [("psF", pz)])
            rsqrt_to(rstd[:], psF[pz][:, :], 1024.0 * EPS, [("psF", pz)], ["rstd"])
            psf_free(pz)
            for k in range(8):
                tm = tmpA[rr["tmp"] % 2]; tk = ("tmpA", rr["tmp"] % 2); rr["tmp"] += 1
                stt("dve", tm[:], mixT2[t][:, k, :], mcol[:, l, 5, k, v:v + 1], rstd[:], ALU.mult, ALU.mult,
                    [("mix2", t), ("mcol", l), "rstd"], [tk])
                tt("dve", xT[t][:, k, :], xT[t][:, k, :], tm[:], ALU.add, [("xT", t), tk], [("xT", t)])

    mixT2 = {"P": mixT, "S": hT2[:].rearrange("p t k n -> p (t k n)").bitcast(F32).rearrange("p (k n) -> p k n", k=8)}
    sq2 = {"P": sqT, "S": arena[:, 22 * 2 * NT:22 * 2 * NT + 8 * NT].rearrange("p (k n) -> p k n", k=8)}

    def av(off, shape, dt=BF16):
        n = 1
        for v_ in shape[1:]:
            n *= v_
        if dt == F32:
            v = arena[0:shape[0], off:off + 2 * n].bitcast(F32)
        else:
            v = arena[0:shape[0], off:off + n]
        if len(shape) == 3:
            v = v.rearrange("p (a b) -> p a b", a=shape[1])
        elif len(shape) == 4:
            v = v.rearrange("p (a b c) -> p a b c", a=shape[1], b=shape[2])
        return v

    xbcpad = {"S": av(0, [128, 8, 516]), "P": av(19232, [128, 8, 520])}
    diagW = av(4128, [128, 40, 128])
    Vt = {"S": av(0, [128, 18, 512]), "P": av(19232, [128, 4, 512])}
    cqn = {"S": av(9248, [128, 2, 512]), "P": av(34656, [128, 2, 512])}
    Ksrc = av(12320, [128, 2, 2304]); kpeT = av(16928, [32, 2304])
    xconvT = av(23392, [128, 8, 512]); xtok = av(27488, [128, 4, 768]); hprev = av(30560, [128, 2, 4, 512])
    x1buf = av(27488, [128, 1568])
    gX = av(30560, [128, 4, 32])
    wuk96 = av(36864, [128, 2, 8, 96])
    smf = arena[:, 35680:36864].bitcast(F32)
    dtraw = {"P": smf[:, 0:64].rearrange("p (a b) -> p a b", a=4), "S": smf[:, 64:128].rearrange("p (a b) -> p a b", a=4)}
    dtw = smf[:, 128:192].rearrange("p (a b) -> p a b", a=4)
    a_ = smf[:, 192:256].rearrange("p (a b) -> p a b", a=4)
    smx = smf[:, 256:384].rearrange("p (a b) -> p a b", a=4)
    E_ = smf[:, 384:448].rearrange("p (a b) -> p a b", a=4)
    CD_ = smf[:, 448:512].rearrange("p (a b) -> p a b", a=4)
    ac2 = smf[:, 512:576].rearrange("p (a b) -> p a b", a=4)
    hflat = hT2[:].rearrange("p t k n -> p (t k n)")
    OT = hflat[0:64, 0:4096].rearrange("p (h n) -> p h n", h=8)
    ssdT = hflat[:, 4096:6144].rearrange("p (k n) -> p k n", k=4)
    zs = {"S": av(10272, [128, 4, 512]), "P": hflat[:, 6144:8192].rearrange("p (k n) -> p k n", k=4)}
    x2buf = hflat[:, 0:2080].bitcast(F32)
    sflat = sqT[:].rearrange("p k n -> p (k n)")
    Kh = sflat[0:96, 0:2304]; Qh = sflat[0:96, 2304:2816]
    PT = [sflat[:, 2816:3328], sflat[:, 3328:3840]]
    mflat = mixT[:].rearrange("p k n -> p (k n)")
    mbf = mflat.bitcast(BF16)
    MT = mbf[:, 0:2048].rearrange("p (h n) -> p h n", h=16)
    CBm = mflat[:, 1024:1536].rearrange("p (h n) -> p h n", h=4)
    ysb = mflat[:, 1536:2048]; yg = mflat[:, 2048:2560]
    xw = mbf[:, 5120:5632]; xd = mbf[:, 5632:6144]
    hrun = mflat[:, 3072:4096].rearrange("p (d n) -> p d n", d=2)
    hcand = mflat[:, 1536:2560].rearrange("p (d n) -> p d n", d=2)
    hin = mflat[:, 0:1024].rearrange("p (d n) -> p d n", d=2)
    acumT = rstd[0:8, 0:256]; stat2 = sb("stat2", [128, 16])
    gS = av(0, [128, 4, 1040], F32)
    SEGS = {"P": [[0, 1], [2, 3]], "S": [[0, 1, 2, 3]]}

    def h8(ap2d):
        return ap2d.rearrange("p (h c) -> p h c", h=8)

    def bc8(ap_8):
        return ap_8.unsqueeze(2).to_broadcast([128, 8, 64])

    def coll(in_ap, out_ap, rk, wk):
        P.dma("pool", lambda e: e.collective_compute("AllGather", ALU.bypass, replica_groups=GROUPS,
                                                     ins=[in_ap.opt()], outs=[out_ap.opt()]), rk, wk, inc=1)

    def inproj(t, l):
        modulate(t, l, 0, 1)
        hk = ("hT", t)
        sA = ring_load([(0, [128, 8, 512], w_in[0, :, 0:512].rearrange("(k p) n -> p k n", p=128))])
        wA = rview(sA, 0, [128, 8, 512])
        for j in range(4):
            pz = psf()
            mm_group(psF[pz][:, :], [(wA[:, k, j * 128:(j + 1) * 128], hT[t][:, k, :]) for k in range(8)],
                     [("ring", sA), hk], [("psF", pz)])
            cp("dve", mixT[:, j, :], psF[pz][:, :], [("psF", pz)], [("cqf", j)])
            act(sqT[:, j, :], mixT[:, j, :], AF.Square, [("cqf", j)], [("sqT", j)])
            psf_free(pz)
        for (j0, scl, dst) in ((0, qn32, None), (2, kvn32, None)):
            pz = psf()
            mm_group(psF[pz][:, :], [(ones_b, sqT[:, j0 + jj, :]) for jj in range(2)],
                     ["consts_b", ("sqT", j0), ("sqT", j0 + 1)], [("psF", pz)])
            rsqrt_to(rstd[:], psF[pz][:, :], 256.0 * EPS, [("psF", pz)], ["rstd"])
            psf_free(pz)
            for jj in range(2):
                if j0 == 0:
                    o, ok = cqn[t][:, jj, :], ("cqn", t)
                elif t == "P":
                    o, ok = Ksrc[:, jj, 0:512], "Ksrc"
                else:
                    o, ok = x1buf[:, jj * 512:(jj + 1) * 512], "x1buf"
                stt("dve", o, mixT[:, j0 + jj, :], scl[:, jj:jj + 1], rstd[:], ALU.mult, ALU.mult,
                    [("cqf", j0 + jj), "qn32", "kvn32", "rstd"], [ok])
        pz = psf()
        mm_group(psF[pz][0:32, :], [(wsm[:, k, 0:32], hT[t][:, k, :]) for k in range(8)], ["wsm", hk], [("psF", pz)])
        if t == "P":
            cp("dve", kpeT[0:32, 0:512], psF[pz][0:32, :], [("psF", pz)], ["kpeT"])
        else:
            pz2 = psf()
            mm_group(psF[pz2][0:32, :], [(wsm_sw[:, k, :], hT[t][:, k, :]) for k in range(8)], ["wsm_sw", hk], [("psF", pz2)])
            tt("dve", tmpA[0][0:32, :], psF[pz][0:32, :], rope_lo[:, 0, :], ALU.mult, [("psF", pz), "rope_lo"], [("tmpA", 0)])
            tt("dve", tmpA[1][0:32, :], psF[pz2][0:32, :], rope_lo[:, 1, :], ALU.mult, [("psF", pz2), "rope_lo"], [("tmpA", 1)])
            tt("dve", x1buf[0:32, 1024:1536], tmpA[0][0:32, :], tmpA[1][0:32, :], ALU.add, [("tmpA", 0), ("tmpA", 1)], ["x1buf"])
            psf_free(pz2)
        psf_free(pz)
        if t == "P":
            for tb in range(4):
                pz = psf()
                mm_group(psF[pz][:, 0:256], [(hT[t][:, k, tb * 128:(tb + 1) * 128], wA[:, k, 256:512]) for k in range(8)],
                         [("ring", sA), hk], [("psF", pz)])
                mm_group(psF[pz][:, 256:288], [(hT[t][:, k, tb * 128:(tb + 1) * 128], wsm[:, k, 0:32]) for k in range(8)],
                         ["wsm", hk], [("psF", pz)])
                ct = tmpA[tb % 2]; ck = ("tmpA", tb % 2)
                P.op("dve", lambda e: e.memset(stat2[:, 0:1], 0.0), (), ["stat2"])
                cp("dve", ct[:, 0:288], psF[pz][:, 0:288], [("psF", pz)], [ck])
                psf_free(pz)
                act(sqT[:, 4, 0:256], ct[:, 0:256], AF.Square, [ck, "stat2"], [("sqT", 4), "stat2"], accum=stat2[:, 0:1])
                rsqrt_to(stat2[:, 0:1], stat2[:, 0:1], EPS, ["stat2"], ["stat2"], scale=1.0 / 256.0)
                stt("dve", ct[:, 0:256], ct[:, 0:256], stat2[:, 0:1], kvn_bc[:], ALU.mult, ALU.mult,
                    [ck, "stat2", "kvn_bc"], [ck])
                dma("sp", ockv[tb * 128:(tb + 1) * 128, :], ct[:, 0:256], [ck], ["ockv"])
                dma("sp", okpe[tb * 128:(tb + 1) * 128, :], ct[:, 256:288], [ck], ["okpe"])
        sZ = ring_load([(0, [128, 8, 512], w_in[0, :, 544:1056].rearrange("(k p) n -> p k n", p=128))])
        wZ = rview(sZ, 0, [128, 8, 512])
        for tb in range(4):
            pz = psf()
            mm_group(psF[pz][:, :], [(hT[t][:, k, tb * 128:(tb + 1) * 128], wZ[:, k, :]) for k in range(8)],
                     [("ring", sZ), hk], [("psF", pz)])
            act(zs[t][:, tb, :], psF[pz][:, :], AF.Silu, [("psF", pz)], [("zs", t)])
            psf_free(pz)
        for tb in range(4):
            pz = psf()
            mm_group(psF[pz][:, 0:16], [(hT[t][:, k, tb * 128:(tb + 1) * 128], wsm[:, k, 32:48]) for k in range(8)],
                     ["wsm", hk], [("psF", pz)])
            tt("dve", dtraw[t][:, tb, :], psF[pz][:, 0:16], sm_bc[:, 0:16], ALU.add, [("psF", pz), "sm_bc"], [("dtraw", t)])
            psf_free(pz)
        for xb in range(2):
            sX = ring_load([(0, [128, 8, 512], w_in[0, :, 1056 + xb * 512:1568 + xb * 512].rearrange("(k p) n -> p k n", p=128))])
            wX = rview(sX, 0, [128, 8, 512])
            for jj in range(4):
                j = xb * 4 + jj
                pz = psf()
                mm_group(psF[pz][:, :], [(wX[:, k, jj * 128:(jj + 1) * 128], hT[t][:, k, :]) for k in range(8)],
                         [("ring", sX), hk], [("psF", pz)])
                if t == "P":
                    alt_cp(xbcpad["P"][:, j, :].rearrange("p (s c) -> p s c", s=2)[:, :, 2:258],
                           psF[pz][:, :].rearrange("p (s c) -> p s c", s=2), [("psF", pz)], [("xbcpad", t)])
                else:
                    alt_cp(xbcpad["S"][:, j, 2:514], psF[pz][:, :], [("psF", pz)], [("xbcpad", t)])
                psf_free(pz)
        if t == "S":
            xe = x1buf[:, 1536:1568].rearrange("p (j c) -> p j c", j=8)
            cp("dve", xe[:, :, 0:2], xbcpad["S"][:, :, 2:4], [("xbcpad", t)], ["x1buf"])
            cp("dve", xe[:, :, 2:4], xbcpad["S"][:, :, 512:514], [("xbcpad", t)], ["x1buf"])

    def x1_exchange():
        dma("sp", x1_in[:, :], x1buf[:, :], ["x1buf"], ["x1_in"])
        coll(x1_in, x1_out, ["x1_in"], ["x1_out"])

    def x1_receive():
        x1r = x1_out.rearrange("(r p) c -> p r c", p=128)
        for kc in range(2):
            dma("sp", Ksrc[:, kc, 256:2304].rearrange("p (r t) -> p r t", r=4), x1r[:, :, kc * 512:(kc + 1) * 512],
                ["x1_out"], ["Ksrc"])
        dma("sp", kpeT[0:32, 256:2304].rearrange("p (r t) -> p r t", r=4), x1r[0:32, :, 1024:1536], ["x1_out"], ["kpeT"])
        dma("sp", gX[:], x1r[:, :, 1536:1568], ["x1_out"], ["gX"])
        cc = mixT[:, 0:2, 0:288]
        for tbk in range(2):
            dma("sp", cc[:, tbk, 0:256], cache_ckv[tbk * 128:(tbk + 1) * 128, :], (), [("cc", tbk)])
            dma("sp", cc[:, tbk, 256:288], cache_kpe[tbk * 128:(tbk + 1) * 128, :], (), [("cc", tbk)])
        for tbk in range(2):
            pz = psf()
            tr(psF[pz][:, 0:128], cc[:, tbk, 0:128], ident_f, ["consts", ("cc", tbk)], [("psF", pz)], signal=False)
            tr(psF[pz][:, 128:256], cc[:, tbk, 128:256], ident_f, ["consts", ("cc", tbk)], [("psF", pz)], signal=False)
            tr(psF[pz][0:32, 256:384], cc[:, tbk, 256:288], ident_f, ["consts", ("cc", tbk)], [("psF", pz)])
            cp("dve", Ksrc[:, :, tbk * 128:(tbk + 1) * 128], psF[pz][:, 0:256].rearrange("p (k n) -> p k n", k=2),
               [("psF", pz)], ["Ksrc"])
            cp("dve", kpeT[0:32, tbk * 128:(tbk + 1) * 128], psF[pz][0:32, 256:384], [("psF", pz)], ["kpeT"])
            psf_free(pz)
        hal = tmpA[0][:, 0:32].rearrange("p (s j c) -> p s j c", s=2, j=8)
        for side, (c0, sb_) in enumerate(((2, 4), (0, 8))):
            for rp in range(4):
                src = gX[:, rp, :].rearrange("p (j c) -> p j c", j=8)[:, :, c0:c0 + 2]
                if rp == 0:
                    ts("dve", hal[:, side], src, sel[:, sb_ + rp:sb_ + rp + 1], None, ALU.mult, None, ["gX", "sel"], [("tmpA", 0)])
                else:
                    stt("dve", hal[:, side], src, sel[:, sb_ + rp:sb_ + rp + 1], hal[:, side], ALU.mult, ALU.add,
                        ["gX", "sel", ("tmpA", 0)], [("tmpA", 0)])
        cp("dve", xbcpad["S"][:, :, 0:2], hal[:, 0], [("tmpA", 0)], [("xbcpad", "S")])
        cp("dve", xbcpad["S"][:, :, 514:516], hal[:, 1], [("tmpA", 0)], [("xbcpad", "S")])

    def conv(t):
        segs = [(0, 0, 256), (260, 256, 256)] if t == "P" else [(0, 0, 512)]
        for j in range(8):
            pz = psf()
            for (pb_, oc, n) in segs:
                mm_group(psF[pz][:, oc:oc + n],
                         [(diagW[:, w * 8 + j, :], xbcpad[t][:, j, pb_ + w:pb_ + w + n]) for w in range(5)],
                         [("diagW", 0), ("xbcpad", t)], [("psF", pz)])
            act(xconvT[:, j, :], psF[pz][:, :], AF.Silu, [("psF", pz), "cols"], [("xconvT", j)],
                bias=cols[:, B_CONVB + j:B_CONVB + j + 1])
            psf_free(pz)
        for tb in range(4):
            pb_ = psb()
            for j in range(6):
                tr(psB[pb_][:, j * 128:(j + 1) * 128], xconvT[:, j, tb * 128:(tb + 1) * 128], ident_b,
                   ["consts_b", ("xconvT", j)], [("psB", pb_)], signal=(j == 5))
            alt_cp(xtok[:, tb, :], psB[pb_][:, 0:768], [("psB", pb_)], [("xtok", tb)])
            psb_free(pb_)

    def ssd_small(t):
        dk = ("dtraw", t)
        act(dtw[:], dtraw[t][:], AF.Exp, [dk], ["dtw"])
        act(dtw[:], dtw[:], AF.Ln, ["dtw"], ["dtw"], bias=1.0)
        tt("dve", a_[:], dtw[:], A_bc[:].unsqueeze(1).to_broadcast([128, 4, 16]), ALU.mult, ["dtw", "A_bc"], ["a_"])
        act(ac2[:], dtw[:], AF.Ln, ["dtw"], ["ac2"])
        for tb in range(4):
            pz = psf()
            mm(psF[pz][:, 0:8], U_f, a_[:, tb, 0:8], True, True, ["consts", "a_"], [("psF", pz)], False)
            mm(psF[pz][:, 8:16], L_f, a_[:, tb, 8:16], True, True, ["consts", "a_"], [("psF", pz)], False)
            mm(psF[pz][:, 16:32], ones_f, a_[:, tb, :], True, True, ["consts", "a_"], [("psF", pz)], True)
            cp("act", smx[:, tb, :], psF[pz][:, 0:32], [("psF", pz)], ["smx"])
            psf_free(pz)
        act(E_[:], smx[:, :, 0:16], AF.Exp, ["smx"], ["E_"])
        act(CD_[:], smx[:, :, 16:32], AF.Exp, ["smx"], ["CD_"])
        tt("dve", ac2[:], smx[:, :, 0:16], ac2[:], ALU.subtract, ["smx", "ac2"], ["ac2"])
        wt = tmpA[0][:, 0:64].rearrange("p (a b) -> p a b", a=4)
        tt("dve", wt, smx[:, :, 16:32], smx[:, :, 0:16], ALU.subtract, ["smx"], [("tmpA", 0)])
        act(wt, wt, AF.Exp, [("tmpA", 0)], [("tmpA", 0)])
        tt("dve", dtw[:], dtw[:], wt, ALU.mult, ["dtw", ("tmpA", 0)], ["dtw"])

    def prepass(t, use_hin, final_cb, dirs=(0, 1)):
        xws = {0: xw, 1: xd}

        def step(d, tb):
            xw_, xk_ = xws[d], ("xw", d)
            cp("act", hprev[:, d, tb, :], hrun[:, d, :], [("hrun", d)], [("hprev", d, tb)])
            tt("dve", h8(xw_), h8(xtok[:, tb, 0:512]), bc8(dtw[:, tb, d * 8:(d + 1) * 8]), ALU.mult,
               [("xtok", tb), "dtw"], [xk_, "xw", "xd"] if False else [xk_])
            pz = psf()
            for g in range(2):
                mm(psF[pz][:, g * 256:(g + 1) * 256], xtok[:, tb, 512 + g * 128:512 + (g + 1) * 128],
                   xw_[:, g * 256:(g + 1) * 256], True, True, [("xtok", tb), xk_], [("psF", pz)], g == 1)
            tt("dve", h8(hrun[:, d, :]), h8(hrun[:, d, :]), bc8(CD_[:, tb, d * 8:(d + 1) * 8]), ALU.mult,
               [("hrun", d), "CD_"], [("hrun", d)])
            tt("dve", hrun[:, d, :], psF[pz][:, :], hrun[:, d, :], ALU.add, [("psF", pz), ("hrun", d)], [("hrun", d)])
            psf_free(pz)

        for seg in SEGS[t]:
            for d in dirs:
                if use_hin:
                    cp("act", hrun[:, d, :], hin[:, d, :], ["hin"], [("hrun", d)])
                else:
                    P.op("dve", lambda e, d=d: e.memset(hrun[:, d, :], 0.0), (), [("hrun", d)])
            orders = {d: (seg if d == 0 else list(reversed(seg))) for d in dirs}
            for i in range(len(seg)):
                for d in dirs:
                    step(d, orders[d][i])
            for d in dirs:
                final_cb(d, seg)

    def final_P(d, seg):
        s_ = seg[0] // 2
        pz = psf()
        for q in range(4):
            tr(psF[pz][:, q * 128:(q + 1) * 128], hrun[:, d, q * 128:(q + 1) * 128], ident_f,
               ["consts", ("hrun", d)], [("psF", pz)], signal=(q == 3))
        st_ = tmpA[d]
        cp("act", st_[:], psF[pz][:, :], [("psF", pz)], [("tmpA", d)])
        psf_free(pz)
        dma("sp", ohs[d][s_].rearrange("(q p) n -> p q n", p=128), st_[:].rearrange("p (q n) -> p q n", q=4),
            [("tmpA", d)], [("ohs", d)])

    def final_S1(d, seg):
        cp("act", x2buf[:, d * 512:(d + 1) * 512], hrun[:, d, :], [("hrun", d)], ["x2buf"])
        tsum = tmpA[0][:, 0:8]
        tt("dve", tsum, smx[:, 0, 16 + d * 8:24 + d * 8], smx[:, 1, 16 + d * 8:24 + d * 8], ALU.add, ["smx"], [("tmpA", 0)])
        tt("dve", tsum, tsum, smx[:, 2, 16 + d * 8:24 + d * 8], ALU.add, ["smx", ("tmpA", 0)], [("tmpA", 0)])
        tt("dve", tsum, tsum, smx[:, 3, 16 + d * 8:24 + d * 8], ALU.add, ["smx", ("tmpA", 0)], [("tmpA", 0)])
        act(x2buf[:, 1024 + d * 8:1032 + d * 8], tsum, AF.Exp, [("tmpA", 0)], ["x2buf"])

    def x2_exchange():
        dma("sp", x2_in[:, :], x2buf[:, :], ["x2buf"], ["x2_in"])
        coll(x2_in, x2_out, ["x2_in"], ["x2_out"])

    gSb = [av(19232, [128, 1040], F32), av(19232 + 2080, [128, 1040], F32)]

    def x2_receive():
        nld = [0]

        def load_rank(rp):
            i = nld[0] % 2; nld[0] += 1
            dma("sp", gSb[i][:, :], x2_out[rp * 128:(rp + 1) * 128, :], ["x2_out"], [("gSb", i)])
            return gSb[i], ("gSb", i)
        for d in range(2):
            st_ = tmpA[d]
            dma("sp", st_[:].rearrange("p (q n) -> p q n", q=4), h0_d[d].rearrange("(q p) n -> p q n", p=128), (), [("tmpA", d)])
            pz = psf()
            for q in range(4):
                tr(psF[pz][:, q * 128:(q + 1) * 128], st_[:, q * 128:(q + 1) * 128], ident_f,
                   ["consts", ("tmpA", d)], [("psF", pz)], signal=(q == 3))
            cp("act", hcand[:, d, :], psF[pz][:, :], [("psF", pz)], [("hcand", d)])
            psf_free(pz)
            ranks = [0, 1, 2] if d == 0 else [3, 2, 1]
            first = 0 if d == 0 else 3
            ts("dve", hin[:, d, :], hcand[:, d, :], sel[:, first:first + 1], None, ALU.mult, None,
               [("hcand", d), "sel"], ["hin"])
            for rp in ranks:
                nxt = rp + 1 if d == 0 else rp - 1
                g_, gk = load_rank(rp)
                tt("dve", h8(hcand[:, d, :]), h8(hcand[:, d, :]), bc8(g_[:, 1024 + d * 8:1032 + d * 8]), ALU.mult,
                   [("hcand", d), gk], [("hcand", d)])
                tt("dve", hcand[:, d, :], hcand[:, d, :], g_[:, d * 512:(d + 1) * 512], ALU.add,
                   [("hcand", d), gk], [("hcand", d)])
                stt("dve", hin[:, d, :], hcand[:, d, :], sel[:, nxt:nxt + 1], hin[:, d, :], ALU.mult, ALU.add,
                    [("hcand", d), "sel", "hin"], ["hin"])

    def ssd_main(t, tb):
        tbc = slice(tb * 128, (tb + 1) * 128)
        pa = psf()
        mm(psF[pa][0:8, 0:128], a_[:, tb, 0:8], U_f, True, True, ["consts", "a_"], [("psF", pa)], False)
        mm(psF[pa][0:8, 128:256], a_[:, tb, 8:16], L_f, True, True, ["consts", "a_"], [("psF", pa)], True)
        cp("act", acumT, psF[pa][0:8, 0:256], [("psF", pa)], ["acumT"])
        psf_free(pa)
        pc = psf()
        for g in range(2):
            mm(psF[pc][:, g * 128:(g + 1) * 128], xconvT[:, 4 + g, tbc], xconvT[:, 6 + g, tbc], True, True,
               [("xconvT", 4 + g), ("xconvT", 6 + g)], [("psF", pc)], g == 1)
        for g in range(2):
            for d in range(2):
                tt("dve", CBm[:, g * 2 + d, :], psF[pc][:, g * 128:(g + 1) * 128], U_f if d == 0 else L_f, ALU.mult,
                   [("psF", pc), "consts"], ["CBm"])
        psf_free(pc)
        for d in range(2):
            prs = []
            for g in range(2):
                pr = psf(); prs.append(pr)
                for hh in range(4):
                    h = g * 4 + hh
                    mm(psF[pr][:, hh * 128:(hh + 1) * 128], selc[0:8, h * 128:(h + 1) * 128], acumT[0:8, d * 128:(d + 1) * 128],
                       True, True, ["selc", "acumT"], [("psF", pr)], hh == 3)
            for g in range(2):
                pr = prs[g]
                for hh in range(4):
                    ci = d * 8 + g * 4 + hh
                    ts("dve", tmpA[g][:, hh * 128:(hh + 1) * 128], psF[pr][:, hh * 128:(hh + 1) * 128],
                       ac2[:, tb, ci:ci + 1], 30.0, ALU.subtract, ALU.min, [("psF", pr), "ac2"], [("tmpA", g)])
                psf_free(pr)
            for g in range(2):
                act(tmpA[g][:], tmpA[g][:], AF.Exp, [("tmpA", g)], [("tmpA", g)])
            for g in range(2):
                D3 = tmpA[g][:].rearrange("p (h n) -> p h n", h=4)
                stt("dve", MT[:, d * 8 + g * 4:d * 8 + g * 4 + 4, :], D3, 1e30,
                    CBm[:, g * 2 + d, :].unsqueeze(1).to_broadcast([128, 4, 128]), ALU.min, ALU.mult,
                    [("tmpA", g), "CBm"], [("MT", d, g)])
        tt("dve", xd, xtok[:, tb, 0:512], dsk_bc[:], ALU.mult, [("xtok", tb), "dsk_bc"], ["xd"])
        py = psf()
        mtk = [("MT", d, g) for d in range(2) for g in range(2)]
        for h in range(8):
            hc = slice(h * 64, (h + 1) * 64)
            mm(psF[py][:, hc], ident_b, xd[:, hc], True, False,
               (["consts_b", "xd"] + mtk + [("xtok", tb)]) if h == 0 else (), [("psF", py)], False)
            for d in range(2):
                mm(psF[py][:, hc], MT[:, d * 8 + h, :], xtok[:, tb, hc], False, d == 1,
                   (), [("psF", py)], (h == 7 and d == 1))
        pof = [psf(), psf()]
        for d in range(2):
            for g in range(2):
                mm(psF[pof[d]][:, g * 256:(g + 1) * 256], xconvT[:, 6 + g, tbc], hprev[:, d, tb, g * 256:(g + 1) * 256],
                   True, True, [("xconvT", 6 + g), ("hprev", d, tb)], [("psF", pof[d])], g == 1)
        cp("act", ysb, psF[py][:, :], [("psF", py)], ["ysb"])
        psf_free(py)
        for d in range(2):
            Dt = tmpA[d]; dk = ("tmpA", d)
            tt("dve", h8(Dt[:]), h8(psF[pof[d]][:, :]), bc8(E_[:, tb, d * 8:(d + 1) * 8]), ALU.mult,
               [("psF", pof[d]), "E_"], [dk])
            tt("dve", ysb, ysb, Dt[:], ALU.add, ["ysb", dk], ["ysb"])
            psf_free(pof[d])
        tt("dve", yg, ysb, zs[t][:, tb, :], ALU.mult, ["ysb", ("zs", t)], ["yg"])
        P.op("dve", lambda e: e.memset(stat2[:, 0:2], 0.0), (), ["stat2"])
        for g in range(2):
            act(tmpA[g][:, 0:256], yg[:, g * 256:(g + 1) * 256], AF.Square, ["yg"], [("tmpA", g), "stat2"],
                accum=stat2[:, g:g + 1])
        rsqrt_to(stat2[:, 0:2], stat2[:, 0:2], EPS, ["stat2"], ["stat2"], scale=1.0 / 256.0)
        for g in range(2):
            stt("dve", xw[:, g * 256:(g + 1) * 256], yg[:, g * 256:(g + 1) * 256], stat2[:, g:g + 1],
                ssdn_bc[:, g * 256:(g + 1) * 256], ALU.mult, ALU.mult, ["yg", "stat2", "ssdn_bc"], ["xw"])
        pb_ = psb()
        for k in range(4):
            tr(psB[pb_][:, k * 128:(k + 1) * 128], xw[:, k * 128:(k + 1) * 128], ident_b, ["consts_b", "xw"],
               [("psB", pb_)], signal=(k == 3))
        alt_cp(ssdT[:, :, tbc], psB[pb_][:, 0:512].rearrange("p (k n) -> p k n", k=4), [("psB", pb_)], ["ssdT"])
        psb_free(pb_)

    def attn_V(t):
        nkb = 18 if t == "S" else 4
        for kb in range(nkb):
            pz = psf()
            mm_group(psF[pz][:, :].rearrange("p (h c) -> p h c", h=8),
                     [(Ksrc[:, kc, kb * 128:(kb + 1) * 128], wukv[:, kc, :, 64:128]) for kc in range(2)],
                     ["Ksrc", "wukv"], [("psF", pz)])
            alt_cp(Vt[t][:, kb, :], psF[pz][:, :], [("psF", pz)], [("V", kb)])
            psf_free(pz)

    def attn_head(t, h):
        nkb = 18 if t == "S" else 4
        NK = nkb * 128
        if True:
            for kt in range((NK + 511) // 512):
                n = min(512, NK - kt * 512)
                kc_ = slice(kt * 512, kt * 512 + n)
                pz = psf()
                mm(psF[pz][0:96, 0:n], padI[0:32, 0:96], kpeT[0:32, kc_], True, False, ["padI", "kpeT"], [("psF", pz)], False)
                for kc in range(2):
                    mm(psF[pz][0:96, 0:n], wuk96[:, kc, h, :], Ksrc[:, kc, kc_], False, kc == 1,
                       ["wuk96", "Ksrc"], [("psF", pz)], kc == 1)
                alt_cp(Kh[:, kc_], psF[pz][0:96, 0:n], [("psF", pz)], ["Kh"])
                psf_free(pz)
            pq = psf()
            mm_group(psF[pq][0:96, :], [(wuq[:, kc, h, :], cqn[t][:, kc, :]) for kc in range(2)], ["wuq", ("cqn", t)], [("psF", pq)])
            if t == "P":
                alt_cp(Qh, psF[pq][0:96, :], [("psF", pq)], ["Qh"])
            else:
                pq2 = psf()
                mm_group(psF[pq2][0:96, :], [(wuq_sw[:, kc, h, :], cqn[t][:, kc, :]) for kc in range(2)],
                         ["wuq_sw", ("cqn", t)], [("psF", pq2)])
                cp("act", Qh[0:64, :], psF[pq][0:64, :], [("psF", pq)], ["Qh"])
                tt("dve", tmpA[0][64:96, :], psF[pq][64:96, :], rope_hi[64:96, 0, :], ALU.mult, [("psF", pq), "rope_hi"], [("tmpA", 0)])
                tt("dve", tmpA[1][64:96, :], psF[pq2][64:96, :], rope_hi[64:96, 1, :], ALU.mult, [("psF", pq2), "rope_hi"], [("tmpA", 1)])
                tt("dve", Qh[64:96, :], tmpA[0][64:96, :], tmpA[1][64:96, :], ALU.add, [("tmpA", 0), ("tmpA", 1)], ["Qh"])
                psf_free(pq2)
            psf_free(pq)
            po = psf(); pl = psf()
            if t == "P":
                plan = [(slice(s_ * 256, (s_ + 1) * 256), [2 * s_, 2 * s_ + 1]) for s_ in range(2)]
            else:
                plan = [(slice(0, 512), list(range(18)))]
            steps = []
            for (qc, kbs) in plan:
                for i, kb in enumerate(kbs):
                    steps.append((qc, kb, i == 0, i == len(kbs) - 1))
            pend = None
            for it, (qc, kb, first, lastk) in enumerate(steps):
                nq = qc.stop - qc.start
                psc = psf()
                mm(psF[psc][:, 0:nq], Kh[:, kb * 128:(kb + 1) * 128], Qh[:, qc], True, True, ["Kh", "Qh"], [("psF", psc)], True)
                if pend is not None:
                    pend()
                pt = PT[it % 2]; pk = ("PT", it % 2)
                act(pt[:, 0:nq], psF[psc][:, 0:nq], AF.Exp, [("psF", psc)], [pk], scale=SCALE)
                psf_free(psc)

                def pend(qc=qc, kb=kb, first=first, lastk=lastk, pt=pt, pk=pk, nq=nq):
                    mm(psF[po][0:64, qc], Vt[t][:, kb, h * 64:(h + 1) * 64], pt[:, 0:nq], first, lastk,
                       [("V", kb), pk], [("psF", po)], True)
                    mm(psF[pl][0:64, qc], ones_b[:, 0:64], pt[:, 0:nq], first, lastk, ["consts_b", pk], [("psF", pl)], True)
            pend()
            rl = tmpA[0][0:64, :]
            act(rl, psF[pl][0:64, :], AF.Ln, [("psF", pl)], [("tmpA", 0)])
            act(rl, rl, AF.Exp, [("tmpA", 0)], [("tmpA", 0)], scale=-1.0)
            tt("dve", OT[:, h, :], psF[po][0:64, :], rl, ALU.mult, [("psF", po), ("tmpA", 0)], [("OT", h)])
            psf_free(po); psf_free(pl)

    def outproj(t, l):
        for half in range(2):
            s1 = ring_load([(0, [64, 8, 512], w_out[0, 0:512, half * 512:(half + 1) * 512].rearrange("(h p) n -> p h n", p=64))])
            s2 = ring_load([(0, [128, 4, 512], w_out[0, 512:1024, half * 512:(half + 1) * 512].rearrange("(k p) n -> p k n", p=128))])
            wa = rview(s1, 0, [64, 8, 512]); ws_ = rview(s2, 0, [128, 4, 512])
            for d4 in range(4):
                dc = half * 4 + d4
                pz = psf()
                pairs = [(wa[:, h, d4 * 128:(d4 + 1) * 128], OT[:, h, :]) for h in range(8)]
                pairs += [(ws_[:, k, d4 * 128:(d4 + 1) * 128], ssdT[:, k, :]) for k in range(4)]
                mm_group(psF[pz][:, :], pairs, [("ring", s1), ("ring", s2), "ssdT"] + [("OT", h) for h in range(8)], [("psF", pz)])
                evac_mix(pz, dc)
                psf_free(pz)
        post_norm_residual(t, l, 2)

    def rest(t, l):
        conv(t)
        ssd_small(t)
        if t == "P":
            prepass(t, False, final_P)
            attn_V(t)
            for h in range(8):
                attn_head(t, h)
            for tb in range(4):
                ssd_main(t, tb)
        else:
            prepass(t, False, final_S1)
            x2_exchange()
            P.barrier()
            nop_ = lambda d, seg: None
            sched = {1: lambda: x2_receive(),
                     2: lambda: prepass(t, True, nop_, dirs=(0,)),
                     3: lambda: prepass(t, True, nop_, dirs=(1,)),
                     4: lambda: ssd_main(t, 0), 5: lambda: ssd_main(t, 1),
                     6: lambda: ssd_main(t, 2), 7: lambda: ssd_main(t, 3)}
            attn_V(t)
            for h in range(8):
                attn_head(t, h)
                if h in sched:
                    sched[h]()
        P.barrier()
        outproj(t, l)
        P.barrier()

    def layer0_mixer(l):
        P.op("dve", lambda e: e.memset(xbcpad["P"][:], 0.0), (), [("xbcpad", "P")])
        P.op("dve", lambda e: e.memset(wuk96[:], 0.0), (), ["wuk96"])
        cp("dve", wuk96[:, :, :, 0:64], wukv[:, :, :, 0:64], ["wukv", "wuk96"], ["wuk96"])
        for i in range(40):
            ts("dve", diagW[:, i, :], ident_f, cols[:, B_CONVW + i:B_CONVW + i + 1], None, ALU.mult, None,
               ["consts", "cols"], [("diagW", 0)])
        if RUN_S:
            P.op("dve", lambda e: e.memset(x1buf[32:64, 1024:1536], 0.0), (), ["x1buf"])
            P.op("dve", lambda e: e.memset(x1buf[64:128, 1024:1536], 0.0), (), ["x1buf"])
            inproj("S", l)
            x1_exchange()
            P.barrier()
        inproj("P", l)
        P.barrier()
        rest("P", l)
        if RUN_S:
            x1_receive()
            P.barrier()
            rest("S", l)

    RUN_S = not os.environ.get("KNO_S")

    def layer1_mixer(l):
        WP = {"P": 544, "S": 528}
        hpad = {"P": av(0, [128, 8, 544]), "S": av(8704, [128, 8, 528])}
        invc = av(21504, [128, 4, 512], F32)
        pooledT = av(25600, [128, 8, 512])
        x3buf = av(29696, [128, 8, 16], F32)
        gE = av(29952, [128, 4, 128], F32)

        def data(ap3, t):
            if t == "S":
                return ap3[:, :, 8:520]
            return ap3.rearrange("p k (s c) -> p k s c", s=2)[:, :, :, 8:264]

        def data2(ap2, t):
            if t == "S":
                return ap2[:, 8:520]
            return ap2.rearrange("p (s c) -> p s c", s=2)[:, :, 8:264]

        def seg2(ap2, t):
            if t == "S":
                return ap2
            return ap2.rearrange("p (s c) -> p s c", s=2)

        def fill(t):
            if t == "P":
                hp4 = hpad[t][:].rearrange("p k (s c) -> p k s c", s=2)
                P.op("dve", lambda e: e.memset(hp4[:, :, :, 0:8], 0.0), (), [("hpad", t)])
                P.op("dve", lambda e: e.memset(hp4[:, :, :, 264:272], 0.0), (), [("hpad", t)])
            modulate(t, l, 0, 1, outf=lambda k, tm, t=t: (data2(hpad[t][:, k, :], t), seg2(tm[:], t), ("hpad", t)))

        def pool_tile(t, ti):
            W = WP[t]
            dma("sp", invc[:], invcnt_d[ti:ti + 1].to_broadcast([128, 4, NT]), (), ["invc"])
            hk = ("hpad", t)
            segs = [(0, 0, 256), (272, 256, 256)] if t == "P" else [(0, 0, 512)]
            for gi, w in enumerate((2, 4, 8, 16)):
                for kk in range(2):
                    kch = 2 * gi + kk
                    pz = psf()
                    for (sb0, oc, n) in segs:
                        mm_group(psF[pz][:, oc:oc + n],
                                 [(ident_b, hpad[t][:, kch, sb0 + 8 + off:sb0 + 8 + off + n]) for off in range(-(w // 2), w // 2)],
                                 ["consts_b", hk], [("psF", pz)])
                    tm = tmpA[kk]; tk = ("tmpA", kk)
                    tt("dve", tm[:], psF[pz][:, :], invc[:, gi, :], ALU.mult, [("psF", pz), "invc"], [tk])
                    psf_free(pz)
                    tt("dve", seg2(pooledT[:, kch, :], t), seg2(tm[:], t), data2(hpad[t][:, kch, :], t), ALU.subtract,
                       [tk, hk], [("pooledT", kch)])
            s = ring_load([(0, [128, 8, 256], pool_w[0].rearrange("g (k p) n -> p (g k) n", p=128))])
            pw = rview(s, 0, [128, 8, 256])
            for dc in range(8):
                gi, co = dc // 2, dc % 2
                pz = psf()
                mm_group(psF[pz][:, :], [(pw[:, gi * 2 + kc, co * 128:(co + 1) * 128], pooledT[:, 2 * gi + kc, :]) for kc in range(2)],
                         [("ring", s), ("pooledT", 2 * gi), ("pooledT", 2 * gi + 1)], [("psF", pz)])
                act(mixT[:, dc, :], psF[pz][:, :], AF.Copy, [("psF", pz), "cols"], ["mixT"],
                    scale=cols[:, B_PSC + dc:B_PSC + dc + 1])
                act(sqT[:, dc, :], mixT[:, dc, :], AF.Square, ["mixT"], [("sqT", dc)])
                psf_free(pz)
            post_norm_residual(t, l, 2)

        if RUN_S:
            fill("S")
            cp("dve", x3buf[:, :, 0:8], hpad["S"][:, :, 8:16], [("hpad", "S")], ["x3buf"])
            cp("dve", x3buf[:, :, 8:16], hpad["S"][:, :, 512:520], [("hpad", "S")], ["x3buf"])
            dma("sp", x3_in[:, :], x3buf[:].rearrange("p k c -> p (k c)"), ["x3buf"], ["x3_in"])
            coll(x3_in, x3_out, ["x3_in"], ["x3_out"])
        fill("P")
        pool_tile("P", 0)
        if RUN_S:
            dma("sp", gE[:], x3_out.rearrange("(r p) c -> p r c", p=128), ["x3_out"], ["gE"])
            for (dst0, c0, sb_) in ((0, 8, 4), (520, 0, 8)):
                dst = hpad["S"][:, :, dst0:dst0 + 8]
                for rp in range(4):
                    src = gE[:, rp, :].rearrange("p (k c) -> p k c", k=8)[:, :, c0:c0 + 8]
                    if rp == 0:
                        ts("dve", dst, src, sel[:, sb_ + rp:sb_ + rp + 1], None, ALU.mult, None, ["gE", "sel"], [("hpad", "S")])
                    else:
                        stt("dve", dst, src, sel[:, sb_ + rp:sb_ + rp + 1], dst, ALU.mult, ALU.add,
                            ["gE", "sel", ("hpad", "S")], [("hpad", "S")])
            pool_tile("S", 1)
    for l in range(n_layers):
        if l % 2 == 0:
            layer0_mixer(l)
        else:
            layer1_mixer(l)
        P.barrier()
        if not os.environ.get('KSKIP_FFN'):
            ffn(l)
        P.barrier()

    P.barrier()
    for t in ("P", "S"):
        for tb in range(4):
            xl = xld[rr["xld"] % 2]; xk = ("xld", rr["xld"] % 2); rr["xld"] += 1
            for half in range(2):
                pz = psf()
                for q in range(4):
                    k = half * 4 + q
                    tr(psF[pz][:, q * 128:(q + 1) * 128], xT[t][:, k, tb * 128:(tb + 1) * 128], ident_f,
                       ["consts", ("xT", t)], [("psF", pz)], signal=(q == 3))
                alt_cp(xl[:, half * 512:(half + 1) * 512], psF[pz][:, :], [("psF", pz)], [xk])
                psf_free(pz)
            dma("sp", yout[t][tb * 128:(tb + 1) * 128, :], xl[:], [xk], [("yout", t)])
    P.final_wait()
    P.emit()
    es.close()
    return nc


def _consts():
    c = np.zeros((128, 512), np.float32)
    c[:, 0:128] = np.eye(128)
    t = np.arange(128)
    c[:, 128:256] = (t[:, None] <= t[None, :])
    c[:, 256:384] = (t[:, None] >= t[None, :])
    c[:, 384:512] = 1.0
    selc = np.zeros((8, 8, 128), np.float32)
    for h in range(8):
        selc[h, h, :] = 1.0
    padI = np.zeros((32, 96), np.float32)
    padI[np.arange(32), 64 + np.arange(32)] = 1.0
    return c, selc.reshape(8, 1024), padI


def _rope(pos):
    half = 16
    inv_freq = np.power(10000.0, -np.arange(0, half, 2, dtype=np.float64) / half)
    row = (pos // 64).astype(np.float64); col = (pos % 64).astype(np.float64)
    ang = np.concatenate([row[:, None] * inv_freq, col[:, None] * inv_freq], axis=-1)
    cos = np.cos(ang).T; sin = np.sin(ang).T
    r = np.zeros((32, 2, len(pos)), np.float32)
    r[0:16, 0] = cos; r[16:32, 0] = cos; r[0:16, 1] = sin; r[16:32, 1] = sin
    return r


def _invcnt(seg_len, nseg, lo_pad, hi_pad):
    out = np.zeros((4, NT), np.float32)
    for gi, w in enumerate((2, 4, 8, 16)):
        for s in range(nseg):
            L = seg_len
            t = np.arange(L)
            lo = t - w // 2; hi = t + w // 2
            if lo_pad: lo = np.clip(lo, 0, None)
            if hi_pad: hi = np.clip(hi, None, L)
            out[gi, s * L:(s + 1) * L] = 1.0 / (hi - lo)
    return out


_NC_CACHE = {}


def kernel(**inp):
    inp = {k: np.ascontiguousarray(np.asarray(v)) for k, v in inp.items()}
    if "nc" not in _NC_CACHE:
        _NC_CACHE["nc"] = build()
    nc = _NC_CACHE["nc"]
    c, selc, padI = _consts()
    shared = {k: inp[k] for k in (
          "w_in_ab",
         "w_uq",  "w_ukv",
         "w_out_ab", "pool_w",
        "ffn_w_gate", "ffn_w_up", "ffn_w_down")}
    shared.update(consts=c, selc=selc, padI=padI)
    bc_all = np.concatenate([inp["kv_norm"].reshape(-1), inp["ssd_norm"].reshape(-1), inp["ssd_dt_bias_fwd"].reshape(-1),
                             inp["ssd_dt_bias_bwd"].reshape(-1), inp["ssd_a_log_fwd"].reshape(-1),
                             inp["ssd_a_log_bwd"].reshape(-1), inp["ssd_d"].reshape(-1)]).reshape(1, 808)
    shared["bc_all"] = bc_all
    stg_common = [inp["norm_pre_mix"].reshape(16, 128), inp["norm_post_mix"].reshape(16, 128),
                  inp["norm_pre_ffn"].reshape(16, 128), inp["norm_post_ffn"].reshape(16, 128),
                  inp["b_mod"].reshape(96, 128)]
    stg_tail = [inp["q_norm"].reshape(2, 128), inp["kv_norm"].reshape(2, 128), inp["ssd_conv_b"].reshape(8, 128),
                inp["ssd_conv_w"][0].reshape(40, 128), inp["ssd_norm"].reshape(4, 128), inp["pool_scale"].reshape(8, 128),
                np.zeros((16, 128), np.float32)]
    in_maps = []
    for core in range(8):
        b, r = core // 4, core % 4
        m = dict(shared)
        m["xp"] = inp["x_prompt"][2 * core:2 * core + 2].reshape(NT, D)
        m["xs"] = inp["x_sample"][b, r * NT:(r + 1) * NT]
        m["stg_all"] = np.concatenate(stg_common + [np.stack([inp["c_ctx"], inp["c"][b]]).reshape(16, 128)] + stg_tail, axis=0)
        m["w_mod_sl"] = inp["w_mod"][:, :, r * 1536:(r + 1) * 1536]
        m["cache_ckv"] = inp["cache_mla_ckv"][b, 0]
        m["cache_kpe"] = inp["cache_mla_krope"][b, 0]
        m["h0f"] = inp["state_ssd_fwd"][b, 0].reshape(512, 128)
        m["h0b"] = inp["state_ssd_bwd"][b, 0].reshape(512, 128)
        m["rope"] = _rope(np.arange(r * NT, (r + 1) * NT))
        sel = np.zeros((128, 16), np.float32)
        sel[:, r] = 1.0
        if r > 0: sel[:, 4 + r - 1] = 1.0
        if r < 3: sel[:, 8 + r + 1] = 1.0
        m["sel"] = sel
        ic = np.zeros((2, 4, NT), np.float32)
        ic[0] = _invcnt(256, 2, True, True)
        ic[1] = _invcnt(NT, 1, r == 0, r == 3)
        m["invcnt"] = ic
        in_maps.append({k: np.ascontiguousarray(v, dtype=np.float32) for k, v in m.items()})
    ncores = int(os.environ.get("KCORES", "8"))
    res = run_bass_kernel_spmd(nc, in_maps[:ncores], core_ids=list(range(ncores)))
    R = list(res.results)
    while len(R) < 8:
        R.append(R[0])
    yp = np.concatenate([R[c_]["yp"].reshape(2, 256, D) for c_ in range(8)], axis=0)
    ys = np.stack([np.concatenate([R[b * 4 + r]["ys"] for r in range(4)], axis=0) for b in range(2)])
    ockv = np.concatenate([R[c_]["ockv"].reshape(2, 1, 256, 256) for c_ in range(8)], axis=0)
    okpe = np.concatenate([R[c_]["okpe"].reshape(2, 1, 256, 32) for c_ in range(8)], axis=0)
    ohf = np.concatenate([R[c_]["ohf"].reshape(2, 1, 8, 64, 128) for c_ in range(8)], axis=0)
    ohb = np.concatenate([R[c_]["ohb"].reshape(2, 1, 8, 64, 128) for c_ in range(8)], axis=0)
    return (yp.astype(np.float32), ys.astype(np.float32), ockv.astype(np.float32), okpe.astype(np.float32),
            ohf.astype(np.float32), ohb.astype(np.float32))
```

```python
import numpy as np
import concourse.bass as bass
import concourse.mybir as mybir

F32 = mybir.dt.float32
BF16 = mybir.dt.bfloat16
AF = mybir.ActivationFunctionType
ALU = mybir.AluOpType
AX = mybir.AxisListType

ENGS = ("pe", "act", "dve", "pool", "sp")


class Prog:
    def __init__(self, nc, n_dma_sems=24):
        self.nc = nc
        self.items = {e: [] for e in ENGS}
        self.cnt = {e: 0 for e in ENGS}
        self.waited = {e: {} for e in ENGS}
        self.lastw = {}
        self.readers = {}
        self.n_dma_sems = n_dma_sems
        self.dma_cnt = [0] * (n_dma_sems + 4)
        self.dma_i = 0
        self.dma_q = 0
        self.dma_c = 0
        self.nops = {e: 0 for e in ENGS}

    def _deps(self, eng, reads, writes):
        deps = []
        for r in reads:
            s = self.lastw.get(r)
            if s is not None:
                deps.append((s, "raw"))
            if isinstance(r, tuple) and r[0] in ("psF", "psB"):
                for s in self.readers.get(r, ()):
                    if s[2] != eng:
                        deps.append((s, "rar"))
        for w in writes:
            s = self.lastw.get(w)
            if s is not None:
                deps.append((s, "waw"))
            for s in self.readers.get(w, ()):
                deps.append((s, "war"))
        out = {}
        for (sem, val, peng), kind in deps:
            if peng == eng:
                if eng == "pe":
                    continue
            if out.get(sem, -1) < val:
                out[sem] = val
        return out

    def _emit_waits(self, eng, deps):
        for sem, val in deps.items():
            if self.waited[eng].get(sem, -1) >= val:
                continue
            self.waited[eng][sem] = val
            self.items[eng].append(("wait", sem, val))

    def _record(self, sig, reads, writes):
        for r in reads:
            self.readers.setdefault(r, []).append(sig)
        for w in writes:
            self.lastw[w] = sig
            self.readers[w] = []

    def op(self, eng, fn, reads=(), writes=(), signal=True):
        deps = self._deps(eng, reads, writes)
        self._emit_waits(eng, deps)
        self.nops[eng] += 1
        if signal:
            self.cnt[eng] += 1
            sig = ("E_" + eng, self.cnt[eng], eng)
            self.items[eng].append(("op", fn, True))
            self._record(sig, reads, writes)
        else:
            self.items[eng].append(("op", fn, False))
            sig = ("E_" + eng, self.cnt[eng] + 1, eng)
            self._record(sig, reads, writes)
        return sig

    def dma(self, eng, fn, reads=(), writes=(), inc=16):
        half = self.n_dma_sems // 2
        if inc == 1:
            i = self.n_dma_sems + (self.dma_c % 4); self.dma_c += 1
        elif eng == "pool":
            i = half + (self.dma_q % half); self.dma_q += 1
        else:
            i = self.dma_i % half; self.dma_i += 1
        sem = "D_%d" % i
        deps = self._deps(eng, reads, writes)
        if self.dma_cnt[i] > 0:
            if deps.get(sem, -1) < self.dma_cnt[i]:
                deps[sem] = self.dma_cnt[i]
        self._emit_waits(eng, deps)
        self.dma_cnt[i] += inc
        sig = (sem, self.dma_cnt[i], None)
        self.items[eng].append(("dma", fn, sem, inc))
        self.nops[eng] += 1
        self._record(sig, reads, writes)
        return sig

    def barrier(self):
        allsig = {}
        for e in ENGS:
            if self.cnt[e] > 0:
                allsig["E_" + e] = self.cnt[e]
        for i in range(self.n_dma_sems + 4):
            if self.dma_cnt[i] > 0:
                allsig["D_%d" % i] = self.dma_cnt[i]
        for e in ENGS:
            d = dict(allsig)
            self._emit_waits(e, d)

    def final_wait(self, eng="sp"):
        self.barrier()

    def emit(self, extra_ctx=()):
        nc = self.nc
        import contextlib
        with contextlib.ExitStack() as st:
            sems = {}
            for e in ENGS:
                sems["E_" + e] = st.enter_context(nc.semaphore("E_" + e))
            for i in range(self.n_dma_sems + 4):
                sems["D_%d" % i] = st.enter_context(nc.semaphore("D_%d" % i))
            block = st.enter_context(nc.Block())
            items = self.items

            def run(engh, ename):
                for it in items[ename]:
                    if it[0] == "wait":
                        engh.wait_ge(sems[it[1]], it[2])
                    elif it[0] == "op":
                        ins = it[1](engh)
                        if it[2]:
                            ins.then_inc(sems["E_" + ename], 1)
                    else:
                        ins = it[1](engh)
                        if it[3] == 1:
                            ins.then_inc(sems[it[2]])
                        else:
                            ins.then_inc(sems[it[2]], it[3])

            @block.sync
            def _(e):
                run(e, "sp")

            @block.scalar
            def _(e):
                run(e, "act")

            @block.vector
            def _(e):
                run(e, "dve")

            @block.gpsimd
            def _(e):
                run(e, "pool")

            @block.tensor
            def _(e):
                run(e, "pe")

from contextlib import ExitStack
import os
from concourse.bass_utils import run_bass_kernel_spmd
import ml_dtypes

D = 1024
NT = 512
EPS = 1e-6
IN_AB = 2096
D_FF = 2816
SCALE = 96 ** -0.5
RING_SLOTS = 3
RING_ELEMS = 4096

B_NPM, B_NPO, B_NFR, B_NFO = 0, 16, 32, 48
B_BMOD = 64
B_CVEC = 160
B_QN, B_KVN = 176, 178
B_CONVB = 180
B_CONVW = 188
B_SSDN = 228
B_PSC = 232
N_ROWS = 240


def build(n_layers=2, dbg=False):
    nc = bass.Bass("TRN2", target_bir_lowering=False)
    P = Prog(nc)
    es = ExitStack()

    def din(name, shape, dt=F32):
        return nc.dram_tensor(name, list(shape), dt, kind="ExternalInput").ap()

    def dout(name, shape, dt=F32):
        return nc.dram_tensor(name, list(shape), dt, kind="ExternalOutput").ap()

    def sb(name, shape, dt=F32):
        return es.enter_context(nc.sbuf_tensor("sb_" + name, list(shape), dt))

    xin = {"P": din("xp", [NT, D]), "S": din("xs", [NT, D])}
    cache_ckv = din("cache_ckv", [256, 256])
    cache_kpe = din("cache_kpe", [256, 32])
    h0_d = [din("h0f", [512, 128]), din("h0b", [512, 128])]
    w_mod = din("w_mod_sl", [2, D, 1536])
    w_in = din("w_in_ab", [1, D, IN_AB])
    w_uq = din("w_uq", [1, 256, 768]); w_ukv = din("w_ukv", [1, 256, 1024])
    w_out = din("w_out_ab", [1, D, D])
    pool_w = din("pool_w", [1, 4, 256, 256])
    w_gate = din("ffn_w_gate", [2, D, D_FF]); w_up = din("ffn_w_up", [2, D, D_FF])
    w_down = din("ffn_w_down", [2, D_FF, D])
    consts_d = din("consts", [128, 512])
    selc_d = din("selc", [8, 1024])
    padI_d = din("padI", [32, 96])
    rope_d = din("rope", [32, 2, NT])
    sel_d = din("sel", [128, 16])
    invcnt_d = din("invcnt", [2, 4, NT])

    yout = {"P": dout("yp", [NT, D]), "S": dout("ys", [NT, D])}
    ockv = dout("ockv", [NT, 256]); okpe = dout("okpe", [NT, 32])
    ohs = [dout("ohf", [2, 512, 128]), dout("ohb", [2, 512, 128])]

    NX1 = 1024 + 512 + 32
    x1_in = nc.dram_tensor("x1_in", [128, NX1], BF16, kind="Internal").ap()
    x1_out = nc.dram_tensor("x1_out", [4 * 128, NX1], BF16, kind="Internal").ap()
    x2_in = nc.dram_tensor("x2_in", [128, 1040], F32, kind="Internal").ap()
    x2_out = nc.dram_tensor("x2_out", [4 * 128, 1040], F32, kind="Internal").ap()
    x3_in = nc.dram_tensor("x3_in", [128, 128], F32, kind="Internal").ap()
    x3_out = nc.dram_tensor("x3_out", [4 * 128, 128], F32, kind="Internal").ap()
    GROUPS = [[0, 1, 2, 3], [4, 5, 6, 7]]

    consts = sb("consts", [128, 512]); consts_b = sb("consts_b", [128, 512], BF16)
    ident_f = consts[:, 0:128]; U_f = consts[:, 128:256]; L_f = consts[:, 256:384]; ones_f = consts[:, 384:512]
    ident_b = consts_b[:, 0:128]; ones_b = consts_b[:, 384:512]
    selc = sb("selc", [8, 1024])
    padI_f = sb("padI_f", [32, 96]); padI = sb("padI", [32, 96], BF16)
    rope_lo = sb("rope_lo", [32, 2, NT], BF16); rope_hi = sb("rope_hi", [96, 2, NT], BF16)
    sel = sb("sel", [128, 16])
    stg = sb("stg", [128, 2, 128]); cols = sb("cols", [128, 256])
    bcp = sb("bcp", [128, 808])
    kvn_bc = bcp[:, 0:256]; ssdn_bc = bcp[:, 256:768]; sm_bc = bcp[:, 768:808]
    A_bc = sb("A_bc", [128, 16]); dsk_bc = sb("dsk_bc", [128, 512], BF16)
    modT = sb("modT", [128, 2, 48, 2])
    csil = sb("csil", [128, 8, 2], BF16)
    mcol = sb("mcol", [128, 2, 6, 8, 2])
    qn32 = sb("qn32", [128, 2]); kvn32 = sb("kvn32", [128, 2])
    xT = {"P": sb("xT_P", [128, 8, NT]), "S": sb("xT_S", [128, 8, NT])}
    hT2 = sb("hT2", [128, 2, 8, NT], BF16)
    hT = {"P": hT2[:, 0], "S": hT2[:, 1]}
    sqT = sb("sqT", [128, 8, NT], BF16)
    rstd = sb("rstd", [128, NT]); tmpA = [sb("tmpA0", [128, NT]), sb("tmpA1", [128, NT])]
    mixT = sb("mixT", [128, 8, NT])
    xld = [mixT[:, 0:2, :].rearrange("p a b -> p (a b)"), mixT[:, 2:4, :].rearrange("p a b -> p (a b)")]
    ring = [sb("ring%d" % i, [128, RING_ELEMS], BF16) for i in range(RING_SLOTS)]
    wsm = sb("wsm", [128, 8, 48], BF16)
    wuq = sb("wuq", [128, 2, 8, 96], BF16); wuq_sw = sb("wuq_sw", [128, 2, 8, 96], BF16)
    wukv = sb("wukv", [128, 2, 8, 128], BF16)
    ARENA = 75 * 1024 // 2
    arena = sb("arena", [128, ARENA], BF16)
    mrow = arena[0:2, 0:6144].bitcast(F32).rearrange("p (l n) -> p l n", l=2)
    Gm = arena[0:16, 12288:12288 + 3072].bitcast(F32)

    rr = {"ring": 0, "tmp": 0, "xld": 0, "alt": 0}

    psF = [es.enter_context(nc.psum_tensor("psF%d" % i, [128, 512], F32)) for i in range(6)]
    psB = [es.enter_context(nc.psum_tensor("psB%d" % i, [128, 1024], BF16)) for i in range(2)]
    freeF = list(range(6)); freeB = [0, 1]

    def psf():
        i = freeF.pop(0); return i

    def psf_free(i):
        freeF.append(i)

    def psb():
        i = freeB.pop(0); return i

    def psb_free(i):
        freeB.append(i)

    def act(out, in_, func, reads, writes, bias=None, scale=None, accum=None):
        kw = {}
        if bias is not None: kw["bias"] = bias
        if scale is not None: kw["scale"] = scale
        if accum is not None: kw["accum_out"] = accum
        return P.op("act", lambda e: e.activation(out=out, in_=in_, func=func, **kw), reads, writes)

    def tt(eng, out, in0, in1, op, reads, writes):
        return P.op(eng, lambda e: e.tensor_tensor(out=out, in0=in0, in1=in1, op=op), reads, writes)

    def ts(eng, out, in0, s1, s2, op0, op1, reads, writes):
        if s2 is None:
            return P.op(eng, lambda e: e.tensor_scalar(out=out, in0=in0, scalar1=s1, scalar2=None, op0=op0), reads, writes)
        return P.op(eng, lambda e: e.tensor_scalar(out=out, in0=in0, scalar1=s1, scalar2=s2, op0=op0, op1=op1), reads, writes)

    def stt(eng, out, in0, scalar, in1, op0, op1, reads, writes):
        return P.op(eng, lambda e: e.scalar_tensor_tensor(out=out, in0=in0, scalar=scalar, in1=in1, op0=op0, op1=op1), reads, writes)

    def cp(eng, out, in_, reads, writes):
        if eng == "act":
            return P.op("act", lambda e: e.copy(out=out, in_=in_), reads, writes)
        return P.op(eng, lambda e: e.tensor_copy(out=out, in_=in_), reads, writes)

    def rsqrt_to(out, in_, c, reads, writes, scale=1.0):
        act(out, in_, AF.Ln, reads, writes, bias=float(c), scale=float(scale))
        act(out, out, AF.Exp, list(writes), list(writes), scale=-0.5)

    def alt_cp(out, in_, reads, writes):
        rr["alt"] ^= 1
        return cp("act" if rr["alt"] else "dve", out, in_, reads, writes)

    def mm(out, lhsT, rhs, start, stop, reads, writes, signal):
        return P.op("pe", lambda e: e.matmul(out, lhsT=lhsT, rhs=rhs, start=start, stop=stop), reads, writes, signal=signal)

    def mm_group(out, pairs, reads, writes):
        n = len(pairs)
        for i, (l, r) in enumerate(pairs):
            mm(out, l, r, i == 0, i == n - 1, reads if i == 0 else (), writes, i == n - 1)

    def tr(out, in_, ident, reads, writes, signal=True):
        return P.op("pe", lambda e: e.transpose(out=out, in_=in_, identity=ident), reads, writes, signal=signal)

    def dma(eng, out, in_, reads, writes):
        return P.dma(eng, lambda e: e.dma_start(out=out, in_=in_), reads, writes)

    def ring_load(parts, eng="pool"):
        s = rr["ring"] % RING_SLOTS
        rr["ring"] += 1
        for (off, shp, src) in parts:
            n = 1
            for v in shp[1:]:
                n *= v
            dst = ring[s][0:shp[0], off:off + n]
            if len(shp) == 3:
                dst = dst.rearrange("p (a b) -> p a b", a=shp[1])
            dma(eng, dst, src, (), [("ring", s)])
        return s

    def rview(s, off, shp):
        n = 1
        for v in shp[1:]:
            n *= v
        v = ring[s][0:shp[0], off:off + n]
        if len(shp) == 3:
            v = v.rearrange("p (a b) -> p a b", a=shp[1])
        return v

    dma("sp", consts[:], consts_d[:, :], (), ["consts"])
    dma("sp", selc[:], selc_d[:, :], (), ["selc"])
    dma("sp", padI_f[:], padI_d[:, :], (), ["padI_f"])
    dma("pool", rope_lo[:], rope_d[:, :, :], (), ["rope_lo"])
    dma("pool", rope_hi[64:96], rope_d[:, :, :], (), ["rope_hi"])
    dma("sp", sel[:], sel_d[:, :], (), ["sel"])
    cp("dve", consts_b[:], consts[:], ["consts"], ["consts_b"])
    cp("dve", padI[:], padI_f[:], ["padI_f"], ["padI"])

    def stage_rows(base, src2d, nrows):
        r = 0
        while r < nrows:
            row = base + r
            t, rin = row // 128, row % 128
            n = min(nrows - r, 128 - rin)
            dma("sp", stg[rin:rin + n, t, :], src2d[r:r + n, :], (), [("stg", t)])
            r += n

    stg_d = din("stg_all", [256, 128])
    bc_d = din("bc_all", [1, 808])
    dma("sp", stg[:, 0, :], stg_d[0:128, :], (), [("stg", 0)])
    dma("sp", stg[0:112, 1, :], stg_d[128:240, :], (), [("stg", 1)])
    pz = psf()
    tr(psF[pz][:, 0:128], stg[:, 0, :], ident_f, ["consts", ("stg", 0)], [("psF", pz)])
    tr(psF[pz][:, 128:128 + 112], stg[0:112, 1, :], ident_f[0:112, 0:112], ["consts", ("stg", 1)], [("psF", pz)])
    cp("dve", cols[:, 0:240], psF[pz][:, 0:240], [("psF", pz)], ["cols"])
    psf_free(pz)

    dma("sp", bcp[:], bc_d[0:1, :].to_broadcast([128, 808]), (), ["kvn_bc", "ssdn_bc", "sm_bc"])
    act(A_bc[:], sm_bc[:, 16:32], AF.Exp, ["sm_bc"], ["A_bc"])
    ts("dve", A_bc[:], A_bc[:], -1.0, None, ALU.mult, None, ["A_bc"], ["A_bc"])
    cp("dve", dsk_bc[:].rearrange("p (h c) -> p h c", h=8),
       sm_bc[:, 32:40].unsqueeze(2).to_broadcast([128, 8, 64]), ["sm_bc"], ["dsk_bc"])
    ts("dve", qn32[:], cols[:, B_QN:B_QN + 2], 16.0, None, ALU.mult, None, ["cols"], ["qn32"])
    ts("dve", kvn32[:], cols[:, B_KVN:B_KVN + 2], 16.0, None, ALU.mult, None, ["cols"], ["kvn32"])

    dma("pool", wsm[:, :, 0:32], w_in[0, :, 512:544].rearrange("(k p) n -> p k n", p=128), (), ["wsm"])
    dma("pool", wsm[:, :, 32:48], w_in[0, :, 2080:2096].rearrange("(k p) n -> p k n", p=128), (), ["wsm"])
    dma("pool", wuq[:].rearrange("p k h c -> p k (h c)"), w_uq[0].rearrange("(k p) n -> p k n", p=128), (), ["wuq"])
    dma("pool", wukv[:].rearrange("p k h c -> p k (h c)"), w_ukv[0].rearrange("(k p) n -> p k n", p=128), (), ["wukv"])
    P.op("dve", lambda e: e.memset(wuq_sw[:], 0.0), (), ["wuq_sw"])
    ts("dve", wuq_sw[:, :, :, 64:80], wuq[:, :, :, 80:96], -1.0, None, ALU.mult, None, ["wuq"], ["wuq_sw"])
    cp("dve", wuq_sw[:, :, :, 80:96], wuq[:, :, :, 64:80], ["wuq"], ["wuq_sw"])
    wsm_sw = sb("wsm_sw", [128, 8, 32], BF16)
    ts("dve", wsm_sw[:, :, 0:16], wsm[:, :, 16:32], -1.0, None, ALU.mult, None, ["wsm"], ["wsm_sw"])
    cp("dve", wsm_sw[:, :, 16:32], wsm[:, :, 0:16], ["wsm"], ["wsm_sw"])

    xm_in = nc.dram_tensor("xm_in", [4, 1536], F32, kind="Internal").ap()
    xm_out = nc.dram_tensor("xm_out", [16, 1536], F32, kind="Internal").ap()
    act(csil[:].rearrange("p k v -> p v k"),
        cols[:, B_CVEC:B_CVEC + 16].rearrange("p (v k) -> p v k", v=2), AF.Silu, ["cols"], ["csil"])
    for l in range(2):
        for cb in range(4):
            s = ring_load([(0, [128, 8, 384], w_mod[l, :, cb * 384:(cb + 1) * 384].rearrange("(k p) n -> p k n", p=128))])
            wv = rview(s, 0, [128, 8, 384])
            pz = psf()
            mm_group(psF[pz][0:2, 0:384], [(csil[:, k, :], wv[:, k, :]) for k in range(8)],
                     ["csil", ("ring", s)], [("psF", pz)])
            alt_cp(mrow[:, l, cb * 384:(cb + 1) * 384], psF[pz][0:2, 0:384], [("psF", pz)], ["mrow"])
            psf_free(pz)
    dma("sp", xm_in.rearrange("(v l) c -> v (l c)", v=2), mrow[:].rearrange("p l n -> p (l n)"), ["mrow"], ["xm_in"])
    P.dma("pool", lambda e: e.collective_compute("AllGather", ALU.bypass, replica_groups=GROUPS,
                                                 ins=[xm_in.opt()], outs=[xm_out.opt()]), ["xm_in"], ["xm_out"], inc=1)

    for t in ("P", "S"):
        for tb in range(4):
            xl = xld[rr["xld"] % 2]; xk = ("xld", rr["xld"] % 2); rr["xld"] += 1
            dma("sp", xl[:], xin[t][tb * 128:(tb + 1) * 128, :], (), [xk])
            for half in range(2):
                pz = psf()
                for q in range(4):
                    k = half * 4 + q
                    tr(psF[pz][:, q * 128:(q + 1) * 128], xl[:, k * 128:(k + 1) * 128], ident_f,
                       ["consts", xk], [("psF", pz)], signal=(q == 3))
                alt_cp(xT[t][:, half * 4:half * 4 + 4, tb * 128:(tb + 1) * 128],
                       psF[pz][:].rearrange("p (q c) -> p q c", q=4), [("psF", pz)], [("xT", t)])
                psf_free(pz)


    dma("sp", Gm[:, :], xm_out[:, :], ["xm_out"], ["Gm"])
    pz = psf()
    for cb in range(12):
        tr(psF[pz][:, cb * 16:(cb + 1) * 16], Gm[:, cb * 128:(cb + 1) * 128], ident_f[0:16, 0:16],
           ["consts", "Gm"], [("psF", pz)], signal=(cb == 11))
    pv = psF[pz][:, 0:192].rearrange("p (cb r v l) -> p r cb v l", cb=12, r=4, v=2, l=2)
    for l in range(2):
        for v2 in range(2):
            tt("dve", modT[:, l, :, v2].rearrange("p (r cb) -> p r cb", r=4), pv[:, :, :, v2, l],
               cols[:, B_BMOD + l * 48:B_BMOD + (l + 1) * 48].rearrange("p (r cb) -> p r cb", r=4), ALU.add,
               [("psF", pz), "cols"], [("modT", l)])
    psf_free(pz)
    for l in range(2):
        def ncol(base):
            return cols[:, base + l * 8:base + (l + 1) * 8].unsqueeze(2).to_broadcast([128, 8, 2])
        for (kind, jscale, nbase) in ((0, 1, B_NPM), (3, 4, B_NFR)):
            ts("dve", mcol[:, l, kind], modT[:, l, jscale * 8:(jscale + 1) * 8, :], 1.0, 32.0, ALU.add, ALU.mult,
               [("modT", l)], [("mcol", l)])
            tt("dve", mcol[:, l, kind], mcol[:, l, kind], ncol(nbase), ALU.mult, [("mcol", l), "cols"], [("mcol", l)])
        for (kind, jsh) in ((1, 0), (4, 3)):
            cp("dve", mcol[:, l, kind], modT[:, l, jsh * 8:(jsh + 1) * 8, :], [("modT", l)], [("mcol", l)])
        for (kind, jg, nbase) in ((2, 2, B_NPO), (5, 5, B_NFO)):
            stt("dve", mcol[:, l, kind], modT[:, l, jg * 8:(jg + 1) * 8, :], 32.0, ncol(nbase), ALU.mult, ALU.mult,
                [("modT", l), "cols"], [("mcol", l)])


    P.barrier()
    VI = {"P": 0, "S": 1}

    def rstd_from_sq(nchunks, scale_const, reads):
        pz = psf()
        mm_group(psF[pz][:, :], [(ones_b, sqT[:, k, :]) for k in range(nchunks)], ["consts_b"] + [("sqT", k_) for k_ in range(nchunks)] + list(reads), [("psF", pz)])
        rsqrt_to(rstd[:], psF[pz][:, :], scale_const * EPS, [("psF", pz)], ["rstd"])
        psf_free(pz)

    def modulate(t, l, kind_gs, kind_sh, outf=None):
        v = VI[t]
        act(sqT[:], xT[t][:], AF.Square, [("xT", t)], [("sqT", k_) for k_ in range(8)])
        rstd_from_sq(8, 1024.0, [])
        for k in range(8):
            tm = tmpA[rr["tmp"] % 2]; tk = ("tmpA", rr["tmp"] % 2); rr["tmp"] += 1
            stt("dve", tm[:], xT[t][:, k, :], mcol[:, l, kind_gs, k, v:v + 1], rstd[:], ALU.mult, ALU.mult,
                [("xT", t), ("mcol", l), "rstd"], [tk])
            if outf is None:
                act(hT[t][:, k, :], tm[:], AF.Identity, [tk, ("mcol", l)], [("hT", t)], bias=mcol[:, l, kind_sh, k, v:v + 1])
            else:
                o_ap, i_ap, o_key = outf(k, tm)
                act(o_ap, i_ap, AF.Identity, [tk, ("mcol", l)], [o_key], bias=mcol[:, l, kind_sh, k, v:v + 1])

    def post_norm_residual(t, l, kind_g):
        v = VI[t]
        rstd_from_sq(8, 1024.0, [])
        for k in range(8):
            tm = tmpA[rr["tmp"] % 2]; tk = ("tmpA", rr["tmp"] % 2); rr["tmp"] += 1
            stt("dve", tm[:], mixT[:, k, :], mcol[:, l, kind_g, k, v:v + 1], rstd[:], ALU.mult, ALU.mult,
                ["mixT", ("mcol", l), "rstd"], [tk])
            tt("dve", xT[t][:, k, :], xT[t][:, k, :], tm[:], ALU.add, [("xT", t), tk], [("xT", t)])

    def evac_mix(pz, k):
        cp("dve", mixT[:, k, :], psF[pz][:, :], [("psF", pz)], ["mixT"])
        act(sqT[:, k, :], mixT[:, k, :], AF.Square, ["mixT"], [("sqT", k)])

    def ffn(l):
        actT = arena[:, 0:22 * 2 * NT].rearrange("p (f n) -> p f n", f=22)
        for t in ("P", "S"):
            modulate(t, l, 3, 4)
        STAGE = int(os.environ.get('KFFN_STAGE', '3'))
        if STAGE < 2:
            return
        for nb in range(11):
            s = ring_load([(0, [128, 8, 256], w_gate[l, :, nb * 256:(nb + 1) * 256].rearrange("(k p) n -> p k n", p=128)),
                           (2048, [128, 8, 256], w_up[l, :, nb * 256:(nb + 1) * 256].rearrange("(k p) n -> p k n", p=128))])
            wg = rview(s, 0, [128, 8, 256]); wu = rview(s, 2048, [128, 8, 256])
            for ti, t in enumerate(("P", "S")):
                for c in range(2):
                    f = nb * 2 + c
                    pg = psf(); pu = psf()
                    mm_group(psF[pg][:, :], [(wg[:, k, c * 128:(c + 1) * 128], hT[t][:, k, :]) for k in range(8)],
                             [("ring", s), ("hT", t)], [("psF", pg)])
                    mm_group(psF[pu][:, :], [(wu[:, k, c * 128:(c + 1) * 128], hT[t][:, k, :]) for k in range(8)],
                             [("ring", s), ("hT", t)], [("psF", pu)])
                    tm = tmpA[rr["tmp"] % 2]; tk = ("tmpA", rr["tmp"] % 2); rr["tmp"] += 1
                    KGU = int(os.environ.get("KGU", "0"))
                    if KGU in (0, 1):
                        act(tm[:], psF[pg][:, :], AF.Silu, [("psF", pg)], [tk])
                    if KGU in (0, 2):
                        tt("dve", actT[:, f, ti * NT:(ti + 1) * NT], psF[pu][:, :], tm[:], ALU.mult,
                           [tk, ("psF", pu)], [("actT", f)])
                    psf_free(pg); psf_free(pu)
        for ti, t in enumerate(("P", "S")):
            pass
        if STAGE < 3:
            return
        wdb = [arena[:, 26624 + i * 5632:26624 + (i + 1) * 5632].rearrange("p (f n) -> p f n", f=22) for i in range(2)]
        for db in range(4):
            wb = wdb[db % 2]; wk = ("wdblk", db % 2)
            for (f0, f1) in ((0, 11), (11, 22)):
                dma("pool", wb[:, f0:f1, :],
                    w_down[l, f0 * 128:f1 * 128, db * 256:(db + 1) * 256].rearrange("(f p) n -> p f n", p=128),
                    (), [wk])
            for d2 in range(2):
                dc = db * 2 + d2
                for ti, t in enumerate(("P", "S")):
                    pz = psf()
                    mm_group(psF[pz][:, :], [(wb[:, f, d2 * 128:(d2 + 1) * 128], actT[:, f, ti * NT:(ti + 1) * NT]) for f in range(22)],
                             [wk] + [("actT", f) for f in range(22)], [("psF", pz)])
                    mb = mixT2[t]
                    mkeys = [("mix2", t), ("hT", "P"), ("hT", "S")] if t == "S" else [("mix2", t)]
                    cp("dve", mb[:, dc, :], psF[pz][:, :], [("psF", pz)], mkeys)
                    act(sq2[t][:, dc, :], mb[:, dc, :], AF.Square, mkeys, [("sq2", t)])
                    psf_free(pz)
        for t in ("P", "S"):
            v = VI[t]
            pz = psf()
            mm_group(psF[pz][:, :], [(ones_b, sq2[t][:, k, :]) for k in range(8)], ["consts_b", ("sq2", t)], [("psF", pz)])
            rsqrt_to(rstd[:], psF[pz][:, :], 1024.0 * EPS, [("psF", pz)], ["rstd"])
            psf_free(pz)
            for k in range(8):
                tm = tmpA[rr["tmp"] % 2]; tk = ("tmpA", rr["tmp"] % 2); rr["tmp"] += 1
                stt("dve", tm[:], mixT2[t][:, k, :], mcol[:, l, 5, k, v:v + 1], rstd[:], ALU.mult, ALU.mult,
                    [("mix2", t), ("mcol", l), "rstd"], [tk])
                tt("dve", xT[t][:, k, :], xT[t][:, k, :], tm[:], ALU.add, [("xT", t), tk], [("xT", t)])

    mixT2 = {"P": mixT, "S": hT2[:].rearrange("p t k n -> p (t k n)").bitcast(F32).rearrange("p (k n) -> p k n", k=8)}
    sq2 = {"P": sqT, "S": arena[:, 22 * 2 * NT:22 * 2 * NT + 8 * NT].rearrange("p (k n) -> p k n", k=8)}

    def av(off, shape, dt=BF16):
        n = 1
        for v_ in shape[1:]:
            n *= v_
        if dt == F32:
            v = arena[0:shape[0], off:off + 2 * n].bitcast(F32)
        else:
            v = arena[0:shape[0], off:off + n]
        if len(shape) == 3:
            v = v.rearrange("p (a b) -> p a b", a=shape[1])
        elif len(shape) == 4:
            v = v.rearrange("p (a b c) -> p a b c", a=shape[1], b=shape[2])
        return v

    xbcpad = {"S": av(0, [128, 8, 516]), "P": av(19232, [128, 8, 520])}
    diagW = av(4128, [128, 40, 128])
    Vt = {"S": av(0, [128, 18, 512]), "P": av(19232, [128, 4, 512])}
    cqn = {"S": av(9248, [128, 2, 512]), "P": av(34656, [128, 2, 512])}
    Ksrc = av(12320, [128, 2, 2304]); kpeT = av(16928, [32, 2304])
    xconvT = av(23392, [128, 8, 512]); xtok = av(27488, [128, 4, 768]); hprev = av(30560, [128, 2, 4, 512])
    x1buf = av(27488, [128, 1568])
    gX = av(30560, [128, 4, 32])
    wuk96 = av(36864, [128, 2, 8, 96])
    smf = arena[:, 35680:36864].bitcast(F32)
    dtraw = {"P": smf[:, 0:64].rearrange("p (a b) -> p a b", a=4), "S": smf[:, 64:128].rearrange("p (a b) -> p a b", a=4)}
    dtw = smf[:, 128:192].rearrange("p (a b) -> p a b", a=4)
    a_ = smf[:, 192:256].rearrange("p (a b) -> p a b", a=4)
    smx = smf[:, 256:384].rearrange("p (a b) -> p a b", a=4)
    E_ = smf[:, 384:448].rearrange("p (a b) -> p a b", a=4)
    CD_ = smf[:, 448:512].rearrange("p (a b) -> p a b", a=4)
    ac2 = smf[:, 512:576].rearrange("p (a b) -> p a b", a=4)
    hflat = hT2[:].rearrange("p t k n -> p (t k n)")
    OT = hflat[0:64, 0:4096].rearrange("p (h n) -> p h n", h=8)
    ssdT = hflat[:, 4096:6144].rearrange("p (k n) -> p k n", k=4)
    zs = {"S": av(10272, [128, 4, 512]), "P": hflat[:, 6144:8192].rearrange("p (k n) -> p k n", k=4)}
    x2buf = hflat[:, 0:2080].bitcast(F32)
    sflat = sqT[:].rearrange("p k n -> p (k n)")
    Kh = sflat[0:96, 0:2304]; Qh = sflat[0:96, 2304:2816]
    PT = [sflat[:, 2816:3328], sflat[:, 3328:3840]]
    mflat = mixT[:].rearrange("p k n -> p (k n)")
    mbf = mflat.bitcast(BF16)
    MT = mbf[:, 0:2048].rearrange("p (h n) -> p h n", h=16)
    CBm = mflat[:, 1024:1536].rearrange("p (h n) -> p h n", h=4)
    ysb = mflat[:, 1536:2048]; yg = mflat[:, 2048:2560]
    xw = mbf[:, 5120:5632]; xd = mbf[:, 5632:6144]
    hrun = mflat[:, 3072:4096].rearrange("p (d n) -> p d n", d=2)
    hcand = mflat[:, 1536:2560].rearrange("p (d n) -> p d n", d=2)
    hin = mflat[:, 0:1024].rearrange("p (d n) -> p d n", d=2)
    acumT = rstd[0:8, 0:256]; stat2 = sb("stat2", [128, 16])
    gS = av(0, [128, 4, 1040], F32)
    SEGS = {"P": [[0, 1], [2, 3]], "S": [[0, 1, 2, 3]]}

    def h8(ap2d):
        return ap2d.rearrange("p (h c) -> p h c", h=8)

    def bc8(ap_8):
        return ap_8.unsqueeze(2).to_broadcast([128, 8, 64])

    def coll(in_ap, out_ap, rk, wk):
        P.dma("pool", lambda e: e.collective_compute("AllGather", ALU.bypass, replica_groups=GROUPS,
                                                     ins=[in_ap.opt()], outs=[out_ap.opt()]), rk, wk, inc=1)

    def inproj(t, l):
        modulate(t, l, 0, 1)
        hk = ("hT", t)
        sA = ring_load([(0, [128, 8, 512], w_in[0, :, 0:512].rearrange("(k p) n -> p k n", p=128))])
        wA = rview(sA, 0, [128, 8, 512])
        for j in range(4):
            pz = psf()
            mm_group(psF[pz][:, :], [(wA[:, k, j * 128:(j + 1) * 128], hT[t][:, k, :]) for k in range(8)],
                     [("ring", sA), hk], [("psF", pz)])
            cp("dve", mixT[:, j, :], psF[pz][:, :], [("psF", pz)], [("cqf", j)])
            act(sqT[:, j, :], mixT[:, j, :], AF.Square, [("cqf", j)], [("sqT", j)])
            psf_free(pz)
        for (j0, scl, dst) in ((0, qn32, None), (2, kvn32, None)):
            pz = psf()
            mm_group(psF[pz][:, :], [(ones_b, sqT[:, j0 + jj, :]) for jj in range(2)],
                     ["consts_b", ("sqT", j0), ("sqT", j0 + 1)], [("psF", pz)])
            rsqrt_to(rstd[:], psF[pz][:, :], 256.0 * EPS, [("psF", pz)], ["rstd"])
            psf_free(pz)
            for jj in range(2):
                if j0 == 0:
                    o, ok = cqn[t][:, jj, :], ("cqn", t)
                elif t == "P":
                    o, ok = Ksrc[:, jj, 0:512], "Ksrc"
                else:
                    o, ok = x1buf[:, jj * 512:(jj + 1) * 512], "x1buf"
                stt("dve", o, mixT[:, j0 + jj, :], scl[:, jj:jj + 1], rstd[:], ALU.mult, ALU.mult,
                    [("cqf", j0 + jj), "qn32", "kvn32", "rstd"], [ok])
        pz = psf()
        mm_group(psF[pz][0:32, :], [(wsm[:, k, 0:32], hT[t][:, k, :]) for k in range(8)], ["wsm", hk], [("psF", pz)])
        if t == "P":
            cp("dve", kpeT[0:32, 0:512], psF[pz][0:32, :], [("psF", pz)], ["kpeT"])
        else:
            pz2 = psf()
            mm_group(psF[pz2][0:32, :], [(wsm_sw[:, k, :], hT[t][:, k, :]) for k in range(8)], ["wsm_sw", hk], [("psF", pz2)])
            tt("dve", tmpA[0][0:32, :], psF[pz][0:32, :], rope_lo[:, 0, :], ALU.mult, [("psF", pz), "rope_lo"], [("tmpA", 0)])
            tt("dve", tmpA[1][0:32, :], psF[pz2][0:32, :], rope_lo[:, 1, :], ALU.mult, [("psF", pz2), "rope_lo"], [("tmpA", 1)])
            tt("dve", x1buf[0:32, 1024:1536], tmpA[0][0:32, :], tmpA[1][0:32, :], ALU.add, [("tmpA", 0), ("tmpA", 1)], ["x1buf"])
            psf_free(pz2)
        psf_free(pz)
        if t == "P":
            for tb in range(4):
                pz = psf()
                mm_group(psF[pz][:, 0:256], [(hT[t][:, k, tb * 128:(tb + 1) * 128], wA[:, k, 256:512]) for k in range(8)],
                         [("ring", sA), hk], [("psF", pz)])
                mm_group(psF[pz][:, 256:288], [(hT[t][:, k, tb * 128:(tb + 1) * 128], wsm[:, k, 0:32]) for k in range(8)],
                         ["wsm", hk], [("psF", pz)])
                ct = tmpA[tb % 2]; ck = ("tmpA", tb % 2)
                P.op("dve", lambda e: e.memset(stat2[:, 0:1], 0.0), (), ["stat2"])
                cp("dve", ct[:, 0:288], psF[pz][:, 0:288], [("psF", pz)], [ck])
                psf_free(pz)
                act(sqT[:, 4, 0:256], ct[:, 0:256], AF.Square, [ck, "stat2"], [("sqT", 4), "stat2"], accum=stat2[:, 0:1])
                rsqrt_to(stat2[:, 0:1], stat2[:, 0:1], EPS, ["stat2"], ["stat2"], scale=1.0 / 256.0)
                stt("dve", ct[:, 0:256], ct[:, 0:256], stat2[:, 0:1], kvn_bc[:], ALU.mult, ALU.mult,
                    [ck, "stat2", "kvn_bc"], [ck])
                dma("sp", ockv[tb * 128:(tb + 1) * 128, :], ct[:, 0:256], [ck], ["ockv"])
                dma("sp", okpe[tb * 128:(tb + 1) * 128, :], ct[:, 256:288], [ck], ["okpe"])
        sZ = ring_load([(0, [128, 8, 512], w_in[0, :, 544:1056].rearrange("(k p) n -> p k n", p=128))])
        wZ = rview(sZ, 0, [128, 8, 512])
        for tb in range(4):
            pz = psf()
            mm_group(psF[pz][:, :], [(hT[t][:, k, tb * 128:(tb + 1) * 128], wZ[:, k, :]) for k in range(8)],
                     [("ring", sZ), hk], [("psF", pz)])
            act(zs[t][:, tb, :], psF[pz][:, :], AF.Silu, [("psF", pz)], [("zs", t)])
            psf_free(pz)
        for tb in range(4):
            pz = psf()
            mm_group(psF[pz][:, 0:16], [(hT[t][:, k, tb * 128:(tb + 1) * 128], wsm[:, k, 32:48]) for k in range(8)],
                     ["wsm", hk], [("psF", pz)])
            tt("dve", dtraw[t][:, tb, :], psF[pz][:, 0:16], sm_bc[:, 0:16], ALU.add, [("psF", pz), "sm_bc"], [("dtraw", t)])
            psf_free(pz)
        for xb in range(2):
            sX = ring_load([(0, [128, 8, 512], w_in[0, :, 1056 + xb * 512:1568 + xb * 512].rearrange("(k p) n -> p k n", p=128))])
            wX = rview(sX, 0, [128, 8, 512])
            for jj in range(4):
                j = xb * 4 + jj
                pz = psf()
                mm_group(psF[pz][:, :], [(wX[:, k, jj * 128:(jj + 1) * 128], hT[t][:, k, :]) for k in range(8)],
                         [("ring", sX), hk], [("psF", pz)])
                if t == "P":
                    alt_cp(xbcpad["P"][:, j, :].rearrange("p (s c) -> p s c", s=2)[:, :, 2:258],
                           psF[pz][:, :].rearrange("p (s c) -> p s c", s=2), [("psF", pz)], [("xbcpad", t)])
                else:
                    alt_cp(xbcpad["S"][:, j, 2:514], psF[pz][:, :], [("psF", pz)], [("xbcpad", t)])
                psf_free(pz)
        if t == "S":
            xe = x1buf[:, 1536:1568].rearrange("p (j c) -> p j c", j=8)
            cp("dve", xe[:, :, 0:2], xbcpad["S"][:, :, 2:4], [("xbcpad", t)], ["x1buf"])
            cp("dve", xe[:, :, 2:4], xbcpad["S"][:, :, 512:514], [("xbcpad", t)], ["x1buf"])

    def x1_exchange():
        dma("sp", x1_in[:, :], x1buf[:, :], ["x1buf"], ["x1_in"])
        coll(x1_in, x1_out, ["x1_in"], ["x1_out"])

    def x1_receive():
        x1r = x1_out.rearrange("(r p) c -> p r c", p=128)
        for kc in range(2):
            dma("sp", Ksrc[:, kc, 256:2304].rearrange("p (r t) -> p r t", r=4), x1r[:, :, kc * 512:(kc + 1) * 512],
                ["x1_out"], ["Ksrc"])
        dma("sp", kpeT[0:32, 256:2304].rearrange("p (r t) -> p r t", r=4), x1r[0:32, :, 1024:1536], ["x1_out"], ["kpeT"])
        dma("sp", gX[:], x1r[:, :, 1536:1568], ["x1_out"], ["gX"])
        cc = mixT[:, 0:2, 0:288]
        for tbk in range(2):
            dma("sp", cc[:, tbk, 0:256], cache_ckv[tbk * 128:(tbk + 1) * 128, :], (), [("cc", tbk)])
            dma("sp", cc[:, tbk, 256:288], cache_kpe[tbk * 128:(tbk + 1) * 128, :], (), [("cc", tbk)])
        for tbk in range(2):
            pz = psf()
            tr(psF[pz][:, 0:128], cc[:, tbk, 0:128], ident_f, ["consts", ("cc", tbk)], [("psF", pz)], signal=False)
            tr(psF[pz][:, 128:256], cc[:, tbk, 128:256], ident_f, ["consts", ("cc", tbk)], [("psF", pz)], signal=False)
            tr(psF[pz][0:32, 256:384], cc[:, tbk, 256:288], ident_f, ["consts", ("cc", tbk)], [("psF", pz)])
            cp("dve", Ksrc[:, :, tbk * 128:(tbk + 1) * 128], psF[pz][:, 0:256].rearrange("p (k n) -> p k n", k=2),
               [("psF", pz)], ["Ksrc"])
            cp("dve", kpeT[0:32, tbk * 128:(tbk + 1) * 128], psF[pz][0:32, 256:384], [("psF", pz)], ["kpeT"])
            psf_free(pz)
        hal = tmpA[0][:, 0:32].rearrange("p (s j c) -> p s j c", s=2, j=8)
        for side, (c0, sb_) in enumerate(((2, 4), (0, 8))):
            for rp in range(4):
                src = gX[:, rp, :].rearrange("p (j c) -> p j c", j=8)[:, :, c0:c0 + 2]
                if rp == 0:
                    ts("dve", hal[:, side], src, sel[:, sb_ + rp:sb_ + rp + 1], None, ALU.mult, None, ["gX", "sel"], [("tmpA", 0)])
                else:
                    stt("dve", hal[:, side], src, sel[:, sb_ + rp:sb_ + rp + 1], hal[:, side], ALU.mult, ALU.add,
                        ["gX", "sel", ("tmpA", 0)], [("tmpA", 0)])
        cp("dve", xbcpad["S"][:, :, 0:2], hal[:, 0], [("tmpA", 0)], [("xbcpad", "S")])
        cp("dve", xbcpad["S"][:, :, 514:516], hal[:, 1], [("tmpA", 0)], [("xbcpad", "S")])

    def conv(t):
        segs = [(0, 0, 256), (260, 256, 256)] if t == "P" else [(0, 0, 512)]
        for j in range(8):
            pz = psf()
            for (pb_, oc, n) in segs:
                mm_group(psF[pz][:, oc:oc + n],
                         [(diagW[:, w * 8 + j, :], xbcpad[t][:, j, pb_ + w:pb_ + w + n]) for w in range(5)],
                         [("diagW", 0), ("xbcpad", t)], [("psF", pz)])
            act(xconvT[:, j, :], psF[pz][:, :], AF.Silu, [("psF", pz), "cols"], [("xconvT", j)],
                bias=cols[:, B_CONVB + j:B_CONVB + j + 1])
            psf_free(pz)
        for tb in range(4):
            pb_ = psb()
            for j in range(6):
                tr(psB[pb_][:, j * 128:(j + 1) * 128], xconvT[:, j, tb * 128:(tb + 1) * 128], ident_b,
                   ["consts_b", ("xconvT", j)], [("psB", pb_)], signal=(j == 5))
            alt_cp(xtok[:, tb, :], psB[pb_][:, 0:768], [("psB", pb_)], [("xtok", tb)])
            psb_free(pb_)

    def ssd_small(t):
        dk = ("dtraw", t)
        act(dtw[:], dtraw[t][:], AF.Exp, [dk], ["dtw"])
        act(dtw[:], dtw[:], AF.Ln, ["dtw"], ["dtw"], bias=1.0)
        tt("dve", a_[:], dtw[:], A_bc[:].unsqueeze(1).to_broadcast([128, 4, 16]), ALU.mult, ["dtw", "A_bc"], ["a_"])
        act(ac2[:], dtw[:], AF.Ln, ["dtw"], ["ac2"])
        for tb in range(4):
            pz = psf()
            mm(psF[pz][:, 0:8], U_f, a_[:, tb, 0:8], True, True, ["consts", "a_"], [("psF", pz)], False)
            mm(psF[pz][:, 8:16], L_f, a_[:, tb, 8:16], True, True, ["consts", "a_"], [("psF", pz)], False)
            mm(psF[pz][:, 16:32], ones_f, a_[:, tb, :], True, True, ["consts", "a_"], [("psF", pz)], True)
            cp("act", smx[:, tb, :], psF[pz][:, 0:32], [("psF", pz)], ["smx"])
            psf_free(pz)
        act(E_[:], smx[:, :, 0:16], AF.Exp, ["smx"], ["E_"])
        act(CD_[:], smx[:, :, 16:32], AF.Exp, ["smx"], ["CD_"])
        tt("dve", ac2[:], smx[:, :, 0:16], ac2[:], ALU.subtract, ["smx", "ac2"], ["ac2"])
        wt = tmpA[0][:, 0:64].rearrange("p (a b) -> p a b", a=4)
        tt("dve", wt, smx[:, :, 16:32], smx[:, :, 0:16], ALU.subtract, ["smx"], [("tmpA", 0)])
        act(wt, wt, AF.Exp, [("tmpA", 0)], [("tmpA", 0)])
        tt("dve", dtw[:], dtw[:], wt, ALU.mult, ["dtw", ("tmpA", 0)], ["dtw"])

    def prepass(t, use_hin, final_cb, dirs=(0, 1)):
        xws = {0: xw, 1: xd}

        def step(d, tb):
            xw_, xk_ = xws[d], ("xw", d)
            cp("act", hprev[:, d, tb, :], hrun[:, d, :], [("hrun", d)], [("hprev", d, tb)])
            tt("dve", h8(xw_), h8(xtok[:, tb, 0:512]), bc8(dtw[:, tb, d * 8:(d + 1) * 8]), ALU.mult,
               [("xtok", tb), "dtw"], [xk_, "xw", "xd"] if False else [xk_])
            pz = psf()
            for g in range(2):
                mm(psF[pz][:, g * 256:(g + 1) * 256], xtok[:, tb, 512 + g * 128:512 + (g + 1) * 128],
                   xw_[:, g * 256:(g + 1) * 256], True, True, [("xtok", tb), xk_], [("psF", pz)], g == 1)
            tt("dve", h8(hrun[:, d, :]), h8(hrun[:, d, :]), bc8(CD_[:, tb, d * 8:(d + 1) * 8]), ALU.mult,
               [("hrun", d), "CD_"], [("hrun", d)])
            tt("dve", hrun[:, d, :], psF[pz][:, :], hrun[:, d, :], ALU.add, [("psF", pz), ("hrun", d)], [("hrun", d)])
            psf_free(pz)

        for seg in SEGS[t]:
            for d in dirs:
                if use_hin:
                    cp("act", hrun[:, d, :], hin[:, d, :], ["hin"], [("hrun", d)])
                else:
                    P.op("dve", lambda e, d=d: e.memset(hrun[:, d, :], 0.0), (), [("hrun", d)])
            orders = {d: (seg if d == 0 else list(reversed(seg))) for d in dirs}
            for i in range(len(seg)):
                for d in dirs:
                    step(d, orders[d][i])
            for d in dirs:
                final_cb(d, seg)

    def final_P(d, seg):
        s_ = seg[0] // 2
        pz = psf()
        for q in range(4):
            tr(psF[pz][:, q * 128:(q + 1) * 128], hrun[:, d, q * 128:(q + 1) * 128], ident_f,
               ["consts", ("hrun", d)], [("psF", pz)], signal=(q == 3))
        st_ = tmpA[d]
        cp("act", st_[:], psF[pz][:, :], [("psF", pz)], [("tmpA", d)])
        psf_free(pz)
        dma("sp", ohs[d][s_].rearrange("(q p) n -> p q n", p=128), st_[:].rearrange("p (q n) -> p q n", q=4),
            [("tmpA", d)], [("ohs", d)])

    def final_S1(d, seg):
        cp("act", x2buf[:, d * 512:(d + 1) * 512], hrun[:, d, :], [("hrun", d)], ["x2buf"])
        tsum = tmpA[0][:, 0:8]
        tt("dve", tsum, smx[:, 0, 16 + d * 8:24 + d * 8], smx[:, 1, 16 + d * 8:24 + d * 8], ALU.add, ["smx"], [("tmpA", 0)])
        tt("dve", tsum, tsum, smx[:, 2, 16 + d * 8:24 + d * 8], ALU.add, ["smx", ("tmpA", 0)], [("tmpA", 0)])
        tt("dve", tsum, tsum, smx[:, 3, 16 + d * 8:24 + d * 8], ALU.add, ["smx", ("tmpA", 0)], [("tmpA", 0)])
        act(x2buf[:, 1024 + d * 8:1032 + d * 8], tsum, AF.Exp, [("tmpA", 0)], ["x2buf"])

    def x2_exchange():
        dma("sp", x2_in[:, :], x2buf[:, :], ["x2buf"], ["x2_in"])
        coll(x2_in, x2_out, ["x2_in"], ["x2_out"])

    gSb = [av(19232, [128, 1040], F32), av(19232 + 2080, [128, 1040], F32)]

    def x2_receive():
        nld = [0]

        def load_rank(rp):
            i = nld[0] % 2; nld[0] += 1
            dma("sp", gSb[i][:, :], x2_out[rp * 128:(rp + 1) * 128, :], ["x2_out"], [("gSb", i)])
            return gSb[i], ("gSb", i)
        for d in range(2):
            st_ = tmpA[d]
            dma("sp", st_[:].rearrange("p (q n) -> p q n", q=4), h0_d[d].rearrange("(q p) n -> p q n", p=128), (), [("tmpA", d)])
            pz = psf()
            for q in range(4):
                tr(psF[pz][:, q * 128:(q + 1) * 128], st_[:, q * 128:(q + 1) * 128], ident_f,
                   ["consts", ("tmpA", d)], [("psF", pz)], signal=(q == 3))
            cp("act", hcand[:, d, :], psF[pz][:, :], [("psF", pz)], [("hcand", d)])
            psf_free(pz)
            ranks = [0, 1, 2] if d == 0 else [3, 2, 1]
            first = 0 if d == 0 else 3
            ts("dve", hin[:, d, :], hcand[:, d, :], sel[:, first:first + 1], None, ALU.mult, None,
               [("hcand", d), "sel"], ["hin"])
            for rp in ranks:
                nxt = rp + 1 if d == 0 else rp - 1
                g_, gk = load_rank(rp)
                tt("dve", h8(hcand[:, d, :]), h8(hcand[:, d, :]), bc8(g_[:, 1024 + d * 8:1032 + d * 8]), ALU.mult,
                   [("hcand", d), gk], [("hcand", d)])
                tt("dve", hcand[:, d, :], hcand[:, d, :], g_[:, d * 512:(d + 1) * 512], ALU.add,
                   [("hcand", d), gk], [("hcand", d)])
                stt("dve", hin[:, d, :], hcand[:, d, :], sel[:, nxt:nxt + 1], hin[:, d, :], ALU.mult, ALU.add,
                    [("hcand", d), "sel", "hin"], ["hin"])

    def ssd_main(t, tb):
        tbc = slice(tb * 128, (tb + 1) * 128)
        pa = psf()
        mm(psF[pa][0:8, 0:128], a_[:, tb, 0:8], U_f, True, True, ["consts", "a_"], [("psF", pa)], False)
        mm(psF[pa][0:8, 128:256], a_[:, tb, 8:16], L_f, True, True, ["consts", "a_"], [("psF", pa)], True)
        cp("act", acumT, psF[pa][0:8, 0:256], [("psF", pa)], ["acumT"])
        psf_free(pa)
        pc = psf()
        for g in range(2):
            mm(psF[pc][:, g * 128:(g + 1) * 128], xconvT[:, 4 + g, tbc], xconvT[:, 6 + g, tbc], True, True,
               [("xconvT", 4 + g), ("xconvT", 6 + g)], [("psF", pc)], g == 1)
        for g in range(2):
            for d in range(2):
                tt("dve", CBm[:, g * 2 + d, :], psF[pc][:, g * 128:(g + 1) * 128], U_f if d == 0 else L_f, ALU.mult,
                   [("psF", pc), "consts"], ["CBm"])
        psf_free(pc)
        for d in range(2):
            prs = []
            for g in range(2):
                pr = psf(); prs.append(pr)
                for hh in range(4):
                    h = g * 4 + hh
                    mm(psF[pr][:, hh * 128:(hh + 1) * 128], selc[0:8, h * 128:(h + 1) * 128], acumT[0:8, d * 128:(d + 1) * 128],
                       True, True, ["selc", "acumT"], [("psF", pr)], hh == 3)
            for g in range(2):
                pr = prs[g]
                for hh in range(4):
                    ci = d * 8 + g * 4 + hh
                    ts("dve", tmpA[g][:, hh * 128:(hh + 1) * 128], psF[pr][:, hh * 128:(hh + 1) * 128],
                       ac2[:, tb, ci:ci + 1], 30.0, ALU.subtract, ALU.min, [("psF", pr), "ac2"], [("tmpA", g)])
                psf_free(pr)
            for g in range(2):
                act(tmpA[g][:], tmpA[g][:], AF.Exp, [("tmpA", g)], [("tmpA", g)])
            for g in range(2):
                D3 = tmpA[g][:].rearrange("p (h n) -> p h n", h=4)
                stt("dve", MT[:, d * 8 + g * 4:d * 8 + g * 4 + 4, :], D3, 1e30,
                    CBm[:, g * 2 + d, :].unsqueeze(1).to_broadcast([128, 4, 128]), ALU.min, ALU.mult,
                    [("tmpA", g), "CBm"], [("MT", d, g)])
        tt("dve", xd, xtok[:, tb, 0:512], dsk_bc[:], ALU.mult, [("xtok", tb), "dsk_bc"], ["xd"])
        py = psf()
        mtk = [("MT", d, g) for d in range(2) for g in range(2)]
        for h in range(8):
            hc = slice(h * 64, (h + 1) * 64)
            mm(psF[py][:, hc], ident_b, xd[:, hc], True, False,
               (["consts_b", "xd"] + mtk + [("xtok", tb)]) if h == 0 else (), [("psF", py)], False)
            for d in range(2):
                mm(psF[py][:, hc], MT[:, d * 8 + h, :], xtok[:, tb, hc], False, d == 1,
                   (), [("psF", py)], (h == 7 and d == 1))
        pof = [psf(), psf()]
        for d in range(2):
            for g in range(2):
                mm(psF[pof[d]][:, g * 256:(g + 1) * 256], xconvT[:, 6 + g, tbc], hprev[:, d, tb, g * 256:(g + 1) * 256],
                   True, True, [("xconvT", 6 + g), ("hprev", d, tb)], [("psF", pof[d])], g == 1)
        cp("act", ysb, psF[py][:, :], [("psF", py)], ["ysb"])
        psf_free(py)
        for d in range(2):
            Dt = tmpA[d]; dk = ("tmpA", d)
            tt("dve", h8(Dt[:]), h8(psF[pof[d]][:, :]), bc8(E_[:, tb, d * 8:(d + 1) * 8]), ALU.mult,
               [("psF", pof[d]), "E_"], [dk])
            tt("dve", ysb, ysb, Dt[:], ALU.add, ["ysb", dk], ["ysb"])
            psf_free(pof[d])
        tt("dve", yg, ysb, zs[t][:, tb, :], ALU.mult, ["ysb", ("zs", t)], ["yg"])
        P.op("dve", lambda e: e.memset(stat2[:, 0:2], 0.0), (), ["stat2"])
        for g in range(2):
            act(tmpA[g][:, 0:256], yg[:, g * 256:(g + 1) * 256], AF.Square, ["yg"], [("tmpA", g), "stat2"],
                accum=stat2[:, g:g + 1])
        rsqrt_to(stat2[:, 0:2], stat2[:, 0:2], EPS, ["stat2"], ["stat2"], scale=1.0 / 256.0)
        for g in range(2):
            stt("dve", xw[:, g * 256:(g + 1) * 256], yg[:, g * 256:(g + 1) * 256], stat2[:, g:g + 1],
                ssdn_bc[:, g * 256:(g + 1) * 256], ALU.mult, ALU.mult, ["yg", "stat2", "ssdn_bc"], ["xw"])
        pb_ = psb()
        for k in range(4):
            tr(psB[pb_][:, k * 128:(k + 1) * 128], xw[:, k * 128:(k + 1) * 128], ident_b, ["consts_b", "xw"],
               [("psB", pb_)], signal=(k == 3))
        alt_cp(ssdT[:, :, tbc], psB[pb_][:, 0:512].rearrange("p (k n) -> p k n", k=4), [("psB", pb_)], ["ssdT"])
        psb_free(pb_)

    def attn_V(t):
        nkb = 18 if t == "S" else 4
        for kb in range(nkb):
            pz = psf()
            mm_group(psF[pz][:, :].rearrange("p (h c) -> p h c", h=8),
                     [(Ksrc[:, kc, kb * 128:(kb + 1) * 128], wukv[:, kc, :, 64:128]) for kc in range(2)],
                     ["Ksrc", "wukv"], [("psF", pz)])
            alt_cp(Vt[t][:, kb, :], psF[pz][:, :], [("psF", pz)], [("V", kb)])
            psf_free(pz)

    def attn_head(t, h):
        nkb = 18 if t == "S" else 4
        NK = nkb * 128
        if True:
            for kt in range((NK + 511) // 512):
                n = min(512, NK - kt * 512)
                kc_ = slice(kt * 512, kt * 512 + n)
                pz = psf()
                mm(psF[pz][0:96, 0:n], padI[0:32, 0:96], kpeT[0:32, kc_], True, False, ["padI", "kpeT"], [("psF", pz)], False)
                for kc in range(2):
                    mm(psF[pz][0:96, 0:n], wuk96[:, kc, h, :], Ksrc[:, kc, kc_], False, kc == 1,
                       ["wuk96", "Ksrc"], [("psF", pz)], kc == 1)
                alt_cp(Kh[:, kc_], psF[pz][0:96, 0:n], [("psF", pz)], ["Kh"])
                psf_free(pz)
            pq = psf()
            mm_group(psF[pq][0:96, :], [(wuq[:, kc, h, :], cqn[t][:, kc, :]) for kc in range(2)], ["wuq", ("cqn", t)], [("psF", pq)])
            if t == "P":
                alt_cp(Qh, psF[pq][0:96, :], [("psF", pq)], ["Qh"])
            else:
                pq2 = psf()
                mm_group(psF[pq2][0:96, :], [(wuq_sw[:, kc, h, :], cqn[t][:, kc, :]) for kc in range(2)],
                         ["wuq_sw", ("cqn", t)], [("psF", pq2)])
                cp("act", Qh[0:64, :], psF[pq][0:64, :], [("psF", pq)], ["Qh"])
                tt("dve", tmpA[0][64:96, :], psF[pq][64:96, :], rope_hi[64:96, 0, :], ALU.mult, [("psF", pq), "rope_hi"], [("tmpA", 0)])
                tt("dve", tmpA[1][64:96, :], psF[pq2][64:96, :], rope_hi[64:96, 1, :], ALU.mult, [("psF", pq2), "rope_hi"], [("tmpA", 1)])
                tt("dve", Qh[64:96, :], tmpA[0][64:96, :], tmpA[1][64:96, :], ALU.add, [("tmpA", 0), ("tmpA", 1)], ["Qh"])
                psf_free(pq2)
            psf_free(pq)
            po = psf(); pl = psf()
            if t == "P":
                plan = [(slice(s_ * 256, (s_ + 1) * 256), [2 * s_, 2 * s_ + 1]) for s_ in range(2)]
            else:
                plan = [(slice(0, 512), list(range(18)))]
            steps = []
            for (qc, kbs) in plan:
                for i, kb in enumerate(kbs):
                    steps.append((qc, kb, i == 0, i == len(kbs) - 1))
            pend = None
            for it, (qc, kb, first, lastk) in enumerate(steps):
                nq = qc.stop - qc.start
                psc = psf()
                mm(psF[psc][:, 0:nq], Kh[:, kb * 128:(kb + 1) * 128], Qh[:, qc], True, True, ["Kh", "Qh"], [("psF", psc)], True)
                if pend is not None:
                    pend()
                pt = PT[it % 2]; pk = ("PT", it % 2)
                act(pt[:, 0:nq], psF[psc][:, 0:nq], AF.Exp, [("psF", psc)], [pk], scale=SCALE)
                psf_free(psc)

                def pend(qc=qc, kb=kb, first=first, lastk=lastk, pt=pt, pk=pk, nq=nq):
                    mm(psF[po][0:64, qc], Vt[t][:, kb, h * 64:(h + 1) * 64], pt[:, 0:nq], first, lastk,
                       [("V", kb), pk], [("psF", po)], True)
                    mm(psF[pl][0:64, qc], ones_b[:, 0:64], pt[:, 0:nq], first, lastk, ["consts_b", pk], [("psF", pl)], True)
            pend()
            rl = tmpA[0][0:64, :]
            act(rl, psF[pl][0:64, :], AF.Ln, [("psF", pl)], [("tmpA", 0)])
            act(rl, rl, AF.Exp, [("tmpA", 0)], [("tmpA", 0)], scale=-1.0)
            tt("dve", OT[:, h, :], psF[po][0:64, :], rl, ALU.mult, [("psF", po), ("tmpA", 0)], [("OT", h)])
            psf_free(po); psf_free(pl)

    def outproj(t, l):
        for half in range(2):
            s1 = ring_load([(0, [64, 8, 512], w_out[0, 0:512, half * 512:(half + 1) * 512].rearrange("(h p) n -> p h n", p=64))])
            s2 = ring_load([(0, [128, 4, 512], w_out[0, 512:1024, half * 512:(half + 1) * 512].rearrange("(k p) n -> p k n", p=128))])
            wa = rview(s1, 0, [64, 8, 512]); ws_ = rview(s2, 0, [128, 4, 512])
            for d4 in range(4):
                dc = half * 4 + d4
                pz = psf()
                pairs = [(wa[:, h, d4 * 128:(d4 + 1) * 128], OT[:, h, :]) for h in range(8)]
                pairs += [(ws_[:, k, d4 * 128:(d4 + 1) * 128], ssdT[:, k, :]) for k in range(4)]
                mm_group(psF[pz][:, :], pairs, [("ring", s1), ("ring", s2), "ssdT"] + [("OT", h) for h in range(8)], [("psF", pz)])
                evac_mix(pz, dc)
                psf_free(pz)
        post_norm_residual(t, l, 2)

    def rest(t, l):
        conv(t)
        ssd_small(t)
        if t == "P":
            prepass(t, False, final_P)
            attn_V(t)
            for h in range(8):
                attn_head(t, h)
            for tb in range(4):
                ssd_main(t, tb)
        else:
            prepass(t, False, final_S1)
            x2_exchange()
            P.barrier()
            nop_ = lambda d, seg: None
            sched = {1: lambda: x2_receive(),
                     2: lambda: prepass(t, True, nop_, dirs=(0,)),
                     3: lambda: prepass(t, True, nop_, dirs=(1,)),
                     4: lambda: ssd_main(t, 0), 5: lambda: ssd_main(t, 1),
                     6: lambda: ssd_main(t, 2), 7: lambda: ssd_main(t, 3)}
            attn_V(t)
            for h in range(8):
                attn_head(t, h)
                if h in sched:
                    sched[h]()
        outproj(t, l)
        P.barrier()

    def layer0_mixer(l):
        P.op("dve", lambda e: e.memset(xbcpad["P"][:], 0.0), (), [("xbcpad", "P")])
        P.op("dve", lambda e: e.memset(wuk96[:], 0.0), (), ["wuk96"])
        cp("dve", wuk96[:, :, :, 0:64], wukv[:, :, :, 0:64], ["wukv", "wuk96"], ["wuk96"])
        for i in range(40):
            ts("dve", diagW[:, i, :], ident_f, cols[:, B_CONVW + i:B_CONVW + i + 1], None, ALU.mult, None,
               ["consts", "cols"], [("diagW", 0)])
        if RUN_S:
            P.op("dve", lambda e: e.memset(x1buf[32:64, 1024:1536], 0.0), (), ["x1buf"])
            P.op("dve", lambda e: e.memset(x1buf[64:128, 1024:1536], 0.0), (), ["x1buf"])
            inproj("S", l)
            x1_exchange()
            P.barrier()
        inproj("P", l)
        P.barrier()
        rest("P", l)
        if RUN_S:
            x1_receive()
            P.barrier()
            rest("S", l)

    RUN_S = not os.environ.get("KNO_S")

    def layer1_mixer(l):
        WP = {"P": 544, "S": 528}
        hpad = {"P": av(0, [128, 8, 544]), "S": av(8704, [128, 8, 528])}
        invc = av(21504, [128, 4, 512], F32)
        pooledT = av(25600, [128, 8, 512])
        x3buf = av(29696, [128, 8, 16], F32)
        gE = av(29952, [128, 4, 128], F32)

        def data(ap3, t):
            if t == "S":
                return ap3[:, :, 8:520]
            return ap3.rearrange("p k (s c) -> p k s c", s=2)[:, :, :, 8:264]

        def data2(ap2, t):
            if t == "S":
                return ap2[:, 8:520]
            return ap2.rearrange("p (s c) -> p s c", s=2)[:, :, 8:264]

        def seg2(ap2, t):
            if t == "S":
                return ap2
            return ap2.rearrange("p (s c) -> p s c", s=2)

        def fill(t):
            if t == "P":
                hp4 = hpad[t][:].rearrange("p k (s c) -> p k s c", s=2)
                P.op("dve", lambda e: e.memset(hp4[:, :, :, 0:8], 0.0), (), [("hpad", t)])
                P.op("dve", lambda e: e.memset(hp4[:, :, :, 264:272], 0.0), (), [("hpad", t)])
            modulate(t, l, 0, 1, outf=lambda k, tm, t=t: (data2(hpad[t][:, k, :], t), seg2(tm[:], t), ("hpad", t)))

        def pool_tile(t, ti):
            W = WP[t]
            dma("sp", invc[:], invcnt_d[ti:ti + 1].to_broadcast([128, 4, NT]), (), ["invc"])
            hk = ("hpad", t)
            segs = [(0, 0, 256), (272, 256, 256)] if t == "P" else [(0, 0, 512)]
            for gi, w in enumerate((2, 4, 8, 16)):
                for kk in range(2):
                    kch = 2 * gi + kk
                    pz = psf()
                    for (sb0, oc, n) in segs:
                        mm_group(psF[pz][:, oc:oc + n],
                                 [(ident_b, hpad[t][:, kch, sb0 + 8 + off:sb0 + 8 + off + n]) for off in range(-(w // 2), w // 2)],
                                 ["consts_b", hk], [("psF", pz)])
                    tm = tmpA[kk]; tk = ("tmpA", kk)
                    tt("dve", tm[:], psF[pz][:, :], invc[:, gi, :], ALU.mult, [("psF", pz), "invc"], [tk])
                    psf_free(pz)
                    tt("dve", seg2(pooledT[:, kch, :], t), seg2(tm[:], t), data2(hpad[t][:, kch, :], t), ALU.subtract,
                       [tk, hk], [("pooledT", kch)])
            s = ring_load([(0, [128, 8, 256], pool_w[0].rearrange("g (k p) n -> p (g k) n", p=128))])
            pw = rview(s, 0, [128, 8, 256])
            for dc in range(8):
                gi, co = dc // 2, dc % 2
                pz = psf()
                mm_group(psF[pz][:, :], [(pw[:, gi * 2 + kc, co * 128:(co + 1) * 128], pooledT[:, 2 * gi + kc, :]) for kc in range(2)],
                         [("ring", s), ("pooledT", 2 * gi), ("pooledT", 2 * gi + 1)], [("psF", pz)])
                act(mixT[:, dc, :], psF[pz][:, :], AF.Copy, [("psF", pz), "cols"], ["mixT"],
                    scale=cols[:, B_PSC + dc:B_PSC + dc + 1])
                act(sqT[:, dc, :], mixT[:, dc, :], AF.Square, ["mixT"], [("sqT", dc)])
                psf_free(pz)
            post_norm_residual(t, l, 2)

        if RUN_S:
            fill("S")
            cp("dve", x3buf[:, :, 0:8], hpad["S"][:, :, 8:16], [("hpad", "S")], ["x3buf"])
            cp("dve", x3buf[:, :, 8:16], hpad["S"][:, :, 512:520], [("hpad", "S")], ["x3buf"])
            dma("sp", x3_in[:, :], x3buf[:].rearrange("p k c -> p (k c)"), ["x3buf"], ["x3_in"])
            coll(x3_in, x3_out, ["x3_in"], ["x3_out"])
        fill("P")
        pool_tile("P", 0)
        if RUN_S:
            dma("sp", gE[:], x3_out.rearrange("(r p) c -> p r c", p=128), ["x3_out"], ["gE"])
            for (dst0, c0, sb_) in ((0, 8, 4), (520, 0, 8)):
                dst = hpad["S"][:, :, dst0:dst0 + 8]
                for rp in range(4):
                    src = gE[:, rp, :].rearrange("p (k c) -> p k c", k=8)[:, :, c0:c0 + 8]
                    if rp == 0:
                        ts("dve", dst, src, sel[:, sb_ + rp:sb_ + rp + 1], None, ALU.mult, None, ["gE", "sel"], [("hpad", "S")])
                    else:
                        stt("dve", dst, src, sel[:, sb_ + rp:sb_ + rp + 1], dst, ALU.mult, ALU.add,
                            ["gE", "sel", ("hpad", "S")], [("hpad", "S")])
            pool_tile("S", 1)
    for l in range(n_layers):
        if l % 2 == 0:
            layer0_mixer(l)
        else:
            layer1_mixer(l)
        P.barrier()
        if not os.environ.get('KSKIP_FFN'):
            ffn(l)
        P.barrier()

    P.barrier()
    for t in ("P", "S"):
        for tb in range(4):
            xl = xld[rr["xld"] % 2]; xk = ("xld", rr["xld"] % 2); rr["xld"] += 1
            for half in range(2):
                pz = psf()
                for q in range(4):
                    k = half * 4 + q
                    tr(psF[pz][:, q * 128:(q + 1) * 128], xT[t][:, k, tb * 128:(tb + 1) * 128], ident_f,
                       ["consts", ("xT", t)], [("psF", pz)], signal=(q == 3))
                alt_cp(xl[:, half * 512:(half + 1) * 512], psF[pz][:, :], [("psF", pz)], [xk])
                psf_free(pz)
            dma("sp", yout[t][tb * 128:(tb + 1) * 128, :], xl[:], [xk], [("yout", t)])
    P.final_wait()
    P.emit()
    es.close()
    return nc


def _consts():
    c = np.zeros((128, 512), np.float32)
    c[:, 0:128] = np.eye(128)
    t = np.arange(128)
    c[:, 128:256] = (t[:, None] <= t[None, :])
    c[:, 256:384] = (t[:, None] >= t[None, :])
    c[:, 384:512] = 1.0
    selc = np.zeros((8, 8, 128), np.float32)
    for h in range(8):
        selc[h, h, :] = 1.0
    padI = np.zeros((32, 96), np.float32)
    padI[np.arange(32), 64 + np.arange(32)] = 1.0
    return c, selc.reshape(8, 1024), padI


def _rope(pos):
    half = 16
    inv_freq = np.power(10000.0, -np.arange(0, half, 2, dtype=np.float64) / half)
    row = (pos // 64).astype(np.float64); col = (pos % 64).astype(np.float64)
    ang = np.concatenate([row[:, None] * inv_freq, col[:, None] * inv_freq], axis=-1)
    cos = np.cos(ang).T; sin = np.sin(ang).T
    r = np.zeros((32, 2, len(pos)), np.float32)
    r[0:16, 0] = cos; r[16:32, 0] = cos; r[0:16, 1] = sin; r[16:32, 1] = sin
    return r


def _invcnt(seg_len, nseg, lo_pad, hi_pad):
    out = np.zeros((4, NT), np.float32)
    for gi, w in enumerate((2, 4, 8, 16)):
        for s in range(nseg):
            L = seg_len
            t = np.arange(L)
            lo = t - w // 2; hi = t + w // 2
            if lo_pad: lo = np.clip(lo, 0, None)
            if hi_pad: hi = np.clip(hi, None, L)
            out[gi, s * L:(s + 1) * L] = 1.0 / (hi - lo)
    return out


_NC_CACHE = {}


def kernel(**inp):
    inp = {k: np.ascontiguousarray(np.asarray(v)) for k, v in inp.items()}
    if "nc" not in _NC_CACHE:
        _NC_CACHE["nc"] = build()
    nc = _NC_CACHE["nc"]
    c, selc, padI = _consts()
    shared = {k: inp[k] for k in (
          "w_in_ab",
         "w_uq",  "w_ukv",
         "w_out_ab", "pool_w",
        "ffn_w_gate", "ffn_w_up", "ffn_w_down")}
    shared.update(consts=c, selc=selc, padI=padI)
    bc_all = np.concatenate([inp["kv_norm"].reshape(-1), inp["ssd_norm"].reshape(-1), inp["ssd_dt_bias_fwd"].reshape(-1),
                             inp["ssd_dt_bias_bwd"].reshape(-1), inp["ssd_a_log_fwd"].reshape(-1),
                             inp["ssd_a_log_bwd"].reshape(-1), inp["ssd_d"].reshape(-1)]).reshape(1, 808)
    shared["bc_all"] = bc_all
    stg_common = [inp["norm_pre_mix"].reshape(16, 128), inp["norm_post_mix"].reshape(16, 128),
                  inp["norm_pre_ffn"].reshape(16, 128), inp["norm_post_ffn"].reshape(16, 128),
                  inp["b_mod"].reshape(96, 128)]
    stg_tail = [inp["q_norm"].reshape(2, 128), inp["kv_norm"].reshape(2, 128), inp["ssd_conv_b"].reshape(8, 128),
                inp["ssd_conv_w"][0].reshape(40, 128), inp["ssd_norm"].reshape(4, 128), inp["pool_scale"].reshape(8, 128),
                np.zeros((16, 128), np.float32)]
    in_maps = []
    for core in range(8):
        b, r = core // 4, core % 4
        m = dict(shared)
        m["xp"] = inp["x_prompt"][2 * core:2 * core + 2].reshape(NT, D)
        m["xs"] = inp["x_sample"][b, r * NT:(r + 1) * NT]
        m["stg_all"] = np.concatenate(stg_common + [np.stack([inp["c_ctx"], inp["c"][b]]).reshape(16, 128)] + stg_tail, axis=0)
        m["w_mod_sl"] = inp["w_mod"][:, :, r * 1536:(r + 1) * 1536]
        m["cache_ckv"] = inp["cache_mla_ckv"][b, 0]
        m["cache_kpe"] = inp["cache_mla_krope"][b, 0]
        m["h0f"] = inp["state_ssd_fwd"][b, 0].reshape(512, 128)
        m["h0b"] = inp["state_ssd_bwd"][b, 0].reshape(512, 128)
        m["rope"] = _rope(np.arange(r * NT, (r + 1) * NT))
        sel = np.zeros((128, 16), np.float32)
        sel[:, r] = 1.0
        if r > 0: sel[:, 4 + r - 1] = 1.0
        if r < 3: sel[:, 8 + r + 1] = 1.0
        m["sel"] = sel
        ic = np.zeros((2, 4, NT), np.float32)
        ic[0] = _invcnt(256, 2, True, True)
        ic[1] = _invcnt(NT, 1, r == 0, r == 3)
        m["invcnt"] = ic
        in_maps.append({k: np.ascontiguousarray(v, dtype=np.float32) for k, v in m.items()})
    ncores = int(os.environ.get("KCORES", "8"))
    res = run_bass_kernel_spmd(nc, in_maps[:ncores], core_ids=list(range(ncores)))
    R = list(res.results)
    while len(R) < 8:
        R.append(R[0])
    yp = np.concatenate([R[c_]["yp"].reshape(2, 256, D) for c_ in range(8)], axis=0)
    ys = np.stack([np.concatenate([R[b * 4 + r]["ys"] for r in range(4)], axis=0) for b in range(2)])
    ockv = np.concatenate([R[c_]["ockv"].reshape(2, 1, 256, 256) for c_ in range(8)], axis=0)
    okpe = np.concatenate([R[c_]["okpe"].reshape(2, 1, 256, 32) for c_ in range(8)], axis=0)
    ohf = np.concatenate([R[c_]["ohf"].reshape(2, 1, 8, 64, 128) for c_ in range(8)], axis=0)
    ohb = np.concatenate([R[c_]["ohb"].reshape(2, 1, 8, 64, 128) for c_ in range(8)], axis=0)
    return (yp.astype(np.float32), ys.astype(np.float32), ockv.astype(np.float32), okpe.astype(np.float32),
            ohf.astype(np.float32), ohb.astype(np.float32))
```

```python
import numpy as np
import concourse.bass as bass
import concourse.mybir as mybir

F32 = mybir.dt.float32
BF16 = mybir.dt.bfloat16
AF = mybir.ActivationFunctionType
ALU = mybir.AluOpType
AX = mybir.AxisListType

ENGS = ("pe", "act", "dve", "pool", "sp")


class Prog:
    def __init__(self, nc, n_dma_sems=24):
        self.nc = nc
        self.items = {e: [] for e in ENGS}
        self.cnt = {e: 0 for e in ENGS}
        self.waited = {e: {} for e in ENGS}
        self.lastw = {}
        self.readers = {}
        self.n_dma_sems = n_dma_sems
        self.dma_cnt = [0] * (n_dma_sems + 4)
        self.dma_i = 0
        self.dma_q = 0
        self.dma_c = 0
        self.nops = {e: 0 for e in ENGS}

    def _deps(self, eng, reads, writes):
        deps = []
        for r in reads:
            s = self.lastw.get(r)
            if s is not None:
                deps.append((s, "raw"))
            if isinstance(r, tuple) and r[0] in ("psF", "psB"):
                for s in self.readers.get(r, ()):
                    if s[2] != eng:
                        deps.append((s, "rar"))
        for w in writes:
            s = self.lastw.get(w)
            if s is not None:
                deps.append((s, "waw"))
            for s in self.readers.get(w, ()):
                deps.append((s, "war"))
        out = {}
        for (sem, val, peng), kind in deps:
            if peng == eng:
                if eng == "pe":
                    continue
            if out.get(sem, -1) < val:
                out[sem] = val
        return out

    def _emit_waits(self, eng, deps):
        for sem, val in deps.items():
            if self.waited[eng].get(sem, -1) >= val:
                continue
            self.waited[eng][sem] = val
            self.items[eng].append(("wait", sem, val))

    def _record(self, sig, reads, writes):
        for r in reads:
            self.readers.setdefault(r, []).append(sig)
        for w in writes:
            self.lastw[w] = sig
            self.readers[w] = []

    def op(self, eng, fn, reads=(), writes=(), signal=True):
        deps = self._deps(eng, reads, writes)
        self._emit_waits(eng, deps)
        self.nops[eng] += 1
        if signal:
            self.cnt[eng] += 1
            sig = ("E_" + eng, self.cnt[eng], eng)
            self.items[eng].append(("op", fn, True))
            self._record(sig, reads, writes)
        else:
            self.items[eng].append(("op", fn, False))
            sig = ("E_" + eng, self.cnt[eng] + 1, eng)
            self._record(sig, reads, writes)
        return sig

    def dma(self, eng, fn, reads=(), writes=(), inc=16):
        half = self.n_dma_sems // 2
        if inc == 1:
            i = self.n_dma_sems + (self.dma_c % 4); self.dma_c += 1
        elif eng == "pool":
            i = half + (self.dma_q % half); self.dma_q += 1
        else:
            i = self.dma_i % half; self.dma_i += 1
        sem = "D_%d" % i
        deps = self._deps(eng, reads, writes)
        if self.dma_cnt[i] > 0:
            if deps.get(sem, -1) < self.dma_cnt[i]:
                deps[sem] = self.dma_cnt[i]
        self._emit_waits(eng, deps)
        self.dma_cnt[i] += inc
        sig = (sem, self.dma_cnt[i], None)
        self.items[eng].append(("dma", fn, sem, inc))
        self.nops[eng] += 1
        self._record(sig, reads, writes)
        return sig

    def barrier(self):
        allsig = {}
        for e in ENGS:
            if self.cnt[e] > 0:
                allsig["E_" + e] = self.cnt[e]
        for i in range(self.n_dma_sems + 4):
            if self.dma_cnt[i] > 0:
                allsig["D_%d" % i] = self.dma_cnt[i]
        for e in ENGS:
            d = dict(allsig)
            self._emit_waits(e, d)

    def final_wait(self, eng="sp"):
        self.barrier()

    def emit(self, extra_ctx=()):
        nc = self.nc
        import contextlib
        with contextlib.ExitStack() as st:
            sems = {}
            for e in ENGS:
                sems["E_" + e] = st.enter_context(nc.semaphore("E_" + e))
            for i in range(self.n_dma_sems + 4):
                sems["D_%d" % i] = st.enter_context(nc.semaphore("D_%d" % i))
            block = st.enter_context(nc.Block())
            items = self.items

            def run(engh, ename):
                for it in items[ename]:
                    if it[0] == "wait":
                        engh.wait_ge(sems[it[1]], it[2])
                    elif it[0] == "op":
                        ins = it[1](engh)
                        if it[2]:
                            ins.then_inc(sems["E_" + ename], 1)
                    else:
                        ins = it[1](engh)
                        if it[3] == 1:
                            ins.then_inc(sems[it[2]])
                        else:
                            ins.then_inc(sems[it[2]], it[3])

            @block.sync
            def _(e):
                run(e, "sp")

            @block.scalar
            def _(e):
                run(e, "act")

            @block.vector
            def _(e):
                run(e, "dve")

            @block.gpsimd
            def _(e):
                run(e, "pool")

            @block.tensor
            def _(e):
                run(e, "pe")

from contextlib import ExitStack
import os
from concourse.bass_utils import run_bass_kernel_spmd
import ml_dtypes

D = 1024
NT = 512
EPS = 1e-6
IN_AB = 2096
D_FF = 2816
SCALE = 96 ** -0.5
RING_SLOTS = 3
RING_ELEMS = 4096

B_NPM, B_NPO, B_NFR, B_NFO = 0, 16, 32, 48
B_BMOD = 64
B_CVEC = 160
B_QN, B_KVN = 176, 178
B_CONVB = 180
B_CONVW = 188
B_SSDN = 228
B_PSC = 232
N_ROWS = 240


def build(n_layers=2, dbg=False):
    nc = bass.Bass("TRN2", target_bir_lowering=False)
    P = Prog(nc)
    es = ExitStack()

    def din(name, shape, dt=F32):
        return nc.dram_tensor(name, list(shape), dt, kind="ExternalInput").ap()

    def dout(name, shape, dt=F32):
        return nc.dram_tensor(name, list(shape), dt, kind="ExternalOutput").ap()

    def sb(name, shape, dt=F32):
        return es.enter_context(nc.sbuf_tensor("sb_" + name, list(shape), dt))

    xin = {"P": din("xp", [NT, D]), "S": din("xs", [NT, D])}
    cache_ckv = din("cache_ckv", [256, 256])
    cache_kpe = din("cache_kpe", [256, 32])
    h0_d = [din("h0f", [512, 128]), din("h0b", [512, 128])]
    w_mod = din("w_mod_sl", [2, D, 1536])
    w_in = din("w_in_ab", [1, D, IN_AB])
    w_uq = din("w_uq", [1, 256, 768]); w_ukv = din("w_ukv", [1, 256, 1024])
    w_out = din("w_out_ab", [1, D, D])
    pool_w = din("pool_w", [1, 4, 256, 256])
    w_gate = din("ffn_w_gate", [2, D, D_FF]); w_up = din("ffn_w_up", [2, D, D_FF])
    w_down = din("ffn_w_down", [2, D_FF, D])
    consts_d = din("consts", [128, 512])
    selc_d = din("selc", [8, 1024])
    padI_d = din("padI", [32, 96])
    rope_d = din("rope", [32, 2, NT])
    sel_d = din("sel", [128, 16])
    invcnt_d = din("invcnt", [2, 4, NT])

    yout = {"P": dout("yp", [NT, D]), "S": dout("ys", [NT, D])}
    ockv = dout("ockv", [NT, 256]); okpe = dout("okpe", [NT, 32])
    ohs = [dout("ohf", [2, 512, 128]), dout("ohb", [2, 512, 128])]

    NX1 = 1024 + 512 + 32
    x1_in = nc.dram_tensor("x1_in", [128, NX1], BF16, kind="Internal").ap()
    x1_out = nc.dram_tensor("x1_out", [4 * 128, NX1], BF16, kind="Internal").ap()
    x2_in = nc.dram_tensor("x2_in", [128, 1040], F32, kind="Internal").ap()
    x2_out = nc.dram_tensor("x2_out", [4 * 128, 1040], F32, kind="Internal").ap()
    x3_in = nc.dram_tensor("x3_in", [128, 128], F32, kind="Internal").ap()
    x3_out = nc.dram_tensor("x3_out", [4 * 128, 128], F32, kind="Internal").ap()
    GROUPS = [[0, 1, 2, 3], [4, 5, 6, 7]]

    consts = sb("consts", [128, 512]); consts_b = sb("consts_b", [128, 512], BF16)
    ident_f = consts[:, 0:128]; U_f = consts[:, 128:256]; L_f = consts[:, 256:384]; ones_f = consts[:, 384:512]
    ident_b = consts_b[:, 0:128]; ones_b = consts_b[:, 384:512]
    selc = sb("selc", [8, 1024])
    padI_f = sb("padI_f", [32, 96]); padI = sb("padI", [32, 96], BF16)
    rope_lo = sb("rope_lo", [32, 2, NT], BF16); rope_hi = sb("rope_hi", [96, 2, NT], BF16)
    sel = sb("sel", [128, 16])
    stg = sb("stg", [128, 2, 128]); cols = sb("cols", [128, 256])
    bcp = sb("bcp", [128, 808])
    kvn_bc = bcp[:, 0:256]; ssdn_bc = bcp[:, 256:768]; sm_bc = bcp[:, 768:808]
    A_bc = sb("A_bc", [128, 16]); dsk_bc = sb("dsk_bc", [128, 512], BF16)
    modT = sb("modT", [128, 2, 48, 2])
    csil = sb("csil", [128, 8, 2], BF16)
    mcol = sb("mcol", [128, 2, 6, 8, 2])
    qn32 = sb("qn32", [128, 2]); kvn32 = sb("kvn32", [128, 2])
    xT = {"P": sb("xT_P", [128, 8, NT]), "S": sb("xT_S", [128, 8, NT])}
    hT2 = sb("hT2", [128, 2, 8, NT], BF16)
    hT = {"P": hT2[:, 0], "S": hT2[:, 1]}
    sqT = sb("sqT", [128, 8, NT], BF16)
    rstd = sb("rstd", [128, NT]); tmpA = [sb("tmpA0", [128, NT]), sb("tmpA1", [128, NT])]
    mixT = sb("mixT", [128, 8, NT])
    xld = [mixT[:, 0:2, :].rearrange("p a b -> p (a b)"), mixT[:, 2:4, :].rearrange("p a b -> p (a b)")]
    ring = [sb("ring%d" % i, [128, RING_ELEMS], BF16) for i in range(RING_SLOTS)]
    wsm = sb("wsm", [128, 8, 48], BF16)
    wuq = sb("wuq", [128, 2, 8, 96], BF16); wuq_sw = sb("wuq_sw", [128, 2, 8, 96], BF16)
    wukv = sb("wukv", [128, 2, 8, 128], BF16)
    ARENA = 75 * 1024 // 2
    arena = sb("arena", [128, ARENA], BF16)
    mrow = arena[0:2, 0:6144].bitcast(F32).rearrange("p (l n) -> p l n", l=2)
    Gm = arena[0:16, 12288:12288 + 3072].bitcast(F32)

    rr = {"ring": 0, "tmp": 0, "xld": 0, "alt": 0}

    psF = [es.enter_context(nc.psum_tensor("psF%d" % i, [128, 512], F32)) for i in range(6)]
    psB = [es.enter_context(nc.psum_tensor("psB%d" % i, [128, 1024], BF16)) for i in range(2)]
    freeF = list(range(6)); freeB = [0, 1]

    def psf():
        i = freeF.pop(0); return i

    def psf_free(i):
        freeF.append(i)

    def psb():
        i = freeB.pop(0); return i

    def psb_free(i):
        freeB.append(i)

    def act(out, in_, func, reads, writes, bias=None, scale=None, accum=None):
        kw = {}
        if bias is not None: kw["bias"] = bias
        if scale is not None: kw["scale"] = scale
        if accum is not None: kw["accum_out"] = accum
        return P.op("act", lambda e: e.activation(out=out, in_=in_, func=func, **kw), reads, writes)

    def tt(eng, out, in0, in1, op, reads, writes):
        return P.op(eng, lambda e: e.tensor_tensor(out=out, in0=in0, in1=in1, op=op), reads, writes)

    def ts(eng, out, in0, s1, s2, op0, op1, reads, writes):
        if s2 is None:
            return P.op(eng, lambda e: e.tensor_scalar(out=out, in0=in0, scalar1=s1, scalar2=None, op0=op0), reads, writes)
        return P.op(eng, lambda e: e.tensor_scalar(out=out, in0=in0, scalar1=s1, scalar2=s2, op0=op0, op1=op1), reads, writes)

    def stt(eng, out, in0, scalar, in1, op0, op1, reads, writes):
        return P.op(eng, lambda e: e.scalar_tensor_tensor(out=out, in0=in0, scalar=scalar, in1=in1, op0=op0, op1=op1), reads, writes)

    def cp(eng, out, in_, reads, writes):
        if eng == "act":
            return P.op("act", lambda e: e.copy(out=out, in_=in_), reads, writes)
        return P.op(eng, lambda e: e.tensor_copy(out=out, in_=in_), reads, writes)

    def rsqrt_to(out, in_, c, reads, writes, scale=1.0):
        act(out, in_, AF.Ln, reads, writes, bias=float(c), scale=float(scale))
        act(out, out, AF.Exp, list(writes), list(writes), scale=-0.5)

    def alt_cp(out, in_, reads, writes):
        rr["alt"] ^= 1
        return cp("act" if rr["alt"] else "dve", out, in_, reads, writes)

    def mm(out, lhsT, rhs, start, stop, reads, writes, signal):
        return P.op("pe", lambda e: e.matmul(out, lhsT=lhsT, rhs=rhs, start=start, stop=stop), reads, writes, signal=signal)

    def mm_group(out, pairs, reads, writes):
        n = len(pairs)
        for i, (l, r) in enumerate(pairs):
            mm(out, l, r, i == 0, i == n - 1, reads if i == 0 else (), writes, i == n - 1)

    def tr(out, in_, ident, reads, writes, signal=True):
        return P.op("pe", lambda e: e.transpose(out=out, in_=in_, identity=ident), reads, writes, signal=signal)

    def dma(eng, out, in_, reads, writes):
        return P.dma(eng, lambda e: e.dma_start(out=out, in_=in_), reads, writes)

    def ring_load(parts, eng="pool"):
        s = rr["ring"] % RING_SLOTS
        rr["ring"] += 1
        for (off, shp, src) in parts:
            n = 1
            for v in shp[1:]:
                n *= v
            dst = ring[s][0:shp[0], off:off + n]
            if len(shp) == 3:
                dst = dst.rearrange("p (a b) -> p a b", a=shp[1])
            dma(eng, dst, src, (), [("ring", s)])
        return s

    def rview(s, off, shp):
        n = 1
        for v in shp[1:]:
            n *= v
        v = ring[s][0:shp[0], off:off + n]
        if len(shp) == 3:
            v = v.rearrange("p (a b) -> p a b", a=shp[1])
        return v

    dma("sp", consts[:], consts_d[:, :], (), ["consts"])
    dma("sp", selc[:], selc_d[:, :], (), ["selc"])
    dma("sp", padI_f[:], padI_d[:, :], (), ["padI_f"])
    dma("pool", rope_lo[:], rope_d[:, :, :], (), ["rope_lo"])
    dma("pool", rope_hi[64:96], rope_d[:, :, :], (), ["rope_hi"])
    dma("sp", sel[:], sel_d[:, :], (), ["sel"])
    cp("dve", consts_b[:], consts[:], ["consts"], ["consts_b"])
    cp("dve", padI[:], padI_f[:], ["padI_f"], ["padI"])

    def stage_rows(base, src2d, nrows):
        r = 0
        while r < nrows:
            row = base + r
            t, rin = row // 128, row % 128
            n = min(nrows - r, 128 - rin)
            dma("sp", stg[rin:rin + n, t, :], src2d[r:r + n, :], (), [("stg", t)])
            r += n

    stg_d = din("stg_all", [256, 128])
    bc_d = din("bc_all", [1, 808])
    dma("sp", stg[:, 0, :], stg_d[0:128, :], (), [("stg", 0)])
    dma("sp", stg[0:112, 1, :], stg_d[128:240, :], (), [("stg", 1)])
    pz = psf()
    tr(psF[pz][:, 0:128], stg[:, 0, :], ident_f, ["consts", ("stg", 0)], [("psF", pz)])
    tr(psF[pz][:, 128:128 + 112], stg[0:112, 1, :], ident_f[0:112, 0:112], ["consts", ("stg", 1)], [("psF", pz)])
    cp("dve", cols[:, 0:240], psF[pz][:, 0:240], [("psF", pz)], ["cols"])
    psf_free(pz)

    dma("sp", bcp[:], bc_d[0:1, :].to_broadcast([128, 808]), (), ["kvn_bc", "ssdn_bc", "sm_bc"])
    act(A_bc[:], sm_bc[:, 16:32], AF.Exp, ["sm_bc"], ["A_bc"])
    ts("dve", A_bc[:], A_bc[:], -1.0, None, ALU.mult, None, ["A_bc"], ["A_bc"])
    cp("dve", dsk_bc[:].rearrange("p (h c) -> p h c", h=8),
       sm_bc[:, 32:40].unsqueeze(2).to_broadcast([128, 8, 64]), ["sm_bc"], ["dsk_bc"])
    ts("dve", qn32[:], cols[:, B_QN:B_QN + 2], 16.0, None, ALU.mult, None, ["cols"], ["qn32"])
    ts("dve", kvn32[:], cols[:, B_KVN:B_KVN + 2], 16.0, None, ALU.mult, None, ["cols"], ["kvn32"])

    dma("pool", wsm[:, :, 0:32], w_in[0, :, 512:544].rearrange("(k p) n -> p k n", p=128), (), ["wsm"])
    dma("pool", wsm[:, :, 32:48], w_in[0, :, 2080:2096].rearrange("(k p) n -> p k n", p=128), (), ["wsm"])
    dma("pool", wuq[:].rearrange("p k h c -> p k (h c)"), w_uq[0].rearrange("(k p) n -> p k n", p=128), (), ["wuq"])
    dma("pool", wukv[:].rearrange("p k h c -> p k (h c)"), w_ukv[0].rearrange("(k p) n -> p k n", p=128), (), ["wukv"])
    P.op("dve", lambda e: e.memset(wuq_sw[:], 0.0), (), ["wuq_sw"])
    ts("dve", wuq_sw[:, :, :, 64:80], wuq[:, :, :, 80:96], -1.0, None, ALU.mult, None, ["wuq"], ["wuq_sw"])
    cp("dve", wuq_sw[:, :, :, 80:96], wuq[:, :, :, 64:80], ["wuq"], ["wuq_sw"])
    wsm_sw = sb("wsm_sw", [128, 8, 32], BF16)
    ts("dve", wsm_sw[:, :, 0:16], wsm[:, :, 16:32], -1.0, None, ALU.mult, None, ["wsm"], ["wsm_sw"])
    cp("dve", wsm_sw[:, :, 16:32], wsm[:, :, 0:16], ["wsm"], ["wsm_sw"])

    xm_in = nc.dram_tensor("xm_in", [4, 1536], F32, kind="Internal").ap()
    xm_out = nc.dram_tensor("xm_out", [16, 1536], F32, kind="Internal").ap()
    act(csil[:].rearrange("p k v -> p v k"),
        cols[:, B_CVEC:B_CVEC + 16].rearrange("p (v k) -> p v k", v=2), AF.Silu, ["cols"], ["csil"])
    for l in range(2):
        for cb in range(4):
            s = ring_load([(0, [128, 8, 384], w_mod[l, :, cb * 384:(cb + 1) * 384].rearrange("(k p) n -> p k n", p=128))])
            wv = rview(s, 0, [128, 8, 384])
            pz = psf()
            mm_group(psF[pz][0:2, 0:384], [(csil[:, k, :], wv[:, k, :]) for k in range(8)],
                     ["csil", ("ring", s)], [("psF", pz)])
            alt_cp(mrow[:, l, cb * 384:(cb + 1) * 384], psF[pz][0:2, 0:384], [("psF", pz)], ["mrow"])
            psf_free(pz)
    dma("sp", xm_in.rearrange("(v l) c -> v (l c)", v=2), mrow[:].rearrange("p l n -> p (l n)"), ["mrow"], ["xm_in"])
    P.dma("pool", lambda e: e.collective_compute("AllGather", ALU.bypass, replica_groups=GROUPS,
                                                 ins=[xm_in.opt()], outs=[xm_out.opt()]), ["xm_in"], ["xm_out"], inc=1)

    for t in ("P", "S"):
        for tb in range(4):
            xl = xld[rr["xld"] % 2]; xk = ("xld", rr["xld"] % 2); rr["xld"] += 1
            dma("sp", xl[:], xin[t][tb * 128:(tb + 1) * 128, :], (), [xk])
            for half in range(2):
                pz = psf()
                for q in range(4):
                    k = half * 4 + q
                    tr(psF[pz][:, q * 128:(q + 1) * 128], xl[:, k * 128:(k + 1) * 128], ident_f,
                       ["consts", xk], [("psF", pz)], signal=(q == 3))
                alt_cp(xT[t][:, half * 4:half * 4 + 4, tb * 128:(tb + 1) * 128],
                       psF[pz][:].rearrange("p (q c) -> p q c", q=4), [("psF", pz)], [("xT", t)])
                psf_free(pz)


    dma("sp", Gm[:, :], xm_out[:, :], ["xm_out"], ["Gm"])
    pz = psf()
    for cb in range(12):
        tr(psF[pz][:, cb * 16:(cb + 1) * 16], Gm[:, cb * 128:(cb + 1) * 128], ident_f[0:16, 0:16],
           ["consts", "Gm"], [("psF", pz)], signal=(cb == 11))
    pv = psF[pz][:, 0:192].rearrange("p (cb r v l) -> p r cb v l", cb=12, r=4, v=2, l=2)
    for l in range(2):
        for v2 in range(2):
            tt("dve", modT[:, l, :, v2].rearrange("p (r cb) -> p r cb", r=4), pv[:, :, :, v2, l],
               cols[:, B_BMOD + l * 48:B_BMOD + (l + 1) * 48].rearrange("p (r cb) -> p r cb", r=4), ALU.add,
               [("psF", pz), "cols"], [("modT", l)])
    psf_free(pz)
    for l in range(2):
        def ncol(base):
            return cols[:, base + l * 8:base + (l + 1) * 8].unsqueeze(2).to_broadcast([128, 8, 2])
        for (kind, jscale, nbase) in ((0, 1, B_NPM), (3, 4, B_NFR)):
            ts("dve", mcol[:, l, kind], modT[:, l, jscale * 8:(jscale + 1) * 8, :], 1.0, 32.0, ALU.add, ALU.mult,
               [("modT", l)], [("mcol", l)])
            tt("dve", mcol[:, l, kind], mcol[:, l, kind], ncol(nbase), ALU.mult, [("mcol", l), "cols"], [("mcol", l)])
        for (kind, jsh) in ((1, 0), (4, 3)):
            cp("dve", mcol[:, l, kind], modT[:, l, jsh * 8:(jsh + 1) * 8, :], [("modT", l)], [("mcol", l)])
        for (kind, jg, nbase) in ((2, 2, B_NPO), (5, 5, B_NFO)):
            stt("dve", mcol[:, l, kind], modT[:, l, jg * 8:(jg + 1) * 8, :], 32.0, ncol(nbase), ALU.mult, ALU.mult,
                [("modT", l), "cols"], [("mcol", l)])


    P.barrier()
    VI = {"P": 0, "S": 1}

    def rstd_from_sq(nchunks, scale_const, reads):
        pz = psf()
        mm_group(psF[pz][:, :], [(ones_b, sqT[:, k, :]) for k in range(nchunks)], ["consts_b"] + [("sqT", k_) for k_ in range(nchunks)] + list(reads), [("psF", pz)])
        rsqrt_to(rstd[:], psF[pz][:, :], scale_const * EPS, [("psF", pz)], ["rstd"])
        psf_free(pz)

    def modulate(t, l, kind_gs, kind_sh, outf=None):
        v = VI[t]
        act(sqT[:], xT[t][:], AF.Square, [("xT", t)], [("sqT", k_) for k_ in range(8)])
        rstd_from_sq(8, 1024.0, [])
        for k in range(8):
            tm = tmpA[rr["tmp"] % 2]; tk = ("tmpA", rr["tmp"] % 2); rr["tmp"] += 1
            stt("dve", tm[:], xT[t][:, k, :], mcol[:, l, kind_gs, k, v:v + 1], rstd[:], ALU.mult, ALU.mult,
                [("xT", t), ("mcol", l), "rstd"], [tk])
            if outf is None:
                act(hT[t][:, k, :], tm[:], AF.Identity, [tk, ("mcol", l)], [("hT", t)], bias=mcol[:, l, kind_sh, k, v:v + 1])
            else:
                o_ap, i_ap, o_key = outf(k, tm)
                act(o_ap, i_ap, AF.Identity, [tk, ("mcol", l)], [o_key], bias=mcol[:, l, kind_sh, k, v:v + 1])

    def post_norm_residual(t, l, kind_g):
        v = VI[t]
        rstd_from_sq(8, 1024.0, [])
        for k in range(8):
            tm = tmpA[rr["tmp"] % 2]; tk = ("tmpA", rr["tmp"] % 2); rr["tmp"] += 1
            stt("dve", tm[:], mixT[:, k, :], mcol[:, l, kind_g, k, v:v + 1], rstd[:], ALU.mult, ALU.mult,
                ["mixT", ("mcol", l), "rstd"], [tk])
            tt("dve", xT[t][:, k, :], xT[t][:, k, :], tm[:], ALU.add, [("xT", t), tk], [("xT", t)])

    def evac_mix(pz, k):
        cp("dve", mixT[:, k, :], psF[pz][:, :], [("psF", pz)], ["mixT"])
        act(sqT[:, k, :], mixT[:, k, :], AF.Square, ["mixT"], [("sqT", k)])

    def ffn(l):
        actT = arena[:, 0:22 * 2 * NT].rearrange("p (f n) -> p f n", f=22)
        for t in ("P", "S"):
            modulate(t, l, 3, 4)
        STAGE = int(os.environ.get('KFFN_STAGE', '3'))
        if STAGE < 2:
            return
        for nb in range(11):
            s = ring_load([(0, [128, 8, 256], w_gate[l, :, nb * 256:(nb + 1) * 256].rearrange("(k p) n -> p k n", p=128)),
                           (2048, [128, 8, 256], w_up[l, :, nb * 256:(nb + 1) * 256].rearrange("(k p) n -> p k n", p=128))])
            wg = rview(s, 0, [128, 8, 256]); wu = rview(s, 2048, [128, 8, 256])
            for ti, t in enumerate(("P", "S")):
                for c in range(2):
                    f = nb * 2 + c
                    pg = psf(); pu = psf()
                    mm_group(psF[pg][:, :], [(wg[:, k, c * 128:(c + 1) * 128], hT[t][:, k, :]) for k in range(8)],
                             [("ring", s), ("hT", t)], [("psF", pg)])
                    mm_group(psF[pu][:, :], [(wu[:, k, c * 128:(c + 1) * 128], hT[t][:, k, :]) for k in range(8)],
                             [("ring", s), ("hT", t)], [("psF", pu)])
                    tm = tmpA[rr["tmp"] % 2]; tk = ("tmpA", rr["tmp"] % 2); rr["tmp"] += 1
                    KGU = int(os.environ.get("KGU", "0"))
                    if KGU in (0, 1):
                        act(tm[:], psF[pg][:, :], AF.Silu, [("psF", pg)], [tk])
                    if KGU in (0, 2):
                        tt("dve", actT[:, f, ti * NT:(ti + 1) * NT], psF[pu][:, :], tm[:], ALU.mult,
                           [tk, ("psF", pu)], [("actT", f)])
                    psf_free(pg); psf_free(pu)
        for ti, t in enumerate(("P", "S")):
            pass
        if STAGE < 3:
            return
        wdb = [arena[:, 26624 + i * 5632:26624 + (i + 1) * 5632].rearrange("p (f n) -> p f n", f=22) for i in range(2)]
        for db in range(4):
            wb = wdb[db % 2]; wk = ("wdblk", db % 2)
            for (f0, f1) in ((0, 11), (11, 22)):
                dma("pool", wb[:, f0:f1, :],
                    w_down[l, f0 * 128:f1 * 128, db * 256:(db + 1) * 256].rearrange("(f p) n -> p f n", p=128),
                    (), [wk])
            for d2 in range(2):
                dc = db * 2 + d2
                for ti, t in enumerate(("P", "S")):
                    pz = psf()
                    mm_group(psF[pz][:, :], [(wb[:, f, d2 * 128:(d2 + 1) * 128], actT[:, f, ti * NT:(ti + 1) * NT]) for f in range(22)],
                             [wk] + [("actT", f) for f in range(22)], [("psF", pz)])
                    mb = mixT2[t]
                    mkeys = [("mix2", t), ("hT", "P"), ("hT", "S")] if t == "S" else [("mix2", t)]
                    cp("dve", mb[:, dc, :], psF[pz][:, :], [("psF", pz)], mkeys)
                    act(sq2[t][:, dc, :], mb[:, dc, :], AF.Square, mkeys, [("sq2", t)])
                    psf_free(pz)
        for t in ("P", "S"):
            v = VI[t]
            pz = psf()
            mm_group(psF[pz][:, :], [(ones_b, sq2[t][:, k, :]) for k in range(8)], ["consts_b", ("sq2", t)], [("psF", pz)])
            rsqrt_to(rstd[:], psF[pz][:, :], 1024.0 * EPS, [("psF", pz)], ["rstd"])
            psf_free(pz)
            for k in range(8):
                tm = tmpA[rr["tmp"] % 2]; tk = ("tmpA", rr["tmp"] % 2); rr["tmp"] += 1
                stt("dve", tm[:], mixT2[t][:, k, :], mcol[:, l, 5, k, v:v + 1], rstd[:], ALU.mult, ALU.mult,
                    [("mix2", t), ("mcol", l), "rstd"], [tk])
                tt("dve", xT[t][:, k, :], xT[t][:, k, :], tm[:], ALU.add, [("xT", t), tk], [("xT", t)])

    mixT2 = {"P": mixT, "S": hT2[:].rearrange("p t k n -> p (t k n)").bitcast(F32).rearrange("p (k n) -> p k n", k=8)}
    sq2 = {"P": sqT, "S": arena[:, 22 * 2 * NT:22 * 2 * NT + 8 * NT].rearrange("p (k n) -> p k n", k=8)}

    def av(off, shape, dt=BF16):
        n = 1
        for v_ in shape[1:]:
            n *= v_
        if dt == F32:
            v = arena[0:shape[0], off:off + 2 * n].bitcast(F32)
        else:
            v = arena[0:shape[0], off:off + n]
        if len(shape) == 3:
            v = v.rearrange("p (a b) -> p a b", a=shape[1])
        elif len(shape) == 4:
            v = v.rearrange("p (a b c) -> p a b c", a=shape[1], b=shape[2])
        return v

    xbcpad = {"S": av(0, [128, 8, 516]), "P": av(19232, [128, 8, 520])}
    diagW = av(4128, [128, 40, 128])
    Vt = {"S": av(0, [128, 18, 512]), "P": av(19232, [128, 4, 512])}
    cqn = {"S": av(9248, [128, 2, 512]), "P": av(34656, [128, 2, 512])}
    Ksrc = av(12320, [128, 2, 2304]); kpeT = av(16928, [32, 2304])
    xconvT = av(23392, [128, 8, 512]); xtok = av(27488, [128, 4, 768]); hprev = av(30560, [128, 2, 4, 512])
    x1buf = av(27488, [128, 1568])
    gX = av(30560, [128, 4, 32])
    wuk96 = av(36864, [128, 2, 8, 96])
    smf = arena[:, 35680:36864].bitcast(F32)
    dtraw = {"P": smf[:, 0:64].rearrange("p (a b) -> p a b", a=4), "S": smf[:, 64:128].rearrange("p (a b) -> p a b", a=4)}
    dtw = smf[:, 128:192].rearrange("p (a b) -> p a b", a=4)
    a_ = smf[:, 192:256].rearrange("p (a b) -> p a b", a=4)
    smx = smf[:, 256:384].rearrange("p (a b) -> p a b", a=4)
    E_ = smf[:, 384:448].rearrange("p (a b) -> p a b", a=4)
    CD_ = smf[:, 448:512].rearrange("p (a b) -> p a b", a=4)
    ac2 = smf[:, 512:576].rearrange("p (a b) -> p a b", a=4)
    hflat = hT2[:].rearrange("p t k n -> p (t k n)")
    OT = hflat[0:64, 0:4096].rearrange("p (h n) -> p h n", h=8)
    ssdT = hflat[:, 4096:6144].rearrange("p (k n) -> p k n", k=4)
    zs = {"S": av(10272, [128, 4, 512]), "P": hflat[:, 6144:8192].rearrange("p (k n) -> p k n", k=4)}
    x2buf = hflat[:, 0:2080].bitcast(F32)
    sflat = sqT[:].rearrange("p k n -> p (k n)")
    Kh = sflat[0:96, 0:2304]; Qh = sflat[0:96, 2304:2816]
    PT = [sflat[:, 2816:3328], sflat[:, 3328:3840]]
    mflat = mixT[:].rearrange("p k n -> p (k n)")
    mbf = mflat.bitcast(BF16)
    MT = mbf[:, 0:2048].rearrange("p (h n) -> p h n", h=16)
    CBm = mflat[:, 1024:1536].rearrange("p (h n) -> p h n", h=4)
    ysb = mflat[:, 1536:2048]; yg = mflat[:, 2048:2560]
    xw = mbf[:, 5120:5632]; xd = mbf[:, 5632:6144]
    hrun = mflat[:, 3072:4096].rearrange("p (d n) -> p d n", d=2)
    hcand = mflat[:, 1536:2560].rearrange("p (d n) -> p d n", d=2)
    hin = mflat[:, 0:1024].rearrange("p (d n) -> p d n", d=2)
    acumT = rstd[0:8, 0:256]; stat2 = sb("stat2", [128, 16])
    gS = av(0, [128, 4, 1040], F32)
    SEGS = {"P": [[0, 1], [2, 3]], "S": [[0, 1, 2, 3]]}

    def h8(ap2d):
        return ap2d.rearrange("p (h c) -> p h c", h=8)

    def bc8(ap_8):
        return ap_8.unsqueeze(2).to_broadcast([128, 8, 64])

    def coll(in_ap, out_ap, rk, wk):
        P.dma("pool", lambda e: e.collective_compute("AllGather", ALU.bypass, replica_groups=GROUPS,
                                                     ins=[in_ap.opt()], outs=[out_ap.opt()]), rk, wk, inc=1)

    def inproj(t, l):
        modulate(t, l, 0, 1)
        hk = ("hT", t)
        sA = ring_load([(0, [128, 8, 512], w_in[0, :, 0:512].rearrange("(k p) n -> p k n", p=128))])
        wA = rview(sA, 0, [128, 8, 512])
        for j in range(4):
            pz = psf()
            mm_group(psF[pz][:, :], [(wA[:, k, j * 128:(j + 1) * 128], hT[t][:, k, :]) for k in range(8)],
                     [("ring", sA), hk], [("psF", pz)])
            cp("dve", mixT[:, j, :], psF[pz][:, :], [("psF", pz)], [("cqf", j)])
            act(sqT[:, j, :], mixT[:, j, :], AF.Square, [("cqf", j)], [("sqT", j)])
            psf_free(pz)
        for (j0, scl, dst) in ((0, qn32, None), (2, kvn32, None)):
            pz = psf()
            mm_group(psF[pz][:, :], [(ones_b, sqT[:, j0 + jj, :]) for jj in range(2)],
                     ["consts_b", ("sqT", j0), ("sqT", j0 + 1)], [("psF", pz)])
            rsqrt_to(rstd[:], psF[pz][:, :], 256.0 * EPS, [("psF", pz)], ["rstd"])
            psf_free(pz)
            for jj in range(2):
                if j0 == 0:
                    o, ok = cqn[t][:, jj, :], ("cqn", t)
                elif t == "P":
                    o, ok = Ksrc[:, jj, 0:512], "Ksrc"
                else:
                    o, ok = x1buf[:, jj * 512:(jj + 1) * 512], "x1buf"
                stt("dve", o, mixT[:, j0 + jj, :], scl[:, jj:jj + 1], rstd[:], ALU.mult, ALU.mult,
                    [("cqf", j0 + jj), "qn32", "kvn32", "rstd"], [ok])
        pz = psf()
        mm_group(psF[pz][0:32, :], [(wsm[:, k, 0:32], hT[t][:, k, :]) for k in range(8)], ["wsm", hk], [("psF", pz)])
        if t == "P":
            cp("dve", kpeT[0:32, 0:512], psF[pz][0:32, :], [("psF", pz)], ["kpeT"])
        else:
            pz2 = psf()
            mm_group(psF[pz2][0:32, :], [(wsm_sw[:, k, :], hT[t][:, k, :]) for k in range(8)], ["wsm_sw", hk], [("psF", pz2)])
            tt("dve", tmpA[0][0:32, :], psF[pz][0:32, :], rope_lo[:, 0, :], ALU.mult, [("psF", pz), "rope_lo"], [("tmpA", 0)])
            tt("dve", tmpA[1][0:32, :], psF[pz2][0:32, :], rope_lo[:, 1, :], ALU.mult, [("psF", pz2), "rope_lo"], [("tmpA", 1)])
            tt("dve", x1buf[0:32, 1024:1536], tmpA[0][0:32, :], tmpA[1][0:32, :], ALU.add, [("tmpA", 0), ("tmpA", 1)], ["x1buf"])
            psf_free(pz2)
        psf_free(pz)
        if t == "P":
            for tb in range(4):
                pz = psf()
                mm_group(psF[pz][:, 0:256], [(hT[t][:, k, tb * 128:(tb + 1) * 128], wA[:, k, 256:512]) for k in range(8)],
                         [("ring", sA), hk], [("psF", pz)])
                mm_group(psF[pz][:, 256:288], [(hT[t][:, k, tb * 128:(tb + 1) * 128], wsm[:, k, 0:32]) for k in range(8)],
                         ["wsm", hk], [("psF", pz)])
                ct = tmpA[tb % 2]; ck = ("tmpA", tb % 2)
                P.op("dve", lambda e: e.memset(stat2[:, 0:1], 0.0), (), ["stat2"])
                cp("dve", ct[:, 0:288], psF[pz][:, 0:288], [("psF", pz)], [ck])
                psf_free(pz)
                act(sqT[:, 4, 0:256], ct[:, 0:256], AF.Square, [ck, "stat2"], [("sqT", 4), "stat2"], accum=stat2[:, 0:1])
                rsqrt_to(stat2[:, 0:1], stat2[:, 0:1], EPS, ["stat2"], ["stat2"], scale=1.0 / 256.0)
                stt("dve", ct[:, 0:256], ct[:, 0:256], stat2[:, 0:1], kvn_bc[:], ALU.mult, ALU.mult,
                    [ck, "stat2", "kvn_bc"], [ck])
                dma("sp", ockv[tb * 128:(tb + 1) * 128, :], ct[:, 0:256], [ck], ["ockv"])
                dma("sp", okpe[tb * 128:(tb + 1) * 128, :], ct[:, 256:288], [ck], ["okpe"])
        sZ = ring_load([(0, [128, 8, 512], w_in[0, :, 544:1056].rearrange("(k p) n -> p k n", p=128))])
        wZ = rview(sZ, 0, [128, 8, 512])
        for tb in range(4):
            pz = psf()
            mm_group(psF[pz][:, :], [(hT[t][:, k, tb * 128:(tb + 1) * 128], wZ[:, k, :]) for k in range(8)],
                     [("ring", sZ), hk], [("psF", pz)])
            act(zs[t][:, tb, :], psF[pz][:, :], AF.Silu, [("psF", pz)], [("zs", t)])
            psf_free(pz)
        for tb in range(4):
            pz = psf()
            mm_group(psF[pz][:, 0:16], [(hT[t][:, k, tb * 128:(tb + 1) * 128], wsm[:, k, 32:48]) for k in range(8)],
                     ["wsm", hk], [("psF", pz)])
            tt("dve", dtraw[t][:, tb, :], psF[pz][:, 0:16], sm_bc[:, 0:16], ALU.add, [("psF", pz), "sm_bc"], [("dtraw", t)])
            psf_free(pz)
        for xb in range(2):
            sX = ring_load([(0, [128, 8, 512], w_in[0, :, 1056 + xb * 512:1568 + xb * 512].rearrange("(k p) n -> p k n", p=128))])
            wX = rview(sX, 0, [128, 8, 512])
            for jj in range(4):
                j = xb * 4 + jj
                pz = psf()
                mm_group(psF[pz][:, :], [(wX[:, k, jj * 128:(jj + 1) * 128], hT[t][:, k, :]) for k in range(8)],
                         [("ring", sX), hk], [("psF", pz)])
                if t == "P":
                    alt_cp(xbcpad["P"][:, j, :].rearrange("p (s c) -> p s c", s=2)[:, :, 2:258],
                           psF[pz][:, :].rearrange("p (s c) -> p s c", s=2), [("psF", pz)], [("xbcpad", t)])
                else:
                    alt_cp(xbcpad["S"][:, j, 2:514], psF[pz][:, :], [("psF", pz)], [("xbcpad", t)])
                psf_free(pz)
        if t == "S":
            xe = x1buf[:, 1536:1568].rearrange("p (j c) -> p j c", j=8)
            cp("dve", xe[:, :, 0:2], xbcpad["S"][:, :, 2:4], [("xbcpad", t)], ["x1buf"])
            cp("dve", xe[:, :, 2:4], xbcpad["S"][:, :, 512:514], [("xbcpad", t)], ["x1buf"])

    def x1_exchange():
        dma("sp", x1_in[:, :], x1buf[:, :], ["x1buf"], ["x1_in"])
        coll(x1_in, x1_out, ["x1_in"], ["x1_out"])

    def x1_receive():
        x1r = x1_out.rearrange("(r p) c -> p r c", p=128)
        for kc in range(2):
            dma("sp", Ksrc[:, kc, 256:2304].rearrange("p (r t) -> p r t", r=4), x1r[:, :, kc * 512:(kc + 1) * 512],
                ["x1_out"], ["Ksrc"])
        dma("sp", kpeT[0:32, 256:2304].rearrange("p (r t) -> p r t", r=4), x1r[0:32, :, 1024:1536], ["x1_out"], ["kpeT"])
        dma("sp", gX[:], x1r[:, :, 1536:1568], ["x1_out"], ["gX"])
        cc = mixT[:, 0:2, 0:288]
        for tbk in range(2):
            dma("sp", cc[:, tbk, 0:256], cache_ckv[tbk * 128:(tbk + 1) * 128, :], (), [("cc", tbk)])
            dma("sp", cc[:, tbk, 256:288], cache_kpe[tbk * 128:(tbk + 1) * 128, :], (), [("cc", tbk)])
        for tbk in range(2):
            pz = psf()
            tr(psF[pz][:, 0:128], cc[:, tbk, 0:128], ident_f, ["consts", ("cc", tbk)], [("psF", pz)], signal=False)
            tr(psF[pz][:, 128:256], cc[:, tbk, 128:256], ident_f, ["consts", ("cc", tbk)], [("psF", pz)], signal=False)
            tr(psF[pz][0:32, 256:384], cc[:, tbk, 256:288], ident_f, ["consts", ("cc", tbk)], [("psF", pz)])
            cp("dve", Ksrc[:, :, tbk * 128:(tbk + 1) * 128], psF[pz][:, 0:256].rearrange("p (k n) -> p k n", k=2),
               [("psF", pz)], ["Ksrc"])
            cp("dve", kpeT[0:32, tbk * 128:(tbk + 1) * 128], psF[pz][0:32, 256:384], [("psF", pz)], ["kpeT"])
            psf_free(pz)
        hal = tmpA[0][:, 0:32].rearrange("p (s j c) -> p s j c", s=2, j=8)
        for side, (c0, sb_) in enumerate(((2, 4), (0, 8))):
            for rp in range(4):
                src = gX[:, rp, :].rearrange("p (j c) -> p j c", j=8)[:, :, c0:c0 + 2]
                if rp == 0:
                    ts("dve", hal[:, side], src, sel[:, sb_ + rp:sb_ + rp + 1], None, ALU.mult, None, ["gX", "sel"], [("tmpA", 0)])
                else:
                    stt("dve", hal[:, side], src, sel[:, sb_ + rp:sb_ + rp + 1], hal[:, side], ALU.mult, ALU.add,
                        ["gX", "sel", ("tmpA", 0)], [("tmpA", 0)])
        cp("dve", xbcpad["S"][:, :, 0:2], hal[:, 0], [("tmpA", 0)], [("xbcpad", "S")])
        cp("dve", xbcpad["S"][:, :, 514:516], hal[:, 1], [("tmpA", 0)], [("xbcpad", "S")])

    def conv(t):
        segs = [(0, 0, 256), (260, 256, 256)] if t == "P" else [(0, 0, 512)]
        for j in range(8):
            pz = psf()
            for (pb_, oc, n) in segs:
                mm_group(psF[pz][:, oc:oc + n],
                         [(diagW[:, w * 8 + j, :], xbcpad[t][:, j, pb_ + w:pb_ + w + n]) for w in range(5)],
                         [("diagW", 0), ("xbcpad", t)], [("psF", pz)])
            act(xconvT[:, j, :], psF[pz][:, :], AF.Silu, [("psF", pz), "cols"], [("xconvT", j)],
                bias=cols[:, B_CONVB + j:B_CONVB + j + 1])
            psf_free(pz)
        for tb in range(4):
            pb_ = psb()
            for j in range(6):
                tr(psB[pb_][:, j * 128:(j + 1) * 128], xconvT[:, j, tb * 128:(tb + 1) * 128], ident_b,
                   ["consts_b", ("xconvT", j)], [("psB", pb_)], signal=(j == 5))
            alt_cp(xtok[:, tb, :], psB[pb_][:, 0:768], [("psB", pb_)], [("xtok", tb)])
            psb_free(pb_)

    def ssd_small(t):
        dk = ("dtraw", t)
        act(dtw[:], dtraw[t][:], AF.Exp, [dk], ["dtw"])
        act(dtw[:], dtw[:], AF.Ln, ["dtw"], ["dtw"], bias=1.0)
        tt("dve", a_[:], dtw[:], A_bc[:].unsqueeze(1).to_broadcast([128, 4, 16]), ALU.mult, ["dtw", "A_bc"], ["a_"])
        act(ac2[:], dtw[:], AF.Ln, ["dtw"], ["ac2"])
        for tb in range(4):
            pz = psf()
            mm(psF[pz][:, 0:8], U_f, a_[:, tb, 0:8], True, True, ["consts", "a_"], [("psF", pz)], False)
            mm(psF[pz][:, 8:16], L_f, a_[:, tb, 8:16], True, True, ["consts", "a_"], [("psF", pz)], False)
            mm(psF[pz][:, 16:32], ones_f, a_[:, tb, :], True, True, ["consts", "a_"], [("psF", pz)], True)
            cp("act", smx[:, tb, :], psF[pz][:, 0:32], [("psF", pz)], ["smx"])
            psf_free(pz)
        act(E_[:], smx[:, :, 0:16], AF.Exp, ["smx"], ["E_"])
        act(CD_[:], smx[:, :, 16:32], AF.Exp, ["smx"], ["CD_"])
        tt("dve", ac2[:], smx[:, :, 0:16], ac2[:], ALU.subtract, ["smx", "ac2"], ["ac2"])
        wt = tmpA[0][:, 0:64].rearrange("p (a b) -> p a b", a=4)
        tt("dve", wt, smx[:, :, 16:32], smx[:, :, 0:16], ALU.subtract, ["smx"], [("tmpA", 0)])
        act(wt, wt, AF.Exp, [("tmpA", 0)], [("tmpA", 0)])
        tt("dve", dtw[:], dtw[:], wt, ALU.mult, ["dtw", ("tmpA", 0)], ["dtw"])

    def prepass(t, use_hin, final_cb, dirs=(0, 1)):
        xws = {0: xw, 1: xd}

        def step(d, tb):
            xw_, xk_ = xws[d], ("xw", d)
            cp("act", hprev[:, d, tb, :], hrun[:, d, :], [("hrun", d)], [("hprev", d, tb)])
            tt("dve", h8(xw_), h8(xtok[:, tb, 0:512]), bc8(dtw[:, tb, d * 8:(d + 1) * 8]), ALU.mult,
               [("xtok", tb), "dtw"], [xk_, "xw", "xd"] if False else [xk_])
            pz = psf()
            for g in range(2):
                mm(psF[pz][:, g * 256:(g + 1) * 256], xtok[:, tb, 512 + g * 128:512 + (g + 1) * 128],
                   xw_[:, g * 256:(g + 1) * 256], True, True, [("xtok", tb), xk_], [("psF", pz)], g == 1)
            tt("dve", h8(hrun[:, d, :]), h8(hrun[:, d, :]), bc8(CD_[:, tb, d * 8:(d + 1) * 8]), ALU.mult,
               [("hrun", d), "CD_"], [("hrun", d)])
            tt("dve", hrun[:, d, :], psF[pz][:, :], hrun[:, d, :], ALU.add, [("psF", pz), ("hrun", d)], [("hrun", d)])
            psf_free(pz)

        for seg in SEGS[t]:
            for d in dirs:
                if use_hin:
                    cp("act", hrun[:, d, :], hin[:, d, :], ["hin"], [("hrun", d)])
                else:
                    P.op("dve", lambda e, d=d: e.memset(hrun[:, d, :], 0.0), (), [("hrun", d)])
            orders = {d: (seg if d == 0 else list(reversed(seg))) for d in dirs}
            for i in range(len(seg)):
                for d in dirs:
                    step(d, orders[d][i])
            for d in dirs:
                final_cb(d, seg)

    def final_P(d, seg):
        s_ = seg[0] // 2
        pz = psf()
        for q in range(4):
            tr(psF[pz][:, q * 128:(q + 1) * 128], hrun[:, d, q * 128:(q + 1) * 128], ident_f,
               ["consts", ("hrun", d)], [("psF", pz)], signal=(q == 3))
        st_ = tmpA[d]
        cp("act", st_[:], psF[pz][:, :], [("psF", pz)], [("tmpA", d)])
        psf_free(pz)
        dma("sp", ohs[d][s_].rearrange("(q p) n -> p q n", p=128), st_[:].rearrange("p (q n) -> p q n", q=4),
            [("tmpA", d)], [("ohs", d)])

    def final_S1(d, seg):
        cp("act", x2buf[:, d * 512:(d + 1) * 512], hrun[:, d, :], [("hrun", d)], ["x2buf"])
        tsum = tmpA[0][:, 0:8]
        tt("dve", tsum, smx[:, 0, 16 + d * 8:24 + d * 8], smx[:, 1, 16 + d * 8:24 + d * 8], ALU.add, ["smx"], [("tmpA", 0)])
        tt("dve", tsum, tsum, smx[:, 2, 16 + d * 8:24 + d * 8], ALU.add, ["smx", ("tmpA", 0)], [("tmpA", 0)])
        tt("dve", tsum, tsum, smx[:, 3, 16 + d * 8:24 + d * 8], ALU.add, ["smx", ("tmpA", 0)], [("tmpA", 0)])
        act(x2buf[:, 1024 + d * 8:1032 + d * 8], tsum, AF.Exp, [("tmpA", 0)], ["x2buf"])

    def x2_exchange():
        dma("sp", x2_in[:, :], x2buf[:, :], ["x2buf"], ["x2_in"])
        coll(x2_in, x2_out, ["x2_in"], ["x2_out"])

    gSb = [av(19232, [128, 1040], F32), av(19232 + 2080, [128, 1040], F32)]

    def x2_receive():
        nld = [0]

        def load_rank(rp):
            i = nld[0] % 2; nld[0] += 1
            dma("sp", gSb[i][:, :], x2_out[rp * 128:(rp + 1) * 128, :], ["x2_out"], [("gSb", i)])
            return gSb[i], ("gSb", i)
        for d in range(2):
            st_ = tmpA[d]
            dma("sp", st_[:].rearrange("p (q n) -> p q n", q=4), h0_d[d].rearrange("(q p) n -> p q n", p=128), (), [("tmpA", d)])
            pz = psf()
            for q in range(4):
                tr(psF[pz][:, q * 128:(q + 1) * 128], st_[:, q * 128:(q + 1) * 128], ident_f,
                   ["consts", ("tmpA", d)], [("psF", pz)], signal=(q == 3))
            cp("act", hcand[:, d, :], psF[pz][:, :], [("psF", pz)], [("hcand", d)])
            psf_free(pz)
            ranks = [0, 1, 2] if d == 0 else [3, 2, 1]
            first = 0 if d == 0 else 3
            ts("dve", hin[:, d, :], hcand[:, d, :], sel[:, first:first + 1], None, ALU.mult, None,
               [("hcand", d), "sel"], ["hin"])
            for rp in ranks:
                nxt = rp + 1 if d == 0 else rp - 1
                g_, gk = load_rank(rp)
                tt("dve", h8(hcand[:, d, :]), h8(hcand[:, d, :]), bc8(g_[:, 1024 + d * 8:1032 + d * 8]), ALU.mult,
                   [("hcand", d), gk], [("hcand", d)])
                tt("dve", hcand[:, d, :], hcand[:, d, :], g_[:, d * 512:(d + 1) * 512], ALU.add,
                   [("hcand", d), gk], [("hcand", d)])
                stt("dve", hin[:, d, :], hcand[:, d, :], sel[:, nxt:nxt + 1], hin[:, d, :], ALU.mult, ALU.add,
                    [("hcand", d), "sel", "hin"], ["hin"])

    def ssd_main(t, tb):
        tbc = slice(tb * 128, (tb + 1) * 128)
        pa = psf()
        mm(psF[pa][0:8, 0:128], a_[:, tb, 0:8], U_f, True, True, ["consts", "a_"], [("psF", pa)], False)
        mm(psF[pa][0:8, 128:256], a_[:, tb, 8:16], L_f, True, True, ["consts", "a_"], [("psF", pa)], True)
        cp("act", acumT, psF[pa][0:8, 0:256], [("psF", pa)], ["acumT"])
        psf_free(pa)
        pc = psf()
        for g in range(2):
            mm(psF[pc][:, g * 128:(g + 1) * 128], xconvT[:, 4 + g, tbc], xconvT[:, 6 + g, tbc], True, True,
               [("xconvT", 4 + g), ("xconvT", 6 + g)], [("psF", pc)], g == 1)
        for g in range(2):
            for d in range(2):
                tt("dve", CBm[:, g * 2 + d, :], psF[pc][:, g * 128:(g + 1) * 128], U_f if d == 0 else L_f, ALU.mult,
                   [("psF", pc), "consts"], ["CBm"])
        psf_free(pc)
        for d in range(2):
            prs = []
            for g in range(2):
                pr = psf(); prs.append(pr)
                for hh in range(4):
                    h = g * 4 + hh
                    mm(psF[pr][:, hh * 128:(hh + 1) * 128], selc[0:8, h * 128:(h + 1) * 128], acumT[0:8, d * 128:(d + 1) * 128],
                       True, True, ["selc", "acumT"], [("psF", pr)], hh == 3)
            for g in range(2):
                pr = prs[g]
                for hh in range(4):
                    ci = d * 8 + g * 4 + hh
                    ts("dve", tmpA[g][:, hh * 128:(hh + 1) * 128], psF[pr][:, hh * 128:(hh + 1) * 128],
                       ac2[:, tb, ci:ci + 1], 30.0, ALU.subtract, ALU.min, [("psF", pr), "ac2"], [("tmpA", g)])
                psf_free(pr)
            for g in range(2):
                act(tmpA[g][:], tmpA[g][:], AF.Exp, [("tmpA", g)], [("tmpA", g)])
            for g in range(2):
                D3 = tmpA[g][:].rearrange("p (h n) -> p h n", h=4)
                stt("dve", MT[:, d * 8 + g * 4:d * 8 + g * 4 + 4, :], D3, 1e30,
                    CBm[:, g * 2 + d, :].unsqueeze(1).to_broadcast([128, 4, 128]), ALU.min, ALU.mult,
                    [("tmpA", g), "CBm"], [("MT", d, g)])
        tt("dve", xd, xtok[:, tb, 0:512], dsk_bc[:], ALU.mult, [("xtok", tb), "dsk_bc"], ["xd"])
        py = psf()
        mtk = [("MT", d, g) for d in range(2) for g in range(2)]
        for h in range(8):
            hc = slice(h * 64, (h + 1) * 64)
            mm(psF[py][:, hc], ident_b, xd[:, hc], True, False,
               (["consts_b", "xd"] + mtk + [("xtok", tb)]) if h == 0 else (), [("psF", py)], False)
            for d in range(2):
                mm(psF[py][:, hc], MT[:, d * 8 + h, :], xtok[:, tb, hc], False, d == 1,
                   (), [("psF", py)], (h == 7 and d == 1))
        pof = [psf(), psf()]
        for d in range(2):
            for g in range(2):
                mm(psF[pof[d]][:, g * 256:(g + 1) * 256], xconvT[:, 6 + g, tbc], hprev[:, d, tb, g * 256:(g + 1) * 256],
                   True, True, [("xconvT", 6 + g), ("hprev", d, tb)], [("psF", pof[d])], g == 1)
        cp("act", ysb, psF[py][:, :], [("psF", py)], ["ysb"])
        psf_free(py)
        for d in range(2):
            Dt = tmpA[d]; dk = ("tmpA", d)
            tt("dve", h8(Dt[:]), h8(psF[pof[d]][:, :]), bc8(E_[:, tb, d * 8:(d + 1) * 8]), ALU.mult,
               [("psF", pof[d]), "E_"], [dk])
            tt("dve", ysb, ysb, Dt[:], ALU.add, ["ysb", dk], ["ysb"])
            psf_free(pof[d])
        tt("dve", yg, ysb, zs[t][:, tb, :], ALU.mult, ["ysb", ("zs", t)], ["yg"])
        P.op("dve", lambda e: e.memset(stat2[:, 0:2], 0.0), (), ["stat2"])
        for g in range(2):
            act(tmpA[g][:, 0:256], yg[:, g * 256:(g + 1) * 256], AF.Square, ["yg"], [("tmpA", g), "stat2"],
                accum=stat2[:, g:g + 1])
        rsqrt_to(stat2[:, 0:2], stat2[:, 0:2], EPS, ["stat2"], ["stat2"], scale=1.0 / 256.0)
        for g in range(2):
            stt("dve", xw[:, g * 256:(g + 1) * 256], yg[:, g * 256:(g + 1) * 256], stat2[:, g:g + 1],
                ssdn_bc[:, g * 256:(g + 1) * 256], ALU.mult, ALU.mult, ["yg", "stat2", "ssdn_bc"], ["xw"])
        pb_ = psb()
        for k in range(4):
            tr(psB[pb_][:, k * 128:(k + 1) * 128], xw[:, k * 128:(k + 1) * 128], ident_b, ["consts_b", "xw"],
               [("psB", pb_)], signal=(k == 3))
        alt_cp(ssdT[:, :, tbc], psB[pb_][:, 0:512].rearrange("p (k n) -> p k n", k=4), [("psB", pb_)], ["ssdT"])
        psb_free(pb_)

    def attn_V(t):
        nkb = 18 if t == "S" else 4
        for kb in range(nkb):
            pz = psf()
            mm_group(psF[pz][:, :].rearrange("p (h c) -> p h c", h=8),
                     [(Ksrc[:, kc, kb * 128:(kb + 1) * 128], wukv[:, kc, :, 64:128]) for kc in range(2)],
                     ["Ksrc", "wukv"], [("psF", pz)])
            alt_cp(Vt[t][:, kb, :], psF[pz][:, :], [("psF", pz)], [("V", kb)])
            psf_free(pz)

    def attn_head(t, h):
        nkb = 18 if t == "S" else 4
        NK = nkb * 128
        if True:
            for kt in range((NK + 511) // 512):
                n = min(512, NK - kt * 512)
                kc_ = slice(kt * 512, kt * 512 + n)
                pz = psf()
                mm(psF[pz][0:96, 0:n], padI[0:32, 0:96], kpeT[0:32, kc_], True, False, ["padI", "kpeT"], [("psF", pz)], False)
                for kc in range(2):
                    mm(psF[pz][0:96, 0:n], wuk96[:, kc, h, :], Ksrc[:, kc, kc_], False, kc == 1,
                       ["wuk96", "Ksrc"], [("psF", pz)], kc == 1)
                alt_cp(Kh[:, kc_], psF[pz][0:96, 0:n], [("psF", pz)], ["Kh"])
                psf_free(pz)
            pq = psf()
            mm_group(psF[pq][0:96, :], [(wuq[:, kc, h, :], cqn[t][:, kc, :]) for kc in range(2)], ["wuq", ("cqn", t)], [("psF", pq)])
            if t == "P":
                alt_cp(Qh, psF[pq][0:96, :], [("psF", pq)], ["Qh"])
            else:
                pq2 = psf()
                mm_group(psF[pq2][0:96, :], [(wuq_sw[:, kc, h, :], cqn[t][:, kc, :]) for kc in range(2)],
                         ["wuq_sw", ("cqn", t)], [("psF", pq2)])
                cp("act", Qh[0:64, :], psF[pq][0:64, :], [("psF", pq)], ["Qh"])
                tt("dve", tmpA[0][64:96, :], psF[pq][64:96, :], rope_hi[64:96, 0, :], ALU.mult, [("psF", pq), "rope_hi"], [("tmpA", 0)])
                tt("dve", tmpA[1][64:96, :], psF[pq2][64:96, :], rope_hi[64:96, 1, :], ALU.mult, [("psF", pq2), "rope_hi"], [("tmpA", 1)])
                tt("dve", Qh[64:96, :], tmpA[0][64:96, :], tmpA[1][64:96, :], ALU.add, [("tmpA", 0), ("tmpA", 1)], ["Qh"])
                psf_free(pq2)
            psf_free(pq)
            po = psf(); pl = psf()
            if t == "P":
                plan = [(slice(s_ * 256, (s_ + 1) * 256), [2 * s_, 2 * s_ + 1]) for s_ in range(2)]
            else:
                plan = [(slice(0, 512), list(range(18)))]
            steps = []
            for (qc, kbs) in plan:
                for i, kb in enumerate(kbs):
                    steps.append((qc, kb, i == 0, i == len(kbs) - 1))
            pend = None
            for it, (qc, kb, first, lastk) in enumerate(steps):
                nq = qc.stop - qc.start
                psc = psf()
                mm(psF[psc][:, 0:nq], Kh[:, kb * 128:(kb + 1) * 128], Qh[:, qc], True, True, ["Kh", "Qh"], [("psF", psc)], True)
                if pend is not None:
                    pend()
                pt = PT[it % 2]; pk = ("PT", it % 2)
                act(pt[:, 0:nq], psF[psc][:, 0:nq], AF.Exp, [("psF", psc)], [pk], scale=SCALE)
                psf_free(psc)

                def pend(qc=qc, kb=kb, first=first, lastk=lastk, pt=pt, pk=pk, nq=nq):
                    mm(psF[po][0:64, qc], Vt[t][:, kb, h * 64:(h + 1) * 64], pt[:, 0:nq], first, lastk,
                       [("V", kb), pk], [("psF", po)], True)
                    mm(psF[pl][0:64, qc], ones_b[:, 0:64], pt[:, 0:nq], first, lastk, ["consts_b", pk], [("psF", pl)], True)
            pend()
            rl = tmpA[0][0:64, :]
            act(rl, psF[pl][0:64, :], AF.Ln, [("psF", pl)], [("tmpA", 0)])
            act(rl, rl, AF.Exp, [("tmpA", 0)], [("tmpA", 0)], scale=-1.0)
            tt("dve", OT[:, h, :], psF[po][0:64, :], rl, ALU.mult, [("psF", po), ("tmpA", 0)], [("OT", h)])
            psf_free(po); psf_free(pl)

    def outproj(t, l):
        for half in range(2):
            s1 = ring_load([(0, [64, 8, 512], w_out[0, 0:512, half * 512:(half + 1) * 512].rearrange("(h p) n -> p h n", p=64))])
            s2 = ring_load([(0, [128, 4, 512], w_out[0, 512:1024, half * 512:(half + 1) * 512].rearrange("(k p) n -> p k n", p=128))])
            wa = rview(s1, 0, [64, 8, 512]); ws_ = rview(s2, 0, [128, 4, 512])
            for d4 in range(4):
                dc = half * 4 + d4
                pz = psf()
                pairs = [(wa[:, h, d4 * 128:(d4 + 1) * 128], OT[:, h, :]) for h in range(8)]
                pairs += [(ws_[:, k, d4 * 128:(d4 + 1) * 128], ssdT[:, k, :]) for k in range(4)]
                mm_group(psF[pz][:, :], pairs, [("ring", s1), ("ring", s2), "ssdT"] + [("OT", h) for h in range(8)], [("psF", pz)])
                evac_mix(pz, dc)
                psf_free(pz)
        post_norm_residual(t, l, 2)

    def rest(t, l):
        conv(t)
        ssd_small(t)
        if t == "P":
            prepass(t, False, final_P)
            attn_V(t)
            for h in range(8):
                attn_head(t, h)
            for tb in range(4):
                ssd_main(t, tb)
        else:
            prepass(t, False, final_S1)
            x2_exchange()
            P.barrier()
            nop_ = lambda d, seg: None
            sched = {1: lambda: x2_receive(),
                     2: lambda: prepass(t, True, nop_, dirs=(0,)),
                     3: lambda: prepass(t, True, nop_, dirs=(1,)),
                     4: lambda: ssd_main(t, 0), 5: lambda: ssd_main(t, 1),
                     6: lambda: ssd_main(t, 2), 7: lambda: ssd_main(t, 3)}
            attn_V(t)
            for h in range(8):
                attn_head(t, h)
                if h in sched:
                    sched[h]()
        outproj(t, l)
        P.barrier()

    def layer0_mixer(l):
        P.op("dve", lambda e: e.memset(xbcpad["P"][:], 0.0), (), [("xbcpad", "P")])
        P.op("dve", lambda e: e.memset(wuk96[:], 0.0), (), ["wuk96"])
        cp("dve", wuk96[:, :, :, 0:64], wukv[:, :, :, 0:64], ["wukv", "wuk96"], ["wuk96"])
        for i in range(40):
            ts("dve", diagW[:, i, :], ident_f, cols[:, B_CONVW + i:B_CONVW + i + 1], None, ALU.mult, None,
               ["consts", "cols"], [("diagW", 0)])
        if RUN_S:
            P.op("dve", lambda e: e.memset(x1buf[32:64, 1024:1536], 0.0), (), ["x1buf"])
            P.op("dve", lambda e: e.memset(x1buf[64:128, 1024:1536], 0.0), (), ["x1buf"])
            inproj("S", l)
            x1_exchange()
            P.barrier()
        inproj("P", l)
        rest("P", l)
        if RUN_S:
            x1_receive()
            P.barrier()
            rest("S", l)

    RUN_S = not os.environ.get("KNO_S")

    def layer1_mixer(l):
        WP = {"P": 544, "S": 528}
        hpad = {"P": av(0, [128, 8, 544]), "S": av(8704, [128, 8, 528])}
        invc = av(21504, [128, 4, 512], F32)
        pooledT = av(25600, [128, 8, 512])
        x3buf = av(29696, [128, 8, 16], F32)
        gE = av(29952, [128, 4, 128], F32)

        def data(ap3, t):
            if t == "S":
                return ap3[:, :, 8:520]
            return ap3.rearrange("p k (s c) -> p k s c", s=2)[:, :, :, 8:264]

        def data2(ap2, t):
            if t == "S":
                return ap2[:, 8:520]
            return ap2.rearrange("p (s c) -> p s c", s=2)[:, :, 8:264]

        def seg2(ap2, t):
            if t == "S":
                return ap2
            return ap2.rearrange("p (s c) -> p s c", s=2)

        def fill(t):
            if t == "P":
                hp4 = hpad[t][:].rearrange("p k (s c) -> p k s c", s=2)
                P.op("dve", lambda e: e.memset(hp4[:, :, :, 0:8], 0.0), (), [("hpad", t)])
                P.op("dve", lambda e: e.memset(hp4[:, :, :, 264:272], 0.0), (), [("hpad", t)])
            modulate(t, l, 0, 1, outf=lambda k, tm, t=t: (data2(hpad[t][:, k, :], t), seg2(tm[:], t), ("hpad", t)))

        def pool_tile(t, ti):
            W = WP[t]
            dma("sp", invc[:], invcnt_d[ti:ti + 1].to_broadcast([128, 4, NT]), (), ["invc"])
            hk = ("hpad", t)
            segs = [(0, 0, 256), (272, 256, 256)] if t == "P" else [(0, 0, 512)]
            for gi, w in enumerate((2, 4, 8, 16)):
                for kk in range(2):
                    kch = 2 * gi + kk
                    pz = psf()
                    for (sb0, oc, n) in segs:
                        mm_group(psF[pz][:, oc:oc + n],
                                 [(ident_b, hpad[t][:, kch, sb0 + 8 + off:sb0 + 8 + off + n]) for off in range(-(w // 2), w // 2)],
                                 ["consts_b", hk], [("psF", pz)])
                    tm = tmpA[kk]; tk = ("tmpA", kk)
                    tt("dve", tm[:], psF[pz][:, :], invc[:, gi, :], ALU.mult, [("psF", pz), "invc"], [tk])
                    psf_free(pz)
                    tt("dve", seg2(pooledT[:, kch, :], t), seg2(tm[:], t), data2(hpad[t][:, kch, :], t), ALU.subtract,
                       [tk, hk], [("pooledT", kch)])
            s = ring_load([(0, [128, 8, 256], pool_w[0].rearrange("g (k p) n -> p (g k) n", p=128))])
            pw = rview(s, 0, [128, 8, 256])
            for dc in range(8):
                gi, co = dc // 2, dc % 2
                pz = psf()
                mm_group(psF[pz][:, :], [(pw[:, gi * 2 + kc, co * 128:(co + 1) * 128], pooledT[:, 2 * gi + kc, :]) for kc in range(2)],
                         [("ring", s), ("pooledT", 2 * gi), ("pooledT", 2 * gi + 1)], [("psF", pz)])
                act(mixT[:, dc, :], psF[pz][:, :], AF.Copy, [("psF", pz), "cols"], ["mixT"],
                    scale=cols[:, B_PSC + dc:B_PSC + dc + 1])
                act(sqT[:, dc, :], mixT[:, dc, :], AF.Square, ["mixT"], [("sqT", dc)])
                psf_free(pz)
            post_norm_residual(t, l, 2)

        if RUN_S:
            fill("S")
            cp("dve", x3buf[:, :, 0:8], hpad["S"][:, :, 8:16], [("hpad", "S")], ["x3buf"])
            cp("dve", x3buf[:, :, 8:16], hpad["S"][:, :, 512:520], [("hpad", "S")], ["x3buf"])
            dma("sp", x3_in[:, :], x3buf[:].rearrange("p k c -> p (k c)"), ["x3buf"], ["x3_in"])
            coll(x3_in, x3_out, ["x3_in"], ["x3_out"])
        fill("P")
        pool_tile("P", 0)
        if RUN_S:
            dma("sp", gE[:], x3_out.rearrange("(r p) c -> p r c", p=128), ["x3_out"], ["gE"])
            for (dst0, c0, sb_) in ((0, 8, 4), (520, 0, 8)):
                dst = hpad["S"][:, :, dst0:dst0 + 8]
                for rp in range(4):
                    src = gE[:, rp, :].rearrange("p (k c) -> p k c", k=8)[:, :, c0:c0 + 8]
                    if rp == 0:
                        ts("dve", dst, src, sel[:, sb_ + rp:sb_ + rp + 1], None, ALU.mult, None, ["gE", "sel"], [("hpad", "S")])
                    else:
                        stt("dve", dst, src, sel[:, sb_ + rp:sb_ + rp + 1], dst, ALU.mult, ALU.add,
                            ["gE", "sel", ("hpad", "S")], [("hpad", "S")])
            pool_tile("S", 1)
    for l in range(n_layers):
        if l % 2 == 0:
            layer0_mixer(l)
        else:
            layer1_mixer(l)
        P.barrier()
        if not os.environ.get('KSKIP_FFN'):
            ffn(l)
        P.barrier()

    P.barrier()
    for t in ("P", "S"):
        for tb in range(4):
            xl = xld[rr["xld"] % 2]; xk = ("xld", rr["xld"] % 2); rr["xld"] += 1
            for half in range(2):
                pz = psf()
                for q in range(4):
                    k = half * 4 + q
                    tr(psF[pz][:, q * 128:(q + 1) * 128], xT[t][:, k, tb * 128:(tb + 1) * 128], ident_f,
                       ["consts", ("xT", t)], [("psF", pz)], signal=(q == 3))
                alt_cp(xl[:, half * 512:(half + 1) * 512], psF[pz][:, :], [("psF", pz)], [xk])
                psf_free(pz)
            dma("sp", yout[t][tb * 128:(tb + 1) * 128, :], xl[:], [xk], [("yout", t)])
    P.final_wait()
    P.emit()
    es.close()
    return nc


def _consts():
    c = np.zeros((128, 512), np.float32)
    c[:, 0:128] = np.eye(128)
    t = np.arange(128)
    c[:, 128:256] = (t[:, None] <= t[None, :])
    c[:, 256:384] = (t[:, None] >= t[None, :])
    c[:, 384:512] = 1.0
    selc = np.zeros((8, 8, 128), np.float32)
    for h in range(8):
        selc[h, h, :] = 1.0
    padI = np.zeros((32, 96), np.float32)
    padI[np.arange(32), 64 + np.arange(32)] = 1.0
    return c, selc.reshape(8, 1024), padI


def _rope(pos):
    half = 16
    inv_freq = np.power(10000.0, -np.arange(0, half, 2, dtype=np.float64) / half)
    row = (pos // 64).astype(np.float64); col = (pos % 64).astype(np.float64)
    ang = np.concatenate([row[:, None] * inv_freq, col[:, None] * inv_freq], axis=-1)
    cos = np.cos(ang).T; sin = np.sin(ang).T
    r = np.zeros((32, 2, len(pos)), np.float32)
    r[0:16, 0] = cos; r[16:32, 0] = cos; r[0:16, 1] = sin; r[16:32, 1] = sin
    return r


def _invcnt(seg_len, nseg, lo_pad, hi_pad):
    out = np.zeros((4, NT), np.float32)
    for gi, w in enumerate((2, 4, 8, 16)):
        for s in range(nseg):
            L = seg_len
            t = np.arange(L)
            lo = t - w // 2; hi = t + w // 2
            if lo_pad: lo = np.clip(lo, 0, None)
            if hi_pad: hi = np.clip(hi, None, L)
            out[gi, s * L:(s + 1) * L] = 1.0 / (hi - lo)
    return out


_NC_CACHE = {}


def kernel(**inp):
    inp = {k: np.ascontiguousarray(np.asarray(v)) for k, v in inp.items()}
    if "nc" not in _NC_CACHE:
        _NC_CACHE["nc"] = build()
    nc = _NC_CACHE["nc"]
    c, selc, padI = _consts()
    shared = {k: inp[k] for k in (
          "w_in_ab",
         "w_uq",  "w_ukv",
         "w_out_ab", "pool_w",
        "ffn_w_gate", "ffn_w_up", "ffn_w_down")}
    shared.update(consts=c, selc=selc, padI=padI)
    bc_all = np.concatenate([inp["kv_norm"].reshape(-1), inp["ssd_norm"].reshape(-1), inp["ssd_dt_bias_fwd"].reshape(-1),
                             inp["ssd_dt_bias_bwd"].reshape(-1), inp["ssd_a_log_fwd"].reshape(-1),
                             inp["ssd_a_log_bwd"].reshape(-1), inp["ssd_d"].reshape(-1)]).reshape(1, 808)
    shared["bc_all"] = bc_all
    stg_common = [inp["norm_pre_mix"].reshape(16, 128), inp["norm_post_mix"].reshape(16, 128),
                  inp["norm_pre_ffn"].reshape(16, 128), inp["norm_post_ffn"].reshape(16, 128),
                  inp["b_mod"].reshape(96, 128)]
    stg_tail = [inp["q_norm"].reshape(2, 128), inp["kv_norm"].reshape(2, 128), inp["ssd_conv_b"].reshape(8, 128),
                inp["ssd_conv_w"][0].reshape(40, 128), inp["ssd_norm"].reshape(4, 128), inp["pool_scale"].reshape(8, 128),
                np.zeros((16, 128), np.float32)]
    in_maps = []
    for core in range(8):
        b, r = core // 4, core % 4
        m = dict(shared)
        m["xp"] = inp["x_prompt"][2 * core:2 * core + 2].reshape(NT, D)
        m["xs"] = inp["x_sample"][b, r * NT:(r + 1) * NT]
        m["stg_all"] = np.concatenate(stg_common + [np.stack([inp["c_ctx"], inp["c"][b]]).reshape(16, 128)] + stg_tail, axis=0)
        m["w_mod_sl"] = inp["w_mod"][:, :, r * 1536:(r + 1) * 1536]
        m["cache_ckv"] = inp["cache_mla_ckv"][b, 0]
        m["cache_kpe"] = inp["cache_mla_krope"][b, 0]
        m["h0f"] = inp["state_ssd_fwd"][b, 0].reshape(512, 128)
        m["h0b"] = inp["state_ssd_bwd"][b, 0].reshape(512, 128)
        m["rope"] = _rope(np.arange(r * NT, (r + 1) * NT))
        sel = np.zeros((128, 16), np.float32)
        sel[:, r] = 1.0
        if r > 0: sel[:, 4 + r - 1] = 1.0
        if r < 3: sel[:, 8 + r + 1] = 1.0
        m["sel"] = sel
        ic = np.zeros((2, 4, NT), np.float32)
        ic[0] = _invcnt(256, 2, True, True)
        ic[1] = _invcnt(NT, 1, r == 0, r == 3)
        m["invcnt"] = ic
        in_maps.append({k: np.ascontiguousarray(v, dtype=np.float32) for k, v in m.items()})
    ncores = int(os.environ.get("KCORES", "8"))
    res = run_bass_kernel_spmd(nc, in_maps[:ncores], core_ids=list(range(ncores)))
    R = list(res.results)
    while len(R) < 8:
        R.append(R[0])
    yp = np.concatenate([R[c_]["yp"].reshape(2, 256, D) for c_ in range(8)], axis=0)
    ys = np.stack([np.concatenate([R[b * 4 + r]["ys"] for r in range(4)], axis=0) for b in range(2)])
    ockv = np.concatenate([R[c_]["ockv"].reshape(2, 1, 256, 256) for c_ in range(8)], axis=0)
    okpe = np.concatenate([R[c_]["okpe"].reshape(2, 1, 256, 32) for c_ in range(8)], axis=0)
    ohf = np.concatenate([R[c_]["ohf"].reshape(2, 1, 8, 64, 128) for c_ in range(8)], axis=0)
    ohb = np.concatenate([R[c_]["ohb"].reshape(2, 1, 8, 64, 128) for c_ in range(8)], axis=0)
    return (yp.astype(np.float32), ys.astype(np.float32), ockv.astype(np.float32), okpe.astype(np.float32),
            ohf.astype(np.float32), ohb.astype(np.float32))
```
